# Optimizing a Trainium2 kernel written in Bass

```python
import math
import jax
import jax.numpy as jnp
from jax import lax
import numpy as np

D_MODEL = 1024
BATCH = 8
SEQ = 4096
DEPTH = 2

CTX_LEN = 256
GRID_W = 64
ROPE_THETA = 10000.0
NORM_EPS = 1e-6
Q_BLOCK = 128
CONV_K = 5
N_BRANCH = 4
D_FF = 4 * D_MODEL

GDN_HEADS = 4
GDN_DK = 128
GDN_DV = 128
GDN_CHUNK = 64
MLA_HEADS = 8
MLA_NOPE = 64
MLA_ROPE = 32
MLA_V = 64
MLA_QK = MLA_NOPE + MLA_ROPE
MLA_Q_LORA = 384
MLA_KV_LORA = 256
DIFF_HEADS = 4
DIFF_HD = 64
LAMBDA_BASE = 0.8
LAMBDA_AMP = 0.6
LAMBDA_RATE = 0.3
SSM_HEADS = 8
SSM_HEAD_DIM = 64
SSM_GROUPS = 2
SSM_STATE = 128
SSM_CHUNK = 64

GDN_QK = GDN_HEADS * GDN_DK
GDN_VW = GDN_HEADS * GDN_DV
DIFF_QK = DIFF_HEADS * 2 * DIFF_HD
DIFF_VW = DIFF_HEADS * 2 * DIFF_HD
SSM_INNER = SSM_HEADS * SSM_HEAD_DIM
SSM_BC = SSM_GROUPS * SSM_STATE
BRANCH_W = 512

GDN_COLS = 2 * GDN_QK + 2 * GDN_VW + 4 * GDN_HEADS
MLA_COLS = MLA_Q_LORA + MLA_KV_LORA + MLA_ROPE
DIFF_COLS = 2 * DIFF_QK + DIFF_VW
SSM_COLS = 2 * SSM_INNER + 2 * SSM_BC + 2 * SSM_HEADS
MIX_COLS = GDN_COLS + MLA_COLS + DIFF_COLS + SSM_COLS
GATE_COLS = N_BRANCH * D_MODEL
IN_COLS = MIX_COLS + GATE_COLS

kernel_name = 'hybrid_gdn_mla_diff_ssd_prefix_block'


def _split(u, sizes):
    idx = np.cumsum(sizes)[:-1].tolist()
    return jnp.split(u, idx, axis=-1)


def rms_norm(x, g):
    xf = x.astype(jnp.float32)
    y = xf * lax.rsqrt(jnp.mean(xf * xf, axis=-1, keepdims=True) + NORM_EPS)
    return (y * g.astype(jnp.float32)).astype(x.dtype)


def l2_normalize(t):
    return t * lax.rsqrt(jnp.sum(t * t, axis=-1, keepdims=True) + 1e-6)


def modulate(h, shift, scale):
    return h * (1 + scale) + shift


def dw_conv(u, w):
    k_w, ch = w.shape
    return lax.conv_general_dilated(u, w[:, None, :].astype(u.dtype), window_strides=(1,),
                                    padding=[(k_w // 2, k_w // 2)],
                                    dimension_numbers=('NWC', 'WIO', 'NWC'),
                                    feature_group_count=ch)


def axial_rope(n_tok, rot_dim):
    rows = n_tok // GRID_W
    row = jnp.repeat(jnp.arange(rows, dtype=jnp.float32), GRID_W)
    col = jnp.tile(jnp.arange(GRID_W, dtype=jnp.float32), rows)
    quarter = rot_dim // 4
    inv = ROPE_THETA ** (-jnp.arange(quarter, dtype=jnp.float32) / quarter)
    ar = row[:, None] * inv
    ac = col[:, None] * inv
    ang = jnp.concatenate([ar, ar, ac, ac], axis=-1)
    return jnp.cos(ang), jnp.sin(ang)


def apply_rope(x, cos, sin):
    half = x.shape[-1] // 2
    qd = half // 2
    rh = lambda t: jnp.concatenate([-t[..., qd:], t[..., :qd]], axis=-1)
    rot = jnp.concatenate([rh(x[..., :half]), rh(x[..., half:])], axis=-1)
    return x * cos[None, :, None, :].astype(x.dtype) + rot * sin[None, :, None, :].astype(x.dtype)


def block_attend(q, k, v):
    bn, h, sq, d = q.shape
    nb = sq // Q_BLOCK
    qb = jnp.moveaxis(q.reshape(bn, h, nb, Q_BLOCK, d), 2, 0)
    scale = d ** -0.5

    def one_block(qi):
        s = jnp.einsum('bhqd,bhkd->bhqk', qi, k, preferred_element_type=jnp.float32) * scale
        p = jax.nn.softmax(s, axis=-1).astype(v.dtype)
        return jnp.einsum('bhqk,bhkd->bhqd', p, v)

    o = lax.map(one_block, qb)
    return jnp.moveaxis(o, 0, 2).reshape(bn, h, sq, v.shape[-1])


def diff_attend(q, k, v, lam):
    bn, h, m, sq, d = q.shape
    nb = sq // Q_BLOCK
    qb = jnp.moveaxis(q.reshape(bn, h, m, nb, Q_BLOCK, d), 3, 0)
    scale = d ** -0.5

    def one_block(qi):
        s = jnp.einsum('bhmqd,bhmkd->bhmqk', qi, k, preferred_element_type=jnp.float32) * scale
        p = jax.nn.softmax(s, axis=-1)
        w = (p[:, :, 0] - lam * p[:, :, 1]).astype(v.dtype)
        return jnp.einsum('bhqk,bhkd->bhqd', w, v)

    o = lax.map(one_block, qb)
    return jnp.moveaxis(o, 0, 2).reshape(bn, h, sq, v.shape[-1])


def gdn_chunk_scan(q, k, v, g, beta, s0):
    bn, h, s, dk = q.shape
    dv = v.shape[-1]
    c = GDN_CHUNK
    n = s // c
    q = q.reshape(bn, h, n, c, dk)
    k = k.reshape(bn, h, n, c, dk)
    v = v.reshape(bn, h, n, c, dv)
    g = jnp.cumsum(g.reshape(bn, h, n, c), axis=-1)
    beta = beta.reshape(bn, h, n, c)
    incl = jnp.tril(jnp.ones((c, c), bool))
    strict = jnp.tril(jnp.ones((c, c), bool), -1)
    diff = g[..., :, None] - g[..., None, :]
    decay = jnp.where(incl, jnp.exp(jnp.where(incl, diff, 0.0)), 0.0)
    kb = k * beta[..., None]
    lmat = jnp.where(strict, jnp.einsum('bhncd,bhnjd->bhncj', kb, k) * decay, 0.0)
    eye = jnp.broadcast_to(jnp.eye(c, dtype=lmat.dtype), lmat.shape)
    tmat = lax.linalg.triangular_solve(lmat + eye, eye, left_side=True, lower=True,
                                       unit_diagonal=True)
    eg = jnp.exp(g)
    u = tmat @ (v * beta[..., None])
    w = tmat @ (kb * eg[..., None])
    attn = jnp.einsum('bhncd,bhnjd->bhncj', q, k) * decay
    qg = q * eg[..., None]
    kd = k * jnp.exp(g[..., -1:] - g)[..., None]
    gl = jnp.exp(g[..., -1])

    def step(state, inp):
        u_c, w_c, a_c, qg_c, kd_c, gl_c = inp
        v_new = u_c - jnp.einsum('bhcd,bhde->bhce', w_c, state)
        o = jnp.einsum('bhcd,bhde->bhce', qg_c, state) + jnp.einsum('bhcj,bhje->bhce', a_c, v_new)
        state = state * gl_c[..., None, None] + jnp.einsum('bhcd,bhce->bhde', kd_c, v_new)
        return state, o

    xs = tuple(jnp.moveaxis(t, 2, 0) for t in (u, w, attn, qg, kd, gl))
    s_fin, o = lax.scan(step, s0, xs)
    return jnp.moveaxis(o, 0, 2).reshape(bn, h, s, dv), s_fin


def ssd_chunk_scan(xs, dt, a, bm, cm, s0):
    bn, s, h, p = xs.shape
    g_n, n_st = bm.shape[2], bm.shape[3]
    r = h // g_n
    c = SSM_CHUNK
    n = s // c
    xdt = (xs * dt[..., None]).reshape(bn, n, c, g_n, r, p)
    a_cum = jnp.moveaxis(jnp.cumsum((dt * a).reshape(bn, n, c, g_n, r), axis=2), 2, -1)
    bc = bm.reshape(bn, n, c, g_n, n_st)
    cc = cm.reshape(bn, n, c, g_n, n_st)
    incl = jnp.tril(jnp.ones((c, c), bool))
    seg = a_cum[..., :, None] - a_cum[..., None, :]
    lmat = jnp.where(incl, jnp.exp(jnp.where(incl, seg, 0.0)), 0.0)
    scores = jnp.einsum('bnlgz,bnsgz->bngls', cc, bc)
    y_diag = jnp.einsum('bngls,bngrls,bnsgrp->bnlgrp', scores, lmat, xdt)
    decay_states = jnp.exp(a_cum[..., -1:] - a_cum)
    chunk_states = jnp.einsum('bnsgz,bngrs,bnsgrp->bngrpz', bc, decay_states, xdt)
    chunk_decay = jnp.exp(a_cum[..., -1])
    in_decay = jnp.exp(a_cum)

    def step(state, inp):
        cs, cd, c_c, idc = inp
        y_off = jnp.einsum('blgz,bgrpz,bgrl->blgrp', c_c, state, idc)
        state = state * cd[..., None, None] + cs
        return state, y_off

    xs_scan = tuple(jnp.moveaxis(t, 1, 0) for t in (chunk_states, chunk_decay, cc, in_decay))
    s_fin, y_off = lax.scan(step, s0, xs_scan)
    y = y_diag + jnp.moveaxis(y_off, 0, 1)
    return y.reshape(bn, s, h, p), s_fin


def gdn_branch(u_lat, u_ctx, conv_w, a_log, dt_bias, norm_g, need_ctx):
    def prep(u):
        bn, s = u.shape[:2]
        q, k, v, z, a, b = _split(u, [GDN_QK, GDN_QK, GDN_VW, GDN_VW, 2 * GDN_HEADS, 2 * GDN_HEADS])
        qkv = jax.nn.silu(dw_conv(jnp.concatenate([q, k, v], axis=-1), conv_w))
        q, k, v = _split(qkv, [GDN_QK, GDN_QK, GDN_VW])
        heads = lambda t, d: t.reshape(bn, s, GDN_HEADS, d).transpose(0, 2, 1, 3).astype(jnp.float32)
        q = l2_normalize(heads(q, GDN_DK)) * (GDN_DK ** -0.5)
        k = l2_normalize(heads(k, GDN_DK))
        v = heads(v, GDN_DV)
        a = a.reshape(bn, s, 2, GDN_HEADS).astype(jnp.float32)
        b = b.reshape(bn, s, 2, GDN_HEADS).astype(jnp.float32)
        g = -jnp.exp(a_log.astype(jnp.float32)) * jax.nn.softplus(a + dt_bias.astype(jnp.float32))
        beta = jax.nn.sigmoid(b)
        return q, k, v, g.transpose(2, 0, 3, 1), beta.transpose(2, 0, 3, 1), z

    def run(pp, d, s_init):
        q, k, v, g, beta = pp[0], pp[1], pp[2], pp[3][d], pp[4][d]
        if d == 1:
            q, k, v = jnp.flip(q, 2), jnp.flip(k, 2), jnp.flip(v, 2)
            g, beta = jnp.flip(g, -1), jnp.flip(beta, -1)
        o, s_fin = gdn_chunk_scan(q, k, v, g, beta, s_init)
        return (jnp.flip(o, 2) if d == 1 else o), s_fin

    def finish(o, z):
        bn, h, s, dv = o.shape
        o = o.transpose(0, 2, 1, 3)
        y = rms_norm(o, norm_g) * jax.nn.silu(z.reshape(bn, s, h, dv).astype(jnp.float32))
        return y.reshape(bn, s, h * dv).astype(z.dtype)

    pc, pl = prep(u_ctx), prep(u_lat)
    s0 = jnp.zeros((u_lat.shape[0], GDN_HEADS, GDN_DK, GDN_DV), jnp.float32)
    oc_f, sc_f = run(pc, 0, s0)
    ol_f, _ = run(pl, 0, sc_f)
    oc_b, sc_b = run(pc, 1, s0)
    ol_b, _ = run(pl, 1, sc_b)
    y_lat = finish(ol_f + ol_b, pl[5])
    y_ctx = finish(oc_f + oc_b, pc[5]) if need_ctx else None
    return y_lat, y_ctx


def mla_branch(u_lat, u_ctx, w_uq, w_ukv, q_lora_g, kv_lora_g, qn_g, kn_g, cos, sin, need_ctx):
    def rope_tail(t):
        return jnp.concatenate([t[..., :MLA_NOPE], apply_rope(t[..., MLA_NOPE:], cos, sin)], axis=-1)

    def prep(u, rope, need_q):
        bn, s = u.shape[:2]
        cq, ckv, kpe = _split(u, [MLA_Q_LORA, MLA_KV_LORA, MLA_ROPE])
        kv = (rms_norm(ckv, kv_lora_g) @ w_ukv).reshape(bn, s, MLA_HEADS, MLA_NOPE + MLA_V)
        k_nope, v = kv[..., :MLA_NOPE], kv[..., MLA_NOPE:]
        k_pe = jnp.broadcast_to(kpe[:, :, None, :], (bn, s, MLA_HEADS, MLA_ROPE))
        k = rms_norm(jnp.concatenate([k_nope, k_pe], axis=-1), kn_g)
        if rope:
            k = rope_tail(k)
        q = None
        if need_q:
            q = rms_norm((rms_norm(cq, q_lora_g) @ w_uq).reshape(bn, s, MLA_HEADS, MLA_QK), qn_g)
            if rope:
                q = rope_tail(q)
            q = q.transpose(0, 2, 1, 3)
        return q, k.transpose(0, 2, 1, 3), v.transpose(0, 2, 1, 3)

    def to_seq(o):
        bn, h, s, dv = o.shape
        return o.transpose(0, 2, 1, 3).reshape(bn, s, h * dv)

    qc, kc, vc = prep(u_ctx, False, need_ctx)
    ql, kl, vl = prep(u_lat, True, True)
    y_lat = to_seq(block_attend(ql, jnp.concatenate([kl, kc], axis=2), jnp.concatenate([vl, vc], axis=2)))
    y_ctx = to_seq(block_attend(qc, kc, vc)) if need_ctx else None
    return y_lat, y_ctx


def diff_branch(u_lat, u_ctx, qn_g, kn_g, lam_p, sub_g, lam_init, cos, sin, need_ctx):
    def prep(u, rope, need_q):
        bn, s = u.shape[:2]
        q, k, v = _split(u, [DIFF_QK, DIFF_QK, DIFF_VW])

        def qk(t, g):
            t = rms_norm(t.reshape(bn, s, 2 * DIFF_HEADS, DIFF_HD), g)
            if rope:
                t = apply_rope(t, cos, sin)
            return t.reshape(bn, s, DIFF_HEADS, 2, DIFF_HD).transpose(0, 2, 3, 1, 4)

        kk = qk(k, kn_g)
        qq = qk(q, qn_g) if need_q else None
        vv = v.reshape(bn, s, DIFF_HEADS, 2 * DIFF_HD).transpose(0, 2, 1, 3)
        return qq, kk, vv

    lp = lam_p.astype(jnp.float32)
    lam = jnp.exp(jnp.sum(lp[0] * lp[1])) - jnp.exp(jnp.sum(lp[2] * lp[3])) + lam_init

    def finish(o):
        bn, h, s, dv = o.shape
        o = rms_norm(o.transpose(0, 2, 1, 3), sub_g) * (1.0 - lam_init)
        return o.reshape(bn, s, h * dv)

    qc, kc, vc = prep(u_ctx, False, need_ctx)
    ql, kl, vl = prep(u_lat, True, True)
    y_lat = finish(diff_attend(ql, jnp.concatenate([kl, kc], axis=3), jnp.concatenate([vl, vc], axis=2), lam))
    y_ctx = finish(diff_attend(qc, kc, vc, lam)) if need_ctx else None
    return y_lat, y_ctx


def ssm_branch(u_lat, u_ctx, conv_w, conv_b, a_log, dt_bias, d_skip, norm_g, need_ctx):
    def prep(u):
        bn, s = u.shape[:2]
        z, xbc, dt = _split(u, [SSM_INNER, SSM_INNER + 2 * SSM_BC, 2 * SSM_HEADS])
        xbc = jax.nn.silu(dw_conv(xbc, conv_w) + conv_b)
        xs, bm, cm = _split(xbc, [SSM_INNER, SSM_BC, SSM_BC])
        xs = xs.reshape(bn, s, SSM_HEADS, SSM_HEAD_DIM).astype(jnp.float32)
        bm = bm.reshape(bn, s, SSM_GROUPS, SSM_STATE).astype(jnp.float32)
        cm = cm.reshape(bn, s, SSM_GROUPS, SSM_STATE).astype(jnp.float32)
        dt = jax.nn.softplus(dt.reshape(bn, s, 2, SSM_HEADS).astype(jnp.float32) + dt_bias.astype(jnp.float32))
        return xs, dt, bm, cm, z

    a = -jnp.exp(a_log.astype(jnp.float32))

    def run(pp, d, s_init):
        xs, dt, bm, cm = pp[0], pp[1][:, :, d], pp[2], pp[3]
        if d == 1:
            xs, dt, bm, cm = (jnp.flip(t, 1) for t in (xs, dt, bm, cm))
        y, s_fin = ssd_chunk_scan(xs, dt, a[d], bm, cm, s_init)
        return (jnp.flip(y, 1) if d == 1 else y), s_fin

    def finish(pp, y):
        xs, z = pp[0], pp[4]
        bn, s = xs.shape[:2]
        y = y + d_skip.astype(jnp.float32)[:, None] * xs
        gsz = SSM_INNER // SSM_GROUPS
        y = y.reshape(bn, s, SSM_GROUPS, gsz) * jax.nn.silu(z.reshape(bn, s, SSM_GROUPS, gsz).astype(jnp.float32))
        return rms_norm(y, norm_g.reshape(SSM_GROUPS, gsz)).reshape(bn, s, SSM_INNER).astype(z.dtype)

    pc, pl = prep(u_ctx), prep(u_lat)
    s0 = jnp.zeros((u_lat.shape[0], SSM_GROUPS, SSM_HEADS // SSM_GROUPS, SSM_HEAD_DIM, SSM_STATE), jnp.float32)
    yc_f, sc_f = run(pc, 0, s0)
    yl_f, _ = run(pl, 0, sc_f)
    yc_b, sc_b = run(pc, 1, s0)
    yl_b, _ = run(pl, 1, sc_b)
    y_lat = finish(pl, yl_f + yl_b)
    y_ctx = finish(pc, yc_f + yc_b) if need_ctx else None
    return y_lat, y_ctx


def merge_branches(ys, gate_logits, w_branch, w_out):
    gates = jnp.split(gate_logits, N_BRANCH, axis=-1)
    m = jax.nn.sigmoid(gates[0]) * (ys[0] @ w_branch[0])
    for i in range(1, N_BRANCH):
        m = m + jax.nn.sigmoid(gates[i]) * (ys[i] @ w_branch[i])
    return m @ w_out


def sq_relu_mlp(h, w1, w2):
    return jnp.square(jax.nn.relu(h @ w1)) @ w2


def setup_inputs(seed: int = 0) -> dict:
    key = jax.random.key(seed)
    ks = list(jax.random.split(key, 64))
    nk = lambda: ks.pop()
    nrm = lambda shape, std: std * jax.random.normal(nk(), shape, jnp.float32)
    gain = lambda shape: 1.0 + nrm(shape, 0.02)

    def dt_bias(shape):
        dt = jnp.exp(jax.random.uniform(nk(), shape, jnp.float32, math.log(1e-3), math.log(1e-1)))
        return dt + jnp.log(-jnp.expm1(-dt))

    def a_log(shape):
        return jnp.log(jax.random.uniform(nk(), shape, jnp.float32, 1.0, 16.0))

    L = DEPTH
    return {
        'x': nrm((BATCH, SEQ, D_MODEL), 1.0),
        'c': nrm((BATCH, D_MODEL), 1.0),
        'ctx': nrm((BATCH, CTX_LEN, D_MODEL), 1.0),
        'c_ctx': nrm((D_MODEL,), 1.0),
        'ada_w': nrm((L, D_MODEL, 6 * D_MODEL), 0.5 * D_MODEL ** -0.5),
        'ada_b': nrm((L, 6 * D_MODEL), 0.01),
        'norm1_g': gain((L, D_MODEL)),
        'norm2_g': gain((L, D_MODEL)),
        'w_in': nrm((L, D_MODEL, IN_COLS), D_MODEL ** -0.5),
        'gdn_conv': nrm((L, CONV_K, 2 * GDN_QK + GDN_VW), CONV_K ** -0.5),
        'gdn_a_log': a_log((L, 2, GDN_HEADS)),
        'gdn_dt_bias': dt_bias((L, 2, GDN_HEADS)),
        'gdn_norm_g': gain((L, GDN_DV)),
        'mla_q_lora_g': gain((L, MLA_Q_LORA)),
        'mla_kv_lora_g': gain((L, MLA_KV_LORA)),
        'mla_w_uq': nrm((L, MLA_Q_LORA, MLA_HEADS * MLA_QK), MLA_Q_LORA ** -0.5),
        'mla_w_ukv': nrm((L, MLA_KV_LORA, MLA_HEADS * (MLA_NOPE + MLA_V)), MLA_KV_LORA ** -0.5),
        'mla_qn_g': gain((L, MLA_QK)),
        'mla_kn_g': gain((L, MLA_QK)),
        'diff_qn_g': gain((L, DIFF_HD)),
        'diff_kn_g': gain((L, DIFF_HD)),
        'diff_lambda': nrm((L, 4, DIFF_HD), 0.1),
        'diff_sub_g': gain((L, 2 * DIFF_HD)),
        'ssm_conv': nrm((L, CONV_K, SSM_INNER + 2 * SSM_BC), CONV_K ** -0.5),
        'ssm_conv_b': nrm((L, SSM_INNER + 2 * SSM_BC), 0.01),
        'ssm_a_log': a_log((L, 2, SSM_HEADS)),
        'ssm_dt_bias': dt_bias((L, 2, SSM_HEADS)),
        'ssm_d': gain((L, SSM_HEADS)),
        'ssm_norm_g': gain((L, SSM_INNER)),
        'w_branch': nrm((L, N_BRANCH, BRANCH_W, D_MODEL), BRANCH_W ** -0.5),
        'w_out': nrm((L, D_MODEL, D_MODEL), D_MODEL ** -0.5),
        'mlp_w1': nrm((L, D_MODEL, D_FF), D_MODEL ** -0.5),
        'mlp_w2': nrm((L, D_FF, D_MODEL), D_FF ** -0.5),
    }


def reference(x, c, ctx, c_ctx, ada_w, ada_b, norm1_g, norm2_g, w_in, gdn_conv, gdn_a_log,
              gdn_dt_bias, gdn_norm_g, mla_q_lora_g, mla_kv_lora_g, mla_w_uq, mla_w_ukv, mla_qn_g,
              mla_kn_g, diff_qn_g, diff_kn_g, diff_lambda, diff_sub_g, ssm_conv, ssm_conv_b,
              ssm_a_log, ssm_dt_bias, ssm_d, ssm_norm_g, w_branch, w_out, mlp_w1, mlp_w2):
    n_tok = x.shape[1]
    cos_m, sin_m = axial_rope(n_tok, MLA_ROPE)
    cos_d, sin_d = axial_rope(n_tok, DIFF_HD)
    xc = ctx
    s_c = jax.nn.silu(c)
    s_cc = jax.nn.silu(c_ctx)
    for l in range(DEPTH):
        last = l == DEPTH - 1
        need_ctx = not last
        mod = jnp.split((s_c @ ada_w[l] + ada_b[l])[:, None, :], 6, axis=-1)
        modc = jnp.split(s_cc @ ada_w[l] + ada_b[l], 6, axis=-1)
        h = modulate(rms_norm(x, norm1_g[l]), mod[0], mod[1])
        hc = modulate(rms_norm(xc, norm1_g[l]), modc[0], modc[1])
        w_in_l = w_in[l]
        u = h @ w_in_l
        uc = hc @ (w_in_l if need_ctx else w_in_l[:, :MIX_COLS])
        ua, ub, ucd, ud, gates = _split(u, [GDN_COLS, MLA_COLS, DIFF_COLS, SSM_COLS, GATE_COLS])
        parts_c = _split(uc, [GDN_COLS, MLA_COLS, DIFF_COLS, SSM_COLS] + ([GATE_COLS] if need_ctx else []))
        lam_init = LAMBDA_BASE - LAMBDA_AMP * math.exp(-LAMBDA_RATE * l)
        ya = gdn_branch(ua, parts_c[0], gdn_conv[l], gdn_a_log[l], gdn_dt_bias[l], gdn_norm_g[l], need_ctx)
        yb = mla_branch(ub, parts_c[1], mla_w_uq[l], mla_w_ukv[l], mla_q_lora_g[l], mla_kv_lora_g[l],
                        mla_qn_g[l], mla_kn_g[l], cos_m, sin_m, need_ctx)
        yc = diff_branch(ucd, parts_c[2], diff_qn_g[l], diff_kn_g[l], diff_lambda[l], diff_sub_g[l],
                         lam_init, cos_d, sin_d, need_ctx)
        yd = ssm_branch(ud, parts_c[3], ssm_conv[l], ssm_conv_b[l], ssm_a_log[l], ssm_dt_bias[l],
                        ssm_d[l], ssm_norm_g[l], need_ctx)
        x = x + mod[2] * merge_branches([ya[0], yb[0], yc[0], yd[0]], gates, w_branch[l], w_out[l])
        if need_ctx:
            xc = xc + modc[2] * merge_branches([ya[1], yb[1], yc[1], yd[1]], parts_c[4], w_branch[l], w_out[l])
        h2 = modulate(rms_norm(x, norm2_g[l]), mod[3], mod[4])
        x = x + mod[5] * sq_relu_mlp(h2, mlp_w1[l], mlp_w2[l])
        if need_ctx:
            hc2 = modulate(rms_norm(xc, norm2_g[l]), modc[3], modc[4])
            xc = xc + modc[5] * sq_relu_mlp(hc2, mlp_w1[l], mlp_w2[l])
    return x
```

```python
import math
import contextlib
import numpy as np
import concourse.bass as bass
import concourse.mybir as mybir
from concourse.bass_utils import run_bass_kernel_spmd

F32 = mybir.dt.float32
BF16 = mybir.dt.bfloat16
ALU = mybir.AluOpType
AF = mybir.ActivationFunctionType


class Buf:
    __slots__ = ("t", "w", "r", "dsem", "name")

    def __init__(self, t, name=""):
        self.t = t
        self.w = {}
        self.r = {}
        self.dsem = None
        self.name = name

    def __getitem__(self, key):
        return self.t[key]


class KB:
    SEM_ROT = 30000

    def __init__(self, nc):
        self.nc = nc
        self.es = contextlib.ExitStack()
        self.engs = {"pe": nc.tensor, "dve": nc.vector, "act": nc.scalar,
                     "pool": nc.gpsimd, "sp": nc.sync}
        self.semh = {}
        self.cnt = {}
        self.isdma = {}
        self.cur = {}
        self.waited = {e: {} for e in self.engs}
        self.nsem = 0
        for e in self.engs:
            self.cur[e] = self.new_sem(False)
        self.ninstr = 0
        self.free_dsems = []
        self.free_dsems_q = {}
        self.phase_dsems = []
        self.persist = False
        self.dma_remap = {}

    def new_sem(self, isdma):
        key = self.nsem
        self.nsem += 1
        self.semh[key] = self.es.enter_context(self.nc.semaphore("s%d" % key))
        self.cnt[key] = 0
        self.isdma[key] = isdma
        return key

    def sb(self, stack, name, shape, dtype):
        t = stack.enter_context(self.nc.sbuf_tensor(name, list(shape), dtype))
        return Buf(t, name)

    def ps(self, stack, name, shape, dtype):
        t = stack.enter_context(self.nc.psum_tensor(name, list(shape), dtype))
        return Buf(t, name)

    def _waits(self, eng, reads, writes):
        need = {}
        for b in reads:
            for s, v in b.w.items():
                if need.get(s, 0) < v:
                    need[s] = v
        for b in writes:
            for d in (b.w, b.r):
                for s, v in d.items():
                    if eng == "pe" and s == self.cur["pe"] and d is b.w:
                        continue
                    if need.get(s, 0) < v:
                        need[s] = v
        e = self.engs[eng]
        wd = self.waited[eng]
        for s, v in need.items():
            if wd.get(s, 0) >= v:
                continue
            if self.isdma[s]:
                v = self.cnt[s]
            e.wait_ge(self.semh[s], v)
            wd[s] = v
            self.ninstr += 1

    def op(self, eng, fn, reads=(), writes=()):
        self._waits(eng, reads, writes)
        ins = fn(self.engs[eng])
        s = self.cur[eng]
        self.cnt[s] += 1
        ins.then_inc(self.semh[s], 1)
        tag = (s, self.cnt[s])
        self._mark(tag, reads, writes)
        if self.cnt[s] >= self.SEM_ROT:
            self.cur[eng] = self.new_sem(False)
        self.ninstr += 1
        return ins

    def _mark(self, tag, reads, writes):
        s, v = tag
        for b in writes:
            b.w = {s: v}
            b.r = {}
        for b in reads:
            if b not in writes:
                b.r[s] = v

    def dma(self, eng, out, in_, reads, writes, sembuf, **kw):
        eng = self.dma_remap.get(eng, eng)
        dmap = sembuf.dsem if isinstance(sembuf.dsem, dict) else {}
        sembuf.dsem = dmap
        if eng not in dmap:
            fl = self.free_dsems_q.setdefault(eng, [])
            if fl and not self.persist:
                dmap[eng] = fl.pop()
            else:
                dmap[eng] = self.new_sem(True)
            if not self.persist:
                self.phase_dsems.append((eng, dmap[eng]))
        self._waits(eng, reads, writes)
        ins = self.engs[eng].dma_start(out=out, in_=in_, **kw)
        s = sembuf.dsem[eng]
        self.cnt[s] += 16
        ins.then_inc(self.semh[s], 16)
        self._mark((s, self.cnt[s]), reads, writes)
        self.ninstr += 1
        return ins

    def barrier(self, engs=None):
        if engs is None:
            for q_, s_ in self.phase_dsems:
                self.free_dsems_q.setdefault(q_, []).append(s_)
            self.phase_dsems = []
        for eng in (engs or self.engs):
            e = self.engs[eng]
            wd = self.waited[eng]
            for s, v in self.cnt.items():
                if v > 0 and wd.get(s, 0) < v:
                    e.wait_ge(self.semh[s], v)
                    wd[s] = v
                    self.ninstr += 1


D = 1024
EPS = 1e-6
NEG = -30000.0
GQ, GK, GV, GZ, GA, GB_ = 0, 512, 1024, 1536, 2048, 2056
MCQ, MCKV, MKPE = 2064, 2448, 2704
DQ, DK, DV = 2736, 3248, 3760
SZ, SX, SB_, SC, SDT = 4272, 4784, 5296, 5552, 5808
GATE0 = 5824
CN = ["ident", "ones", "triF", "triB", "mnegF", "mnegB", "strF", "strB", "bd64", "perm64", "perm32", "sel65"]


def consts_np(S, C):
    T = C + S
    k = np.arange(128)
    d = {}
    d["ident"] = np.eye(128)
    d["ones"] = np.ones((128, 128))
    d["triF"] = (k[:, None] <= k[None, :])
    d["triB"] = (k[:, None] >= k[None, :])
    d["mnegF"] = np.where(k[None, :] >= k[:, None], 0.0, NEG)
    d["mnegB"] = np.where(k[None, :] <= k[:, None], 0.0, NEG)
    d["strF"] = (k[None, :] > k[:, None])
    d["strB"] = (k[None, :] < k[:, None])
    d["bd64"] = (k[:, None] // 64 == k[None, :] // 64)

    def perm(n_rot, total):
        P = np.zeros((128, 128))
        half = n_rot // 2
        qd = half // 2
        for base in range(0, total, half):
            for i in range(half):
                m = base + i
                if i < qd:
                    P[m + qd, m] = -1.0
                else:
                    P[m - qd, m] = 1.0
        return P
    d["perm64"] = perm(64, 128)
    d["perm32"] = perm(32, 32)
    s65 = np.zeros((128, 128))
    s65[64, :] = 1.0
    d["sel65"] = s65
    cst = np.stack([np.asarray(d[n], np.float32) for n in CN], axis=1)

    def rope(rot_dim):
        rows = S // 64
        row = np.repeat(np.arange(rows, dtype=np.float32), 64)
        col = np.tile(np.arange(64, dtype=np.float32), rows)
        quarter = rot_dim // 4
        inv = (np.float32(10000.0) ** (-np.arange(quarter, dtype=np.float32) / np.float32(quarter))).astype(np.float32)
        ar = row[:, None] * inv
        ac = col[:, None] * inv
        ang = np.concatenate([ar, ar, ac, ac], axis=-1).astype(np.float32)
        cos = np.ones((rot_dim, T), np.float32)
        sin = np.zeros((rot_dim, T), np.float32)
        cos[:, C:] = np.cos(ang).T
        sin[:, C:] = np.sin(ang).T
        return cos, sin
    cm, sm = rope(32)
    cd, sd = rope(64)
    ropem = np.stack([cm, sm], axis=1)
    roped = np.stack([np.tile(cd, (2, 1)), np.tile(sd, (2, 1))], axis=1)
    gm = []
    for lv in range(7):
        b = 1 << lv
        mU = ((k[:, None] // (2 * b) == k[None, :] // (2 * b)) & (k[:, None] % (2 * b) < b) & (k[None, :] % (2 * b) >= b))
        gm.append(mU)
    gm = gm + [m_.T for m_ in gm]
    gmask = np.stack([np.asarray(m_, np.float32) for m_ in gm], axis=1)
    return {"cst": np.ascontiguousarray(cst), "ropem": np.ascontiguousarray(ropem),
            "roped": np.ascontiguousarray(roped), "gmask": np.ascontiguousarray(gmask)}


WSPEC = [
    ("ada_w", [D, 6 * D]), ("ada_b", [6 * D]), ("norm1_g", [D]), ("norm2_g", [D]),
    ("w_in", [D, 9920]), ("gdn_conv", [5, 1536]), ("gdn_a_log", [2, 4]), ("gdn_dt_bias", [2, 4]),
    ("gdn_norm_g", [128]), ("mla_q_lora_g", [384]), ("mla_kv_lora_g", [256]),
    ("mla_w_uq", [384, 768]), ("mla_w_ukv", [256, 1024]), ("mla_qn_g", [96]), ("mla_kn_g", [96]),
    ("diff_qn_g", [64]), ("diff_kn_g", [64]), ("diff_lambda", [4, 64]), ("diff_sub_g", [128]),
    ("ssm_conv", [5, 1024]), ("ssm_conv_b", [1024]), ("ssm_a_log", [2, 8]), ("ssm_dt_bias", [2, 8]),
    ("ssm_d", [8]), ("ssm_norm_g", [512]), ("w_branch", [4, 512, D]), ("w_out", [D, D]),
    ("mlp_w1", [D, 4 * D]), ("mlp_w2", [4 * D, D]),
]


class Mod:
    def __init__(self, S, C, depth, dbg=()):
        self.S, self.C, self.depth = S, C, depth
        self.T = T = S + C
        self.NB = T // 128
        self.dbg = dbg
        nc = self.nc = bass.Bass("TRN2", target_bir_lowering=False)
        self.k = KB(nc)
        self.uid = 0
        di = lambda n, sh, dt=F32: nc.dram_tensor(n, list(sh), dt, kind="ExternalInput").ap()
        self.x_in = di("x", [S, D])
        self.ctx_in = di("ctx", [C, D])
        self.cc_in = di("cc", [2, D])
        self.cst_in = di("cst", [128, len(CN), 128])
        self.ropem_in = di("ropem", [32, 2, T])
        self.roped_in = di("roped", [128, 2, T])
        self.gmask_in = di("gmask", [128, 14, 128])
        self.W = {n: di(n, [depth] + sh) for n, sh in WSPEC}
        self.out = nc.dram_tensor("out", [S, D], F32, kind="ExternalOutput").ap()
        dscr = lambda n, sh, dt=F32: Buf(nc.dram_tensor(n, list(sh), dt, kind=("ExternalOutput" if n in dbg else "Internal")).ap(), n)
        self.XT = dscr("XT", [D, T])
        self.U = dscr("U", [80 * 128, T])
        self.Y = dscr("Y", [2048, T], BF16)
        self.MT = dscr("MT", [D, T])
        self.HID = dscr("HID", [4 * D, T], BF16)
        self.QD = dscr("QD", [512, T], BF16)
        self.KD = dscr("KD", [512, T], BF16)
        self.VD = dscr("VD", [T, 512], BF16)
        self.KN = dscr("KN", [8 * 64, T], BF16)
        self.KR = dscr("KR", [8 * 32, T], BF16)
        self.QN = dscr("QN", [8 * 64, T], BF16)
        self.QR = dscr("QR", [8 * 32, T], BF16)
        self.VA = dscr("VA", [T, 8 * 65], BF16)
        self.XS = dscr("XS", [512, T])
        self.YS = dscr("YS", [2, 512, T])
        self.GO = dscr("GO", [2, 512, T])
        self.tiles = [(0, C)] + [(C + 512 * i, 512) for i in range(S // 512)]
        self.uchunks = []
        slot = 0
        self.uslot = {}
        for (a, b) in [(0, 2048), (2048, 2064), (2064, 2448), (2448, 2704), (2704, 2736), (2736, 4272),
                       (4272, 5808), (5808, 5824), (5824, 9920)]:
            c = a
            while c < b:
                e = min(c + 128, b)
                self.uchunks.append((c, e, slot))
                self.uslot[c] = slot
                slot += 1
                c = e
        assert slot == 80

    def nm(self, p):
        self.uid += 1
        return "%s_%d" % (p, self.uid)

    def sb(self, st, shape, dt=F32, p="t"):
        return self.k.sb(st, self.nm(p), shape, dt)

    def psum(self):
        self.psi = (self.psi + 1) % len(self.psb)
        return self.psb[self.psi]

    def urow(self, col):
        return self.uslot[col] * 128

    def build(self):
        k = self.k
        with k.es, contextlib.ExitStack() as gs:
            self.psb = [k.ps(gs, "psb%d" % i, [128, 512], F32) for i in range(8)]
            self.psi = 0
            self.cst = self.sb(gs, [128, len(CN), 128], F32, "cst")
            k.persist = True
            k.dma("sp", self.cst[:, :, :], self.cst_in, [], [self.cst], self.cst)
            k.persist = False
            self.cbf = self.sb(gs, [128, len(CN), 128], BF16, "cbf")
            k.op("dve", lambda e: e.tensor_copy(out=self.cbf[:, :, :], in_=self.cst[:, :, :]), [self.cst], [self.cbf])
            self.epsb = self.sb(gs, [128, 4], F32, "epsb")
            k.op("dve", lambda e: e.memset(self.epsb[:, :], EPS), [], [self.epsb])
            k.op("dve", lambda e: e.memset(self.epsb[:, 1:2], 1.0), [self.epsb], [self.epsb])
            self.XTt = [Buf(None, "XTt%d" % i) for i in range(len(self.tiles))]
            self.MTt = [Buf(None, "MTt%d" % i) for i in range(len(self.tiles))]
            self.phase_init()
            for l in range(self.depth):
                self.layer(l)
            self.phase_final()
            k.barrier()
        return self.nc

    def C_(self, name, bf=False):
        i = CN.index(name)
        return (self.cbf if bf else self.cst)[:, i, :]

    def phase_init(self):
        k = self.k
        with contextlib.ExitStack() as st:
            xin = [self.sb(st, [128, D], F32, "xin") for _ in range(2)]
            xo = [self.sb(st, [128, 8, 128], F32, "xo") for _ in range(2)]
            XTv = self.XT.t.rearrange("(kc p) t -> p kc t", p=128)
            for blk in range(self.NB):
                src = self.ctx_in[blk * 128:(blk + 1) * 128, :] if blk < self.C // 128 else \
                    self.x_in[blk * 128 - self.C:(blk + 1) * 128 - self.C, :]
                xi = xin[blk % 2]
                o = xo[blk % 2]
                k.dma("sp", xi[:, :], src, [], [xi], xi)
                for half in range(2):
                    ps = self.psum()
                    for j in range(4):
                        kc = half * 4 + j
                        k.op("pe", lambda e: e.transpose(out=ps[:, j * 128:(j + 1) * 128], in_=xi[:, kc * 128:(kc + 1) * 128],
                                                         identity=self.C_("ident")), [xi, self.cst], [ps])
                    eng = "dve" if half == 0 else "act"
                    if eng == "dve":
                        k.op("dve", lambda e: e.tensor_copy(out=o[:, half * 4:half * 4 + 4, :], in_=ps[:, :].rearrange("p (a b) -> p a b", a=4)), [ps], [o])
                    else:
                        k.op("act", lambda e: e.copy(out=o[:, half * 4:half * 4 + 4, :], in_=ps[:, :].rearrange("p (a b) -> p a b", a=4)), [ps], [o])
                k.dma("pool", XTv[:, :, blk * 128:(blk + 1) * 128], o[:, :, :], [o], [], o)
            k.barrier()

    def phase_final(self):
        k = self.k
        with contextlib.ExitStack() as st:
            xi = [self.sb(st, [128, 8, 128], F32, "fxi") for _ in range(2)]
            xo = [self.sb(st, [128, D], F32, "fxo") for _ in range(2)]
            XTv = self.XT.t.rearrange("(kc p) t -> p kc t", p=128)
            dout = Buf(self.out, "out")
            for b in range(self.S // 128):
                blk = b + self.C // 128
                a = xi[b % 2]
                o = xo[b % 2]
                k.dma("sp", a[:, :, :], XTv[:, :, blk * 128:(blk + 1) * 128], [], [a], a)
                for half in range(2):
                    ps = self.psum()
                    for j in range(4):
                        kc = half * 4 + j
                        k.op("pe", lambda e: e.transpose(out=ps[:, j * 128:(j + 1) * 128], in_=a[:, kc, :],
                                                         identity=self.C_("ident")), [a, self.cst], [ps])
                    if half == 0:
                        k.op("dve", lambda e: e.tensor_copy(out=o[:, 0:512], in_=ps[:, :]), [ps], [o])
                    else:
                        k.op("act", lambda e: e.copy(out=o[:, 512:1024], in_=ps[:, :]), [ps], [o])
                k.dma("pool", self.out[b * 128:(b + 1) * 128, :], o[:, :], [o], [dout], o)
            k.barrier()

    def layer(self, l):
        k = self.k
        self.l = l
        self.last = (l == self.depth - 1) and ("forcectx" not in self.dbg)
        with contextlib.ExitStack() as ls:
            self.phase_mod(l, ls)
            self.phase_inproj(l)
            if "U" in self.dbg and l == 0 and "stopU" in self.dbg:
                return
            for ph in ("gdn", "mla", "diff", "ssm", "merge", "mlp"):
                if ph not in self.dbg:
                    getattr(self, "phase_" + ph)(l)
            k.barrier()

    def phase_mod(self, l, ls):
        k = self.k
        W = self.W
        self.modv = self.sb(ls, [128, 48, 2], F32, "modv")
        self.A1 = self.sb(ls, [128, 8, 2], F32, "A1")
        self.A2 = self.sb(ls, [128, 8, 2], F32, "A2")
        with contextlib.ExitStack() as st:
            cv = self.sb(st, [128, 2, 8], F32, "cv")
            sv = self.sb(st, [128, 8, 2], F32, "sv")
            ab = self.sb(st, [128, 48], F32, "ab")
            g12 = self.sb(st, [128, 2, 8], F32, "g12")
            k.dma("sp", cv[:, :, :], self.cc_in.rearrange("r (kc p) -> p r kc", p=128), [], [cv], cv, allow_slow_non_contiguous=True)
            k.dma("sp", ab[:, :], W["ada_b"][l].rearrange("(j p) -> p j", p=128), [], [ab], ab, allow_slow_non_contiguous=True)
            k.dma("sp", g12[:, 0, :], W["norm1_g"][l].rearrange("(kc p) -> p kc", p=128), [], [g12], g12, allow_slow_non_contiguous=True)
            k.dma("sp", g12[:, 1, :], W["norm2_g"][l].rearrange("(kc p) -> p kc", p=128), [], [g12], g12, allow_slow_non_contiguous=True)
            k.op("act", lambda e: e.activation(out=sv[:, :, :], in_=cv[:, :, :].rearrange("p r kc -> p kc r"), func=AF.Silu), [cv], [sv])
            wst = [self.sb(st, [128, 8, 512], F32, "adaw") for _ in range(2)]
            aw = W["ada_w"][l].rearrange("(kc p) c -> p kc c", p=128)
            ps = self.psum()
            for g in range(12):
                w = wst[g % 2]
                k.dma("sp", w[:, :, :], aw[:, :, g * 512:(g + 1) * 512], [], [w], w)
                for jj in range(4):
                    j = g * 4 + jj
                    for kc in range(8):
                        k.op("pe", lambda e: e.matmul(ps[:, j * 2:j * 2 + 2], lhsT=w[:, kc, jj * 128:(jj + 1) * 128], rhs=sv[:, kc, :],
                                                      start=(kc == 0), stop=(kc == 7)), [w, sv], [ps])
            k.op("dve", lambda e: e.tensor_tensor(out=self.modv[:, :, :], in0=ps[:, 0:96].rearrange("p (j r) -> p j r", r=2),
                                                  in1=ab[:, :].unsqueeze(2).to_broadcast([128, 48, 2]), op=ALU.add), [ps, ab], [self.modv])
            for (A, gi, mi) in ((self.A1, 0, 1), (self.A2, 1, 4)):
                k.op("dve", lambda e: e.scalar_tensor_tensor(out=A[:, :, :], in0=self.modv[:, mi * 8:mi * 8 + 8, :], scalar=1.0,
                                                             in1=g12[:, gi, :].unsqueeze(2).to_broadcast([128, 8, 2]), op0=ALU.add, op1=ALU.mult),
                     [self.modv, g12], [A])
            k.barrier()

    def mcol(self, ti):
        return 1 if ti == 0 else 0

    def norm_mod(self, st, A, shift_idx, dst):
        k = self.k
        xt_ = [self.sb(st, [128, 8, 512], F32, "nx") for _ in range(2)]
        sq_ = [self.sb(st, [128, 8, 512], F32, "nsq") for _ in range(2)]
        rs_ = [self.sb(st, [128, 512], F32, "nrs") for _ in range(2)]
        tmp_ = [self.sb(st, [128, 512], F32, "ntmp") for _ in range(3)]
        XTv = self.XT.t.rearrange("(kc p) t -> p kc t", p=128)
        ci = 0
        for ti, (t0, n) in enumerate(self.tiles):
            col = self.mcol(ti)
            xt, sq, rs = xt_[ti % 2], sq_[ti % 2], rs_[ti % 2]
            k.dma("sp", xt[:, :, 0:n], XTv[:, :, t0:t0 + n], [self.XTt[ti]], [xt], xt)
            k.op("act", lambda e: e.activation(out=sq[:, :, 0:n], in_=xt[:, :, 0:n], func=AF.Square), [xt], [sq])
            ps = self.psum()
            for kc in range(8):
                k.op("pe", lambda e: e.matmul(ps[:, 0:n], lhsT=self.C_("ones"), rhs=sq[:, kc, 0:n], start=(kc == 0), stop=(kc == 7)),
                     [self.cst, sq], [ps])
            self.rstd_from(ps, n, rs, 1.0 / D)
            for kc in range(8):
                tmp = tmp_[ci % 3]
                ci += 1
                k.op("dve", lambda e: e.scalar_tensor_tensor(out=tmp[:, 0:n], in0=xt[:, kc, 0:n], scalar=A[:, kc, col:col + 1], in1=rs[:, 0:n],
                                                             op0=ALU.mult, op1=ALU.mult), [xt, A, rs], [tmp])
                k.op("act", lambda e: e.activation(out=dst[:, kc, t0:t0 + n], in_=tmp[:, 0:n], func=AF.Identity,
                                                   bias=self.modv[:, shift_idx * 8 + kc, col:col + 1], scale=1.0), [tmp, self.modv], [dst])

    def gemm_fm(self, st, in_sb, KC, wsrc, chunks, tiles, epi, krows=128):
        k = self.k
        groups, cur = [], []
        for ch in chunks:
            if cur and (ch[0] != cur[-1][1] or ch[1] - cur[0][0] > 512):
                groups.append(cur)
                cur = []
            cur.append(ch)
        if cur:
            groups.append(cur)
        wst = [self.sb(st, [128, KC, 512], F32, "wst") for _ in range(2)]
        wbf = [self.sb(st, [128, KC, 512], BF16, "wbf") for _ in range(2)]
        for gi, g in enumerate(groups):
            c0, c1 = g[0][0], g[-1][1]
            w = c1 - c0
            s, b = wst[gi % 2], wbf[gi % 2]
            k.dma("sp", s[0:krows, :, 0:w], wsrc(c0, c1), [], [s], s)
            k.op("pool", lambda e: e.tensor_copy(out=b[0:krows, :, 0:w], in_=s[0:krows, :, 0:w]), [s], [b])
            for ti, (t0, n) in enumerate(tiles):
                for ch in g:
                    rows = ch[1] - ch[0]
                    off = ch[0] - c0
                    ps = self.psum()
                    for kk in range(KC):
                        k.op("pe", lambda e: e.matmul(ps[0:rows, 0:n], lhsT=b[0:krows, kk, off:off + rows], rhs=in_sb[0:krows, kk, t0:t0 + n],
                                                      start=(kk == 0), stop=(kk == KC - 1)), [b, in_sb], [ps])
                    epi(ch, ti, t0, n, ps, rows)

    def phase_inproj(self, l):
        k = self.k
        with contextlib.ExitStack() as st:
            hT = self.sb(st, [128, 8, self.T], BF16, "hT")
            with contextlib.ExitStack() as st2:
                self.norm_mod(st2, self.A1, 0, hT)
                k.barrier()
            stg = [self.sb(st, [128, 512], F32, "ustg") for _ in range(4)]
            cnt = [0]
            win = self.W["w_in"][l].rearrange("(kc p) c -> p kc c", p=128)

            def epi(ch, ti, t0, n, ps, rows):
                s = stg[cnt[0] % 4]
                if cnt[0] % 2 == 0:
                    k.op("dve", lambda e: e.tensor_copy(out=s[0:rows, 0:n], in_=ps[0:rows, 0:n]), [ps], [s])
                else:
                    k.op("act", lambda e: e.copy(out=s[0:rows, 0:n], in_=ps[0:rows, 0:n]), [ps], [s])
                cnt[0] += 1
                r0 = ch[2] * 128
                k.dma("pool", self.U.t[r0:r0 + rows, t0:t0 + n], s[0:rows, 0:n], [s], [], s)
            self.gemm_fm(st, hT, 8, lambda c0, c1: win[:, :, c0:c1], self.uchunks, self.tiles, epi)
            vst = [self.sb(st, [128, 512], BF16, "vst") for _ in range(2)]

            def epiv(blk, ps):
                v = vst[blk % 2]
                k.op("act", lambda e: e.copy(out=v[:, :], in_=ps[:, :]), [ps], [v])
                k.dma("pool", self.VD.t[blk * 128:(blk + 1) * 128, :], v[:, :], [v], [], v)
            self.gemm_tm(st, hT, 8, lambda s_: k.dma("sp", s_[:, :, :], win[:, :, DV:DV + 512], [], [s_], s_), 512, list(range(self.NB)), epiv)
            k.barrier()

    def gemm_tm(self, st, in_sb, KC, wsrc, width, blocks, epi, krows=128):
        k = self.k
        s = self.sb(st, [128, KC, width], F32, "wtm")
        b = self.sb(st, [128, KC, width], BF16, "wtmb")
        wsrc(s)
        k.op("pool", lambda e: e.tensor_copy(out=b[0:krows, :, :], in_=s[0:krows, :, :]), [s], [b])
        for blk in blocks:
            ps = self.psum()
            for kk in range(KC):
                k.op("pe", lambda e: e.matmul(ps[:, 0:width], lhsT=in_sb[0:krows, kk, blk * 128:(blk + 1) * 128], rhs=b[0:krows, kk, :],
                                              start=(kk == 0), stop=(kk == KC - 1)), [in_sb, b], [ps])
            epi(blk, ps)

    def rstd_from(self, ps, n, rs, scale, rows=128):
        k = self.k
        k.op("act", lambda e: e.activation(out=rs[0:rows, 0:n], in_=ps[0:rows, 0:n], func=AF.Ln, bias=self.epsb[0:rows, 0:1], scale=scale), [ps, self.epsb], [rs])
        k.op("act", lambda e: e.activation(out=rs[0:rows, 0:n], in_=rs[0:rows, 0:n], func=AF.Exp, scale=-0.5), [rs], [rs])

    def vec_col(self, st, src_ap, rows, p="vc"):
        t = self.sb(st, [128, 1], F32, p)
        self.k.dma("sp", t[0:rows, :], src_ap.rearrange("(p o) -> p o", o=1), [], [t], t, allow_slow_non_contiguous=True)
        return t

    def phase_merge(self, l):
        k = self.k
        W = self.W
        tiles = self.tiles[1:] if self.last else self.tiles
        MTv = self.MT.t
        for i in range(4):
            with contextlib.ExitStack() as st:
                yin = self.sb(st, [128, 4, self.T], BF16, "yin")
                k.dma("sp", yin[:, :, :], self.Y.t[i * 512:(i + 1) * 512, :].rearrange("(kc p) t -> p kc t", p=128), [], [yin], yin)
                gt_ = [self.sb(st, [128, 512], F32, "gt") for _ in range(3)]
                mt_ = [self.sb(st, [128, 512], F32, "mt") for _ in range(3)]
                cnt = [0]
                wb = W["w_branch"][l, i].rearrange("(kc p) c -> p kc c", p=128)

                def epi(ch, ti, t0, n, ps, rows):
                    c = ch[0] // 128
                    gt, mt = gt_[cnt[0] % 3], mt_[cnt[0] % 3]
                    cnt[0] += 1
                    r0 = self.urow(GATE0 + i * 1024 + c * 128)
                    k.dma("sp", gt[:, 0:n], self.U.t[r0:r0 + 128, t0:t0 + n], [], [gt], gt)
                    k.op("act", lambda e: e.activation(out=gt[:, 0:n], in_=gt[:, 0:n], func=AF.Sigmoid), [gt], [gt])
                    if i > 0:
                        k.dma("sp", mt[:, 0:n], MTv[c * 128:(c + 1) * 128, t0:t0 + n], [], [mt], mt)
                        k.op("dve", lambda e: e.tensor_tensor(out=gt[:, 0:n], in0=ps[:, 0:n], in1=gt[:, 0:n], op=ALU.mult), [ps, gt], [gt])
                        k.op("dve", lambda e: e.tensor_tensor(out=mt[:, 0:n], in0=mt[:, 0:n], in1=gt[:, 0:n], op=ALU.add), [mt, gt], [mt])
                    else:
                        k.op("dve", lambda e: e.tensor_tensor(out=mt[:, 0:n], in0=ps[:, 0:n], in1=gt[:, 0:n], op=ALU.mult), [ps, gt], [mt])
                    k.dma("pool", MTv[c * 128:(c + 1) * 128, t0:t0 + n], mt[:, 0:n], [mt], [], mt)
                self.gemm_fm(st, yin, 4, lambda c0, c1: wb[:, :, c0:c1], [(c * 128, (c + 1) * 128) for c in range(8)], tiles, epi)
                k.barrier()
        with contextlib.ExitStack() as st:
            mT = self.sb(st, [128, 8, self.T], BF16, "mTb")
            with contextlib.ExitStack() as st2:
                ml = [self.sb(st2, [128, 8, 512], F32, "ml") for _ in range(2)]
                for ti, (t0, n) in enumerate(tiles):
                    m = ml[ti % 2]
                    k.dma("sp", m[:, :, 0:n], MTv.rearrange("(kc p) t -> p kc t", p=128)[:, :, t0:t0 + n], [], [m], m)
                    k.op("dve", lambda e: e.tensor_copy(out=mT[:, :, t0:t0 + n], in_=m[:, :, 0:n]), [m], [mT])
                k.barrier()
            self.resid_gemm(st, mT, 8, self.W["w_out"][l].rearrange("(kc p) c -> p kc c", p=128), 16, tiles)
            k.barrier()

    def resid_gemm(self, st, in_sb, KC, wv, gate_idx, tiles):
        k = self.k
        xt_ = [self.sb(st, [128, 512], F32, "rx") for _ in range(4)]
        cnt = [0]
        XTv = self.XT.t

        def epi(ch, ti, t0, n, ps, rows):
            c = ch[0] // 128
            col = 1 if t0 == 0 else 0
            xt = xt_[cnt[0] % 4]
            cnt[0] += 1
            k.dma("sp", xt[:, 0:n], XTv[c * 128:(c + 1) * 128, t0:t0 + n], [], [xt], xt)
            k.op("dve", lambda e: e.scalar_tensor_tensor(out=xt[:, 0:n], in0=ps[:, 0:n], scalar=self.modv[:, gate_idx + c, col:col + 1], in1=xt[:, 0:n],
                                                         op0=ALU.mult, op1=ALU.add), [ps, self.modv, xt], [xt])
            k.dma("pool", XTv[c * 128:(c + 1) * 128, t0:t0 + n], xt[:, 0:n], [xt], [], xt)
        self.gemm_fm(st, in_sb, KC, lambda c0, c1: wv[:, :, c0:c1], [(c * 128, (c + 1) * 128) for c in range(8)], tiles, epi)

    def phase_mlp(self, l):
        k = self.k
        tiles = self.tiles[1:] if self.last else self.tiles
        with contextlib.ExitStack() as st:
            hT = self.sb(st, [128, 8, self.T], BF16, "h2T")
            with contextlib.ExitStack() as st2:
                self.norm_mod(st2, self.A2, 3, hT)
                k.barrier()
            stg = [self.sb(st, [128, 512], F32, "hs") for _ in range(3)]
            stb = [self.sb(st, [128, 512], BF16, "hb") for _ in range(3)]
            cnt = [0]
            w1 = self.W["mlp_w1"][l].rearrange("(kc p) c -> p kc c", p=128)

            def epi(ch, ti, t0, n, ps, rows):
                s, b = stg[cnt[0] % 3], stb[cnt[0] % 3]
                cnt[0] += 1
                k.op("dve", lambda e: e.tensor_scalar_max(out=s[:, 0:n], in0=ps[:, 0:n], scalar1=0.0), [ps], [s])
                k.op("act", lambda e: e.activation(out=b[:, 0:n], in_=s[:, 0:n], func=AF.Square), [s], [b])
                k.dma("pool", self.HID.t[ch[0]:ch[1], t0:t0 + n], b[:, 0:n], [b], [], b)
            self.gemm_fm(st, hT, 8, lambda c0, c1: w1[:, :, c0:c1], [(c * 128, (c + 1) * 128) for c in range(32)], tiles, epi)
            k.barrier()
        for q in range(4):
            with contextlib.ExitStack() as st:
                hin = self.sb(st, [128, 8, self.T], BF16, "hin")
                k.dma("sp", hin[:, :, :], self.HID.t[q * 1024:(q + 1) * 1024, :].rearrange("(kc p) t -> p kc t", p=128), [], [hin], hin)
                w2 = self.W["mlp_w2"][l, q * 1024:(q + 1) * 1024, :].rearrange("(kc p) c -> p kc c", p=128)
                self.resid_gemm(st, hin, 8, w2, 40, tiles)
                k.barrier()

    def attend(self, st, terms, vfn, M, kblocks, qtiles, scale, ones_sum, epi, ptag=0):
        k = self.k
        pt_ = self.pt_
        for qi, (qc0, t0, n) in enumerate(qtiles):
            O = self.psb[4 + 2 * ptag]
            Sps = self.psb[5 + 2 * ptag]
            nk = len(kblocks)
            sps = {}

            def score(i):
                sp = self.psb[self.sci % 4]
                self.sci += 1
                sps[i] = sp
                kb = kblocks[i]
                for j, (K_sb, Q_sb, r0, rows) in enumerate(terms):
                    k.op("pe", lambda e: e.matmul(sp[:, 0:n], lhsT=K_sb[r0:r0 + rows, kb * 128:(kb + 1) * 128], rhs=Q_sb[r0:r0 + rows, qc0:qc0 + n],
                                                  start=(j == 0), stop=(j == len(terms) - 1)), [K_sb, Q_sb], [sp])
            score(0)
            for i in range(nk):
                if i + 1 < nk:
                    score(i + 1)
                pt = pt_[self.pti % 3]
                self.pti += 1
                sp = sps.pop(i)
                k.op("act", lambda e: e.activation(out=pt[:, 0:n], in_=sp[:, 0:n], func=AF.Exp, scale=scale), [sp], [pt])
                k.op("pe", lambda e: e.matmul(O[0:M, 0:n], lhsT=vfn(kblocks[i]), rhs=pt[:, 0:n], start=(i == 0), stop=(i == nk - 1)), [self.vbuf, pt], [O])
                if ones_sum:
                    k.op("pe", lambda e: e.matmul(Sps[:, 0:n], lhsT=self.C_("ones", True), rhs=pt[:, 0:n], start=(i == 0), stop=(i == nk - 1)), [self.cbf, pt], [Sps])
            epi(qi, t0, n, O, Sps)

    def attn_common(self, st):
        self.pt_ = [self.sb(st, [128, 512], BF16, "pt") for _ in range(3)]
        self.sci = 0
        self.pti = 0

    def qtiles(self, lat):
        if lat:
            return [(t0, t0, n) for (t0, n) in self.tiles[1:]]
        return [(0, 0, self.C)]

    def phase_diff(self, l):
        k = self.k
        W = self.W
        T, NB = self.T, self.NB
        lam_init = 0.8 - 0.6 * math.exp(-0.3 * l)
        QD = self.QD
        KD = self.KD
        with contextlib.ExitStack() as st:
            gq = self.sb(st, [128, 2], F32, "dg")
            for j, nm_ in enumerate(("diff_qn_g", "diff_kn_g")):
                for hh in range(2):
                    k.dma("sp", gq[hh * 64:(hh + 1) * 64, j:j + 1], W[nm_][l].rearrange("(p o) -> p o", o=1), [], [gq], gq, allow_slow_non_contiguous=True)
            u_ = [self.sb(st, [128, 512], F32, "du") for _ in range(2)]
            sq_ = [self.sb(st, [128, 512], F32, "dsq") for _ in range(2)]
            rs_ = [self.sb(st, [128, 512], F32, "drs") for _ in range(2)]
            cs_ = [self.sb(st, [128, 2, 512], F32, "dcs") for _ in range(2)]
            o_ = [self.sb(st, [128, 512], F32, "do") for _ in range(2)]
            ob_ = [self.sb(st, [128, 512], BF16, "dob") for _ in range(2)]
            it = 0
            for j, (col0, dst) in enumerate(((DQ, QD), (DK, KD))):
                for h in range(4):
                    r0 = self.urow(col0 + h * 128)
                    for ti, (t0, n) in enumerate(self.tiles):
                        u, sq, rs, cs, o, ob = u_[it % 2], sq_[it % 2], rs_[it % 2], cs_[it % 2], o_[it % 2], ob_[it % 2]
                        it += 1
                        k.dma("sp", u[:, 0:n], self.U.t[r0:r0 + 128, t0:t0 + n], [], [u], u)
                        k.dma("sp", cs[:, :, 0:n], self.roped_in[:, :, t0:t0 + n], [], [cs], cs)
                        k.op("act", lambda e: e.activation(out=sq[:, 0:n], in_=u[:, 0:n], func=AF.Square), [u], [sq])
                        ps = self.psum()
                        k.op("pe", lambda e: e.matmul(ps[:, 0:n], lhsT=self.C_("bd64"), rhs=sq[:, 0:n], start=True, stop=True), [self.cst, sq], [ps])
                        self.rstd_from(ps, n, rs, 1.0 / 64)
                        k.op("dve", lambda e: e.scalar_tensor_tensor(out=u[:, 0:n], in0=u[:, 0:n], scalar=gq[:, j:j + 1], in1=rs[:, 0:n], op0=ALU.mult, op1=ALU.mult), [u, gq, rs], [u])
                        ps2 = self.psum()
                        k.op("pe", lambda e: e.matmul(ps2[:, 0:n], lhsT=self.C_("perm64"), rhs=u[:, 0:n], start=True, stop=True), [self.cst, u], [ps2])
                        k.op("dve", lambda e: e.tensor_tensor(out=o[:, 0:n], in0=u[:, 0:n], in1=cs[:, 0, 0:n], op=ALU.mult), [u, cs], [o])
                        k.op("dve", lambda e: e.tensor_tensor(out=sq[:, 0:n], in0=ps2[:, 0:n], in1=cs[:, 1, 0:n], op=ALU.mult), [ps2, cs], [sq])
                        k.op("dve", lambda e: e.tensor_tensor(out=ob[:, 0:n], in0=o[:, 0:n], in1=sq[:, 0:n], op=ALU.add), [o, sq], [ob])
                        k.dma("pool", dst.t[h * 128:(h + 1) * 128, t0:t0 + n], ob[:, 0:n], [ob], [], ob)
            k.barrier()
        with contextlib.ExitStack() as st:
            self.attn_common(st)
            lamt2 = self.sb(st, [128, 256], F32, "lamt")
            k.dma("sp", lamt2[:, :], W["diff_lambda"][l:l + 1].rearrange("o a b -> o (a b)").partition_broadcast(128), [], [lamt2], lamt2)

            lv = self.sb(st, [128, 4], F32, "lv")
            lt = self.sb(st, [128, 2, 64], F32, "lt")
            for j in range(2):
                k.op("dve", lambda e: e.tensor_tensor(out=lt[:, j, :], in0=lamt2[:, (2 * j) * 64:(2 * j + 1) * 64], in1=lamt2[:, (2 * j + 1) * 64:(2 * j + 2) * 64], op=ALU.mult), [lamt2], [lt])
            k.op("dve", lambda e: e.reduce_sum(out=lv[:, 0:2], in_=lt[:, :, :], axis=mybir.AxisListType.X), [lt], [lv])
            k.op("act", lambda e: e.activation(out=lv[:, 0:2], in_=lv[:, 0:2], func=AF.Exp), [lv], [lv])
            k.op("dve", lambda e: e.scalar_tensor_tensor(out=lv[:, 2:3], in0=lv[:, 1:2], scalar=-lam_init, in1=lv[:, 0:1], op0=ALU.add, op1=ALU.subtract), [lv], [lv])
            sg = self.vec_col(st, W["diff_sub_g"][l], 128, "sg")
            k.op("dve", lambda e: e.tensor_scalar(out=sg[:, :], in0=sg[:, :], scalar1=(1.0 - lam_init), scalar2=None, op0=ALU.mult), [sg], [sg])
            qh = self.sb(st, [128, T], BF16, "dqh")
            khm = [self.sb(st, [128, T], BF16, "dkh") for _ in range(2)]
            for m_ in range(2):
                k.op("pool", lambda e: e.memset(khm[m_][:, :], 0.0), [], [khm[m_]])
            vh = self.sb(st, [128, NB, 128], BF16, "dvh")
            self.vbuf = vh
            ra = [self.sb(st, [128, 512], F32, "ra") for _ in range(2)]
            aa = [self.sb(st, [128, 512], F32, "aa") for _ in range(2)]
            dd = self.sb(st, [128, 512], F32, "dd")
            yb = [self.sb(st, [128, 512], BF16, "dyb") for _ in range(2)]
            for h in range(4):
                k.dma("sp", qh[:, :], QD.t[h * 128:(h + 1) * 128, :], [], [qh], qh)
                for m_ in range(2):
                    k.dma("sp", khm[m_][m_ * 64:(m_ + 1) * 64, :], KD.t[h * 128 + m_ * 64:h * 128 + (m_ + 1) * 64, :], [], [khm[m_]], khm[m_])
                k.dma("sp", vh[:, :, :], self.VD.t[:, h * 128:(h + 1) * 128].rearrange("(b p) d -> p b d", p=128), [], [vh], vh)
                passes = [(True, list(range(NB)))]
                if not self.last:
                    passes.append((False, list(range(self.C // 128))))
                for lat, kbl in passes:
                    for qi, qt in enumerate(self.qtiles(lat)):
                        res = {}
                        for m in range(2):
                            def epi(qi_, t0, n, O, Sps, m=m):
                                res[m] = (O, Sps)
                            self.attend(st, [(khm[m], qh, 0, 128)], lambda kb: vh[:, kb, :], 128, kbl, [qt], 64 ** -0.5, True, epi, ptag=m)
                        (qc0, t0, n) = qt
                        for m in range(2):
                            O, Sps = res[m]
                            k.op("dve", lambda e: e.reciprocal(out=ra[m][:, 0:n], in_=Sps[:, 0:n]), [Sps], [ra[m]])
                            k.op("dve", lambda e: e.tensor_tensor(out=aa[m][:, 0:n], in0=O[:, 0:n], in1=ra[m][:, 0:n], op=ALU.mult), [O, ra[m]], [aa[m]])
                        k.op("dve", lambda e: e.scalar_tensor_tensor(out=dd[:, 0:n], in0=aa[1][:, 0:n], scalar=lv[:, 2:3], in1=aa[0][:, 0:n], op0=ALU.mult, op1=ALU.add), [aa[0], aa[1], lv], [dd])
                        k.op("act", lambda e: e.activation(out=aa[0][:, 0:n], in_=dd[:, 0:n], func=AF.Square), [dd], [aa[0]])
                        ps = self.psb[self.sci % 4]
                        self.sci += 1
                        k.op("pe", lambda e: e.matmul(ps[:, 0:n], lhsT=self.C_("ones"), rhs=aa[0][:, 0:n], start=True, stop=True), [self.cst, aa[0]], [ps])
                        self.rstd_from(ps, n, ra[0], 1.0 / 128)
                        y = yb[qi % 2]
                        k.op("dve", lambda e: e.scalar_tensor_tensor(out=y[:, 0:n], in0=dd[:, 0:n], scalar=sg[:, 0:1], in1=ra[0][:, 0:n], op0=ALU.mult, op1=ALU.mult), [dd, sg, ra[0]], [y])
                        k.dma("pool", self.Y.t[1024 + h * 128:1024 + (h + 1) * 128, t0:t0 + n], y[:, 0:n], [y], [], y)
            k.barrier()

    def phase_mla(self, l):
        k = self.k
        W = self.W
        T, NB = self.T, self.NB
        with contextlib.ExitStack() as st:
            cn = self.sb(st, [128, 5, T], BF16, "cn")
            RK = self.sb(st, [32, T], F32, "RK")
            SQPE = self.sb(st, [32, T], F32, "SQPE")
            gkv = self.sb(st, [128, 5], F32, "gkv")
            k.dma("sp", gkv[:, 0:2], W["mla_kv_lora_g"][l].rearrange("(kc p) -> p kc", p=128), [], [gkv], gkv, allow_slow_non_contiguous=True)
            k.dma("sp", gkv[:, 2:5], W["mla_q_lora_g"][l].rearrange("(kc p) -> p kc", p=128), [], [gkv], gkv, allow_slow_non_contiguous=True)
            gk = self.sb(st, [128, 4], F32, "gk")
            for j, nm_ in enumerate(("mla_kn_g", "mla_qn_g")):
                k.dma("sp", gk[0:64, 2 * j:2 * j + 1], W[nm_][l, 0:64].rearrange("(p o) -> p o", o=1), [], [gk], gk, allow_slow_non_contiguous=True)
                k.dma("sp", gk[0:32, 2 * j + 1:2 * j + 2], W[nm_][l, 64:96].rearrange("(p o) -> p o", o=1), [], [gk], gk, allow_slow_non_contiguous=True)
            with contextlib.ExitStack() as st2:
                u_ = [self.sb(st2, [128, 3, 512], F32, "mu") for _ in range(2)]
                sq_ = [self.sb(st2, [128, 3, 512], F32, "msq") for _ in range(2)]
                rs_ = [self.sb(st2, [128, 512], F32, "mrs") for _ in range(2)]
                it = 0
                for (col0, kc_n, dst0, nfeat) in ((MCKV, 2, 0, 256), (MCQ, 3, 2, 384)):
                    r0 = self.urow(col0)
                    for ti, (t0, n) in enumerate(self.tiles):
                        u, sq, rs = u_[it % 2], sq_[it % 2], rs_[it % 2]
                        it += 1
                        k.dma("sp", u[:, 0:kc_n, 0:n], self.U.t[r0:r0 + kc_n * 128, t0:t0 + n].rearrange("(kc p) t -> p kc t", p=128), [], [u], u)
                        k.op("act", lambda e: e.activation(out=sq[:, 0:kc_n, 0:n], in_=u[:, 0:kc_n, 0:n], func=AF.Square), [u], [sq])
                        ps = self.psum()
                        for kc in range(kc_n):
                            k.op("pe", lambda e: e.matmul(ps[:, 0:n], lhsT=self.C_("ones"), rhs=sq[:, kc, 0:n], start=(kc == 0), stop=(kc == kc_n - 1)), [self.cst, sq], [ps])
                        self.rstd_from(ps, n, rs, 1.0 / nfeat)
                        for kc in range(kc_n):
                            k.op("dve", lambda e: e.scalar_tensor_tensor(out=cn[:, dst0 + kc, t0:t0 + n], in0=u[:, kc, 0:n], scalar=gkv[:, dst0 + kc:dst0 + kc + 1], in1=rs[:, 0:n],
                                                                         op0=ALU.mult, op1=ALU.mult), [u, gkv, rs], [cn])
                r0 = self.urow(MKPE)
                kp = self.sb(st2, [32, T], F32, "kp")
                k.dma("sp", kp[:, :], self.U.t[r0:r0 + 32, :], [], [kp], kp)
                k.op("act", lambda e: e.activation(out=SQPE[:, :], in_=kp[:, :], func=AF.Square), [kp], [SQPE])
                k.op("dve", lambda e: e.tensor_scalar(out=kp[:, :], in0=kp[:, :], scalar1=gk[0:32, 1:2], scalar2=None, op0=ALU.mult), [kp, gk], [kp])
                self.rope32(st2, kp, RK)
                k.barrier()
            with contextlib.ExitStack() as st2:
                sq_ = [self.sb(st2, [64, 512], F32, "ksq") for _ in range(2)]
                rs_ = [self.sb(st2, [128, 512], F32, "krs") for _ in range(2)]
                kn_ = [self.sb(st2, [64, 512], BF16, "kn") for _ in range(2)]
                kr_ = [self.sb(st2, [32, 512], BF16, "kr") for _ in range(2)]
                cnt = [0]
                wkv = W["mla_w_ukv"][l].rearrange("(kc p) c -> p kc c", p=128)

                def epik(ch, ti, t0, n, ps, rows):
                    h = ch[0] // 128
                    i = cnt[0] % 2
                    cnt[0] += 1
                    sq, rs, kn, kr = sq_[i], rs_[i], kn_[i], kr_[i]
                    k.op("act", lambda e: e.activation(out=sq[:, 0:n], in_=ps[0:64, 0:n], func=AF.Square), [ps], [sq])
                    p2 = self.psum()
                    k.op("pe", lambda e: e.matmul(p2[:, 0:n], lhsT=self.cst[0:64, 1, :], rhs=sq[0:64, 0:n], start=True, stop=False), [self.cst, sq], [p2])
                    k.op("pe", lambda e: e.matmul(p2[:, 0:n], lhsT=self.cst[0:32, 1, :], rhs=SQPE[0:32, t0:t0 + n], start=False, stop=True), [self.cst, SQPE], [p2])
                    self.rstd_from(p2, n, rs, 1.0 / 96)
                    k.op("dve", lambda e: e.scalar_tensor_tensor(out=kn[:, 0:n], in0=ps[0:64, 0:n], scalar=gk[0:64, 0:1], in1=rs[0:64, 0:n], op0=ALU.mult, op1=ALU.mult), [ps, gk, rs], [kn])
                    k.op("dve", lambda e: e.tensor_tensor(out=kr[:, 0:n], in0=RK[:, t0:t0 + n], in1=rs[0:32, 0:n], op=ALU.mult), [RK, rs], [kr])
                    k.dma("pool", self.KN.t[h * 64:(h + 1) * 64, t0:t0 + n], kn[:, 0:n], [kn], [], kn)
                    k.dma("pool", self.KR.t[h * 32:(h + 1) * 32, t0:t0 + n], kr[:, 0:n], [kr], [], kr)
                self.gemm_fm(st2, cn, 2, lambda c0, c1: wkv[:, :, c0:c1], [(h * 128, h * 128 + 64) for h in range(8)], self.tiles, epik)
                va_ = [self.sb(st2, [128, 8, 65], BF16, "va") for _ in range(2)]
                for v in va_:
                    k.op("dve", lambda e: e.memset(v[:, :, :], 1.0), [], [v])

                def epiv(blk, ps):
                    v = va_[blk % 2]
                    k.op("act", lambda e: e.copy(out=v[:, :, 0:64], in_=ps[:, 0:512].rearrange("p (h d) -> p h d", h=8)), [ps], [v])
                    k.dma("pool", self.VA.t[blk * 128:(blk + 1) * 128, :], v[:, :, :].rearrange("p h d -> p (h d)"), [v], [], v)
                wv5 = W["mla_w_ukv"][l].rearrange("(kc p) (h two d) -> p kc h two d", p=128, two=2, d=64)

                def wsrc(s_):
                    for kc in range(2):
                        k.dma("sp", s_[:, kc, :].rearrange("p (h d) -> p h d", h=8), wv5[:, kc, :, 1, :], [], [s_], s_)
                self.gemm_tm(st2, cn, 2, wsrc, 512, list(range(NB)), epiv)
                k.barrier()
            with contextlib.ExitStack() as st2:
                sqn_ = [self.sb(st2, [64, 512], F32, "qsq") for _ in range(2)]
                sqr_ = [self.sb(st2, [32, 512], F32, "qsr") for _ in range(2)]
                rs_ = [self.sb(st2, [128, 512], F32, "qrs") for _ in range(2)]
                qn_ = [self.sb(st2, [64, 512], BF16, "qn") for _ in range(2)]
                qr_ = [self.sb(st2, [32, 512], BF16, "qr") for _ in range(2)]
                xr_ = [self.sb(st2, [32, 512], F32, "xr") for _ in range(2)]
                ro_ = [self.sb(st2, [32, 512], F32, "ro") for _ in range(2)]
                cs_ = [self.sb(st2, [32, 2, 512], F32, "qcs") for _ in range(2)]
                cnt = [0]
                held = {}
                wq = W["mla_w_uq"][l].rearrange("(kc p) c -> p kc c", p=128)

                def epiq(ch, ti, t0, n, ps, rows):
                    if rows == 64:
                        held["n"] = ps
                        return
                    psn, psr = held["n"], ps
                    h = ch[0] // 96
                    i = cnt[0] % 2
                    cnt[0] += 1
                    sqn, sqr, rs, qn, qr, xr, ro, cs = sqn_[i], sqr_[i], rs_[i], qn_[i], qr_[i], xr_[i], ro_[i], cs_[i]
                    k.dma("sp", cs[:, :, 0:n], self.ropem_in[:, :, t0:t0 + n], [], [cs], cs)
                    k.op("act", lambda e: e.activation(out=sqn[:, 0:n], in_=psn[0:64, 0:n], func=AF.Square), [psn], [sqn])
                    k.op("act", lambda e: e.activation(out=sqr[:, 0:n], in_=psr[0:32, 0:n], func=AF.Square), [psr], [sqr])
                    p2 = self.psum()
                    k.op("pe", lambda e: e.matmul(p2[:, 0:n], lhsT=self.cst[0:64, 1, :], rhs=sqn[0:64, 0:n], start=True, stop=False), [self.cst, sqn], [p2])
                    k.op("pe", lambda e: e.matmul(p2[:, 0:n], lhsT=self.cst[0:32, 1, :], rhs=sqr[0:32, 0:n], start=False, stop=True), [self.cst, sqr], [p2])
                    self.rstd_from(p2, n, rs, 1.0 / 96)
                    k.op("dve", lambda e: e.scalar_tensor_tensor(out=qn[:, 0:n], in0=psn[0:64, 0:n], scalar=gk[0:64, 2:3], in1=rs[0:64, 0:n], op0=ALU.mult, op1=ALU.mult), [psn, gk, rs], [qn])
                    k.op("dve", lambda e: e.tensor_scalar(out=xr[:, 0:n], in0=psr[0:32, 0:n], scalar1=gk[0:32, 3:4], scalar2=None, op0=ALU.mult), [psr, gk], [xr])
                    p3 = self.psum()
                    k.op("pe", lambda e: e.matmul(p3[0:32, 0:n], lhsT=self.cst[0:32, CN.index("perm32"), 0:32], rhs=xr[0:32, 0:n], start=True, stop=True), [self.cst, xr], [p3])
                    k.op("dve", lambda e: e.tensor_tensor(out=ro[:, 0:n], in0=p3[0:32, 0:n], in1=cs[:, 1, 0:n], op=ALU.mult), [p3, cs], [ro])
                    k.op("dve", lambda e: e.tensor_tensor(out=xr[:, 0:n], in0=xr[:, 0:n], in1=cs[:, 0, 0:n], op=ALU.mult), [xr, cs], [xr])
                    k.op("dve", lambda e: e.tensor_tensor(out=xr[:, 0:n], in0=xr[:, 0:n], in1=ro[:, 0:n], op=ALU.add), [xr, ro], [xr])
                    k.op("dve", lambda e: e.tensor_tensor(out=qr[:, 0:n], in0=xr[:, 0:n], in1=rs[0:32, 0:n], op=ALU.mult), [xr, rs], [qr])
                    k.dma("pool", self.QN.t[h * 64:(h + 1) * 64, t0:t0 + n], qn[:, 0:n], [qn], [], qn)
                    k.dma("pool", self.QR.t[h * 32:(h + 1) * 32, t0:t0 + n], qr[:, 0:n], [qr], [], qr)
                chq = []
                for h in range(8):
                    chq += [(h * 96, h * 96 + 64), (h * 96 + 64, h * 96 + 96)]
                self.gemm_fm(st2, Buf(cn.t[:, 2:5, :]), 3, lambda c0, c1: wq[:, :, c0:c1], chq, self.tiles, epiq)
                k.barrier()
        with contextlib.ExitStack() as st:
            self.attn_common(st)
            kqh = self.sb(st, [96, T], BF16, "kqh")
            qqh = self.sb(st, [96, T], BF16, "qqh")
            vah = self.sb(st, [128, NB, 65], BF16, "vah")
            self.vbuf = vah
            osb = [self.sb(st, [65, 512], F32, "osb") for _ in range(2)]
            rr = [self.sb(st, [64, 512], F32, "rr") for _ in range(2)]
            yb = [self.sb(st, [64, 512], BF16, "myb") for _ in range(2)]
            cnt = [0]
            for h in range(8):
                k.dma("sp", kqh[0:64, :], self.KN.t[h * 64:(h + 1) * 64, :], [], [kqh], kqh)
                k.dma("sp", kqh[64:96, :], self.KR.t[h * 32:(h + 1) * 32, :], [], [kqh], kqh)
                k.dma("sp", qqh[0:64, :], self.QN.t[h * 64:(h + 1) * 64, :], [], [qqh], qqh)
                k.dma("sp", qqh[64:96, :], self.QR.t[h * 32:(h + 1) * 32, :], [], [qqh], qqh)
                k.dma("sp", vah[:, :, :], self.VA.t[:, h * 65:(h + 1) * 65].rearrange("(b p) d -> p b d", p=128), [], [vah], vah)

                def epi(qi, t0, n, O, Sps):
                    i = cnt[0] % 2
                    cnt[0] += 1
                    o, r, y = osb[i], rr[i], yb[i]
                    k.op("act", lambda e: e.copy(out=o[0:65, 0:n], in_=O[0:65, 0:n]), [O], [o])
                    ps = self.psb[self.sci % 4]
                    self.sci += 1
                    k.op("pe", lambda e: e.matmul(ps[0:64, 0:n], lhsT=self.cst[0:65, CN.index("sel65"), 0:64], rhs=o[0:65, 0:n], start=True, stop=True), [self.cst, o], [ps])
                    k.op("dve", lambda e: e.reciprocal(out=r[:, 0:n], in_=ps[0:64, 0:n]), [ps], [r])
                    k.op("dve", lambda e: e.tensor_tensor(out=y[:, 0:n], in0=o[0:64, 0:n], in1=r[:, 0:n], op=ALU.mult), [o, r], [y])
                    k.dma("pool", self.Y.t[512 + h * 64:512 + (h + 1) * 64, t0:t0 + n], y[:, 0:n], [y], [], y)
                terms = [(kqh, qqh, 0, 96)]
                self.attend(st, terms, lambda kb: vah[:, kb, :], 65, list(range(NB)), self.qtiles(True), 96 ** -0.5, False, epi)
                if not self.last:
                    self.attend(st, terms, lambda kb: vah[:, kb, :], 65, list(range(self.C // 128)), self.qtiles(False), 96 ** -0.5, False, epi)
            k.barrier()

    def rope32(self, st, xin, dst):
        k = self.k
        cs_ = [self.sb(st, [32, 2, 512], F32, "rcs") for _ in range(2)]
        ro_ = [self.sb(st, [32, 512], F32, "rro") for _ in range(2)]
        for ti, (t0, n) in enumerate(self.tiles):
            cs, ro = cs_[ti % 2], ro_[ti % 2]
            k.dma("sp", cs[:, :, 0:n], self.ropem_in[:, :, t0:t0 + n], [], [cs], cs)
            ps = self.psum()
            k.op("pe", lambda e: e.matmul(ps[0:32, 0:n], lhsT=self.cst[0:32, CN.index("perm32"), 0:32], rhs=xin[0:32, t0:t0 + n], start=True, stop=True), [self.cst, xin], [ps])
            k.op("dve", lambda e: e.tensor_tensor(out=ro[:, 0:n], in0=ps[0:32, 0:n], in1=cs[:, 1, 0:n], op=ALU.mult), [ps, cs], [ro])
            k.op("dve", lambda e: e.tensor_tensor(out=dst[:, t0:t0 + n], in0=xin[0:32, t0:t0 + n], in1=cs[:, 0, 0:n], op=ALU.mult), [xin, cs], [dst])
            k.op("dve", lambda e: e.tensor_tensor(out=dst[:, t0:t0 + n], in0=dst[:, t0:t0 + n], in1=ro[:, 0:n], op=ALU.add), [dst, ro], [dst])

    def conv_silu(self, st, u, acc, wc, bias, out):
        k = self.k
        C, T = self.C, self.T
        k.op("dve", lambda e: e.tensor_scalar(out=acc[:, :], in0=u[:, :], scalar1=wc[:, 2:3], scalar2=None, op0=ALU.mult), [u, wc], [acc])
        for j in (0, 1, 3, 4):
            s = j - 2
            for (s0, s1) in ((0, C), (C, T)):
                a = max(s0, s0 - s)
                b = min(s1, s1 - s)
                k.op("dve", lambda e: e.scalar_tensor_tensor(out=acc[:, a:b], in0=u[:, a + s:b + s], scalar=wc[:, j:j + 1], in1=acc[:, a:b], op0=ALU.mult, op1=ALU.add), [u, wc, acc], [acc])
        if bias is None:
            k.op("act", lambda e: e.activation(out=out, in_=acc[:, :], func=AF.Silu), [acc], [acc])
        else:
            k.op("act", lambda e: e.activation(out=out, in_=acc[:, :], func=AF.Silu, bias=bias, scale=1.0), [acc, wc], [acc])

    def softplus(self, st, xb, xap, n):
        k = self.k
        t = self.sb(st, [128, n], F32, "spt")

        class _X:
            def __getitem__(s_, key):
                return xap
        x = _X()
        k.op("act", lambda e: e.activation(out=t[:, :], in_=xap, func=AF.Abs), [xb], [t])
        k.op("act", lambda e: e.activation(out=t[:, :], in_=t[:, :], func=AF.Exp, scale=-1.0), [t], [t])
        k.op("act", lambda e: e.activation(out=t[:, :], in_=t[:, :], func=AF.Ln, bias=self.epsb[:, 1:2], scale=1.0), [t, self.epsb], [t])
        k.op("dve", lambda e: e.scalar_tensor_tensor(out=xap, in0=xap, scalar=0.0, in1=t[:, :], op0=ALU.max, op1=ALU.add), [xb, t], [xb])

    def tok_scalars(self, st, col0, ncols, dst):
        k = self.k
        r0 = self.urow(col0)
        raw = self.sb(st, [ncols, self.T], F32, "tsr")
        k.dma("sp", raw[:, :], self.U.t[r0:r0 + ncols, :], [], [raw], raw)
        for blk in range(self.NB):
            ps = self.psum()
            k.op("pe", lambda e: e.transpose(out=ps[:, 0:ncols], in_=raw[0:ncols, blk * 128:(blk + 1) * 128], identity=self.cst[0:ncols, 0, 0:ncols]), [raw, self.cst], [ps])
            k.op("act", lambda e: e.copy(out=dst[:, blk, :], in_=ps[:, 0:ncols]), [ps], [dst])

    def blk_order(self, d):
        nc_ = self.C // 128
        if d == 0:
            return list(range(self.NB))
        return list(range(nc_ - 1, -1, -1)) + list(range(self.NB - 1, nc_ - 1, -1))

    def phase_ssm(self, l):
        k = self.k
        W = self.W
        T, NB = self.T, self.NB
        with contextlib.ExitStack() as st:
            XTOK = self.sb(st, [128, NB, 512], F32, "XTOK")
            BT = self.sb(st, [128, 2, T], BF16, "BT")
            CT = self.sb(st, [128, 2, T], BF16, "CT")
            BTOK = self.sb(st, [128, NB, 2, 128], BF16, "BTOK")
            DT = self.sb(st, [128, NB, 16], F32, "DT")
            DA = self.sb(st, [128, NB, 16], F32, "DA")
            ACUM = self.sb(st, [128, NB, 16], F32, "ACUM")
            ATOT = self.sb(st, [128, NB, 16], F32, "ATOT")
            CD = self.sb(st, [128, NB, 16], F32, "CD")
            DTDS = self.sb(st, [128, NB, 16], F32, "DTDS")
            with contextlib.ExitStack() as st2:
                wc_ = [self.sb(st2, [128, 6], F32, "swc") for _ in range(2)]
                st2a = contextlib.ExitStack()
                u_ = [self.sb(st2a, [128, T], F32, "su") for _ in range(1)] * 2
                acc_ = [self.sb(st2a, [128, T], F32, "sacc") for _ in range(1)] * 2
                for c in range(8):
                    u, acc, wc = u_[c % 2], acc_[c % 2], wc_[c % 2]
                    r0 = self.urow(SX + c * 128)
                    k.dma("sp", u[:, :], self.U.t[r0:r0 + 128, :], [], [u], u)
                    k.dma("sp", wc[:, 0:5], W["ssm_conv"][l][:, c * 128:(c + 1) * 128].rearrange("j c -> c j"), [], [wc], wc, allow_slow_non_contiguous=True)
                    k.dma("sp", wc[:, 5:6], W["ssm_conv_b"][l, c * 128:(c + 1) * 128].rearrange("(p o) -> p o", o=1), [], [wc], wc, allow_slow_non_contiguous=True)
                    if c < 4:
                        self.conv_silu(st2, u, acc, wc, wc[:, 5:6], acc[:, :])
                        k.dma("pool", self.XS.t[c * 128:(c + 1) * 128, :], acc[:, :], [acc], [], acc)
                        for blk in range(NB):
                            ps = self.psum()
                            k.op("pe", lambda e: e.transpose(out=ps[:, 0:128], in_=acc[:, blk * 128:(blk + 1) * 128], identity=self.C_("ident")), [acc, self.cst], [ps])
                            k.op("act", lambda e: e.copy(out=XTOK[:, blk, c * 128:(c + 1) * 128], in_=ps[:, 0:128]), [ps], [XTOK])
                    elif c < 6:
                        g = c - 4
                        self.conv_silu(st2, u, acc, wc, wc[:, 5:6], acc[:, :])
                        k.op("dve", lambda e: e.tensor_copy(out=BT[:, g, :], in_=acc[:, :]), [acc], [BT])
                        for blk in range(NB):
                            ps = self.psum()
                            k.op("pe", lambda e: e.transpose(out=ps[:, 0:128], in_=acc[:, blk * 128:(blk + 1) * 128], identity=self.C_("ident")), [acc, self.cst], [ps])
                            k.op("act", lambda e: e.copy(out=BTOK[:, blk, g, :], in_=ps[:, 0:128]), [ps], [BTOK])
                    else:
                        g = c - 6
                        self.conv_silu(st2, u, acc, wc, wc[:, 5:6], acc[:, :])
                        k.op("dve", lambda e: e.tensor_copy(out=CT[:, g, :], in_=acc[:, :]), [acc], [CT])
                k.barrier()
                st2a.close()
                self.tok_scalars(st2, SDT, 16, DT)
                pb = self.sb(st2, [128, 2, 16], F32, "spb")
                k.dma("sp", pb[:, 0, :], W["ssm_dt_bias"][l:l + 1].rearrange("o a b -> o (a b)").partition_broadcast(128), [], [pb], pb)
                k.dma("sp", pb[:, 1, :], W["ssm_a_log"][l:l + 1].rearrange("o a b -> o (a b)").partition_broadcast(128), [], [pb], pb)
                k.op("dve", lambda e: e.tensor_tensor(out=DT[:, :, :], in0=DT[:, :, :], in1=pb[:, 0, :].unsqueeze(1).to_broadcast([128, NB, 16]), op=ALU.add), [DT, pb], [DT])
                self.softplus(st2, DT, DT.t[:, :, :].rearrange("p b c -> p (b c)"), NB * 16)
                k.op("act", lambda e: e.activation(out=pb[:, 1, :], in_=pb[:, 1, :], func=AF.Exp), [pb], [pb])
                k.op("dve", lambda e: e.scalar_tensor_tensor(out=DA[:, :, :], in0=DT[:, :, :], scalar=-1.0, in1=pb[:, 1, :].unsqueeze(1).to_broadcast([128, NB, 16]), op0=ALU.mult, op1=ALU.mult), [DT, pb], [DA])
                self.cum_stats(st2, DA, ACUM, ATOT, 8)
                k.op("act", lambda e: e.activation(out=CD[:, :, :], in_=ATOT[:, :, :], func=AF.Exp), [ATOT], [CD])
                k.op("dve", lambda e: e.tensor_tensor(out=DTDS[:, :, :], in0=ATOT[:, :, :], in1=ACUM[:, :, :], op=ALU.subtract), [ATOT, ACUM], [DTDS])
                k.op("act", lambda e: e.activation(out=DTDS[:, :, :], in_=DTDS[:, :, :], func=AF.Exp), [DTDS], [DTDS])
                k.op("dve", lambda e: e.tensor_tensor(out=DTDS[:, :, :], in0=DTDS[:, :, :], in1=DT[:, :, :], op=ALU.mult), [DTDS, DT], [DTDS])
                k.barrier()
            ST = self.sb(st, [128, 4, 4, 64], F32, "ST")
            STb = self.sb(st, [128, 4, 4, 64], BF16, "STb")
            k.op("dve", lambda e: e.memset(ST[:, :, :, :], 0.0), [], [ST])
            k.op("dve", lambda e: e.memset(STb[:, :, :, :], 0.0), [], [STb])
            R = 2
            rhsb = [self.sb(st, [128, 4, 128], F32, "srhs") for _ in range(R)]
            Dm = [self.sb(st, [128, 4, 128], F32, "sD") for _ in range(R)]
            LT = [self.sb(st, [128, 4, 128], F32, "sLT") for _ in range(R)]
            RE = [self.sb(st, [128, 4, 128], F32, "sRE") for _ in range(R)]
            WT = [self.sb(st, [128, 4, 128], BF16, "sWT") for _ in range(R)]
            CdT = [self.sb(st, [128, 4, 128], BF16, "sCd") for _ in range(R)]
            xdt = [self.sb(st, [128, 4, 64], BF16, "sxdt") for _ in range(R)]
            xdd = [self.sb(st, [128, 4, 64], BF16, "sxdd") for _ in range(R)]
            SCs = [self.sb(st, [128, 128], F32, "sSC") for _ in range(R)]
            yo = [self.sb(st, [64, 4, 128], F32, "syo") for _ in range(R)]
            STs = {(d, g): Buf(None) for d in range(2) for g in range(2)}
            it = 0
            orders = [self.blk_order(0), self.blk_order(1)]
            for i in range(NB):
                for d in range(2):
                    blk = orders[d][i]
                    tri = self.C_("triF" if d == 0 else "triB")
                    mneg = self.C_("mnegF" if d == 0 else "mnegB")
                    for g in range(2):
                        j = it % R
                        it += 1
                        ch = d * 2 + g
                        sbuf_ = STs[(d, g)]
                        hs = slice(d * 8 + g * 4, d * 8 + g * 4 + 4)
                        k.op("dve", lambda e: e.tensor_tensor(out=rhsb[j][:, :, :], in0=tri.unsqueeze(1).to_broadcast([128, 4, 128]),
                                                              in1=DA[:, blk, hs].unsqueeze(2).to_broadcast([128, 4, 128]), op=ALU.mult), [self.cst, DA], [rhsb[j]])
                        pa = self.psum()
                        k.op("pe", lambda e: e.matmul(pa[:, :], lhsT=self.C_("ones"), rhs=rhsb[j][:, :, :].rearrange("p h c -> p (h c)"), start=True, stop=True), [self.cst, rhsb[j]], [pa])
                        for h in range(4):
                            k.op("dve", lambda e: e.scalar_tensor_tensor(out=Dm[j][:, h, :], in0=pa[:, h * 128:(h + 1) * 128], scalar=ACUM[:, blk, d * 8 + g * 4 + h:d * 8 + g * 4 + h + 1],
                                                                         in1=mneg, op0=ALU.subtract, op1=ALU.add), [pa, ACUM, self.cst], [Dm[j]])
                        k.op("act", lambda e: e.activation(out=LT[j][:, :, :], in_=Dm[j][:, :, :], func=AF.Exp), [Dm[j]], [LT[j]])
                        k.op("act", lambda e: e.activation(out=RE[j][:, :, :].rearrange("p h c -> p (h c)"), in_=pa[:, :], func=AF.Exp), [pa], [RE[j]])
                        psc = self.psum()
                        k.op("pe", lambda e: e.matmul(psc[:, 0:128], lhsT=BT[:, g, blk * 128:(blk + 1) * 128], rhs=CT[:, g, blk * 128:(blk + 1) * 128], start=True, stop=True), [BT, CT], [psc])
                        k.op("act", lambda e: e.copy(out=SCs[j][:, :], in_=psc[:, 0:128]), [psc], [SCs[j]])
                        k.op("dve", lambda e: e.tensor_tensor(out=WT[j][:, :, :], in0=LT[j][:, :, :], in1=SCs[j][:, :].unsqueeze(1).to_broadcast([128, 4, 128]), op=ALU.mult), [LT[j], SCs[j]], [WT[j]])
                        k.op("dve", lambda e: e.tensor_tensor(out=CdT[j][:, :, :], in0=RE[j][:, :, :], in1=CT[:, g, blk * 128:(blk + 1) * 128].unsqueeze(1).to_broadcast([128, 4, 128]), op=ALU.mult), [RE[j], CT], [CdT[j]])
                        xv = XTOK[:, blk, g * 256:(g + 1) * 256].rearrange("p (h q) -> p h q", h=4)
                        k.op("dve", lambda e: e.tensor_tensor(out=xdt[j][:, :, :], in0=xv, in1=DT[:, blk, hs].unsqueeze(2).to_broadcast([128, 4, 64]), op=ALU.mult), [XTOK, DT], [xdt[j]])
                        k.op("dve", lambda e: e.tensor_tensor(out=xdd[j][:, :, :], in0=xv, in1=DTDS[:, blk, hs].unsqueeze(2).to_broadcast([128, 4, 64]), op=ALU.mult), [XTOK, DTDS], [xdd[j]])
                        py = self.psum()
                        for h in range(4):
                            k.op("pe", lambda e: e.matmul(py[0:64, h * 128:(h + 1) * 128], lhsT=xdt[j][:, h, :], rhs=WT[j][:, h, :], start=True, stop=False), [xdt[j], WT[j]], [py])
                            k.op("pe", lambda e: e.matmul(py[0:64, h * 128:(h + 1) * 128], lhsT=STb[:, ch, h, :], rhs=CdT[j][:, h, :], start=False, stop=True), [sbuf_, CdT[j]], [py])
                        k.op("act", lambda e: e.copy(out=yo[j][:, :, :].rearrange("p h c -> p (h c)"), in_=py[0:64, :]), [py], [yo[j]])
                        k.dma("pool", self.YS.t[d, g * 256:(g + 1) * 256, blk * 128:(blk + 1) * 128].rearrange("(h p) c -> p h c", p=64), yo[j][:, :, :], [yo[j]], [], yo[j])
                        pst = self.psum()
                        for h in range(4):
                            k.op("pe", lambda e: e.matmul(pst[:, h * 64:(h + 1) * 64], lhsT=BTOK[:, blk, g, :], rhs=xdd[j][:, h, :], start=True, stop=True), [BTOK, xdd[j]], [pst])
                        k.op("dve", lambda e: e.tensor_tensor(out=ST[:, ch, :, :], in0=ST[:, ch, :, :], in1=CD[:, blk, hs].unsqueeze(2).to_broadcast([128, 4, 64]), op=ALU.mult), [sbuf_, CD], [sbuf_])
                        k.op("dve", lambda e: e.tensor_tensor(out=ST[:, ch, :, :], in0=ST[:, ch, :, :], in1=pst[:, 0:256].rearrange("p (h q) -> p h q", h=4), op=ALU.add), [sbuf_, pst], [sbuf_])
                        k.op("act", lambda e: e.copy(out=STb[:, ch, :, :], in_=ST[:, ch, :, :]), [sbuf_], [sbuf_])
            k.barrier()
        with contextlib.ExitStack() as st:
            dsk = self.sb(st, [128, 4], F32, "dsk")
            gn = self.sb(st, [128, 4], F32, "sgn")
            for c in range(4):
                for hh in range(2):
                    k.dma("sp", dsk[hh * 64:(hh + 1) * 64, c:c + 1], W["ssm_d"][l:l + 1, 2 * c + hh:2 * c + hh + 1].partition_broadcast(64), [], [dsk], dsk)
            k.dma("sp", gn[:, :], W["ssm_norm_g"][l].rearrange("(c p) -> p c", p=128), [], [gn], gn, allow_slow_non_contiguous=True)
            ya = [self.sb(st, [128, 2, 512], F32, "fya") for _ in range(2)]
            yb_ = [self.sb(st, [128, 2, 512], F32, "fyb") for _ in range(2)]
            xs_ = [self.sb(st, [128, 2, 512], F32, "fxs") for _ in range(2)]
            z_ = [self.sb(st, [128, 2, 512], F32, "fz") for _ in range(2)]
            sq_ = [self.sb(st, [128, 2, 512], F32, "fsq") for _ in range(2)]
            rs_ = [self.sb(st, [128, 512], F32, "frs") for _ in range(2)]
            ob_ = [self.sb(st, [128, 2, 512], BF16, "fob") for _ in range(2)]
            it = 0
            rz = self.urow(SZ)
            tiles = self.tiles[1:] if self.last else self.tiles
            for gi in range(2):
                for ti, (t0, n) in enumerate(tiles):
                    j = it % 2
                    it += 1
                    r0 = gi * 256
                    v = lambda ap: ap.rearrange("(c p) t -> p c t", p=128)
                    k.dma("sp", ya[j][:, :, 0:n], v(self.YS.t[0, r0:r0 + 256, t0:t0 + n]), [], [ya[j]], ya[j])
                    k.dma("sp", yb_[j][:, :, 0:n], v(self.YS.t[1, r0:r0 + 256, t0:t0 + n]), [], [yb_[j]], yb_[j])
                    k.dma("sp", xs_[j][:, :, 0:n], v(self.XS.t[r0:r0 + 256, t0:t0 + n]), [], [xs_[j]], xs_[j])
                    k.dma("sp", z_[j][:, :, 0:n], v(self.U.t[rz + r0:rz + r0 + 256, t0:t0 + n]), [], [z_[j]], z_[j])
                    k.op("dve", lambda e: e.tensor_tensor(out=ya[j][:, :, 0:n], in0=ya[j][:, :, 0:n], in1=yb_[j][:, :, 0:n], op=ALU.add), [ya[j], yb_[j]], [ya[j]])
                    k.op("act", lambda e: e.activation(out=z_[j][:, :, 0:n], in_=z_[j][:, :, 0:n], func=AF.Silu), [z_[j]], [z_[j]])
                    for cc in range(2):
                        c = gi * 2 + cc
                        k.op("dve", lambda e: e.scalar_tensor_tensor(out=ya[j][:, cc, 0:n], in0=xs_[j][:, cc, 0:n], scalar=dsk[:, c:c + 1], in1=ya[j][:, cc, 0:n], op0=ALU.mult, op1=ALU.add), [xs_[j], dsk, ya[j]], [ya[j]])
                    k.op("dve", lambda e: e.tensor_tensor(out=ya[j][:, :, 0:n], in0=ya[j][:, :, 0:n], in1=z_[j][:, :, 0:n], op=ALU.mult), [ya[j], z_[j]], [ya[j]])
                    k.op("act", lambda e: e.activation(out=sq_[j][:, :, 0:n], in_=ya[j][:, :, 0:n], func=AF.Square), [ya[j]], [sq_[j]])
                    ps = self.psum()
                    for cc in range(2):
                        k.op("pe", lambda e: e.matmul(ps[:, 0:n], lhsT=self.C_("ones"), rhs=sq_[j][:, cc, 0:n], start=(cc == 0), stop=(cc == 1)), [self.cst, sq_[j]], [ps])
                    self.rstd_from(ps, n, rs_[j], 1.0 / 256)
                    for cc in range(2):
                        c = gi * 2 + cc
                        k.op("dve", lambda e: e.scalar_tensor_tensor(out=ob_[j][:, cc, 0:n], in0=ya[j][:, cc, 0:n], scalar=gn[:, c:c + 1], in1=rs_[j][:, 0:n], op0=ALU.mult, op1=ALU.mult), [ya[j], gn, rs_[j]], [ob_[j]])
                    k.dma("pool", v(self.Y.t[1536 + r0:1536 + r0 + 256, t0:t0 + n]), ob_[j][:, :, 0:n], [ob_[j]], [], ob_[j])
            k.barrier()

    def cum_stats(self, st, G, GCUM, GTOT, nh):
        k = self.k
        NB = self.NB
        for d in range(2):
            tri = self.C_("triF" if d == 0 else "triB")
            ps = self.psum()
            k.op("pe", lambda e: e.matmul(ps[:, 0:NB * nh].rearrange("p (b h) -> p b h", h=nh), lhsT=tri, rhs=G[:, :, d * nh:(d + 1) * nh], start=True, stop=True), [self.cst, G], [ps])
            k.op("act", lambda e: e.copy(out=GCUM[:, :, d * nh:(d + 1) * nh], in_=ps[:, 0:NB * nh].rearrange("p (b h) -> p b h", h=nh)), [ps], [GCUM])
            ps2 = self.psum()
            k.op("pe", lambda e: e.matmul(ps2[:, 0:NB * nh].rearrange("p (b h) -> p b h", h=nh), lhsT=self.C_("ones"), rhs=G[:, :, d * nh:(d + 1) * nh], start=True, stop=True), [self.cst, G], [ps2])
            k.op("dve", lambda e: e.tensor_copy(out=GTOT[:, :, d * nh:(d + 1) * nh], in_=ps2[:, 0:NB * nh].rearrange("p (b h) -> p b h", h=nh)), [ps2], [GTOT])

    def phase_gdn(self, l):
        k = self.k
        W = self.W
        T, NB = self.T, self.NB
        with contextlib.ExitStack() as st:
            QT = self.sb(st, [128, 4, T], BF16, "gQT")
            KT = self.sb(st, [128, 4, T], BF16, "gKT")
            KTOK = self.sb(st, [128, NB, 4, 128], BF16, "gKTOK")
            VTOK = self.sb(st, [128, NB, 4, 128], BF16, "gVTOK")
            G = self.sb(st, [128, NB, 8], F32, "gG")
            GCUM = self.sb(st, [128, NB, 8], F32, "gGCUM")
            GTOT = self.sb(st, [128, NB, 8], F32, "gGTOT")
            EG = self.sb(st, [128, NB, 8], F32, "gEG")
            GL = self.sb(st, [128, NB, 8], F32, "gGL")
            KDS = self.sb(st, [128, NB, 8], F32, "gKDS")
            BETA = self.sb(st, [128, NB, 8], F32, "gBETA")
            NBETA = self.sb(st, [128, NB, 8], F32, "gNBETA")
            with contextlib.ExitStack() as st2:
                wc_ = [self.sb(st2, [128, 6], F32, "gwc") for _ in range(2)]
                sq_ = [self.sb(st2, [128, 512], F32, "gsq") for _ in range(2)]
                rs_ = [self.sb(st2, [128, 512], F32, "grs") for _ in range(2)]
                st2a = contextlib.ExitStack()
                u_ = [self.sb(st2a, [128, T], F32, "gu") for _ in range(1)] * 2
                acc_ = [self.sb(st2a, [128, T], F32, "gacc") for _ in range(1)] * 2
                it = 0
                for c in range(12):
                    u, acc, wc = u_[c % 2], acc_[c % 2], wc_[c % 2]
                    r0 = self.urow(c * 128)
                    kind, h = c // 4, c % 4
                    k.dma("sp", u[:, :], self.U.t[r0:r0 + 128, :], [], [u], u)
                    k.dma("sp", wc[:, 0:5], W["gdn_conv"][l][:, c * 128:(c + 1) * 128].rearrange("j c -> c j"), [], [wc], wc, allow_slow_non_contiguous=True)
                    self.conv_silu(st2, u, acc, wc, None, acc[:, :])
                    if kind < 2:
                        dstT = QT if kind == 0 else KT
                        for ti, (t0, n) in enumerate(self.tiles):
                            sq, rs = sq_[it % 2], rs_[it % 2]
                            it += 1
                            k.op("act", lambda e: e.activation(out=sq[:, 0:n], in_=acc[:, t0:t0 + n], func=AF.Square), [acc], [sq])
                            ps = self.psum()
                            k.op("pe", lambda e: e.matmul(ps[:, 0:n], lhsT=self.C_("ones"), rhs=sq[:, 0:n], start=True, stop=True), [self.cst, sq], [ps])
                            self.rstd_from(ps, n, rs, 1.0)
                            if kind == 0:
                                k.op("dve", lambda e: e.scalar_tensor_tensor(out=dstT[:, h, t0:t0 + n], in0=acc[:, t0:t0 + n], scalar=128.0 ** -0.5, in1=rs[:, 0:n], op0=ALU.mult, op1=ALU.mult), [acc, rs], [dstT])
                            else:
                                k.op("dve", lambda e: e.tensor_tensor(out=acc[:, t0:t0 + n], in0=acc[:, t0:t0 + n], in1=rs[:, 0:n], op=ALU.mult), [acc, rs], [acc])
                                k.op("act", lambda e: e.copy(out=dstT[:, h, t0:t0 + n], in_=acc[:, t0:t0 + n]), [acc], [dstT])
                    if kind >= 1:
                        dst = KTOK if kind == 1 else VTOK
                        for blk in range(NB):
                            ps = self.psum()
                            k.op("pe", lambda e: e.transpose(out=ps[:, 0:128], in_=acc[:, blk * 128:(blk + 1) * 128], identity=self.C_("ident")), [acc, self.cst], [ps])
                            k.op("act", lambda e: e.copy(out=dst[:, blk, h, :], in_=ps[:, 0:128]), [ps], [dst])
                k.barrier()
                st2a.close()
                AB = self.sb(st2, [128, NB, 16], F32, "gAB")
                self.tok_scalars(st2, GA, 16, AB)
                pb = self.sb(st2, [128, 2, 8], F32, "gpb")
                k.dma("sp", pb[:, 0, :], W["gdn_dt_bias"][l:l + 1].rearrange("o a b -> o (a b)").partition_broadcast(128), [], [pb], pb)
                k.dma("sp", pb[:, 1, :], W["gdn_a_log"][l:l + 1].rearrange("o a b -> o (a b)").partition_broadcast(128), [], [pb], pb)
                k.op("dve", lambda e: e.tensor_tensor(out=G[:, :, :], in0=AB[:, :, 0:8], in1=pb[:, 0, :].unsqueeze(1).to_broadcast([128, NB, 8]), op=ALU.add), [AB, pb], [G])
                self.softplus(st2, G, G.t[:, :, :].rearrange("p b c -> p (b c)"), NB * 8)
                k.op("act", lambda e: e.activation(out=pb[:, 1, :], in_=pb[:, 1, :], func=AF.Exp), [pb], [pb])
                k.op("dve", lambda e: e.scalar_tensor_tensor(out=G[:, :, :], in0=G[:, :, :], scalar=-1.0, in1=pb[:, 1, :].unsqueeze(1).to_broadcast([128, NB, 8]), op0=ALU.mult, op1=ALU.mult), [G, pb], [G])
                k.op("act", lambda e: e.activation(out=BETA[:, :, :], in_=AB[:, :, 8:16], func=AF.Sigmoid), [AB], [BETA])
                k.op("dve", lambda e: e.tensor_scalar(out=NBETA[:, :, :], in0=BETA[:, :, :], scalar1=-1.0, scalar2=None, op0=ALU.mult), [BETA], [NBETA])
                self.cum_stats(st2, G, GCUM, GTOT, 4)
                k.op("act", lambda e: e.activation(out=EG[:, :, :], in_=GCUM[:, :, :], func=AF.Exp), [GCUM], [EG])
                k.op("act", lambda e: e.activation(out=GL[:, :, :], in_=GTOT[:, :, :], func=AF.Exp), [GTOT], [GL])
                k.op("dve", lambda e: e.tensor_tensor(out=KDS[:, :, :], in0=GTOT[:, :, :], in1=GCUM[:, :, :], op=ALU.subtract), [GTOT, GCUM], [KDS])
                k.op("act", lambda e: e.activation(out=KDS[:, :, :], in_=KDS[:, :, :], func=AF.Exp), [KDS], [KDS])
                k.barrier()
            S_ = self.sb(st, [128, 8, 128], F32, "gS")
            Sb = self.sb(st, [128, 8, 128], BF16, "gSb")
            k.op("dve", lambda e: e.memset(S_[:, :, :], 0.0), [], [S_])
            k.op("dve", lambda e: e.memset(Sb[:, :, :], 0.0), [], [Sb])
            R = 2
            mk = lambda p, dt=F32: [self.sb(st, [128, 128], dt, p) for _ in range(R)]
            rhsb, Dm, E, RE, t1 = mk("grh"), mk("gDm"), mk("gE"), mk("gRE"), mk("gt1")
            AttnT, QgT, Kd, Xb, Rp, vnew = mk("gAt", BF16), mk("gQg", BF16), mk("gKd", BF16), mk("gXb", BF16), mk("gRp", BF16), mk("gvn", BF16)
            Pk = [mk("gP%d" % i) for i in range(1)]
            PTk = [mk("gPT%d" % i) for i in range(1)]
            XTb, CTb, Cb, Zb, Z2b = mk("gXTb", BF16), mk("gCTb", BF16), mk("gCb", BF16), mk("gZb", BF16), mk("gZ2b", BF16)
            GM = self.sb(st, [128, 14, 128], F32, "gGM")
            k.dma("sp", GM[:, :, :], self.gmask_in, [], [GM], GM)
            X = mk("gX")
            oo = mk("goo")
            Sbufs = {(h, d): Buf(None) for h in range(4) for d in range(2)}
            ident = self.C_("ident")
            orders = [self.blk_order(0), self.blk_order(1)]
            it = 0
            evi = [0]

            def evac(out_ap, ps_ap, rd, wr):
                evi[0] += 1
                if evi[0] % 2:
                    k.op("act", lambda e: e.copy(out=out_ap, in_=ps_ap), rd, wr)
                else:
                    k.op("dve", lambda e: e.tensor_copy(out=out_ap, in_=ps_ap), rd, wr)
            for i in range(0 if "gdn_noscan" in self.dbg else NB):
                for h in range(4):
                    for d in range(2):
                        blk = orders[d][i]
                        j = it % R
                        it += 1
                        ci = d * 4 + h
                        sb_ = Sbufs[(h, d)]
                        tri = self.C_("triF" if d == 0 else "triB")
                        mneg = self.C_("mnegF" if d == 0 else "mnegB")
                        strict = self.C_("strF" if d == 0 else "strB")
                        cs = slice(blk * 128, (blk + 1) * 128)
                        sc = lambda Tn: Tn[:, blk, ci:ci + 1]
                        k.op("dve", lambda e: e.tensor_scalar(out=rhsb[j][:, :], in0=tri, scalar1=sc(G), scalar2=None, op0=ALU.mult), [self.cst, G], [rhsb[j]])
                        pa = self.psum()
                        k.op("pe", lambda e: e.matmul(pa[:, 0:128], lhsT=self.C_("ones"), rhs=rhsb[j][:, :], start=True, stop=True), [self.cst, rhsb[j]], [pa])
                        k.op("dve", lambda e: e.scalar_tensor_tensor(out=Dm[j][:, :], in0=pa[:, 0:128], scalar=sc(GCUM), in1=mneg, op0=ALU.subtract, op1=ALU.add), [pa, GCUM, self.cst], [Dm[j]])
                        k.op("act", lambda e: e.activation(out=E[j][:, :], in_=Dm[j][:, :], func=AF.Exp), [Dm[j]], [E[j]])
                        k.op("act", lambda e: e.activation(out=RE[j][:, :], in_=pa[:, 0:128], func=AF.Exp), [pa], [RE[j]])
                        pA = self.psum()
                        k.op("pe", lambda e: e.matmul(pA[:, 0:128], lhsT=KT[:, h, cs], rhs=KT[:, h, cs], start=True, stop=True), [KT], [pA])
                        k.op("pe", lambda e: e.matmul(pA[:, 128:256], lhsT=KT[:, h, cs], rhs=QT[:, h, cs], start=True, stop=True), [KT, QT], [pA])
                        k.op("dve", lambda e: e.scalar_tensor_tensor(out=t1[j][:, :], in0=pA[:, 0:128], scalar=sc(BETA), in1=E[j][:, :], op0=ALU.mult, op1=ALU.mult), [pA, BETA, E[j]], [t1[j]])
                        P0, PT0 = Pk[0][j], PTk[0][j]
                        k.op("dve", lambda e: e.tensor_tensor(out=P0[:, :], in0=t1[j][:, :], in1=strict, op=ALU.mult), [t1[j], self.cst], [P0])
                        k.op("dve", lambda e: e.tensor_tensor(out=AttnT[j][:, :], in0=pA[:, 128:256], in1=E[j][:, :], op=ALU.mult), [pA, E[j]], [AttnT[j]])
                        k.op("dve", lambda e: e.tensor_tensor(out=QgT[j][:, :], in0=QT[:, h, cs], in1=RE[j][:, :], op=ALU.mult), [QT, RE[j]], [QgT[j]])
                        k.op("dve", lambda e: e.tensor_scalar(out=Kd[j][:, :], in0=KTOK[:, blk, h, :], scalar1=sc(KDS), scalar2=None, op0=ALU.mult), [KTOK, KDS], [Kd[j]])
                        pt = self.psum()
                        k.op("pe", lambda e: e.transpose(out=pt[:, 0:128], in_=P0[:, :], identity=ident), [P0, self.cst], [pt])
                        evac(PT0[:, :], pt[:, 0:128], [pt], [PT0])
                        mC = lambda lv: GM[:, (0 if d == 0 else 7) + lv, :]
                        mCT = lambda lv: GM[:, (7 if d == 0 else 0) + lv, :]
                        k.op("dve", lambda e: e.tensor_tensor(out=t1[j][:, :], in0=P0[:, :], in1=mC(0), op=ALU.mult), [P0, GM], [t1[j]])
                        k.op("dve", lambda e: e.tensor_tensor(out=Xb[j][:, :], in0=ident, in1=t1[j][:, :], op=ALU.subtract), [self.cst, t1[j]], [Xb[j]])
                        k.op("dve", lambda e: e.tensor_tensor(out=t1[j][:, :], in0=PT0[:, :], in1=mCT(0), op=ALU.mult), [PT0, GM], [t1[j]])
                        k.op("dve", lambda e: e.tensor_tensor(out=XTb[j][:, :], in0=ident, in1=t1[j][:, :], op=ALU.subtract), [self.cst, t1[j]], [XTb[j]])
                        for lv in range(1, 1 if 'gdn_noneu' in self.dbg else 7):
                            lastlv = (lv == 6)
                            k.op("dve", lambda e: e.tensor_tensor(out=CTb[j][:, :], in0=PT0[:, :], in1=mCT(lv), op=ALU.mult), [PT0, GM], [CTb[j]])
                            pz = self.psum()
                            k.op("pe", lambda e: e.matmul(pz[:, 0:128], lhsT=CTb[j][:, :], rhs=Xb[j][:, :], start=True, stop=True), [CTb[j], Xb[j]], [pz])
                            evac(Zb[j][:, :], pz[:, 0:128], [pz], [Zb[j]])
                            py_ = self.psum()
                            k.op("pe", lambda e: e.matmul(py_[:, 0:128], lhsT=XTb[j][:, :], rhs=Zb[j][:, :], start=True, stop=True), [XTb[j], Zb[j]], [py_])
                            if not lastlv:
                                k.op("pool", lambda e: e.tensor_tensor(out=Cb[j][:, :], in0=P0[:, :], in1=mC(lv), op=ALU.mult), [P0, GM], [Cb[j]])
                                pz2 = self.psum()
                                k.op("pe", lambda e: e.matmul(pz2[:, 0:128], lhsT=Cb[j][:, :], rhs=XTb[j][:, :], start=True, stop=True), [Cb[j], XTb[j]], [pz2])
                                evac(Z2b[j][:, :], pz2[:, 0:128], [pz2], [Z2b[j]])
                                py2 = self.psum()
                                k.op("pe", lambda e: e.matmul(py2[:, 0:128], lhsT=Xb[j][:, :], rhs=Z2b[j][:, :], start=True, stop=True), [Xb[j], Z2b[j]], [py2])
                            k.op("dve", lambda e: e.tensor_tensor(out=Xb[j][:, :], in0=Xb[j][:, :], in1=py_[:, 0:128], op=ALU.subtract), [Xb[j], py_], [Xb[j]])
                            if not lastlv:
                                k.op("dve", lambda e: e.tensor_tensor(out=XTb[j][:, :], in0=XTb[j][:, :], in1=py2[:, 0:128], op=ALU.subtract), [XTb[j], py2], [XTb[j]])
                        pk = self.psum()
                        k.op("pe", lambda e: e.matmul(pk[:, 0:128], lhsT=KT[:, h, cs], rhs=Sb[:, ci, :], start=True, stop=True), [KT, sb_], [pk])
                        k.op("dve", lambda e: e.scalar_tensor_tensor(out=Rp[j][:, :], in0=pk[:, 0:128], scalar=sc(EG), in1=VTOK[:, blk, h, :], op0=ALU.mult, op1=ALU.subtract), [pk, EG, VTOK], [Rp[j]])
                        k.op("pe", lambda e: e.matmul(pk[:, 128:256], lhsT=Xb[j][:, :], rhs=Rp[j][:, :], start=True, stop=True), [Xb[j], Rp[j]], [pk])
                        k.op("dve", lambda e: e.tensor_scalar(out=vnew[j][:, :], in0=pk[:, 128:256], scalar1=sc(NBETA), scalar2=None, op0=ALU.mult), [pk, NBETA], [vnew[j]])
                        po = self.psum()
                        k.op("pe", lambda e: e.matmul(po[:, 0:128], lhsT=Sb[:, ci, :], rhs=QgT[j][:, :], start=True, stop=False), [sb_, QgT[j]], [po])
                        k.op("pe", lambda e: e.matmul(po[:, 0:128], lhsT=vnew[j][:, :], rhs=AttnT[j][:, :], start=False, stop=True), [vnew[j], AttnT[j]], [po])
                        k.op("act", lambda e: e.copy(out=oo[j][:, :], in_=po[:, 0:128]), [po], [oo[j]])
                        k.dma("pool", self.GO.t[d, h * 128:(h + 1) * 128, cs], oo[j][:, :], [oo[j]], [], oo[j])
                        k.op("pe", lambda e: e.matmul(po[:, 128:256], lhsT=Kd[j][:, :], rhs=vnew[j][:, :], start=True, stop=True), [Kd[j], vnew[j]], [po])
                        k.op("dve", lambda e: e.scalar_tensor_tensor(out=S_[:, ci, :], in0=S_[:, ci, :], scalar=sc(GL), in1=po[:, 128:256], op0=ALU.mult, op1=ALU.add), [sb_, GL, po], [sb_])
                        k.op("act", lambda e: e.copy(out=Sb[:, ci, :], in_=S_[:, ci, :]), [sb_], [sb_])
            k.barrier()
        with contextlib.ExitStack() as st:
            gn = self.vec_col(st, W["gdn_norm_g"][l], 128, "ggn")
            oa = [self.sb(st, [128, 512], F32, "goa") for _ in range(2)]
            ob = [self.sb(st, [128, 512], F32, "gob") for _ in range(2)]
            z_ = [self.sb(st, [128, 512], F32, "gz") for _ in range(2)]
            sq_ = [self.sb(st, [128, 512], F32, "gfsq") for _ in range(2)]
            rs_ = [self.sb(st, [128, 512], F32, "gfrs") for _ in range(2)]
            yb = [self.sb(st, [128, 512], BF16, "gyb") for _ in range(2)]
            tiles = self.tiles[1:] if self.last else self.tiles
            it = 0
            for h in range(4):
                rz = self.urow(GZ + h * 128)
                for ti, (t0, n) in enumerate(tiles):
                    j = it % 2
                    it += 1
                    k.dma("sp", oa[j][:, 0:n], self.GO.t[0, h * 128:(h + 1) * 128, t0:t0 + n], [], [oa[j]], oa[j])
                    k.dma("sp", ob[j][:, 0:n], self.GO.t[1, h * 128:(h + 1) * 128, t0:t0 + n], [], [ob[j]], ob[j])
                    k.dma("sp", z_[j][:, 0:n], self.U.t[rz:rz + 128, t0:t0 + n], [], [z_[j]], z_[j])
                    k.op("dve", lambda e: e.tensor_tensor(out=oa[j][:, 0:n], in0=oa[j][:, 0:n], in1=ob[j][:, 0:n], op=ALU.add), [oa[j], ob[j]], [oa[j]])
                    k.op("act", lambda e: e.activation(out=sq_[j][:, 0:n], in_=oa[j][:, 0:n], func=AF.Square), [oa[j]], [sq_[j]])
                    k.op("act", lambda e: e.activation(out=z_[j][:, 0:n], in_=z_[j][:, 0:n], func=AF.Silu), [z_[j]], [z_[j]])
                    ps = self.psum()
                    k.op("pe", lambda e: e.matmul(ps[:, 0:n], lhsT=self.C_("ones"), rhs=sq_[j][:, 0:n], start=True, stop=True), [self.cst, sq_[j]], [ps])
                    self.rstd_from(ps, n, rs_[j], 1.0 / 128)
                    k.op("dve", lambda e: e.scalar_tensor_tensor(out=oa[j][:, 0:n], in0=oa[j][:, 0:n], scalar=gn[:, 0:1], in1=rs_[j][:, 0:n], op0=ALU.mult, op1=ALU.mult), [oa[j], gn, rs_[j]], [oa[j]])
                    k.op("dve", lambda e: e.tensor_tensor(out=yb[j][:, 0:n], in0=oa[j][:, 0:n], in1=z_[j][:, 0:n], op=ALU.mult), [oa[j], z_[j]], [yb[j]])
                    k.dma("pool", self.Y.t[h * 128:(h + 1) * 128, t0:t0 + n], yb[j][:, 0:n], [yb[j]], [], yb[j])
            k.barrier()


_CACHE = {}


def kernel(**inputs):
    S, C, DEPTH = 4096, 256, 2
    inputs = {k_: np.asarray(v) for k_, v in inputs.items()}
    if "nc" not in _CACHE:
        _CACHE["nc"] = Mod(S, C, DEPTH).build()
    nc = _CACHE["nc"]
    consts = consts_np(S, C)
    in_maps = []
    for b in range(8):
        m = {"x": np.ascontiguousarray(inputs["x"][b], dtype=np.float32),
             "ctx": np.ascontiguousarray(inputs["ctx"][b], dtype=np.float32),
             "cc": np.ascontiguousarray(np.stack([inputs["c"][b], inputs["c_ctx"]]), dtype=np.float32)}
        m.update(consts)
        for n, sh in WSPEC:
            m[n] = np.ascontiguousarray(inputs[n], dtype=np.float32)
        in_maps.append(m)
    res = run_bass_kernel_spmd(nc, in_maps, core_ids=list(range(8)))
    return np.stack([np.asarray(r["out"], dtype=np.float32) for r in res.results], axis=0)
```

```python
import math
import contextlib
import numpy as np
import concourse.bass as bass
import concourse.mybir as mybir
from concourse.bass_utils import run_bass_kernel_spmd

F32 = mybir.dt.float32
BF16 = mybir.dt.bfloat16
ALU = mybir.AluOpType
AF = mybir.ActivationFunctionType


class Buf:
    __slots__ = ("t", "w", "r", "dsem", "name")

    def __init__(self, t, name=""):
        self.t = t
        self.w = {}
        self.r = {}
        self.dsem = None
        self.name = name

    def __getitem__(self, key):
        return self.t[key]


class KB:
    SEM_ROT = 30000

    def __init__(self, nc):
        self.nc = nc
        self.es = contextlib.ExitStack()
        self.engs = {"pe": nc.tensor, "dve": nc.vector, "act": nc.scalar,
                     "pool": nc.gpsimd, "sp": nc.sync}
        self.semh = {}
        self.cnt = {}
        self.isdma = {}
        self.cur = {}
        self.waited = {e: {} for e in self.engs}
        self.nsem = 0
        for e in self.engs:
            self.cur[e] = self.new_sem(False)
        self.ninstr = 0
        self.free_dsems = []
        self.free_dsems_q = {}
        self.phase_dsems = []
        self.persist = False
        self.dma_remap = {}

    def new_sem(self, isdma):
        key = self.nsem
        self.nsem += 1
        self.semh[key] = self.es.enter_context(self.nc.semaphore("s%d" % key))
        self.cnt[key] = 0
        self.isdma[key] = isdma
        return key

    def sb(self, stack, name, shape, dtype):
        t = stack.enter_context(self.nc.sbuf_tensor(name, list(shape), dtype))
        return Buf(t, name)

    def ps(self, stack, name, shape, dtype):
        t = stack.enter_context(self.nc.psum_tensor(name, list(shape), dtype))
        return Buf(t, name)

    def _waits(self, eng, reads, writes):
        need = {}
        for b in reads:
            for s, v in b.w.items():
                if need.get(s, 0) < v:
                    need[s] = v
        for b in writes:
            for d in (b.w, b.r):
                for s, v in d.items():
                    if eng == "pe" and s == self.cur["pe"] and d is b.w:
                        continue
                    if need.get(s, 0) < v:
                        need[s] = v
        e = self.engs[eng]
        wd = self.waited[eng]
        for s, v in need.items():
            if wd.get(s, 0) >= v:
                continue
            if self.isdma[s]:
                v = self.cnt[s]
            e.wait_ge(self.semh[s], v)
            wd[s] = v
            self.ninstr += 1

    def op(self, eng, fn, reads=(), writes=()):
        self._waits(eng, reads, writes)
        ins = fn(self.engs[eng])
        s = self.cur[eng]
        self.cnt[s] += 1
        ins.then_inc(self.semh[s], 1)
        tag = (s, self.cnt[s])
        self._mark(tag, reads, writes)
        if self.cnt[s] >= self.SEM_ROT:
            self.cur[eng] = self.new_sem(False)
        self.ninstr += 1
        return ins

    def _mark(self, tag, reads, writes):
        s, v = tag
        for b in writes:
            b.w = {s: v}
            b.r = {}
        for b in reads:
            if b not in writes:
                b.r[s] = v

    def dma(self, eng, out, in_, reads, writes, sembuf, **kw):
        eng = self.dma_remap.get(eng, eng)
        dmap = sembuf.dsem if isinstance(sembuf.dsem, dict) else {}
        sembuf.dsem = dmap
        if eng not in dmap:
            fl = self.free_dsems_q.setdefault(eng, [])
            if fl and not self.persist:
                dmap[eng] = fl.pop()
            else:
                dmap[eng] = self.new_sem(True)
            if not self.persist:
                self.phase_dsems.append((eng, dmap[eng]))
        self._waits(eng, reads, writes)
        ins = self.engs[eng].dma_start(out=out, in_=in_, **kw)
        s = sembuf.dsem[eng]
        self.cnt[s] += 16
        ins.then_inc(self.semh[s], 16)
        self._mark((s, self.cnt[s]), reads, writes)
        self.ninstr += 1
        return ins

    def barrier(self, engs=None):
        if engs is None:
            for q_, s_ in self.phase_dsems:
                self.free_dsems_q.setdefault(q_, []).append(s_)
            self.phase_dsems = []
        for eng in (engs or self.engs):
            e = self.engs[eng]
            wd = self.waited[eng]
            for s, v in self.cnt.items():
                if v > 0 and wd.get(s, 0) < v:
                    e.wait_ge(self.semh[s], v)
                    wd[s] = v
                    self.ninstr += 1


D = 1024
EPS = 1e-6
NEG = -30000.0
GQ, GK, GV, GZ, GA, GB_ = 0, 512, 1024, 1536, 2048, 2056
MCQ, MCKV, MKPE = 2064, 2448, 2704
DQ, DK, DV = 2736, 3248, 3760
SZ, SX, SB_, SC, SDT = 4272, 4784, 5296, 5552, 5808
GATE0 = 5824
CN = ["ident", "ones", "triF", "triB", "mnegF", "mnegB", "strF", "strB", "bd64", "perm64", "perm32", "sel65"]


def consts_np(S, C):
    T = C + S
    k = np.arange(128)
    d = {}
    d["ident"] = np.eye(128)
    d["ones"] = np.ones((128, 128))
    d["triF"] = (k[:, None] <= k[None, :])
    d["triB"] = (k[:, None] >= k[None, :])
    d["mnegF"] = np.where(k[None, :] >= k[:, None], 0.0, NEG)
    d["mnegB"] = np.where(k[None, :] <= k[:, None], 0.0, NEG)
    d["strF"] = (k[None, :] > k[:, None])
    d["strB"] = (k[None, :] < k[:, None])
    d["bd64"] = (k[:, None] // 64 == k[None, :] // 64)

    def perm(n_rot, total):
        P = np.zeros((128, 128))
        half = n_rot // 2
        qd = half // 2
        for base in range(0, total, half):
            for i in range(half):
                m = base + i
                if i < qd:
                    P[m + qd, m] = -1.0
                else:
                    P[m - qd, m] = 1.0
        return P
    d["perm64"] = perm(64, 128)
    d["perm32"] = perm(32, 32)
    s65 = np.zeros((128, 128))
    s65[64, :] = 1.0
    d["sel65"] = s65
    cst = np.stack([np.asarray(d[n], np.float32) for n in CN], axis=1)

    def rope(rot_dim):
        rows = S // 64
        row = np.repeat(np.arange(rows, dtype=np.float32), 64)
        col = np.tile(np.arange(64, dtype=np.float32), rows)
        quarter = rot_dim // 4
        inv = (np.float32(10000.0) ** (-np.arange(quarter, dtype=np.float32) / np.float32(quarter))).astype(np.float32)
        ar = row[:, None] * inv
        ac = col[:, None] * inv
        ang = np.concatenate([ar, ar, ac, ac], axis=-1).astype(np.float32)
        cos = np.ones((rot_dim, T), np.float32)
        sin = np.zeros((rot_dim, T), np.float32)
        cos[:, C:] = np.cos(ang).T
        sin[:, C:] = np.sin(ang).T
        return cos, sin
    cm, sm = rope(32)
    cd, sd = rope(64)
    ropem = np.stack([cm, sm], axis=1)
    roped = np.stack([np.tile(cd, (2, 1)), np.tile(sd, (2, 1))], axis=1)
    gm = []
    for lv in range(7):
        b = 1 << lv
        mU = ((k[:, None] // (2 * b) == k[None, :] // (2 * b)) & (k[:, None] % (2 * b) < b) & (k[None, :] % (2 * b) >= b))
        gm.append(mU)
    gm = gm + [m_.T for m_ in gm]
    gmask = np.stack([np.asarray(m_, np.float32) for m_ in gm], axis=1)
    return {"cst": np.ascontiguousarray(cst), "ropem": np.ascontiguousarray(ropem),
            "roped": np.ascontiguousarray(roped), "gmask": np.ascontiguousarray(gmask)}


WSPEC = [
    ("ada_w", [D, 6 * D]), ("ada_b", [6 * D]), ("norm1_g", [D]), ("norm2_g", [D]),
    ("w_in", [D, 9920]), ("gdn_conv", [5, 1536]), ("gdn_a_log", [2, 4]), ("gdn_dt_bias", [2, 4]),
    ("gdn_norm_g", [128]), ("mla_q_lora_g", [384]), ("mla_kv_lora_g", [256]),
    ("mla_w_uq", [384, 768]), ("mla_w_ukv", [256, 1024]), ("mla_qn_g", [96]), ("mla_kn_g", [96]),
    ("diff_qn_g", [64]), ("diff_kn_g", [64]), ("diff_lambda", [4, 64]), ("diff_sub_g", [128]),
    ("ssm_conv", [5, 1024]), ("ssm_conv_b", [1024]), ("ssm_a_log", [2, 8]), ("ssm_dt_bias", [2, 8]),
    ("ssm_d", [8]), ("ssm_norm_g", [512]), ("w_branch", [4, 512, D]), ("w_out", [D, D]),
    ("mlp_w1", [D, 4 * D]), ("mlp_w2", [4 * D, D]),
]


class Mod:
    def __init__(self, S, C, depth, dbg=()):
        self.S, self.C, self.depth = S, C, depth
        self.T = T = S + C
        self.NB = T // 128
        self.dbg = dbg
        nc = self.nc = bass.Bass("TRN2", target_bir_lowering=False)
        self.k = KB(nc)
        self.uid = 0
        di = lambda n, sh, dt=F32: nc.dram_tensor(n, list(sh), dt, kind="ExternalInput").ap()
        self.x_in = di("x", [S, D])
        self.ctx_in = di("ctx", [C, D])
        self.cc_in = di("cc", [2, D])
        self.cst_in = di("cst", [128, len(CN), 128])
        self.ropem_in = di("ropem", [32, 2, T])
        self.roped_in = di("roped", [128, 2, T])
        self.gmask_in = di("gmask", [128, 14, 128])
        self.W = {n: di(n, [depth] + sh) for n, sh in WSPEC}
        self.out = nc.dram_tensor("out", [S, D], F32, kind="ExternalOutput").ap()
        dscr = lambda n, sh, dt=F32: Buf(nc.dram_tensor(n, list(sh), dt, kind=("ExternalOutput" if n in dbg else "Internal")).ap(), n)
        self.XT = dscr("XT", [D, T])
        self.U = dscr("U", [80 * 128, T])
        self.Y = dscr("Y", [2048, T], BF16)
        self.MT = dscr("MT", [D, T])
        self.HID = dscr("HID", [4 * D, T], BF16)
        self.QD = dscr("QD", [512, T], BF16)
        self.KD = dscr("KD", [512, T], BF16)
        self.VD = dscr("VD", [T, 512], BF16)
        self.KN = dscr("KN", [8 * 64, T], BF16)
        self.KR = dscr("KR", [8 * 32, T], BF16)
        self.QN = dscr("QN", [8 * 64, T], BF16)
        self.QR = dscr("QR", [8 * 32, T], BF16)
        self.VA = dscr("VA", [T, 8 * 65], BF16)
        self.XS = dscr("XS", [512, T])
        self.YS = dscr("YS", [2, 512, T])
        self.GO = dscr("GO", [2, 512, T])
        self.tiles = [(0, C)] + [(C + 512 * i, 512) for i in range(S // 512)]
        self.uchunks = []
        slot = 0
        self.uslot = {}
        for (a, b) in [(0, 2048), (2048, 2064), (2064, 2448), (2448, 2704), (2704, 2736), (2736, 4272),
                       (4272, 5808), (5808, 5824), (5824, 9920)]:
            c = a
            while c < b:
                e = min(c + 128, b)
                self.uchunks.append((c, e, slot))
                self.uslot[c] = slot
                slot += 1
                c = e
        assert slot == 80

    def nm(self, p):
        self.uid += 1
        return "%s_%d" % (p, self.uid)

    def sb(self, st, shape, dt=F32, p="t"):
        return self.k.sb(st, self.nm(p), shape, dt)

    def psum(self):
        self.psi = (self.psi + 1) % len(self.psb)
        return self.psb[self.psi]

    def urow(self, col):
        return self.uslot[col] * 128

    def build(self):
        k = self.k
        with k.es, contextlib.ExitStack() as gs:
            self.psb = [k.ps(gs, "psb%d" % i, [128, 512], F32) for i in range(8)]
            self.psi = 0
            self.cst = self.sb(gs, [128, len(CN), 128], F32, "cst")
            k.persist = True
            k.dma("sp", self.cst[:, :, :], self.cst_in, [], [self.cst], self.cst)
            k.persist = False
            self.cbf = self.sb(gs, [128, len(CN), 128], BF16, "cbf")
            k.op("dve", lambda e: e.tensor_copy(out=self.cbf[:, :, :], in_=self.cst[:, :, :]), [self.cst], [self.cbf])
            self.epsb = self.sb(gs, [128, 4], F32, "epsb")
            k.op("dve", lambda e: e.memset(self.epsb[:, :], EPS), [], [self.epsb])
            k.op("dve", lambda e: e.memset(self.epsb[:, 1:2], 1.0), [self.epsb], [self.epsb])
            self.XTt = [Buf(None, "XTt%d" % i) for i in range(len(self.tiles))]
            self.MTt = [Buf(None, "MTt%d" % i) for i in range(len(self.tiles))]
            self.phase_init()
            for l in range(self.depth):
                self.layer(l)
            self.phase_final()
            k.barrier()
        return self.nc

    def C_(self, name, bf=False):
        i = CN.index(name)
        return (self.cbf if bf else self.cst)[:, i, :]

    def phase_init(self):
        k = self.k
        with contextlib.ExitStack() as st:
            xin = [self.sb(st, [128, D], F32, "xin") for _ in range(2)]
            xo = [self.sb(st, [128, 8, 128], F32, "xo") for _ in range(2)]
            XTv = self.XT.t.rearrange("(kc p) t -> p kc t", p=128)
            for blk in range(self.NB):
                src = self.ctx_in[blk * 128:(blk + 1) * 128, :] if blk < self.C // 128 else \
                    self.x_in[blk * 128 - self.C:(blk + 1) * 128 - self.C, :]
                xi = xin[blk % 2]
                o = xo[blk % 2]
                k.dma("sp", xi[:, :], src, [], [xi], xi)
                for half in range(2):
                    ps = self.psum()
                    for j in range(4):
                        kc = half * 4 + j
                        k.op("pe", lambda e: e.transpose(out=ps[:, j * 128:(j + 1) * 128], in_=xi[:, kc * 128:(kc + 1) * 128],
                                                         identity=self.C_("ident")), [xi, self.cst], [ps])
                    eng = "dve" if half == 0 else "act"
                    if eng == "dve":
                        k.op("dve", lambda e: e.tensor_copy(out=o[:, half * 4:half * 4 + 4, :], in_=ps[:, :].rearrange("p (a b) -> p a b", a=4)), [ps], [o])
                    else:
                        k.op("act", lambda e: e.copy(out=o[:, half * 4:half * 4 + 4, :], in_=ps[:, :].rearrange("p (a b) -> p a b", a=4)), [ps], [o])
                k.dma("pool", XTv[:, :, blk * 128:(blk + 1) * 128], o[:, :, :], [o], [], o)
            k.barrier()

    def phase_final(self):
        k = self.k
        with contextlib.ExitStack() as st:
            xi = [self.sb(st, [128, 8, 128], F32, "fxi") for _ in range(2)]
            xo = [self.sb(st, [128, D], F32, "fxo") for _ in range(2)]
            XTv = self.XT.t.rearrange("(kc p) t -> p kc t", p=128)
            dout = Buf(self.out, "out")
            for b in range(self.S // 128):
                blk = b + self.C // 128
                a = xi[b % 2]
                o = xo[b % 2]
                k.dma("sp", a[:, :, :], XTv[:, :, blk * 128:(blk + 1) * 128], [], [a], a)
                for half in range(2):
                    ps = self.psum()
                    for j in range(4):
                        kc = half * 4 + j
                        k.op("pe", lambda e: e.transpose(out=ps[:, j * 128:(j + 1) * 128], in_=a[:, kc, :],
                                                         identity=self.C_("ident")), [a, self.cst], [ps])
                    if half == 0:
                        k.op("dve", lambda e: e.tensor_copy(out=o[:, 0:512], in_=ps[:, :]), [ps], [o])
                    else:
                        k.op("act", lambda e: e.copy(out=o[:, 512:1024], in_=ps[:, :]), [ps], [o])
                k.dma("pool", self.out[b * 128:(b + 1) * 128, :], o[:, :], [o], [dout], o)
            k.barrier()

    def layer(self, l):
        k = self.k
        self.l = l
        self.last = (l == self.depth - 1) and ("forcectx" not in self.dbg)
        with contextlib.ExitStack() as ls:
            self.phase_mod(l, ls)
            self.phase_inproj(l)
            if "U" in self.dbg and l == 0 and "stopU" in self.dbg:
                return
            for ph in ("gdn", "mla", "diff", "ssm", "merge", "mlp"):
                if ph not in self.dbg:
                    getattr(self, "phase_" + ph)(l)
            k.barrier()

    def phase_mod(self, l, ls):
        k = self.k
        W = self.W
        self.modv = self.sb(ls, [128, 48, 2], F32, "modv")
        self.A1 = self.sb(ls, [128, 8, 2], F32, "A1")
        self.A2 = self.sb(ls, [128, 8, 2], F32, "A2")
        with contextlib.ExitStack() as st:
            cv = self.sb(st, [128, 2, 8], F32, "cv")
            sv = self.sb(st, [128, 8, 2], F32, "sv")
            ab = self.sb(st, [128, 48], F32, "ab")
            g12 = self.sb(st, [128, 2, 8], F32, "g12")
            k.dma("sp", cv[:, :, :], self.cc_in.rearrange("r (kc p) -> p r kc", p=128), [], [cv], cv, allow_slow_non_contiguous=True)
            k.dma("sp", ab[:, :], W["ada_b"][l].rearrange("(j p) -> p j", p=128), [], [ab], ab, allow_slow_non_contiguous=True)
            k.dma("sp", g12[:, 0, :], W["norm1_g"][l].rearrange("(kc p) -> p kc", p=128), [], [g12], g12, allow_slow_non_contiguous=True)
            k.dma("sp", g12[:, 1, :], W["norm2_g"][l].rearrange("(kc p) -> p kc", p=128), [], [g12], g12, allow_slow_non_contiguous=True)
            k.op("act", lambda e: e.activation(out=sv[:, :, :], in_=cv[:, :, :].rearrange("p r kc -> p kc r"), func=AF.Silu), [cv], [sv])
            wst = [self.sb(st, [128, 8, 512], F32, "adaw") for _ in range(2)]
            aw = W["ada_w"][l].rearrange("(kc p) c -> p kc c", p=128)
            ps = self.psum()
            for g in range(12):
                w = wst[g % 2]
                k.dma("sp", w[:, :, :], aw[:, :, g * 512:(g + 1) * 512], [], [w], w)
                for jj in range(4):
                    j = g * 4 + jj
                    for kc in range(8):
                        k.op("pe", lambda e: e.matmul(ps[:, j * 2:j * 2 + 2], lhsT=w[:, kc, jj * 128:(jj + 1) * 128], rhs=sv[:, kc, :],
                                                      start=(kc == 0), stop=(kc == 7)), [w, sv], [ps])
            k.op("dve", lambda e: e.tensor_tensor(out=self.modv[:, :, :], in0=ps[:, 0:96].rearrange("p (j r) -> p j r", r=2),
                                                  in1=ab[:, :].unsqueeze(2).to_broadcast([128, 48, 2]), op=ALU.add), [ps, ab], [self.modv])
            for (A, gi, mi) in ((self.A1, 0, 1), (self.A2, 1, 4)):
                k.op("dve", lambda e: e.scalar_tensor_tensor(out=A[:, :, :], in0=self.modv[:, mi * 8:mi * 8 + 8, :], scalar=1.0,
                                                             in1=g12[:, gi, :].unsqueeze(2).to_broadcast([128, 8, 2]), op0=ALU.add, op1=ALU.mult),
                     [self.modv, g12], [A])
            k.barrier()

    def mcol(self, ti):
        return 1 if ti == 0 else 0

    def norm_mod(self, st, A, shift_idx, dst):
        k = self.k
        xt_ = [self.sb(st, [128, 8, 512], F32, "nx") for _ in range(2)]
        sq_ = [self.sb(st, [128, 8, 512], F32, "nsq") for _ in range(2)]
        rs_ = [self.sb(st, [128, 512], F32, "nrs") for _ in range(2)]
        tmp_ = [self.sb(st, [128, 512], F32, "ntmp") for _ in range(3)]
        XTv = self.XT.t.rearrange("(kc p) t -> p kc t", p=128)
        ci = 0
        for ti, (t0, n) in enumerate(self.tiles):
            col = self.mcol(ti)
            xt, sq, rs = xt_[ti % 2], sq_[ti % 2], rs_[ti % 2]
            k.dma("sp", xt[:, :, 0:n], XTv[:, :, t0:t0 + n], [self.XTt[ti]], [xt], xt)
            k.op("act", lambda e: e.activation(out=sq[:, :, 0:n], in_=xt[:, :, 0:n], func=AF.Square), [xt], [sq])
            ps = self.psum()
            for kc in range(8):
                k.op("pe", lambda e: e.matmul(ps[:, 0:n], lhsT=self.C_("ones"), rhs=sq[:, kc, 0:n], start=(kc == 0), stop=(kc == 7)),
                     [self.cst, sq], [ps])
            self.rstd_from(ps, n, rs, 1.0 / D)
            for kc in range(8):
                tmp = tmp_[ci % 3]
                ci += 1
                k.op("dve", lambda e: e.scalar_tensor_tensor(out=tmp[:, 0:n], in0=xt[:, kc, 0:n], scalar=A[:, kc, col:col + 1], in1=rs[:, 0:n],
                                                             op0=ALU.mult, op1=ALU.mult), [xt, A, rs], [tmp])
                k.op("act", lambda e: e.activation(out=dst[:, kc, t0:t0 + n], in_=tmp[:, 0:n], func=AF.Identity,
                                                   bias=self.modv[:, shift_idx * 8 + kc, col:col + 1], scale=1.0), [tmp, self.modv], [dst])

    def gemm_fm(self, st, in_sb, KC, wsrc, chunks, tiles, epi, krows=128):
        k = self.k
        groups, cur = [], []
        for ch in chunks:
            if cur and (ch[0] != cur[-1][1] or ch[1] - cur[0][0] > 512):
                groups.append(cur)
                cur = []
            cur.append(ch)
        if cur:
            groups.append(cur)
        wst = [self.sb(st, [128, KC, 512], F32, "wst") for _ in range(2)]
        wbf = [self.sb(st, [128, KC, 512], BF16, "wbf") for _ in range(2)]
        for gi, g in enumerate(groups):
            c0, c1 = g[0][0], g[-1][1]
            w = c1 - c0
            s, b = wst[gi % 2], wbf[gi % 2]
            k.dma("sp", s[0:krows, :, 0:w], wsrc(c0, c1), [], [s], s)
            k.op("pool", lambda e: e.tensor_copy(out=b[0:krows, :, 0:w], in_=s[0:krows, :, 0:w]), [s], [b])
            for ti, (t0, n) in enumerate(tiles):
                for ch in g:
                    rows = ch[1] - ch[0]
                    off = ch[0] - c0
                    ps = self.psum()
                    for kk in range(KC):
                        k.op("pe", lambda e: e.matmul(ps[0:rows, 0:n], lhsT=b[0:krows, kk, off:off + rows], rhs=in_sb[0:krows, kk, t0:t0 + n],
                                                      start=(kk == 0), stop=(kk == KC - 1)), [b, in_sb], [ps])
                    epi(ch, ti, t0, n, ps, rows)

    def phase_inproj(self, l):
        k = self.k
        with contextlib.ExitStack() as st:
            hT = self.sb(st, [128, 8, self.T], BF16, "hT")
            with contextlib.ExitStack() as st2:
                self.norm_mod(st2, self.A1, 0, hT)
                k.barrier()
            stg = [self.sb(st, [128, 512], F32, "ustg") for _ in range(4)]
            cnt = [0]
            win = self.W["w_in"][l].rearrange("(kc p) c -> p kc c", p=128)

            def epi(ch, ti, t0, n, ps, rows):
                s = stg[cnt[0] % 4]
                if cnt[0] % 2 == 0:
                    k.op("dve", lambda e: e.tensor_copy(out=s[0:rows, 0:n], in_=ps[0:rows, 0:n]), [ps], [s])
                else:
                    k.op("act", lambda e: e.copy(out=s[0:rows, 0:n], in_=ps[0:rows, 0:n]), [ps], [s])
                cnt[0] += 1
                r0 = ch[2] * 128
                k.dma("pool", self.U.t[r0:r0 + rows, t0:t0 + n], s[0:rows, 0:n], [s], [], s)
            self.gemm_fm(st, hT, 8, lambda c0, c1: win[:, :, c0:c1], self.uchunks, self.tiles, epi)
            vst = [self.sb(st, [128, 512], BF16, "vst") for _ in range(2)]

            def epiv(blk, ps):
                v = vst[blk % 2]
                k.op("act", lambda e: e.copy(out=v[:, :], in_=ps[:, :]), [ps], [v])
                k.dma("pool", self.VD.t[blk * 128:(blk + 1) * 128, :], v[:, :], [v], [], v)
            self.gemm_tm(st, hT, 8, lambda s_: k.dma("sp", s_[:, :, :], win[:, :, DV:DV + 512], [], [s_], s_), 512, list(range(self.NB)), epiv)
            k.barrier()

    def gemm_tm(self, st, in_sb, KC, wsrc, width, blocks, epi, krows=128):
        k = self.k
        s = self.sb(st, [128, KC, width], F32, "wtm")
        b = self.sb(st, [128, KC, width], BF16, "wtmb")
        wsrc(s)
        k.op("pool", lambda e: e.tensor_copy(out=b[0:krows, :, :], in_=s[0:krows, :, :]), [s], [b])
        for blk in blocks:
            ps = self.psum()
            for kk in range(KC):
                k.op("pe", lambda e: e.matmul(ps[:, 0:width], lhsT=in_sb[0:krows, kk, blk * 128:(blk + 1) * 128], rhs=b[0:krows, kk, :],
                                              start=(kk == 0), stop=(kk == KC - 1)), [in_sb, b], [ps])
            epi(blk, ps)

    def rstd_from(self, ps, n, rs, scale, rows=128):
        k = self.k
        k.op("act", lambda e: e.activation(out=rs[0:rows, 0:n], in_=ps[0:rows, 0:n], func=AF.Ln, bias=self.epsb[0:rows, 0:1], scale=scale), [ps, self.epsb], [rs])
        k.op("act", lambda e: e.activation(out=rs[0:rows, 0:n], in_=rs[0:rows, 0:n], func=AF.Exp, scale=-0.5), [rs], [rs])

    def vec_col(self, st, src_ap, rows, p="vc"):
        t = self.sb(st, [128, 1], F32, p)
        self.k.dma("sp", t[0:rows, :], src_ap.rearrange("(p o) -> p o", o=1), [], [t], t, allow_slow_non_contiguous=True)
        return t

    def phase_merge(self, l):
        k = self.k
        W = self.W
        tiles = self.tiles[1:] if self.last else self.tiles
        MTv = self.MT.t
        for i in range(4):
            with contextlib.ExitStack() as st:
                yin = self.sb(st, [128, 4, self.T], BF16, "yin")
                k.dma("sp", yin[:, :, :], self.Y.t[i * 512:(i + 1) * 512, :].rearrange("(kc p) t -> p kc t", p=128), [], [yin], yin)
                gt_ = [self.sb(st, [128, 512], F32, "gt") for _ in range(3)]
                mt_ = [self.sb(st, [128, 512], F32, "mt") for _ in range(3)]
                cnt = [0]
                wb = W["w_branch"][l, i].rearrange("(kc p) c -> p kc c", p=128)

                def epi(ch, ti, t0, n, ps, rows):
                    c = ch[0] // 128
                    gt, mt = gt_[cnt[0] % 3], mt_[cnt[0] % 3]
                    cnt[0] += 1
                    r0 = self.urow(GATE0 + i * 1024 + c * 128)
                    k.dma("sp", gt[:, 0:n], self.U.t[r0:r0 + 128, t0:t0 + n], [], [gt], gt)
                    k.op("act", lambda e: e.activation(out=gt[:, 0:n], in_=gt[:, 0:n], func=AF.Sigmoid), [gt], [gt])
                    if i > 0:
                        k.dma("sp", mt[:, 0:n], MTv[c * 128:(c + 1) * 128, t0:t0 + n], [], [mt], mt)
                        k.op("dve", lambda e: e.tensor_tensor(out=gt[:, 0:n], in0=ps[:, 0:n], in1=gt[:, 0:n], op=ALU.mult), [ps, gt], [gt])
                        k.op("dve", lambda e: e.tensor_tensor(out=mt[:, 0:n], in0=mt[:, 0:n], in1=gt[:, 0:n], op=ALU.add), [mt, gt], [mt])
                    else:
                        k.op("dve", lambda e: e.tensor_tensor(out=mt[:, 0:n], in0=ps[:, 0:n], in1=gt[:, 0:n], op=ALU.mult), [ps, gt], [mt])
                    k.dma("pool", MTv[c * 128:(c + 1) * 128, t0:t0 + n], mt[:, 0:n], [mt], [], mt)
                self.gemm_fm(st, yin, 4, lambda c0, c1: wb[:, :, c0:c1], [(c * 128, (c + 1) * 128) for c in range(8)], tiles, epi)
                k.barrier()
        with contextlib.ExitStack() as st:
            mT = self.sb(st, [128, 8, self.T], BF16, "mTb")
            with contextlib.ExitStack() as st2:
                ml = [self.sb(st2, [128, 8, 512], F32, "ml") for _ in range(2)]
                for ti, (t0, n) in enumerate(tiles):
                    m = ml[ti % 2]
                    k.dma("sp", m[:, :, 0:n], MTv.rearrange("(kc p) t -> p kc t", p=128)[:, :, t0:t0 + n], [], [m], m)
                    k.op("dve", lambda e: e.tensor_copy(out=mT[:, :, t0:t0 + n], in_=m[:, :, 0:n]), [m], [mT])
                k.barrier()
            self.resid_gemm(st, mT, 8, self.W["w_out"][l].rearrange("(kc p) c -> p kc c", p=128), 16, tiles)
            k.barrier()

    def resid_gemm(self, st, in_sb, KC, wv, gate_idx, tiles):
        k = self.k
        xt_ = [self.sb(st, [128, 512], F32, "rx") for _ in range(4)]
        cnt = [0]
        XTv = self.XT.t

        def epi(ch, ti, t0, n, ps, rows):
            c = ch[0] // 128
            col = 1 if t0 == 0 else 0
            xt = xt_[cnt[0] % 4]
            cnt[0] += 1
            k.dma("sp", xt[:, 0:n], XTv[c * 128:(c + 1) * 128, t0:t0 + n], [], [xt], xt)
            k.op("dve", lambda e: e.scalar_tensor_tensor(out=xt[:, 0:n], in0=ps[:, 0:n], scalar=self.modv[:, gate_idx + c, col:col + 1], in1=xt[:, 0:n],
                                                         op0=ALU.mult, op1=ALU.add), [ps, self.modv, xt], [xt])
            k.dma("pool", XTv[c * 128:(c + 1) * 128, t0:t0 + n], xt[:, 0:n], [xt], [], xt)
        self.gemm_fm(st, in_sb, KC, lambda c0, c1: wv[:, :, c0:c1], [(c * 128, (c + 1) * 128) for c in range(8)], tiles, epi)

    def phase_mlp(self, l):
        k = self.k
        tiles = self.tiles[1:] if self.last else self.tiles
        with contextlib.ExitStack() as st:
            hT = self.sb(st, [128, 8, self.T], BF16, "h2T")
            with contextlib.ExitStack() as st2:
                self.norm_mod(st2, self.A2, 3, hT)
                k.barrier()
            stg = [self.sb(st, [128, 512], F32, "hs") for _ in range(3)]
            stb = [self.sb(st, [128, 512], BF16, "hb") for _ in range(3)]
            cnt = [0]
            w1 = self.W["mlp_w1"][l].rearrange("(kc p) c -> p kc c", p=128)

            def epi(ch, ti, t0, n, ps, rows):
                s, b = stg[cnt[0] % 3], stb[cnt[0] % 3]
                cnt[0] += 1
                k.op("dve", lambda e: e.tensor_scalar_max(out=s[:, 0:n], in0=ps[:, 0:n], scalar1=0.0), [ps], [s])
                k.op("act", lambda e: e.activation(out=b[:, 0:n], in_=s[:, 0:n], func=AF.Square), [s], [b])
                k.dma("pool", self.HID.t[ch[0]:ch[1], t0:t0 + n], b[:, 0:n], [b], [], b)
            self.gemm_fm(st, hT, 8, lambda c0, c1: w1[:, :, c0:c1], [(c * 128, (c + 1) * 128) for c in range(32)], tiles, epi)
            k.barrier()
        for q in range(4):
            with contextlib.ExitStack() as st:
                hin = self.sb(st, [128, 8, self.T], BF16, "hin")
                k.dma("sp", hin[:, :, :], self.HID.t[q * 1024:(q + 1) * 1024, :].rearrange("(kc p) t -> p kc t", p=128), [], [hin], hin)
                w2 = self.W["mlp_w2"][l, q * 1024:(q + 1) * 1024, :].rearrange("(kc p) c -> p kc c", p=128)
                self.resid_gemm(st, hin, 8, w2, 40, tiles)
                k.barrier()

    def attend(self, st, terms, vfn, M, kblocks, qtiles, scale, ones_sum, epi, ptag=0):
        k = self.k
        pt_ = self.pt_
        for qi, (qc0, t0, n) in enumerate(qtiles):
            O = self.psb[4 + 2 * ptag]
            Sps = self.psb[5 + 2 * ptag]
            nk = len(kblocks)
            sps = {}

            def score(i):
                sp = self.psb[self.sci % 4]
                self.sci += 1
                sps[i] = sp
                kb = kblocks[i]
                for j, (K_sb, Q_sb, r0, rows) in enumerate(terms):
                    k.op("pe", lambda e: e.matmul(sp[:, 0:n], lhsT=K_sb[r0:r0 + rows, kb * 128:(kb + 1) * 128], rhs=Q_sb[r0:r0 + rows, qc0:qc0 + n],
                                                  start=(j == 0), stop=(j == len(terms) - 1)), [K_sb, Q_sb], [sp])
            score(0)
            for i in range(nk):
                if i + 1 < nk:
                    score(i + 1)
                pt = pt_[self.pti % 3]
                self.pti += 1
                sp = sps.pop(i)
                k.op("act", lambda e: e.activation(out=pt[:, 0:n], in_=sp[:, 0:n], func=AF.Exp, scale=scale), [sp], [pt])
                k.op("pe", lambda e: e.matmul(O[0:M, 0:n], lhsT=vfn(kblocks[i]), rhs=pt[:, 0:n], start=(i == 0), stop=(i == nk - 1)), [self.vbuf, pt], [O])
                if ones_sum:
                    k.op("pe", lambda e: e.matmul(Sps[:, 0:n], lhsT=self.C_("ones", True), rhs=pt[:, 0:n], start=(i == 0), stop=(i == nk - 1)), [self.cbf, pt], [Sps])
            epi(qi, t0, n, O, Sps)

    def attn_common(self, st):
        self.pt_ = [self.sb(st, [128, 512], BF16, "pt") for _ in range(3)]
        self.sci = 0
        self.pti = 0

    def qtiles(self, lat):
        if lat:
            return [(t0, t0, n) for (t0, n) in self.tiles[1:]]
        return [(0, 0, self.C)]

    def phase_diff(self, l):
        k = self.k
        W = self.W
        T, NB = self.T, self.NB
        lam_init = 0.8 - 0.6 * math.exp(-0.3 * l)
        QD = self.QD
        KD = self.KD
        with contextlib.ExitStack() as st:
            gq = self.sb(st, [128, 2], F32, "dg")
            for j, nm_ in enumerate(("diff_qn_g", "diff_kn_g")):
                for hh in range(2):
                    k.dma("sp", gq[hh * 64:(hh + 1) * 64, j:j + 1], W[nm_][l].rearrange("(p o) -> p o", o=1), [], [gq], gq, allow_slow_non_contiguous=True)
            u_ = [self.sb(st, [128, 512], F32, "du") for _ in range(2)]
            sq_ = [self.sb(st, [128, 512], F32, "dsq") for _ in range(2)]
            rs_ = [self.sb(st, [128, 512], F32, "drs") for _ in range(2)]
            cs_ = [self.sb(st, [128, 2, 512], F32, "dcs") for _ in range(2)]
            o_ = [self.sb(st, [128, 512], F32, "do") for _ in range(2)]
            ob_ = [self.sb(st, [128, 512], BF16, "dob") for _ in range(2)]
            it = 0
            for j, (col0, dst) in enumerate(((DQ, QD), (DK, KD))):
                for h in range(4):
                    r0 = self.urow(col0 + h * 128)
                    for ti, (t0, n) in enumerate(self.tiles):
                        u, sq, rs, cs, o, ob = u_[it % 2], sq_[it % 2], rs_[it % 2], cs_[it % 2], o_[it % 2], ob_[it % 2]
                        it += 1
                        k.dma("sp", u[:, 0:n], self.U.t[r0:r0 + 128, t0:t0 + n], [], [u], u)
                        k.dma("sp", cs[:, :, 0:n], self.roped_in[:, :, t0:t0 + n], [], [cs], cs)
                        k.op("act", lambda e: e.activation(out=sq[:, 0:n], in_=u[:, 0:n], func=AF.Square), [u], [sq])
                        ps = self.psum()
                        k.op("pe", lambda e: e.matmul(ps[:, 0:n], lhsT=self.C_("bd64"), rhs=sq[:, 0:n], start=True, stop=True), [self.cst, sq], [ps])
                        self.rstd_from(ps, n, rs, 1.0 / 64)
                        k.op("dve", lambda e: e.scalar_tensor_tensor(out=u[:, 0:n], in0=u[:, 0:n], scalar=gq[:, j:j + 1], in1=rs[:, 0:n], op0=ALU.mult, op1=ALU.mult), [u, gq, rs], [u])
                        ps2 = self.psum()
                        k.op("pe", lambda e: e.matmul(ps2[:, 0:n], lhsT=self.C_("perm64"), rhs=u[:, 0:n], start=True, stop=True), [self.cst, u], [ps2])
                        k.op("dve", lambda e: e.tensor_tensor(out=o[:, 0:n], in0=u[:, 0:n], in1=cs[:, 0, 0:n], op=ALU.mult), [u, cs], [o])
                        k.op("dve", lambda e: e.tensor_tensor(out=sq[:, 0:n], in0=ps2[:, 0:n], in1=cs[:, 1, 0:n], op=ALU.mult), [ps2, cs], [sq])
                        k.op("dve", lambda e: e.tensor_tensor(out=ob[:, 0:n], in0=o[:, 0:n], in1=sq[:, 0:n], op=ALU.add), [o, sq], [ob])
                        k.dma("pool", dst.t[h * 128:(h + 1) * 128, t0:t0 + n], ob[:, 0:n], [ob], [], ob)
            k.barrier()
        with contextlib.ExitStack() as st:
            self.attn_common(st)
            lamt2 = self.sb(st, [128, 256], F32, "lamt")
            k.dma("sp", lamt2[:, :], W["diff_lambda"][l:l + 1].rearrange("o a b -> o (a b)").partition_broadcast(128), [], [lamt2], lamt2)

            lv = self.sb(st, [128, 4], F32, "lv")
            lt = self.sb(st, [128, 2, 64], F32, "lt")
            for j in range(2):
                k.op("dve", lambda e: e.tensor_tensor(out=lt[:, j, :], in0=lamt2[:, (2 * j) * 64:(2 * j + 1) * 64], in1=lamt2[:, (2 * j + 1) * 64:(2 * j + 2) * 64], op=ALU.mult), [lamt2], [lt])
            k.op("dve", lambda e: e.reduce_sum(out=lv[:, 0:2], in_=lt[:, :, :], axis=mybir.AxisListType.X), [lt], [lv])
            k.op("act", lambda e: e.activation(out=lv[:, 0:2], in_=lv[:, 0:2], func=AF.Exp), [lv], [lv])
            k.op("dve", lambda e: e.scalar_tensor_tensor(out=lv[:, 2:3], in0=lv[:, 1:2], scalar=-lam_init, in1=lv[:, 0:1], op0=ALU.add, op1=ALU.subtract), [lv], [lv])
            sg = self.vec_col(st, W["diff_sub_g"][l], 128, "sg")
            k.op("dve", lambda e: e.tensor_scalar(out=sg[:, :], in0=sg[:, :], scalar1=(1.0 - lam_init), scalar2=None, op0=ALU.mult), [sg], [sg])
            qh = self.sb(st, [128, T], BF16, "dqh")
            khm = [self.sb(st, [128, T], BF16, "dkh") for _ in range(2)]
            for m_ in range(2):
                k.op("pool", lambda e: e.memset(khm[m_][:, :], 0.0), [], [khm[m_]])
            vh = self.sb(st, [128, NB, 128], BF16, "dvh")
            self.vbuf = vh
            ra = [self.sb(st, [128, 512], F32, "ra") for _ in range(2)]
            aa = [self.sb(st, [128, 512], F32, "aa") for _ in range(2)]
            dd = self.sb(st, [128, 512], F32, "dd")
            yb = [self.sb(st, [128, 512], BF16, "dyb") for _ in range(2)]
            for h in range(4):
                k.dma("sp", qh[:, :], QD.t[h * 128:(h + 1) * 128, :], [], [qh], qh)
                for m_ in range(2):
                    k.dma("sp", khm[m_][m_ * 64:(m_ + 1) * 64, :], KD.t[h * 128 + m_ * 64:h * 128 + (m_ + 1) * 64, :], [], [khm[m_]], khm[m_])
                k.dma("sp", vh[:, :, :], self.VD.t[:, h * 128:(h + 1) * 128].rearrange("(b p) d -> p b d", p=128), [], [vh], vh)
                passes = [(True, list(range(NB)))]
                if not self.last:
                    passes.append((False, list(range(self.C // 128))))
                for lat, kbl in passes:
                    for qi, qt in enumerate(self.qtiles(lat)):
                        res = {}
                        for m in range(2):
                            def epi(qi_, t0, n, O, Sps, m=m):
                                res[m] = (O, Sps)
                            self.attend(st, [(khm[m], qh, 0, 128)], lambda kb: vh[:, kb, :], 128, kbl, [qt], 64 ** -0.5, True, epi, ptag=m)
                        (qc0, t0, n) = qt
                        for m in range(2):
                            O, Sps = res[m]
                            k.op("dve", lambda e: e.reciprocal(out=ra[m][:, 0:n], in_=Sps[:, 0:n]), [Sps], [ra[m]])
                            k.op("dve", lambda e: e.tensor_tensor(out=aa[m][:, 0:n], in0=O[:, 0:n], in1=ra[m][:, 0:n], op=ALU.mult), [O, ra[m]], [aa[m]])
                        k.op("dve", lambda e: e.scalar_tensor_tensor(out=dd[:, 0:n], in0=aa[1][:, 0:n], scalar=lv[:, 2:3], in1=aa[0][:, 0:n], op0=ALU.mult, op1=ALU.add), [aa[0], aa[1], lv], [dd])
                        k.op("act", lambda e: e.activation(out=aa[0][:, 0:n], in_=dd[:, 0:n], func=AF.Square), [dd], [aa[0]])
                        ps = self.psb[self.sci % 4]
                        self.sci += 1
                        k.op("pe", lambda e: e.matmul(ps[:, 0:n], lhsT=self.C_("ones"), rhs=aa[0][:, 0:n], start=True, stop=True), [self.cst, aa[0]], [ps])
                        self.rstd_from(ps, n, ra[0], 1.0 / 128)
                        y = yb[qi % 2]
                        k.op("dve", lambda e: e.scalar_tensor_tensor(out=y[:, 0:n], in0=dd[:, 0:n], scalar=sg[:, 0:1], in1=ra[0][:, 0:n], op0=ALU.mult, op1=ALU.mult), [dd, sg, ra[0]], [y])
                        k.dma("pool", self.Y.t[1024 + h * 128:1024 + (h + 1) * 128, t0:t0 + n], y[:, 0:n], [y], [], y)
            k.barrier()

    def phase_mla(self, l):
        k = self.k
        W = self.W
        T, NB = self.T, self.NB
        with contextlib.ExitStack() as st:
            cn = self.sb(st, [128, 5, T], BF16, "cn")
            RK = self.sb(st, [32, T], F32, "RK")
            SQPE = self.sb(st, [32, T], F32, "SQPE")
            gkv = self.sb(st, [128, 5], F32, "gkv")
            k.dma("sp", gkv[:, 0:2], W["mla_kv_lora_g"][l].rearrange("(kc p) -> p kc", p=128), [], [gkv], gkv, allow_slow_non_contiguous=True)
            k.dma("sp", gkv[:, 2:5], W["mla_q_lora_g"][l].rearrange("(kc p) -> p kc", p=128), [], [gkv], gkv, allow_slow_non_contiguous=True)
            gk = self.sb(st, [128, 4], F32, "gk")
            for j, nm_ in enumerate(("mla_kn_g", "mla_qn_g")):
                k.dma("sp", gk[0:64, 2 * j:2 * j + 1], W[nm_][l, 0:64].rearrange("(p o) -> p o", o=1), [], [gk], gk, allow_slow_non_contiguous=True)
                k.dma("sp", gk[0:32, 2 * j + 1:2 * j + 2], W[nm_][l, 64:96].rearrange("(p o) -> p o", o=1), [], [gk], gk, allow_slow_non_contiguous=True)
            with contextlib.ExitStack() as st2:
                u_ = [self.sb(st2, [128, 3, 512], F32, "mu") for _ in range(2)]
                sq_ = [self.sb(st2, [128, 3, 512], F32, "msq") for _ in range(2)]
                rs_ = [self.sb(st2, [128, 512], F32, "mrs") for _ in range(2)]
                it = 0
                for (col0, kc_n, dst0, nfeat) in ((MCKV, 2, 0, 256), (MCQ, 3, 2, 384)):
                    r0 = self.urow(col0)
                    for ti, (t0, n) in enumerate(self.tiles):
                        u, sq, rs = u_[it % 2], sq_[it % 2], rs_[it % 2]
                        it += 1
                        k.dma("sp", u[:, 0:kc_n, 0:n], self.U.t[r0:r0 + kc_n * 128, t0:t0 + n].rearrange("(kc p) t -> p kc t", p=128), [], [u], u)
                        k.op("act", lambda e: e.activation(out=sq[:, 0:kc_n, 0:n], in_=u[:, 0:kc_n, 0:n], func=AF.Square), [u], [sq])
                        ps = self.psum()
                        for kc in range(kc_n):
                            k.op("pe", lambda e: e.matmul(ps[:, 0:n], lhsT=self.C_("ones"), rhs=sq[:, kc, 0:n], start=(kc == 0), stop=(kc == kc_n - 1)), [self.cst, sq], [ps])
                        self.rstd_from(ps, n, rs, 1.0 / nfeat)
                        for kc in range(kc_n):
                            k.op("dve", lambda e: e.scalar_tensor_tensor(out=cn[:, dst0 + kc, t0:t0 + n], in0=u[:, kc, 0:n], scalar=gkv[:, dst0 + kc:dst0 + kc + 1], in1=rs[:, 0:n],
                                                                         op0=ALU.mult, op1=ALU.mult), [u, gkv, rs], [cn])
                r0 = self.urow(MKPE)
                kp = self.sb(st2, [32, T], F32, "kp")
                k.dma("sp", kp[:, :], self.U.t[r0:r0 + 32, :], [], [kp], kp)
                k.op("act", lambda e: e.activation(out=SQPE[:, :], in_=kp[:, :], func=AF.Square), [kp], [SQPE])
                k.op("dve", lambda e: e.tensor_scalar(out=kp[:, :], in0=kp[:, :], scalar1=gk[0:32, 1:2], scalar2=None, op0=ALU.mult), [kp, gk], [kp])
                self.rope32(st2, kp, RK)
                k.barrier()
            with contextlib.ExitStack() as st2:
                sq_ = [self.sb(st2, [64, 512], F32, "ksq") for _ in range(2)]
                rs_ = [self.sb(st2, [128, 512], F32, "krs") for _ in range(2)]
                kn_ = [self.sb(st2, [64, 512], BF16, "kn") for _ in range(2)]
                kr_ = [self.sb(st2, [32, 512], BF16, "kr") for _ in range(2)]
                cnt = [0]
                wkv = W["mla_w_ukv"][l].rearrange("(kc p) c -> p kc c", p=128)

                def epik(ch, ti, t0, n, ps, rows):
                    h = ch[0] // 128
                    i = cnt[0] % 2
                    cnt[0] += 1
                    sq, rs, kn, kr = sq_[i], rs_[i], kn_[i], kr_[i]
                    k.op("act", lambda e: e.activation(out=sq[:, 0:n], in_=ps[0:64, 0:n], func=AF.Square), [ps], [sq])
                    p2 = self.psum()
                    k.op("pe", lambda e: e.matmul(p2[:, 0:n], lhsT=self.cst[0:64, 1, :], rhs=sq[0:64, 0:n], start=True, stop=False), [self.cst, sq], [p2])
                    k.op("pe", lambda e: e.matmul(p2[:, 0:n], lhsT=self.cst[0:32, 1, :], rhs=SQPE[0:32, t0:t0 + n], start=False, stop=True), [self.cst, SQPE], [p2])
                    self.rstd_from(p2, n, rs, 1.0 / 96)
                    k.op("dve", lambda e: e.scalar_tensor_tensor(out=kn[:, 0:n], in0=ps[0:64, 0:n], scalar=gk[0:64, 0:1], in1=rs[0:64, 0:n], op0=ALU.mult, op1=ALU.mult), [ps, gk, rs], [kn])
                    k.op("dve", lambda e: e.tensor_tensor(out=kr[:, 0:n], in0=RK[:, t0:t0 + n], in1=rs[0:32, 0:n], op=ALU.mult), [RK, rs], [kr])
                    k.dma("pool", self.KN.t[h * 64:(h + 1) * 64, t0:t0 + n], kn[:, 0:n], [kn], [], kn)
                    k.dma("pool", self.KR.t[h * 32:(h + 1) * 32, t0:t0 + n], kr[:, 0:n], [kr], [], kr)
                self.gemm_fm(st2, cn, 2, lambda c0, c1: wkv[:, :, c0:c1], [(h * 128, h * 128 + 64) for h in range(8)], self.tiles, epik)
                va_ = [self.sb(st2, [128, 8, 65], BF16, "va") for _ in range(2)]
                for v in va_:
                    k.op("dve", lambda e: e.memset(v[:, :, :], 1.0), [], [v])

                def epiv(blk, ps):
                    v = va_[blk % 2]
                    k.op("act", lambda e: e.copy(out=v[:, :, 0:64], in_=ps[:, 0:512].rearrange("p (h d) -> p h d", h=8)), [ps], [v])
                    k.dma("pool", self.VA.t[blk * 128:(blk + 1) * 128, :], v[:, :, :].rearrange("p h d -> p (h d)"), [v], [], v)
                wv5 = W["mla_w_ukv"][l].rearrange("(kc p) (h two d) -> p kc h two d", p=128, two=2, d=64)

                def wsrc(s_):
                    for kc in range(2):
                        k.dma("sp", s_[:, kc, :].rearrange("p (h d) -> p h d", h=8), wv5[:, kc, :, 1, :], [], [s_], s_)
                self.gemm_tm(st2, cn, 2, wsrc, 512, list(range(NB)), epiv)
                k.barrier()
            with contextlib.ExitStack() as st2:
                sqn_ = [self.sb(st2, [64, 512], F32, "qsq") for _ in range(2)]
                sqr_ = [self.sb(st2, [32, 512], F32, "qsr") for _ in range(2)]
                rs_ = [self.sb(st2, [128, 512], F32, "qrs") for _ in range(2)]
                qn_ = [self.sb(st2, [64, 512], BF16, "qn") for _ in range(2)]
                qr_ = [self.sb(st2, [32, 512], BF16, "qr") for _ in range(2)]
                xr_ = [self.sb(st2, [32, 512], F32, "xr") for _ in range(2)]
                ro_ = [self.sb(st2, [32, 512], F32, "ro") for _ in range(2)]
                cs_ = [self.sb(st2, [32, 2, 512], F32, "qcs") for _ in range(2)]
                cnt = [0]
                held = {}
                wq = W["mla_w_uq"][l].rearrange("(kc p) c -> p kc c", p=128)

                def epiq(ch, ti, t0, n, ps, rows):
                    if rows == 64:
                        held["n"] = ps
                        return
                    psn, psr = held["n"], ps
                    h = ch[0] // 96
                    i = cnt[0] % 2
                    cnt[0] += 1
                    sqn, sqr, rs, qn, qr, xr, ro, cs = sqn_[i], sqr_[i], rs_[i], qn_[i], qr_[i], xr_[i], ro_[i], cs_[i]
                    k.dma("sp", cs[:, :, 0:n], self.ropem_in[:, :, t0:t0 + n], [], [cs], cs)
                    k.op("act", lambda e: e.activation(out=sqn[:, 0:n], in_=psn[0:64, 0:n], func=AF.Square), [psn], [sqn])
                    k.op("act", lambda e: e.activation(out=sqr[:, 0:n], in_=psr[0:32, 0:n], func=AF.Square), [psr], [sqr])
                    p2 = self.psum()
                    k.op("pe", lambda e: e.matmul(p2[:, 0:n], lhsT=self.cst[0:64, 1, :], rhs=sqn[0:64, 0:n], start=True, stop=False), [self.cst, sqn], [p2])
                    k.op("pe", lambda e: e.matmul(p2[:, 0:n], lhsT=self.cst[0:32, 1, :], rhs=sqr[0:32, 0:n], start=False, stop=True), [self.cst, sqr], [p2])
                    self.rstd_from(p2, n, rs, 1.0 / 96)
                    k.op("dve", lambda e: e.scalar_tensor_tensor(out=qn[:, 0:n], in0=psn[0:64, 0:n], scalar=gk[0:64, 2:3], in1=rs[0:64, 0:n], op0=ALU.mult, op1=ALU.mult), [psn, gk, rs], [qn])
                    k.op("dve", lambda e: e.tensor_scalar(out=xr[:, 0:n], in0=psr[0:32, 0:n], scalar1=gk[0:32, 3:4], scalar2=None, op0=ALU.mult), [psr, gk], [xr])
                    p3 = self.psum()
                    k.op("pe", lambda e: e.matmul(p3[0:32, 0:n], lhsT=self.cst[0:32, CN.index("perm32"), 0:32], rhs=xr[0:32, 0:n], start=True, stop=True), [self.cst, xr], [p3])
                    k.op("dve", lambda e: e.tensor_tensor(out=ro[:, 0:n], in0=p3[0:32, 0:n], in1=cs[:, 1, 0:n], op=ALU.mult), [p3, cs], [ro])
                    k.op("dve", lambda e: e.tensor_tensor(out=xr[:, 0:n], in0=xr[:, 0:n], in1=cs[:, 0, 0:n], op=ALU.mult), [xr, cs], [xr])
                    k.op("dve", lambda e: e.tensor_tensor(out=xr[:, 0:n], in0=xr[:, 0:n], in1=ro[:, 0:n], op=ALU.add), [xr, ro], [xr])
                    k.op("dve", lambda e: e.tensor_tensor(out=qr[:, 0:n], in0=xr[:, 0:n], in1=rs[0:32, 0:n], op=ALU.mult), [xr, rs], [qr])
                    k.dma("pool", self.QN.t[h * 64:(h + 1) * 64, t0:t0 + n], qn[:, 0:n], [qn], [], qn)
                    k.dma("pool", self.QR.t[h * 32:(h + 1) * 32, t0:t0 + n], qr[:, 0:n], [qr], [], qr)
                chq = []
                for h in range(8):
                    chq += [(h * 96, h * 96 + 64), (h * 96 + 64, h * 96 + 96)]
                self.gemm_fm(st2, Buf(cn.t[:, 2:5, :]), 3, lambda c0, c1: wq[:, :, c0:c1], chq, self.tiles, epiq)
                k.barrier()
        with contextlib.ExitStack() as st:
            self.attn_common(st)
            kqh = self.sb(st, [96, T], BF16, "kqh")
            qqh = self.sb(st, [96, T], BF16, "qqh")
            vah = self.sb(st, [128, NB, 65], BF16, "vah")
            self.vbuf = vah
            osb = [self.sb(st, [65, 512], F32, "osb") for _ in range(2)]
            rr = [self.sb(st, [64, 512], F32, "rr") for _ in range(2)]
            yb = [self.sb(st, [64, 512], BF16, "myb") for _ in range(2)]
            cnt = [0]
            for h in range(8):
                k.dma("sp", kqh[0:64, :], self.KN.t[h * 64:(h + 1) * 64, :], [], [kqh], kqh)
                k.dma("sp", kqh[64:96, :], self.KR.t[h * 32:(h + 1) * 32, :], [], [kqh], kqh)
                k.dma("sp", qqh[0:64, :], self.QN.t[h * 64:(h + 1) * 64, :], [], [qqh], qqh)
                k.dma("sp", qqh[64:96, :], self.QR.t[h * 32:(h + 1) * 32, :], [], [qqh], qqh)
                k.dma("sp", vah[:, :, :], self.VA.t[:, h * 65:(h + 1) * 65].rearrange("(b p) d -> p b d", p=128), [], [vah], vah)

                def epi(qi, t0, n, O, Sps):
                    i = cnt[0] % 2
                    cnt[0] += 1
                    o, r, y = osb[i], rr[i], yb[i]
                    k.op("act", lambda e: e.copy(out=o[0:65, 0:n], in_=O[0:65, 0:n]), [O], [o])
                    ps = self.psb[self.sci % 4]
                    self.sci += 1
                    k.op("pe", lambda e: e.matmul(ps[0:64, 0:n], lhsT=self.cst[0:65, CN.index("sel65"), 0:64], rhs=o[0:65, 0:n], start=True, stop=True), [self.cst, o], [ps])
                    k.op("dve", lambda e: e.reciprocal(out=r[:, 0:n], in_=ps[0:64, 0:n]), [ps], [r])
                    k.op("dve", lambda e: e.tensor_tensor(out=y[:, 0:n], in0=o[0:64, 0:n], in1=r[:, 0:n], op=ALU.mult), [o, r], [y])
                    k.dma("pool", self.Y.t[512 + h * 64:512 + (h + 1) * 64, t0:t0 + n], y[:, 0:n], [y], [], y)
                terms = [(kqh, qqh, 0, 96)]
                self.attend(st, terms, lambda kb: vah[:, kb, :], 65, list(range(NB)), self.qtiles(True), 96 ** -0.5, False, epi)
                if not self.last:
                    self.attend(st, terms, lambda kb: vah[:, kb, :], 65, list(range(self.C // 128)), self.qtiles(False), 96 ** -0.5, False, epi)
            k.barrier()

    def rope32(self, st, xin, dst):
        k = self.k
        cs_ = [self.sb(st, [32, 2, 512], F32, "rcs") for _ in range(2)]
        ro_ = [self.sb(st, [32, 512], F32, "rro") for _ in range(2)]
        for ti, (t0, n) in enumerate(self.tiles):
            cs, ro = cs_[ti % 2], ro_[ti % 2]
            k.dma("sp", cs[:, :, 0:n], self.ropem_in[:, :, t0:t0 + n], [], [cs], cs)
            ps = self.psum()
            k.op("pe", lambda e: e.matmul(ps[0:32, 0:n], lhsT=self.cst[0:32, CN.index("perm32"), 0:32], rhs=xin[0:32, t0:t0 + n], start=True, stop=True), [self.cst, xin], [ps])
            k.op("dve", lambda e: e.tensor_tensor(out=ro[:, 0:n], in0=ps[0:32, 0:n], in1=cs[:, 1, 0:n], op=ALU.mult), [ps, cs], [ro])
            k.op("dve", lambda e: e.tensor_tensor(out=dst[:, t0:t0 + n], in0=xin[0:32, t0:t0 + n], in1=cs[:, 0, 0:n], op=ALU.mult), [xin, cs], [dst])
            k.op("dve", lambda e: e.tensor_tensor(out=dst[:, t0:t0 + n], in0=dst[:, t0:t0 + n], in1=ro[:, 0:n], op=ALU.add), [dst, ro], [dst])

    def conv_silu(self, st, u, acc, wc, bias, out):
        k = self.k
        C, T = self.C, self.T
        k.op("dve", lambda e: e.tensor_scalar(out=acc[:, :], in0=u[:, :], scalar1=wc[:, 2:3], scalar2=None, op0=ALU.mult), [u, wc], [acc])
        for j in (0, 1, 3, 4):
            s = j - 2
            for (s0, s1) in ((0, C), (C, T)):
                a = max(s0, s0 - s)
                b = min(s1, s1 - s)
                k.op("dve", lambda e: e.scalar_tensor_tensor(out=acc[:, a:b], in0=u[:, a + s:b + s], scalar=wc[:, j:j + 1], in1=acc[:, a:b], op0=ALU.mult, op1=ALU.add), [u, wc, acc], [acc])
        if bias is None:
            k.op("act", lambda e: e.activation(out=out, in_=acc[:, :], func=AF.Silu), [acc], [acc])
        else:
            k.op("act", lambda e: e.activation(out=out, in_=acc[:, :], func=AF.Silu, bias=bias, scale=1.0), [acc, wc], [acc])

    def softplus(self, st, xb, xap, n):
        k = self.k
        t = self.sb(st, [128, n], F32, "spt")

        class _X:
            def __getitem__(s_, key):
                return xap
        x = _X()
        k.op("act", lambda e: e.activation(out=t[:, :], in_=xap, func=AF.Abs), [xb], [t])
        k.op("act", lambda e: e.activation(out=t[:, :], in_=t[:, :], func=AF.Exp, scale=-1.0), [t], [t])
        k.op("act", lambda e: e.activation(out=t[:, :], in_=t[:, :], func=AF.Ln, bias=self.epsb[:, 1:2], scale=1.0), [t, self.epsb], [t])
        k.op("dve", lambda e: e.scalar_tensor_tensor(out=xap, in0=xap, scalar=0.0, in1=t[:, :], op0=ALU.max, op1=ALU.add), [xb, t], [xb])

    def tok_scalars(self, st, col0, ncols, dst):
        k = self.k
        r0 = self.urow(col0)
        raw = self.sb(st, [ncols, self.T], F32, "tsr")
        k.dma("sp", raw[:, :], self.U.t[r0:r0 + ncols, :], [], [raw], raw)
        for blk in range(self.NB):
            ps = self.psum()
            k.op("pe", lambda e: e.transpose(out=ps[:, 0:ncols], in_=raw[0:ncols, blk * 128:(blk + 1) * 128], identity=self.cst[0:ncols, 0, 0:ncols]), [raw, self.cst], [ps])
            k.op("act", lambda e: e.copy(out=dst[:, blk, :], in_=ps[:, 0:ncols]), [ps], [dst])

    def blk_order(self, d):
        nc_ = self.C // 128
        if d == 0:
            return list(range(self.NB))
        return list(range(nc_ - 1, -1, -1)) + list(range(self.NB - 1, nc_ - 1, -1))

    def phase_ssm(self, l):
        k = self.k
        W = self.W
        T, NB = self.T, self.NB
        with contextlib.ExitStack() as st:
            XTOK = self.sb(st, [128, NB, 512], F32, "XTOK")
            BT = self.sb(st, [128, 2, T], BF16, "BT")
            CT = self.sb(st, [128, 2, T], BF16, "CT")
            BTOK = self.sb(st, [128, NB, 2, 128], BF16, "BTOK")
            DT = self.sb(st, [128, NB, 16], F32, "DT")
            DA = self.sb(st, [128, NB, 16], F32, "DA")
            ACUM = self.sb(st, [128, NB, 16], F32, "ACUM")
            ATOT = self.sb(st, [128, NB, 16], F32, "ATOT")
            CD = self.sb(st, [128, NB, 16], F32, "CD")
            DTDS = self.sb(st, [128, NB, 16], F32, "DTDS")
            with contextlib.ExitStack() as st2:
                wc_ = [self.sb(st2, [128, 6], F32, "swc") for _ in range(2)]
                st2a = contextlib.ExitStack()
                u_ = [self.sb(st2a, [128, T], F32, "su") for _ in range(1)] * 2
                acc_ = [self.sb(st2a, [128, T], F32, "sacc") for _ in range(1)] * 2
                for c in range(8):
                    u, acc, wc = u_[c % 2], acc_[c % 2], wc_[c % 2]
                    r0 = self.urow(SX + c * 128)
                    k.dma("sp", u[:, :], self.U.t[r0:r0 + 128, :], [], [u], u)
                    k.dma("sp", wc[:, 0:5], W["ssm_conv"][l][:, c * 128:(c + 1) * 128].rearrange("j c -> c j"), [], [wc], wc, allow_slow_non_contiguous=True)
                    k.dma("sp", wc[:, 5:6], W["ssm_conv_b"][l, c * 128:(c + 1) * 128].rearrange("(p o) -> p o", o=1), [], [wc], wc, allow_slow_non_contiguous=True)
                    if c < 4:
                        self.conv_silu(st2, u, acc, wc, wc[:, 5:6], acc[:, :])
                        k.dma("pool", self.XS.t[c * 128:(c + 1) * 128, :], acc[:, :], [acc], [], acc)
                        for blk in range(NB):
                            ps = self.psum()
                            k.op("pe", lambda e: e.transpose(out=ps[:, 0:128], in_=acc[:, blk * 128:(blk + 1) * 128], identity=self.C_("ident")), [acc, self.cst], [ps])
                            k.op("act", lambda e: e.copy(out=XTOK[:, blk, c * 128:(c + 1) * 128], in_=ps[:, 0:128]), [ps], [XTOK])
                    elif c < 6:
                        g = c - 4
                        self.conv_silu(st2, u, acc, wc, wc[:, 5:6], acc[:, :])
                        k.op("dve", lambda e: e.tensor_copy(out=BT[:, g, :], in_=acc[:, :]), [acc], [BT])
                        for blk in range(NB):
                            ps = self.psum()
                            k.op("pe", lambda e: e.transpose(out=ps[:, 0:128], in_=acc[:, blk * 128:(blk + 1) * 128], identity=self.C_("ident")), [acc, self.cst], [ps])
                            k.op("act", lambda e: e.copy(out=BTOK[:, blk, g, :], in_=ps[:, 0:128]), [ps], [BTOK])
                    else:
                        g = c - 6
                        self.conv_silu(st2, u, acc, wc, wc[:, 5:6], acc[:, :])
                        k.op("dve", lambda e: e.tensor_copy(out=CT[:, g, :], in_=acc[:, :]), [acc], [CT])
                k.barrier()
                st2a.close()
                self.tok_scalars(st2, SDT, 16, DT)
                pb = self.sb(st2, [128, 2, 16], F32, "spb")
                k.dma("sp", pb[:, 0, :], W["ssm_dt_bias"][l:l + 1].rearrange("o a b -> o (a b)").partition_broadcast(128), [], [pb], pb)
                k.dma("sp", pb[:, 1, :], W["ssm_a_log"][l:l + 1].rearrange("o a b -> o (a b)").partition_broadcast(128), [], [pb], pb)
                k.op("dve", lambda e: e.tensor_tensor(out=DT[:, :, :], in0=DT[:, :, :], in1=pb[:, 0, :].unsqueeze(1).to_broadcast([128, NB, 16]), op=ALU.add), [DT, pb], [DT])
                self.softplus(st2, DT, DT.t[:, :, :].rearrange("p b c -> p (b c)"), NB * 16)
                k.op("act", lambda e: e.activation(out=pb[:, 1, :], in_=pb[:, 1, :], func=AF.Exp), [pb], [pb])
                k.op("dve", lambda e: e.scalar_tensor_tensor(out=DA[:, :, :], in0=DT[:, :, :], scalar=-1.0, in1=pb[:, 1, :].unsqueeze(1).to_broadcast([128, NB, 16]), op0=ALU.mult, op1=ALU.mult), [DT, pb], [DA])
                self.cum_stats(st2, DA, ACUM, ATOT, 8)
                k.op("act", lambda e: e.activation(out=CD[:, :, :], in_=ATOT[:, :, :], func=AF.Exp), [ATOT], [CD])
                k.op("dve", lambda e: e.tensor_tensor(out=DTDS[:, :, :], in0=ATOT[:, :, :], in1=ACUM[:, :, :], op=ALU.subtract), [ATOT, ACUM], [DTDS])
                k.op("act", lambda e: e.activation(out=DTDS[:, :, :], in_=DTDS[:, :, :], func=AF.Exp), [DTDS], [DTDS])
                k.op("dve", lambda e: e.tensor_tensor(out=DTDS[:, :, :], in0=DTDS[:, :, :], in1=DT[:, :, :], op=ALU.mult), [DTDS, DT], [DTDS])
                k.barrier()
            ST = self.sb(st, [128, 4, 4, 64], F32, "ST")
            STb = self.sb(st, [128, 4, 4, 64], BF16, "STb")
            k.op("dve", lambda e: e.memset(ST[:, :, :, :], 0.0), [], [ST])
            k.op("dve", lambda e: e.memset(STb[:, :, :, :], 0.0), [], [STb])
            R = 2
            rhsb = [self.sb(st, [128, 4, 128], F32, "srhs") for _ in range(R)]
            Dm = [self.sb(st, [128, 4, 128], F32, "sD") for _ in range(R)]
            LT = [self.sb(st, [128, 4, 128], F32, "sLT") for _ in range(R)]
            RE = [self.sb(st, [128, 4, 128], F32, "sRE") for _ in range(R)]
            WT = [self.sb(st, [128, 4, 128], BF16, "sWT") for _ in range(R)]
            CdT = [self.sb(st, [128, 4, 128], BF16, "sCd") for _ in range(R)]
            xdt = [self.sb(st, [128, 4, 64], BF16, "sxdt") for _ in range(R)]
            xdd = [self.sb(st, [128, 4, 64], BF16, "sxdd") for _ in range(R)]
            SCs = [self.sb(st, [128, 128], F32, "sSC") for _ in range(R)]
            yo = [self.sb(st, [64, 4, 128], F32, "syo") for _ in range(R)]
            STs = {(d, g): Buf(None) for d in range(2) for g in range(2)}
            it = 0
            orders = [self.blk_order(0), self.blk_order(1)]
            for i in range(NB):
                for d in range(2):
                    blk = orders[d][i]
                    tri = self.C_("triF" if d == 0 else "triB")
                    mneg = self.C_("mnegF" if d == 0 else "mnegB")
                    for g in range(2):
                        j = it % R
                        it += 1
                        ch = d * 2 + g
                        sbuf_ = STs[(d, g)]
                        hs = slice(d * 8 + g * 4, d * 8 + g * 4 + 4)
                        k.op("dve", lambda e: e.tensor_tensor(out=rhsb[j][:, :, :], in0=tri.unsqueeze(1).to_broadcast([128, 4, 128]),
                                                              in1=DA[:, blk, hs].unsqueeze(2).to_broadcast([128, 4, 128]), op=ALU.mult), [self.cst, DA], [rhsb[j]])
                        pa = self.psum()
                        k.op("pe", lambda e: e.matmul(pa[:, :], lhsT=self.C_("ones"), rhs=rhsb[j][:, :, :].rearrange("p h c -> p (h c)"), start=True, stop=True), [self.cst, rhsb[j]], [pa])
                        for h in range(4):
                            k.op("dve", lambda e: e.scalar_tensor_tensor(out=Dm[j][:, h, :], in0=pa[:, h * 128:(h + 1) * 128], scalar=ACUM[:, blk, d * 8 + g * 4 + h:d * 8 + g * 4 + h + 1],
                                                                         in1=mneg, op0=ALU.subtract, op1=ALU.add), [pa, ACUM, self.cst], [Dm[j]])
                        k.op("act", lambda e: e.activation(out=LT[j][:, :, :], in_=Dm[j][:, :, :], func=AF.Exp), [Dm[j]], [LT[j]])
                        k.op("act", lambda e: e.activation(out=RE[j][:, :, :].rearrange("p h c -> p (h c)"), in_=pa[:, :], func=AF.Exp), [pa], [RE[j]])
                        psc = self.psum()
                        k.op("pe", lambda e: e.matmul(psc[:, 0:128], lhsT=BT[:, g, blk * 128:(blk + 1) * 128], rhs=CT[:, g, blk * 128:(blk + 1) * 128], start=True, stop=True), [BT, CT], [psc])
                        k.op("act", lambda e: e.copy(out=SCs[j][:, :], in_=psc[:, 0:128]), [psc], [SCs[j]])
                        k.op("dve", lambda e: e.tensor_tensor(out=WT[j][:, :, :], in0=LT[j][:, :, :], in1=SCs[j][:, :].unsqueeze(1).to_broadcast([128, 4, 128]), op=ALU.mult), [LT[j], SCs[j]], [WT[j]])
                        k.op("dve", lambda e: e.tensor_tensor(out=CdT[j][:, :, :], in0=RE[j][:, :, :], in1=CT[:, g, blk * 128:(blk + 1) * 128].unsqueeze(1).to_broadcast([128, 4, 128]), op=ALU.mult), [RE[j], CT], [CdT[j]])
                        xv = XTOK[:, blk, g * 256:(g + 1) * 256].rearrange("p (h q) -> p h q", h=4)
                        k.op("dve", lambda e: e.tensor_tensor(out=xdt[j][:, :, :], in0=xv, in1=DT[:, blk, hs].unsqueeze(2).to_broadcast([128, 4, 64]), op=ALU.mult), [XTOK, DT], [xdt[j]])
                        k.op("dve", lambda e: e.tensor_tensor(out=xdd[j][:, :, :], in0=xv, in1=DTDS[:, blk, hs].unsqueeze(2).to_broadcast([128, 4, 64]), op=ALU.mult), [XTOK, DTDS], [xdd[j]])
                        py = self.psum()
                        for h in range(4):
                            k.op("pe", lambda e: e.matmul(py[0:64, h * 128:(h + 1) * 128], lhsT=xdt[j][:, h, :], rhs=WT[j][:, h, :], start=True, stop=False), [xdt[j], WT[j]], [py])
                            k.op("pe", lambda e: e.matmul(py[0:64, h * 128:(h + 1) * 128], lhsT=STb[:, ch, h, :], rhs=CdT[j][:, h, :], start=False, stop=True), [sbuf_, CdT[j]], [py])
                        k.op("act", lambda e: e.copy(out=yo[j][:, :, :].rearrange("p h c -> p (h c)"), in_=py[0:64, :]), [py], [yo[j]])
                        k.dma("pool", self.YS.t[d, g * 256:(g + 1) * 256, blk * 128:(blk + 1) * 128].rearrange("(h p) c -> p h c", p=64), yo[j][:, :, :], [yo[j]], [], yo[j])
                        pst = self.psum()
                        for h in range(4):
                            k.op("pe", lambda e: e.matmul(pst[:, h * 64:(h + 1) * 64], lhsT=BTOK[:, blk, g, :], rhs=xdd[j][:, h, :], start=True, stop=True), [BTOK, xdd[j]], [pst])
                        k.op("dve", lambda e: e.tensor_tensor(out=ST[:, ch, :, :], in0=ST[:, ch, :, :], in1=CD[:, blk, hs].unsqueeze(2).to_broadcast([128, 4, 64]), op=ALU.mult), [sbuf_, CD], [sbuf_])
                        k.op("dve", lambda e: e.tensor_tensor(out=ST[:, ch, :, :], in0=ST[:, ch, :, :], in1=pst[:, 0:256].rearrange("p (h q) -> p h q", h=4), op=ALU.add), [sbuf_, pst], [sbuf_])
                        k.op("act", lambda e: e.copy(out=STb[:, ch, :, :], in_=ST[:, ch, :, :]), [sbuf_], [sbuf_])
            k.barrier()
        with contextlib.ExitStack() as st:
            dsk = self.sb(st, [128, 4], F32, "dsk")
            gn = self.sb(st, [128, 4], F32, "sgn")
            for c in range(4):
                for hh in range(2):
                    k.dma("sp", dsk[hh * 64:(hh + 1) * 64, c:c + 1], W["ssm_d"][l:l + 1, 2 * c + hh:2 * c + hh + 1].partition_broadcast(64), [], [dsk], dsk)
            k.dma("sp", gn[:, :], W["ssm_norm_g"][l].rearrange("(c p) -> p c", p=128), [], [gn], gn, allow_slow_non_contiguous=True)
            ya = [self.sb(st, [128, 2, 512], F32, "fya") for _ in range(2)]
            yb_ = [self.sb(st, [128, 2, 512], F32, "fyb") for _ in range(2)]
            xs_ = [self.sb(st, [128, 2, 512], F32, "fxs") for _ in range(2)]
            z_ = [self.sb(st, [128, 2, 512], F32, "fz") for _ in range(2)]
            sq_ = [self.sb(st, [128, 2, 512], F32, "fsq") for _ in range(2)]
            rs_ = [self.sb(st, [128, 512], F32, "frs") for _ in range(2)]
            ob_ = [self.sb(st, [128, 2, 512], BF16, "fob") for _ in range(2)]
            it = 0
            rz = self.urow(SZ)
            tiles = self.tiles[1:] if self.last else self.tiles
            for gi in range(2):
                for ti, (t0, n) in enumerate(tiles):
                    j = it % 2
                    it += 1
                    r0 = gi * 256
                    v = lambda ap: ap.rearrange("(c p) t -> p c t", p=128)
                    k.dma("sp", ya[j][:, :, 0:n], v(self.YS.t[0, r0:r0 + 256, t0:t0 + n]), [], [ya[j]], ya[j])
                    k.dma("sp", yb_[j][:, :, 0:n], v(self.YS.t[1, r0:r0 + 256, t0:t0 + n]), [], [yb_[j]], yb_[j])
                    k.dma("sp", xs_[j][:, :, 0:n], v(self.XS.t[r0:r0 + 256, t0:t0 + n]), [], [xs_[j]], xs_[j])
                    k.dma("sp", z_[j][:, :, 0:n], v(self.U.t[rz + r0:rz + r0 + 256, t0:t0 + n]), [], [z_[j]], z_[j])
                    k.op("dve", lambda e: e.tensor_tensor(out=ya[j][:, :, 0:n], in0=ya[j][:, :, 0:n], in1=yb_[j][:, :, 0:n], op=ALU.add), [ya[j], yb_[j]], [ya[j]])
                    k.op("act", lambda e: e.activation(out=z_[j][:, :, 0:n], in_=z_[j][:, :, 0:n], func=AF.Silu), [z_[j]], [z_[j]])
                    for cc in range(2):
                        c = gi * 2 + cc
                        k.op("dve", lambda e: e.scalar_tensor_tensor(out=ya[j][:, cc, 0:n], in0=xs_[j][:, cc, 0:n], scalar=dsk[:, c:c + 1], in1=ya[j][:, cc, 0:n], op0=ALU.mult, op1=ALU.add), [xs_[j], dsk, ya[j]], [ya[j]])
                    k.op("dve", lambda e: e.tensor_tensor(out=ya[j][:, :, 0:n], in0=ya[j][:, :, 0:n], in1=z_[j][:, :, 0:n], op=ALU.mult), [ya[j], z_[j]], [ya[j]])
                    k.op("act", lambda e: e.activation(out=sq_[j][:, :, 0:n], in_=ya[j][:, :, 0:n], func=AF.Square), [ya[j]], [sq_[j]])
                    ps = self.psum()
                    for cc in range(2):
                        k.op("pe", lambda e: e.matmul(ps[:, 0:n], lhsT=self.C_("ones"), rhs=sq_[j][:, cc, 0:n], start=(cc == 0), stop=(cc == 1)), [self.cst, sq_[j]], [ps])
                    self.rstd_from(ps, n, rs_[j], 1.0 / 256)
                    for cc in range(2):
                        c = gi * 2 + cc
                        k.op("dve", lambda e: e.scalar_tensor_tensor(out=ob_[j][:, cc, 0:n], in0=ya[j][:, cc, 0:n], scalar=gn[:, c:c + 1], in1=rs_[j][:, 0:n], op0=ALU.mult, op1=ALU.mult), [ya[j], gn, rs_[j]], [ob_[j]])
                    k.dma("pool", v(self.Y.t[1536 + r0:1536 + r0 + 256, t0:t0 + n]), ob_[j][:, :, 0:n], [ob_[j]], [], ob_[j])
            k.barrier()

    def cum_stats(self, st, G, GCUM, GTOT, nh):
        k = self.k
        NB = self.NB
        for d in range(2):
            tri = self.C_("triF" if d == 0 else "triB")
            ps = self.psum()
            k.op("pe", lambda e: e.matmul(ps[:, 0:NB * nh].rearrange("p (b h) -> p b h", h=nh), lhsT=tri, rhs=G[:, :, d * nh:(d + 1) * nh], start=True, stop=True), [self.cst, G], [ps])
            k.op("act", lambda e: e.copy(out=GCUM[:, :, d * nh:(d + 1) * nh], in_=ps[:, 0:NB * nh].rearrange("p (b h) -> p b h", h=nh)), [ps], [GCUM])
            ps2 = self.psum()
            k.op("pe", lambda e: e.matmul(ps2[:, 0:NB * nh].rearrange("p (b h) -> p b h", h=nh), lhsT=self.C_("ones"), rhs=G[:, :, d * nh:(d + 1) * nh], start=True, stop=True), [self.cst, G], [ps2])
            k.op("dve", lambda e: e.tensor_copy(out=GTOT[:, :, d * nh:(d + 1) * nh], in_=ps2[:, 0:NB * nh].rearrange("p (b h) -> p b h", h=nh)), [ps2], [GTOT])

    def phase_gdn(self, l):
        k = self.k
        W = self.W
        T, NB = self.T, self.NB
        with contextlib.ExitStack() as st:
            QT = self.sb(st, [128, 4, T], BF16, "gQT")
            KT = self.sb(st, [128, 4, T], BF16, "gKT")
            KTOK = self.sb(st, [128, NB, 4, 128], BF16, "gKTOK")
            VTOK = self.sb(st, [128, NB, 4, 128], BF16, "gVTOK")
            G = self.sb(st, [128, NB, 8], F32, "gG")
            GCUM = self.sb(st, [128, NB, 8], F32, "gGCUM")
            GTOT = self.sb(st, [128, NB, 8], F32, "gGTOT")
            EG = self.sb(st, [128, NB, 8], F32, "gEG")
            GL = self.sb(st, [128, NB, 8], F32, "gGL")
            KDS = self.sb(st, [128, NB, 8], F32, "gKDS")
            BETA = self.sb(st, [128, NB, 8], F32, "gBETA")
            NBETA = self.sb(st, [128, NB, 8], F32, "gNBETA")
            with contextlib.ExitStack() as st2:
                wc_ = [self.sb(st2, [128, 6], F32, "gwc") for _ in range(2)]
                sq_ = [self.sb(st2, [128, 512], F32, "gsq") for _ in range(2)]
                rs_ = [self.sb(st2, [128, 512], F32, "grs") for _ in range(2)]
                st2a = contextlib.ExitStack()
                u_ = [self.sb(st2a, [128, T], F32, "gu") for _ in range(1)] * 2
                acc_ = [self.sb(st2a, [128, T], F32, "gacc") for _ in range(1)] * 2
                it = 0
                for c in range(12):
                    u, acc, wc = u_[c % 2], acc_[c % 2], wc_[c % 2]
                    r0 = self.urow(c * 128)
                    kind, h = c // 4, c % 4
                    k.dma("sp", u[:, :], self.U.t[r0:r0 + 128, :], [], [u], u)
                    k.dma("sp", wc[:, 0:5], W["gdn_conv"][l][:, c * 128:(c + 1) * 128].rearrange("j c -> c j"), [], [wc], wc, allow_slow_non_contiguous=True)
                    self.conv_silu(st2, u, acc, wc, None, acc[:, :])
                    if kind < 2:
                        dstT = QT if kind == 0 else KT
                        for ti, (t0, n) in enumerate(self.tiles):
                            sq, rs = sq_[it % 2], rs_[it % 2]
                            it += 1
                            k.op("act", lambda e: e.activation(out=sq[:, 0:n], in_=acc[:, t0:t0 + n], func=AF.Square), [acc], [sq])
                            ps = self.psum()
                            k.op("pe", lambda e: e.matmul(ps[:, 0:n], lhsT=self.C_("ones"), rhs=sq[:, 0:n], start=True, stop=True), [self.cst, sq], [ps])
                            self.rstd_from(ps, n, rs, 1.0)
                            if kind == 0:
                                k.op("dve", lambda e: e.scalar_tensor_tensor(out=dstT[:, h, t0:t0 + n], in0=acc[:, t0:t0 + n], scalar=128.0 ** -0.5, in1=rs[:, 0:n], op0=ALU.mult, op1=ALU.mult), [acc, rs], [dstT])
                            else:
                                k.op("dve", lambda e: e.tensor_tensor(out=acc[:, t0:t0 + n], in0=acc[:, t0:t0 + n], in1=rs[:, 0:n], op=ALU.mult), [acc, rs], [acc])
                                k.op("act", lambda e: e.copy(out=dstT[:, h, t0:t0 + n], in_=acc[:, t0:t0 + n]), [acc], [dstT])
                    if kind >= 1:
                        dst = KTOK if kind == 1 else VTOK
                        for blk in range(NB):
                            ps = self.psum()
                            k.op("pe", lambda e: e.transpose(out=ps[:, 0:128], in_=acc[:, blk * 128:(blk + 1) * 128], identity=self.C_("ident")), [acc, self.cst], [ps])
                            k.op("act", lambda e: e.copy(out=dst[:, blk, h, :], in_=ps[:, 0:128]), [ps], [dst])
                k.barrier()
                st2a.close()
                AB = self.sb(st2, [128, NB, 16], F32, "gAB")
                self.tok_scalars(st2, GA, 16, AB)
                pb = self.sb(st2, [128, 2, 8], F32, "gpb")
                k.dma("sp", pb[:, 0, :], W["gdn_dt_bias"][l:l + 1].rearrange("o a b -> o (a b)").partition_broadcast(128), [], [pb], pb)
                k.dma("sp", pb[:, 1, :], W["gdn_a_log"][l:l + 1].rearrange("o a b -> o (a b)").partition_broadcast(128), [], [pb], pb)
                k.op("dve", lambda e: e.tensor_tensor(out=G[:, :, :], in0=AB[:, :, 0:8], in1=pb[:, 0, :].unsqueeze(1).to_broadcast([128, NB, 8]), op=ALU.add), [AB, pb], [G])
                self.softplus(st2, G, G.t[:, :, :].rearrange("p b c -> p (b c)"), NB * 8)
                k.op("act", lambda e: e.activation(out=pb[:, 1, :], in_=pb[:, 1, :], func=AF.Exp), [pb], [pb])
                k.op("dve", lambda e: e.scalar_tensor_tensor(out=G[:, :, :], in0=G[:, :, :], scalar=-1.0, in1=pb[:, 1, :].unsqueeze(1).to_broadcast([128, NB, 8]), op0=ALU.mult, op1=ALU.mult), [G, pb], [G])
                k.op("act", lambda e: e.activation(out=BETA[:, :, :], in_=AB[:, :, 8:16], func=AF.Sigmoid), [AB], [BETA])
                k.op("dve", lambda e: e.tensor_scalar(out=NBETA[:, :, :], in0=BETA[:, :, :], scalar1=-1.0, scalar2=None, op0=ALU.mult), [BETA], [NBETA])
                self.cum_stats(st2, G, GCUM, GTOT, 4)
                k.op("act", lambda e: e.activation(out=EG[:, :, :], in_=GCUM[:, :, :], func=AF.Exp), [GCUM], [EG])
                k.op("act", lambda e: e.activation(out=GL[:, :, :], in_=GTOT[:, :, :], func=AF.Exp), [GTOT], [GL])
                k.op("dve", lambda e: e.tensor_tensor(out=KDS[:, :, :], in0=GTOT[:, :, :], in1=GCUM[:, :, :], op=ALU.subtract), [GTOT, GCUM], [KDS])
                k.op("act", lambda e: e.activation(out=KDS[:, :, :], in_=KDS[:, :, :], func=AF.Exp), [KDS], [KDS])
                k.barrier()
            S_ = self.sb(st, [128, 8, 128], F32, "gS")
            Sb = self.sb(st, [128, 8, 128], BF16, "gSb")
            k.op("dve", lambda e: e.memset(S_[:, :, :], 0.0), [], [S_])
            k.op("dve", lambda e: e.memset(Sb[:, :, :], 0.0), [], [Sb])
            R = 2
            mk = lambda p, dt=F32: [self.sb(st, [128, 128], dt, p) for _ in range(R)]
            rhsb, Dm, E, RE, t1 = mk("grh"), mk("gDm"), mk("gE"), mk("gRE"), mk("gt1")
            AttnT, QgT, Kd, Xb, Rp, vnew = mk("gAt", BF16), mk("gQg", BF16), mk("gKd", BF16), mk("gXb", BF16), mk("gRp", BF16), mk("gvn", BF16)
            Pk = [mk("gP%d" % i) for i in range(1)]
            PTk = [mk("gPT%d" % i) for i in range(1)]
            XTb, CTb, Cb, Zb, Z2b = mk("gXTb", BF16), mk("gCTb", BF16), mk("gCb", BF16), mk("gZb", BF16), mk("gZ2b", BF16)
            GM = self.sb(st, [128, 14, 128], F32, "gGM")
            k.dma("sp", GM[:, :, :], self.gmask_in, [], [GM], GM)
            X = mk("gX")
            oo = mk("goo")
            Sbufs = {(h, d): Buf(None) for h in range(4) for d in range(2)}
            ident = self.C_("ident")
            orders = [self.blk_order(0), self.blk_order(1)]
            it = 0
            evi = [0]

            def evac(out_ap, ps_ap, rd, wr):
                evi[0] += 1
                if evi[0] % 2:
                    k.op("act", lambda e: e.copy(out=out_ap, in_=ps_ap), rd, wr)
                else:
                    k.op("dve", lambda e: e.tensor_copy(out=out_ap, in_=ps_ap), rd, wr)
            def make_unit(i, h, d, j):
                blk = orders[d][i]
                ci = d * 4 + h
                sb_ = Sbufs[(h, d)]
                tri = self.C_("triF" if d == 0 else "triB")
                mneg = self.C_("mnegF" if d == 0 else "mnegB")
                strict = self.C_("strF" if d == 0 else "strB")
                cs = slice(blk * 128, (blk + 1) * 128)
                sc = lambda Tn: Tn[:, blk, ci:ci + 1]

                def pre_fn():
                    k.op("dve", lambda e: e.tensor_scalar(out=rhsb[j][:, :], in0=tri, scalar1=sc(G), scalar2=None, op0=ALU.mult), [self.cst, G], [rhsb[j]])
                    pa = self.psum()
                    k.op("pe", lambda e: e.matmul(pa[:, 0:128], lhsT=self.C_("ones"), rhs=rhsb[j][:, :], start=True, stop=True), [self.cst, rhsb[j]], [pa])
                    k.op("dve", lambda e: e.scalar_tensor_tensor(out=Dm[j][:, :], in0=pa[:, 0:128], scalar=sc(GCUM), in1=mneg, op0=ALU.subtract, op1=ALU.add), [pa, GCUM, self.cst], [Dm[j]])
                    k.op("act", lambda e: e.activation(out=E[j][:, :], in_=Dm[j][:, :], func=AF.Exp), [Dm[j]], [E[j]])
                    k.op("act", lambda e: e.activation(out=RE[j][:, :], in_=pa[:, 0:128], func=AF.Exp), [pa], [RE[j]])
                    pA = self.psum()
                    k.op("pe", lambda e: e.matmul(pA[:, 0:128], lhsT=KT[:, h, cs], rhs=KT[:, h, cs], start=True, stop=True), [KT], [pA])
                    k.op("pe", lambda e: e.matmul(pA[:, 128:256], lhsT=KT[:, h, cs], rhs=QT[:, h, cs], start=True, stop=True), [KT, QT], [pA])
                    k.op("dve", lambda e: e.scalar_tensor_tensor(out=t1[j][:, :], in0=pA[:, 0:128], scalar=sc(BETA), in1=E[j][:, :], op0=ALU.mult, op1=ALU.mult), [pA, BETA, E[j]], [t1[j]])
                    P0, PT0 = Pk[0][j], PTk[0][j]
                    k.op("dve", lambda e: e.tensor_tensor(out=P0[:, :], in0=t1[j][:, :], in1=strict, op=ALU.mult), [t1[j], self.cst], [P0])
                    k.op("dve", lambda e: e.tensor_tensor(out=AttnT[j][:, :], in0=pA[:, 128:256], in1=E[j][:, :], op=ALU.mult), [pA, E[j]], [AttnT[j]])
                    k.op("dve", lambda e: e.tensor_tensor(out=QgT[j][:, :], in0=QT[:, h, cs], in1=RE[j][:, :], op=ALU.mult), [QT, RE[j]], [QgT[j]])
                    k.op("dve", lambda e: e.tensor_scalar(out=Kd[j][:, :], in0=KTOK[:, blk, h, :], scalar1=sc(KDS), scalar2=None, op0=ALU.mult), [KTOK, KDS], [Kd[j]])
                    pt = self.psum()
                    k.op("pe", lambda e: e.transpose(out=pt[:, 0:128], in_=P0[:, :], identity=ident), [P0, self.cst], [pt])
                    evac(PT0[:, :], pt[:, 0:128], [pt], [PT0])
                    mC = lambda lv: GM[:, (0 if d == 0 else 7) + lv, :]
                    mCT = lambda lv: GM[:, (7 if d == 0 else 0) + lv, :]
                    k.op("dve", lambda e: e.tensor_tensor(out=t1[j][:, :], in0=P0[:, :], in1=mC(0), op=ALU.mult), [P0, GM], [t1[j]])
                    k.op("dve", lambda e: e.tensor_tensor(out=Xb[j][:, :], in0=ident, in1=t1[j][:, :], op=ALU.subtract), [self.cst, t1[j]], [Xb[j]])
                    k.op("dve", lambda e: e.tensor_tensor(out=t1[j][:, :], in0=PT0[:, :], in1=mCT(0), op=ALU.mult), [PT0, GM], [t1[j]])
                    k.op("dve", lambda e: e.tensor_tensor(out=XTb[j][:, :], in0=ident, in1=t1[j][:, :], op=ALU.subtract), [self.cst, t1[j]], [XTb[j]])
                    for lv in range(1, 1 if 'gdn_noneu' in self.dbg else 7):
                        lastlv = (lv == 6)
                        k.op("dve", lambda e: e.tensor_tensor(out=CTb[j][:, :], in0=PT0[:, :], in1=mCT(lv), op=ALU.mult), [PT0, GM], [CTb[j]])
                        pz = self.psum()
                        k.op("pe", lambda e: e.matmul(pz[:, 0:128], lhsT=CTb[j][:, :], rhs=Xb[j][:, :], start=True, stop=True), [CTb[j], Xb[j]], [pz])
                        evac(Zb[j][:, :], pz[:, 0:128], [pz], [Zb[j]])
                        py_ = self.psum()
                        k.op("pe", lambda e: e.matmul(py_[:, 0:128], lhsT=XTb[j][:, :], rhs=Zb[j][:, :], start=True, stop=True), [XTb[j], Zb[j]], [py_])
                        if not lastlv:
                            k.op("pool", lambda e: e.tensor_tensor(out=Cb[j][:, :], in0=P0[:, :], in1=mC(lv), op=ALU.mult), [P0, GM], [Cb[j]])
                            pz2 = self.psum()
                            k.op("pe", lambda e: e.matmul(pz2[:, 0:128], lhsT=Cb[j][:, :], rhs=XTb[j][:, :], start=True, stop=True), [Cb[j], XTb[j]], [pz2])
                            evac(Z2b[j][:, :], pz2[:, 0:128], [pz2], [Z2b[j]])
                            py2 = self.psum()
                            k.op("pe", lambda e: e.matmul(py2[:, 0:128], lhsT=Xb[j][:, :], rhs=Z2b[j][:, :], start=True, stop=True), [Xb[j], Z2b[j]], [py2])
                        k.op("dve", lambda e: e.tensor_tensor(out=Xb[j][:, :], in0=Xb[j][:, :], in1=py_[:, 0:128], op=ALU.subtract), [Xb[j], py_], [Xb[j]])
                        if not lastlv:
                            k.op("dve", lambda e: e.tensor_tensor(out=XTb[j][:, :], in0=XTb[j][:, :], in1=py2[:, 0:128], op=ALU.subtract), [XTb[j], py2], [XTb[j]])

                def seq_fn():
                    pk = self.psum()
                    k.op("pe", lambda e: e.matmul(pk[:, 0:128], lhsT=KT[:, h, cs], rhs=Sb[:, ci, :], start=True, stop=True), [KT, sb_], [pk])
                    k.op("dve", lambda e: e.scalar_tensor_tensor(out=Rp[j][:, :], in0=pk[:, 0:128], scalar=sc(EG), in1=VTOK[:, blk, h, :], op0=ALU.mult, op1=ALU.subtract), [pk, EG, VTOK], [Rp[j]])
                    k.op("pe", lambda e: e.matmul(pk[:, 128:256], lhsT=Xb[j][:, :], rhs=Rp[j][:, :], start=True, stop=True), [Xb[j], Rp[j]], [pk])
                    k.op("dve", lambda e: e.tensor_scalar(out=vnew[j][:, :], in0=pk[:, 128:256], scalar1=sc(NBETA), scalar2=None, op0=ALU.mult), [pk, NBETA], [vnew[j]])
                    po = self.psum()
                    k.op("pe", lambda e: e.matmul(po[:, 0:128], lhsT=Sb[:, ci, :], rhs=QgT[j][:, :], start=True, stop=False), [sb_, QgT[j]], [po])
                    k.op("pe", lambda e: e.matmul(po[:, 0:128], lhsT=vnew[j][:, :], rhs=AttnT[j][:, :], start=False, stop=True), [vnew[j], AttnT[j]], [po])
                    k.op("act", lambda e: e.copy(out=oo[j][:, :], in_=po[:, 0:128]), [po], [oo[j]])
                    k.dma("pool", self.GO.t[d, h * 128:(h + 1) * 128, cs], oo[j][:, :], [oo[j]], [], oo[j])
                    k.op("pe", lambda e: e.matmul(po[:, 128:256], lhsT=Kd[j][:, :], rhs=vnew[j][:, :], start=True, stop=True), [Kd[j], vnew[j]], [po])
                    k.op("dve", lambda e: e.scalar_tensor_tensor(out=S_[:, ci, :], in0=S_[:, ci, :], scalar=sc(GL), in1=po[:, 128:256], op0=ALU.mult, op1=ALU.add), [sb_, GL, po], [sb_])
                    k.op("act", lambda e: e.copy(out=Sb[:, ci, :], in_=S_[:, ci, :]), [sb_], [sb_])
                return pre_fn, seq_fn
            ulist = []
            for i in range(0 if "gdn_noscan" in self.dbg else NB):
                for h in range(4):
                    for d in range(2):
                        ulist.append(make_unit(i, h, d, len(ulist) % R))
            if ulist:
                ulist[0][0]()
            for n_ in range(len(ulist)):
                if n_ + 1 < len(ulist):
                    ulist[n_ + 1][0]()
                ulist[n_][1]()
            k.barrier()
        with contextlib.ExitStack() as st:
            gn = self.vec_col(st, W["gdn_norm_g"][l], 128, "ggn")
            oa = [self.sb(st, [128, 512], F32, "goa") for _ in range(2)]
            ob = [self.sb(st, [128, 512], F32, "gob") for _ in range(2)]
            z_ = [self.sb(st, [128, 512], F32, "gz") for _ in range(2)]
            sq_ = [self.sb(st, [128, 512], F32, "gfsq") for _ in range(2)]
            rs_ = [self.sb(st, [128, 512], F32, "gfrs") for _ in range(2)]
            yb = [self.sb(st, [128, 512], BF16, "gyb") for _ in range(2)]
            tiles = self.tiles[1:] if self.last else self.tiles
            it = 0
            for h in range(4):
                rz = self.urow(GZ + h * 128)
                for ti, (t0, n) in enumerate(tiles):
                    j = it % 2
                    it += 1
                    k.dma("sp", oa[j][:, 0:n], self.GO.t[0, h * 128:(h + 1) * 128, t0:t0 + n], [], [oa[j]], oa[j])
                    k.dma("sp", ob[j][:, 0:n], self.GO.t[1, h * 128:(h + 1) * 128, t0:t0 + n], [], [ob[j]], ob[j])
                    k.dma("sp", z_[j][:, 0:n], self.U.t[rz:rz + 128, t0:t0 + n], [], [z_[j]], z_[j])
                    k.op("dve", lambda e: e.tensor_tensor(out=oa[j][:, 0:n], in0=oa[j][:, 0:n], in1=ob[j][:, 0:n], op=ALU.add), [oa[j], ob[j]], [oa[j]])
                    k.op("act", lambda e: e.activation(out=sq_[j][:, 0:n], in_=oa[j][:, 0:n], func=AF.Square), [oa[j]], [sq_[j]])
                    k.op("act", lambda e: e.activation(out=z_[j][:, 0:n], in_=z_[j][:, 0:n], func=AF.Silu), [z_[j]], [z_[j]])
                    ps = self.psum()
                    k.op("pe", lambda e: e.matmul(ps[:, 0:n], lhsT=self.C_("ones"), rhs=sq_[j][:, 0:n], start=True, stop=True), [self.cst, sq_[j]], [ps])
                    self.rstd_from(ps, n, rs_[j], 1.0 / 128)
                    k.op("dve", lambda e: e.scalar_tensor_tensor(out=oa[j][:, 0:n], in0=oa[j][:, 0:n], scalar=gn[:, 0:1], in1=rs_[j][:, 0:n], op0=ALU.mult, op1=ALU.mult), [oa[j], gn, rs_[j]], [oa[j]])
                    k.op("dve", lambda e: e.tensor_tensor(out=yb[j][:, 0:n], in0=oa[j][:, 0:n], in1=z_[j][:, 0:n], op=ALU.mult), [oa[j], z_[j]], [yb[j]])
                    k.dma("pool", self.Y.t[h * 128:(h + 1) * 128, t0:t0 + n], yb[j][:, 0:n], [yb[j]], [], yb[j])
            k.barrier()


_CACHE = {}


def kernel(**inputs):
    S, C, DEPTH = 4096, 256, 2
    inputs = {k_: np.asarray(v) for k_, v in inputs.items()}
    if "nc" not in _CACHE:
        _CACHE["nc"] = Mod(S, C, DEPTH).build()
    nc = _CACHE["nc"]
    consts = consts_np(S, C)
    in_maps = []
    for b in range(8):
        m = {"x": np.ascontiguousarray(inputs["x"][b], dtype=np.float32),
             "ctx": np.ascontiguousarray(inputs["ctx"][b], dtype=np.float32),
             "cc": np.ascontiguousarray(np.stack([inputs["c"][b], inputs["c_ctx"]]), dtype=np.float32)}
        m.update(consts)
        for n, sh in WSPEC:
            m[n] = np.ascontiguousarray(inputs[n], dtype=np.float32)
        in_maps.append(m)
    res = run_bass_kernel_spmd(nc, in_maps, core_ids=list(range(8)))
    return np.stack([np.asarray(r["out"], dtype=np.float32) for r in res.results], axis=0)
```

```python
import math
import contextlib
import numpy as np
import concourse.bass as bass
import concourse.mybir as mybir
from concourse.bass_utils import run_bass_kernel_spmd

F32 = mybir.dt.float32
BF16 = mybir.dt.bfloat16
ALU = mybir.AluOpType
AF = mybir.ActivationFunctionType


class Buf:
    __slots__ = ("t", "w", "r", "dsem", "name")

    def __init__(self, t, name=""):
        self.t = t
        self.w = {}
        self.r = {}
        self.dsem = None
        self.name = name

    def __getitem__(self, key):
        return self.t[key]


class KB:
    SEM_ROT = 30000

    def __init__(self, nc):
        self.nc = nc
        self.es = contextlib.ExitStack()
        self.engs = {"pe": nc.tensor, "dve": nc.vector, "act": nc.scalar,
                     "pool": nc.gpsimd, "sp": nc.sync}
        self.semh = {}
        self.cnt = {}
        self.isdma = {}
        self.cur = {}
        self.waited = {e: {} for e in self.engs}
        self.nsem = 0
        for e in self.engs:
            self.cur[e] = self.new_sem(False)
        self.ninstr = 0
        self.free_dsems = []
        self.free_dsems_q = {}
        self.phase_dsems = []
        self.persist = False
        self.dma_remap = {}

    def new_sem(self, isdma):
        key = self.nsem
        self.nsem += 1
        self.semh[key] = self.es.enter_context(self.nc.semaphore("s%d" % key))
        self.cnt[key] = 0
        self.isdma[key] = isdma
        return key

    def sb(self, stack, name, shape, dtype):
        t = stack.enter_context(self.nc.sbuf_tensor(name, list(shape), dtype))
        return Buf(t, name)

    def ps(self, stack, name, shape, dtype):
        t = stack.enter_context(self.nc.psum_tensor(name, list(shape), dtype))
        return Buf(t, name)

    def _waits(self, eng, reads, writes):
        need = {}
        for b in reads:
            for s, v in b.w.items():
                if need.get(s, 0) < v:
                    need[s] = v
        for b in writes:
            for d in (b.w, b.r):
                for s, v in d.items():
                    if eng == "pe" and s == self.cur["pe"] and d is b.w:
                        continue
                    if need.get(s, 0) < v:
                        need[s] = v
        e = self.engs[eng]
        wd = self.waited[eng]
        for s, v in need.items():
            if wd.get(s, 0) >= v:
                continue
            if self.isdma[s]:
                v = self.cnt[s]
            e.wait_ge(self.semh[s], v)
            wd[s] = v
            self.ninstr += 1

    def op(self, eng, fn, reads=(), writes=()):
        self._waits(eng, reads, writes)
        ins = fn(self.engs[eng])
        s = self.cur[eng]
        self.cnt[s] += 1
        ins.then_inc(self.semh[s], 1)
        tag = (s, self.cnt[s])
        self._mark(tag, reads, writes)
        if self.cnt[s] >= self.SEM_ROT:
            self.cur[eng] = self.new_sem(False)
        self.ninstr += 1
        return ins

    def _mark(self, tag, reads, writes):
        s, v = tag
        for b in writes:
            b.w = {s: v}
            b.r = {}
        for b in reads:
            if b not in writes:
                b.r[s] = v

    def dma(self, eng, out, in_, reads, writes, sembuf, **kw):
        eng = self.dma_remap.get(eng, eng)
        dmap = sembuf.dsem if isinstance(sembuf.dsem, dict) else {}
        sembuf.dsem = dmap
        if eng not in dmap:
            fl = self.free_dsems_q.setdefault(eng, [])
            if fl and not self.persist:
                dmap[eng] = fl.pop()
            else:
                dmap[eng] = self.new_sem(True)
            if not self.persist:
                self.phase_dsems.append((eng, dmap[eng]))
        self._waits(eng, reads, writes)
        ins = self.engs[eng].dma_start(out=out, in_=in_, **kw)
        s = sembuf.dsem[eng]
        self.cnt[s] += 16
        ins.then_inc(self.semh[s], 16)
        self._mark((s, self.cnt[s]), reads, writes)
        self.ninstr += 1
        return ins

    def barrier(self, engs=None):
        if engs is None:
            for q_, s_ in self.phase_dsems:
                self.free_dsems_q.setdefault(q_, []).append(s_)
            self.phase_dsems = []
        for eng in (engs or self.engs):
            e = self.engs[eng]
            wd = self.waited[eng]
            for s, v in self.cnt.items():
                if v > 0 and wd.get(s, 0) < v:
                    e.wait_ge(self.semh[s], v)
                    wd[s] = v
                    self.ninstr += 1


D = 1024
EPS = 1e-6
NEG = -30000.0
GQ, GK, GV, GZ, GA, GB_ = 0, 512, 1024, 1536, 2048, 2056
MCQ, MCKV, MKPE = 2064, 2448, 2704
DQ, DK, DV = 2736, 3248, 3760
SZ, SX, SB_, SC, SDT = 4272, 4784, 5296, 5552, 5808
GATE0 = 5824
CN = ["ident", "ones", "triF", "triB", "mnegF", "mnegB", "strF", "strB", "bd64", "perm64", "perm32", "sel65"]


def consts_np(S, C):
    T = C + S
    k = np.arange(128)
    d = {}
    d["ident"] = np.eye(128)
    d["ones"] = np.ones((128, 128))
    d["triF"] = (k[:, None] <= k[None, :])
    d["triB"] = (k[:, None] >= k[None, :])
    d["mnegF"] = np.where(k[None, :] >= k[:, None], 0.0, NEG)
    d["mnegB"] = np.where(k[None, :] <= k[:, None], 0.0, NEG)
    d["strF"] = (k[None, :] > k[:, None])
    d["strB"] = (k[None, :] < k[:, None])
    d["bd64"] = (k[:, None] // 64 == k[None, :] // 64)

    def perm(n_rot, total):
        P = np.zeros((128, 128))
        half = n_rot // 2
        qd = half // 2
        for base in range(0, total, half):
            for i in range(half):
                m = base + i
                if i < qd:
                    P[m + qd, m] = -1.0
                else:
                    P[m - qd, m] = 1.0
        return P
    d["perm64"] = perm(64, 128)
    d["perm32"] = perm(32, 32)
    s65 = np.zeros((128, 128))
    s65[64, :] = 1.0
    d["sel65"] = s65
    cst = np.stack([np.asarray(d[n], np.float32) for n in CN], axis=1)

    def rope(rot_dim):
        rows = S // 64
        row = np.repeat(np.arange(rows, dtype=np.float32), 64)
        col = np.tile(np.arange(64, dtype=np.float32), rows)
        quarter = rot_dim // 4
        inv = (np.float32(10000.0) ** (-np.arange(quarter, dtype=np.float32) / np.float32(quarter))).astype(np.float32)
        ar = row[:, None] * inv
        ac = col[:, None] * inv
        ang = np.concatenate([ar, ar, ac, ac], axis=-1).astype(np.float32)
        cos = np.ones((rot_dim, T), np.float32)
        sin = np.zeros((rot_dim, T), np.float32)
        cos[:, C:] = np.cos(ang).T
        sin[:, C:] = np.sin(ang).T
        return cos, sin
    cm, sm = rope(32)
    cd, sd = rope(64)
    ropem = np.stack([cm, sm], axis=1)
    roped = np.stack([np.tile(cd, (2, 1)), np.tile(sd, (2, 1))], axis=1)
    gm = []
    for lv in range(7):
        b = 1 << lv
        mU = ((k[:, None] // (2 * b) == k[None, :] // (2 * b)) & (k[:, None] % (2 * b) < b) & (k[None, :] % (2 * b) >= b))
        gm.append(mU)
    gm = gm + [m_.T for m_ in gm]
    gmask = np.stack([np.asarray(m_, np.float32) for m_ in gm], axis=1)
    return {"cst": np.ascontiguousarray(cst), "ropem": np.ascontiguousarray(ropem),
            "roped": np.ascontiguousarray(roped), "gmask": np.ascontiguousarray(gmask)}


WSPEC = [
    ("ada_w", [D, 6 * D]), ("ada_b", [6 * D]), ("norm1_g", [D]), ("norm2_g", [D]),
    ("w_in", [D, 9920]), ("gdn_conv", [5, 1536]), ("gdn_a_log", [2, 4]), ("gdn_dt_bias", [2, 4]),
    ("gdn_norm_g", [128]), ("mla_q_lora_g", [384]), ("mla_kv_lora_g", [256]),
    ("mla_w_uq", [384, 768]), ("mla_w_ukv", [256, 1024]), ("mla_qn_g", [96]), ("mla_kn_g", [96]),
    ("diff_qn_g", [64]), ("diff_kn_g", [64]), ("diff_lambda", [4, 64]), ("diff_sub_g", [128]),
    ("ssm_conv", [5, 1024]), ("ssm_conv_b", [1024]), ("ssm_a_log", [2, 8]), ("ssm_dt_bias", [2, 8]),
    ("ssm_d", [8]), ("ssm_norm_g", [512]), ("w_branch", [4, 512, D]), ("w_out", [D, D]),
    ("mlp_w1", [D, 4 * D]), ("mlp_w2", [4 * D, D]),
]


class Mod:
    def __init__(self, S, C, depth, dbg=()):
        self.S, self.C, self.depth = S, C, depth
        self.T = T = S + C
        self.NB = T // 128
        self.dbg = dbg
        nc = self.nc = bass.Bass("TRN2", target_bir_lowering=False)
        self.k = KB(nc)
        self.uid = 0
        di = lambda n, sh, dt=F32: nc.dram_tensor(n, list(sh), dt, kind="ExternalInput").ap()
        self.x_in = di("x", [S, D])
        self.ctx_in = di("ctx", [C, D])
        self.cc_in = di("cc", [2, D])
        self.cst_in = di("cst", [128, len(CN), 128])
        self.ropem_in = di("ropem", [32, 2, T])
        self.roped_in = di("roped", [128, 2, T])
        self.gmask_in = di("gmask", [128, 14, 128])
        self.W = {n: di(n, [depth] + sh) for n, sh in WSPEC}
        self.out = nc.dram_tensor("out", [S, D], F32, kind="ExternalOutput").ap()
        dscr = lambda n, sh, dt=F32: Buf(nc.dram_tensor(n, list(sh), dt, kind=("ExternalOutput" if n in dbg else "Internal")).ap(), n)
        self.XT = dscr("XT", [D, T])
        self.U = dscr("U", [80 * 128, T])
        self.Y = dscr("Y", [2048, T], BF16)
        self.MT = dscr("MT", [D, T])
        self.HID = dscr("HID", [4 * D, T], BF16)
        self.QD = dscr("QD", [512, T], BF16)
        self.KD = dscr("KD", [512, T], BF16)
        self.VD = dscr("VD", [T, 512], BF16)
        self.KN = dscr("KN", [8 * 64, T], BF16)
        self.KR = dscr("KR", [8 * 32, T], BF16)
        self.QN = dscr("QN", [8 * 64, T], BF16)
        self.QR = dscr("QR", [8 * 32, T], BF16)
        self.VA = dscr("VA", [T, 8 * 65], BF16)
        self.XS = dscr("XS", [512, T])
        self.YS = dscr("YS", [2, 512, T])
        self.GO = dscr("GO", [2, 512, T])
        self.tiles = [(0, C)] + [(C + 512 * i, 512) for i in range(S // 512)]
        self.uchunks = []
        slot = 0
        self.uslot = {}
        for (a, b) in [(0, 2048), (2048, 2064), (2064, 2448), (2448, 2704), (2704, 2736), (2736, 4272),
                       (4272, 5808), (5808, 5824), (5824, 9920)]:
            c = a
            while c < b:
                e = min(c + 128, b)
                self.uchunks.append((c, e, slot))
                self.uslot[c] = slot
                slot += 1
                c = e
        assert slot == 80

    def nm(self, p):
        self.uid += 1
        return "%s_%d" % (p, self.uid)

    def sb(self, st, shape, dt=F32, p="t"):
        return self.k.sb(st, self.nm(p), shape, dt)

    def psum(self):
        self.psi = (self.psi + 1) % len(self.psb)
        return self.psb[self.psi]

    def urow(self, col):
        return self.uslot[col] * 128

    def build(self):
        k = self.k
        with k.es, contextlib.ExitStack() as gs:
            self.psb = [k.ps(gs, "psb%d" % i, [128, 512], F32) for i in range(8)]
            self.psi = 0
            self.cst = self.sb(gs, [128, len(CN), 128], F32, "cst")
            k.persist = True
            k.dma("sp", self.cst[:, :, :], self.cst_in, [], [self.cst], self.cst)
            k.persist = False
            self.cbf = self.sb(gs, [128, len(CN), 128], BF16, "cbf")
            k.op("dve", lambda e: e.tensor_copy(out=self.cbf[:, :, :], in_=self.cst[:, :, :]), [self.cst], [self.cbf])
            self.epsb = self.sb(gs, [128, 4], F32, "epsb")
            k.op("dve", lambda e: e.memset(self.epsb[:, :], EPS), [], [self.epsb])
            k.op("dve", lambda e: e.memset(self.epsb[:, 1:2], 1.0), [self.epsb], [self.epsb])
            self.XTt = [Buf(None, "XTt%d" % i) for i in range(len(self.tiles))]
            self.MTt = [Buf(None, "MTt%d" % i) for i in range(len(self.tiles))]
            self.phase_init()
            for l in range(self.depth):
                self.layer(l)
            self.phase_final()
            k.barrier()
        return self.nc

    def C_(self, name, bf=False):
        i = CN.index(name)
        return (self.cbf if bf else self.cst)[:, i, :]

    def phase_init(self):
        k = self.k
        with contextlib.ExitStack() as st:
            xin = [self.sb(st, [128, D], F32, "xin") for _ in range(2)]
            xo = [self.sb(st, [128, 8, 128], F32, "xo") for _ in range(2)]
            XTv = self.XT.t.rearrange("(kc p) t -> p kc t", p=128)
            for blk in range(self.NB):
                src = self.ctx_in[blk * 128:(blk + 1) * 128, :] if blk < self.C // 128 else \
                    self.x_in[blk * 128 - self.C:(blk + 1) * 128 - self.C, :]
                xi = xin[blk % 2]
                o = xo[blk % 2]
                k.dma("sp", xi[:, :], src, [], [xi], xi)
                for half in range(2):
                    ps = self.psum()
                    for j in range(4):
                        kc = half * 4 + j
                        k.op("pe", lambda e: e.transpose(out=ps[:, j * 128:(j + 1) * 128], in_=xi[:, kc * 128:(kc + 1) * 128],
                                                         identity=self.C_("ident")), [xi, self.cst], [ps])
                    eng = "dve" if half == 0 else "act"
                    if eng == "dve":
                        k.op("dve", lambda e: e.tensor_copy(out=o[:, half * 4:half * 4 + 4, :], in_=ps[:, :].rearrange("p (a b) -> p a b", a=4)), [ps], [o])
                    else:
                        k.op("act", lambda e: e.copy(out=o[:, half * 4:half * 4 + 4, :], in_=ps[:, :].rearrange("p (a b) -> p a b", a=4)), [ps], [o])
                k.dma("pool", XTv[:, :, blk * 128:(blk + 1) * 128], o[:, :, :], [o], [], o)
            k.barrier()

    def phase_final(self):
        k = self.k
        with contextlib.ExitStack() as st:
            xi = [self.sb(st, [128, 8, 128], F32, "fxi") for _ in range(2)]
            xo = [self.sb(st, [128, D], F32, "fxo") for _ in range(2)]
            XTv = self.XT.t.rearrange("(kc p) t -> p kc t", p=128)
            dout = Buf(self.out, "out")
            for b in range(self.S // 128):
                blk = b + self.C // 128
                a = xi[b % 2]
                o = xo[b % 2]
                k.dma("sp", a[:, :, :], XTv[:, :, blk * 128:(blk + 1) * 128], [], [a], a)
                for half in range(2):
                    ps = self.psum()
                    for j in range(4):
                        kc = half * 4 + j
                        k.op("pe", lambda e: e.transpose(out=ps[:, j * 128:(j + 1) * 128], in_=a[:, kc, :],
                                                         identity=self.C_("ident")), [a, self.cst], [ps])
                    if half == 0:
                        k.op("dve", lambda e: e.tensor_copy(out=o[:, 0:512], in_=ps[:, :]), [ps], [o])
                    else:
                        k.op("act", lambda e: e.copy(out=o[:, 512:1024], in_=ps[:, :]), [ps], [o])
                k.dma("pool", self.out[b * 128:(b + 1) * 128, :], o[:, :], [o], [dout], o)
            k.barrier()

    def layer(self, l):
        k = self.k
        self.l = l
        self.last = (l == self.depth - 1) and ("forcectx" not in self.dbg)
        with contextlib.ExitStack() as ls:
            self.phase_mod(l, ls)
            self.phase_inproj(l)
            if "U" in self.dbg and l == 0 and "stopU" in self.dbg:
                return
            for ph in ("gdn", "mla", "diff", "ssm", "merge", "mlp"):
                if ph not in self.dbg:
                    getattr(self, "phase_" + ph)(l)
            k.barrier()

    def phase_mod(self, l, ls):
        k = self.k
        W = self.W
        self.modv = self.sb(ls, [128, 48, 2], F32, "modv")
        self.A1 = self.sb(ls, [128, 8, 2], F32, "A1")
        self.A2 = self.sb(ls, [128, 8, 2], F32, "A2")
        with contextlib.ExitStack() as st:
            cv = self.sb(st, [128, 2, 8], F32, "cv")
            sv = self.sb(st, [128, 8, 2], F32, "sv")
            ab = self.sb(st, [128, 48], F32, "ab")
            g12 = self.sb(st, [128, 2, 8], F32, "g12")
            k.dma("sp", cv[:, :, :], self.cc_in.rearrange("r (kc p) -> p r kc", p=128), [], [cv], cv, allow_slow_non_contiguous=True)
            k.dma("sp", ab[:, :], W["ada_b"][l].rearrange("(j p) -> p j", p=128), [], [ab], ab, allow_slow_non_contiguous=True)
            k.dma("sp", g12[:, 0, :], W["norm1_g"][l].rearrange("(kc p) -> p kc", p=128), [], [g12], g12, allow_slow_non_contiguous=True)
            k.dma("sp", g12[:, 1, :], W["norm2_g"][l].rearrange("(kc p) -> p kc", p=128), [], [g12], g12, allow_slow_non_contiguous=True)
            k.op("act", lambda e: e.activation(out=sv[:, :, :], in_=cv[:, :, :].rearrange("p r kc -> p kc r"), func=AF.Silu), [cv], [sv])
            wst = [self.sb(st, [128, 8, 512], F32, "adaw") for _ in range(2)]
            aw = W["ada_w"][l].rearrange("(kc p) c -> p kc c", p=128)
            ps = self.psum()
            for g in range(12):
                w = wst[g % 2]
                k.dma("sp", w[:, :, :], aw[:, :, g * 512:(g + 1) * 512], [], [w], w)
                for jj in range(4):
                    j = g * 4 + jj
                    for kc in range(8):
                        k.op("pe", lambda e: e.matmul(ps[:, j * 2:j * 2 + 2], lhsT=w[:, kc, jj * 128:(jj + 1) * 128], rhs=sv[:, kc, :],
                                                      start=(kc == 0), stop=(kc == 7)), [w, sv], [ps])
            k.op("dve", lambda e: e.tensor_tensor(out=self.modv[:, :, :], in0=ps[:, 0:96].rearrange("p (j r) -> p j r", r=2),
                                                  in1=ab[:, :].unsqueeze(2).to_broadcast([128, 48, 2]), op=ALU.add), [ps, ab], [self.modv])
            for (A, gi, mi) in ((self.A1, 0, 1), (self.A2, 1, 4)):
                k.op("dve", lambda e: e.scalar_tensor_tensor(out=A[:, :, :], in0=self.modv[:, mi * 8:mi * 8 + 8, :], scalar=1.0,
                                                             in1=g12[:, gi, :].unsqueeze(2).to_broadcast([128, 8, 2]), op0=ALU.add, op1=ALU.mult),
                     [self.modv, g12], [A])
            k.barrier()

    def mcol(self, ti):
        return 1 if ti == 0 else 0

    def norm_mod(self, st, A, shift_idx, dst):
        k = self.k
        xt_ = [self.sb(st, [128, 8, 512], F32, "nx") for _ in range(2)]
        sq_ = [self.sb(st, [128, 8, 512], F32, "nsq") for _ in range(2)]
        rs_ = [self.sb(st, [128, 512], F32, "nrs") for _ in range(2)]
        tmp_ = [self.sb(st, [128, 512], F32, "ntmp") for _ in range(3)]
        XTv = self.XT.t.rearrange("(kc p) t -> p kc t", p=128)
        ci = 0
        for ti, (t0, n) in enumerate(self.tiles):
            col = self.mcol(ti)
            xt, sq, rs = xt_[ti % 2], sq_[ti % 2], rs_[ti % 2]
            k.dma("sp", xt[:, :, 0:n], XTv[:, :, t0:t0 + n], [self.XTt[ti]], [xt], xt)
            k.op("act", lambda e: e.activation(out=sq[:, :, 0:n], in_=xt[:, :, 0:n], func=AF.Square), [xt], [sq])
            ps = self.psum()
            for kc in range(8):
                k.op("pe", lambda e: e.matmul(ps[:, 0:n], lhsT=self.C_("ones"), rhs=sq[:, kc, 0:n], start=(kc == 0), stop=(kc == 7)),
                     [self.cst, sq], [ps])
            self.rstd_from(ps, n, rs, 1.0 / D)
            for kc in range(8):
                tmp = tmp_[ci % 3]
                ci += 1
                k.op("dve", lambda e: e.scalar_tensor_tensor(out=tmp[:, 0:n], in0=xt[:, kc, 0:n], scalar=A[:, kc, col:col + 1], in1=rs[:, 0:n],
                                                             op0=ALU.mult, op1=ALU.mult), [xt, A, rs], [tmp])
                k.op("act", lambda e: e.activation(out=dst[:, kc, t0:t0 + n], in_=tmp[:, 0:n], func=AF.Identity,
                                                   bias=self.modv[:, shift_idx * 8 + kc, col:col + 1], scale=1.0), [tmp, self.modv], [dst])

    def gemm_fm(self, st, in_sb, KC, wsrc, chunks, tiles, epi, krows=128):
        k = self.k
        groups, cur = [], []
        for ch in chunks:
            if cur and (ch[0] != cur[-1][1] or ch[1] - cur[0][0] > 512):
                groups.append(cur)
                cur = []
            cur.append(ch)
        if cur:
            groups.append(cur)
        wst = [self.sb(st, [128, KC, 512], F32, "wst") for _ in range(2)]
        wbf = [self.sb(st, [128, KC, 512], BF16, "wbf") for _ in range(2)]
        for gi, g in enumerate(groups):
            c0, c1 = g[0][0], g[-1][1]
            w = c1 - c0
            s, b = wst[gi % 2], wbf[gi % 2]
            k.dma("sp", s[0:krows, :, 0:w], wsrc(c0, c1), [], [s], s)
            k.op("pool", lambda e: e.tensor_copy(out=b[0:krows, :, 0:w], in_=s[0:krows, :, 0:w]), [s], [b])
            for ti, (t0, n) in enumerate(tiles):
                for ch in g:
                    rows = ch[1] - ch[0]
                    off = ch[0] - c0
                    ps = self.psum()
                    for kk in range(KC):
                        k.op("pe", lambda e: e.matmul(ps[0:rows, 0:n], lhsT=b[0:krows, kk, off:off + rows], rhs=in_sb[0:krows, kk, t0:t0 + n],
                                                      start=(kk == 0), stop=(kk == KC - 1)), [b, in_sb], [ps])
                    epi(ch, ti, t0, n, ps, rows)

    def phase_inproj(self, l):
        k = self.k
        with contextlib.ExitStack() as st:
            hT = self.sb(st, [128, 8, self.T], BF16, "hT")
            with contextlib.ExitStack() as st2:
                self.norm_mod(st2, self.A1, 0, hT)
                k.barrier()
            stg = [self.sb(st, [128, 512], F32, "ustg") for _ in range(4)]
            cnt = [0]
            win = self.W["w_in"][l].rearrange("(kc p) c -> p kc c", p=128)

            def epi(ch, ti, t0, n, ps, rows):
                s = stg[cnt[0] % 4]
                if cnt[0] % 2 == 0:
                    k.op("dve", lambda e: e.tensor_copy(out=s[0:rows, 0:n], in_=ps[0:rows, 0:n]), [ps], [s])
                else:
                    k.op("act", lambda e: e.copy(out=s[0:rows, 0:n], in_=ps[0:rows, 0:n]), [ps], [s])
                cnt[0] += 1
                r0 = ch[2] * 128
                k.dma("pool", self.U.t[r0:r0 + rows, t0:t0 + n], s[0:rows, 0:n], [s], [], s)
            self.gemm_fm(st, hT, 8, lambda c0, c1: win[:, :, c0:c1], self.uchunks, self.tiles, epi)
            vst = [self.sb(st, [128, 512], BF16, "vst") for _ in range(2)]

            def epiv(blk, ps):
                v = vst[blk % 2]
                k.op("act", lambda e: e.copy(out=v[:, :], in_=ps[:, :]), [ps], [v])
                k.dma("pool", self.VD.t[blk * 128:(blk + 1) * 128, :], v[:, :], [v], [], v)
            self.gemm_tm(st, hT, 8, lambda s_: k.dma("sp", s_[:, :, :], win[:, :, DV:DV + 512], [], [s_], s_), 512, list(range(self.NB)), epiv)
            k.barrier()

    def gemm_tm(self, st, in_sb, KC, wsrc, width, blocks, epi, krows=128):
        k = self.k
        s = self.sb(st, [128, KC, width], F32, "wtm")
        b = self.sb(st, [128, KC, width], BF16, "wtmb")
        wsrc(s)
        k.op("pool", lambda e: e.tensor_copy(out=b[0:krows, :, :], in_=s[0:krows, :, :]), [s], [b])
        for blk in blocks:
            ps = self.psum()
            for kk in range(KC):
                k.op("pe", lambda e: e.matmul(ps[:, 0:width], lhsT=in_sb[0:krows, kk, blk * 128:(blk + 1) * 128], rhs=b[0:krows, kk, :],
                                              start=(kk == 0), stop=(kk == KC - 1)), [in_sb, b], [ps])
            epi(blk, ps)

    def rstd_from(self, ps, n, rs, scale, rows=128):
        k = self.k
        k.op("act", lambda e: e.activation(out=rs[0:rows, 0:n], in_=ps[0:rows, 0:n], func=AF.Ln, bias=self.epsb[0:rows, 0:1], scale=scale), [ps, self.epsb], [rs])
        k.op("act", lambda e: e.activation(out=rs[0:rows, 0:n], in_=rs[0:rows, 0:n], func=AF.Exp, scale=-0.5), [rs], [rs])

    def vec_col(self, st, src_ap, rows, p="vc"):
        t = self.sb(st, [128, 1], F32, p)
        self.k.dma("sp", t[0:rows, :], src_ap.rearrange("(p o) -> p o", o=1), [], [t], t, allow_slow_non_contiguous=True)
        return t

    def phase_merge(self, l):
        k = self.k
        W = self.W
        tiles = self.tiles[1:] if self.last else self.tiles
        MTv = self.MT.t
        for i in range(4):
            with contextlib.ExitStack() as st:
                yin = self.sb(st, [128, 4, self.T], BF16, "yin")
                k.dma("sp", yin[:, :, :], self.Y.t[i * 512:(i + 1) * 512, :].rearrange("(kc p) t -> p kc t", p=128), [], [yin], yin)
                gt_ = [self.sb(st, [128, 512], F32, "gt") for _ in range(3)]
                mt_ = [self.sb(st, [128, 512], F32, "mt") for _ in range(3)]
                cnt = [0]
                wb = W["w_branch"][l, i].rearrange("(kc p) c -> p kc c", p=128)

                def epi(ch, ti, t0, n, ps, rows):
                    c = ch[0] // 128
                    gt, mt = gt_[cnt[0] % 3], mt_[cnt[0] % 3]
                    cnt[0] += 1
                    r0 = self.urow(GATE0 + i * 1024 + c * 128)
                    k.dma("sp", gt[:, 0:n], self.U.t[r0:r0 + 128, t0:t0 + n], [], [gt], gt)
                    k.op("act", lambda e: e.activation(out=gt[:, 0:n], in_=gt[:, 0:n], func=AF.Sigmoid), [gt], [gt])
                    if i > 0:
                        k.dma("sp", mt[:, 0:n], MTv[c * 128:(c + 1) * 128, t0:t0 + n], [], [mt], mt)
                        k.op("dve", lambda e: e.tensor_tensor(out=gt[:, 0:n], in0=ps[:, 0:n], in1=gt[:, 0:n], op=ALU.mult), [ps, gt], [gt])
                        k.op("dve", lambda e: e.tensor_tensor(out=mt[:, 0:n], in0=mt[:, 0:n], in1=gt[:, 0:n], op=ALU.add), [mt, gt], [mt])
                    else:
                        k.op("dve", lambda e: e.tensor_tensor(out=mt[:, 0:n], in0=ps[:, 0:n], in1=gt[:, 0:n], op=ALU.mult), [ps, gt], [mt])
                    k.dma("pool", MTv[c * 128:(c + 1) * 128, t0:t0 + n], mt[:, 0:n], [mt], [], mt)
                self.gemm_fm(st, yin, 4, lambda c0, c1: wb[:, :, c0:c1], [(c * 128, (c + 1) * 128) for c in range(8)], tiles, epi)
                k.barrier()
        with contextlib.ExitStack() as st:
            mT = self.sb(st, [128, 8, self.T], BF16, "mTb")
            with contextlib.ExitStack() as st2:
                ml = [self.sb(st2, [128, 8, 512], F32, "ml") for _ in range(2)]
                for ti, (t0, n) in enumerate(tiles):
                    m = ml[ti % 2]
                    k.dma("sp", m[:, :, 0:n], MTv.rearrange("(kc p) t -> p kc t", p=128)[:, :, t0:t0 + n], [], [m], m)
                    k.op("dve", lambda e: e.tensor_copy(out=mT[:, :, t0:t0 + n], in_=m[:, :, 0:n]), [m], [mT])
                k.barrier()
            self.resid_gemm(st, mT, 8, self.W["w_out"][l].rearrange("(kc p) c -> p kc c", p=128), 16, tiles)
            k.barrier()

    def resid_gemm(self, st, in_sb, KC, wv, gate_idx, tiles):
        k = self.k
        xt_ = [self.sb(st, [128, 512], F32, "rx") for _ in range(4)]
        cnt = [0]
        XTv = self.XT.t

        def epi(ch, ti, t0, n, ps, rows):
            c = ch[0] // 128
            col = 1 if t0 == 0 else 0
            xt = xt_[cnt[0] % 4]
            cnt[0] += 1
            k.dma("sp", xt[:, 0:n], XTv[c * 128:(c + 1) * 128, t0:t0 + n], [], [xt], xt)
            k.op("dve", lambda e: e.scalar_tensor_tensor(out=xt[:, 0:n], in0=ps[:, 0:n], scalar=self.modv[:, gate_idx + c, col:col + 1], in1=xt[:, 0:n],
                                                         op0=ALU.mult, op1=ALU.add), [ps, self.modv, xt], [xt])
            k.dma("pool", XTv[c * 128:(c + 1) * 128, t0:t0 + n], xt[:, 0:n], [xt], [], xt)
        self.gemm_fm(st, in_sb, KC, lambda c0, c1: wv[:, :, c0:c1], [(c * 128, (c + 1) * 128) for c in range(8)], tiles, epi)

    def phase_mlp(self, l):
        k = self.k
        tiles = self.tiles[1:] if self.last else self.tiles
        with contextlib.ExitStack() as st:
            hT = self.sb(st, [128, 8, self.T], BF16, "h2T")
            with contextlib.ExitStack() as st2:
                self.norm_mod(st2, self.A2, 3, hT)
                k.barrier()
            stg = [self.sb(st, [128, 512], F32, "hs") for _ in range(3)]
            stb = [self.sb(st, [128, 512], BF16, "hb") for _ in range(3)]
            cnt = [0]
            w1 = self.W["mlp_w1"][l].rearrange("(kc p) c -> p kc c", p=128)

            def epi(ch, ti, t0, n, ps, rows):
                s, b = stg[cnt[0] % 3], stb[cnt[0] % 3]
                cnt[0] += 1
                k.op("dve", lambda e: e.tensor_scalar_max(out=s[:, 0:n], in0=ps[:, 0:n], scalar1=0.0), [ps], [s])
                k.op("act", lambda e: e.activation(out=b[:, 0:n], in_=s[:, 0:n], func=AF.Square), [s], [b])
                k.dma("pool", self.HID.t[ch[0]:ch[1], t0:t0 + n], b[:, 0:n], [b], [], b)
            self.gemm_fm(st, hT, 8, lambda c0, c1: w1[:, :, c0:c1], [(c * 128, (c + 1) * 128) for c in range(32)], tiles, epi)
            k.barrier()
        for q in range(4):
            with contextlib.ExitStack() as st:
                hin = self.sb(st, [128, 8, self.T], BF16, "hin")
                k.dma("sp", hin[:, :, :], self.HID.t[q * 1024:(q + 1) * 1024, :].rearrange("(kc p) t -> p kc t", p=128), [], [hin], hin)
                w2 = self.W["mlp_w2"][l, q * 1024:(q + 1) * 1024, :].rearrange("(kc p) c -> p kc c", p=128)
                self.resid_gemm(st, hin, 8, w2, 40, tiles)
                k.barrier()

    def attend(self, st, terms, vfn, M, kblocks, qtiles, scale, ones_sum, epi, ptag=0):
        k = self.k
        pt_ = self.pt_
        for qi, (qc0, t0, n) in enumerate(qtiles):
            O = self.psb[4 + 2 * ptag]
            Sps = self.psb[5 + 2 * ptag]
            nk = len(kblocks)
            sps = {}

            def score(i):
                sp = self.psb[self.sci % 4]
                self.sci += 1
                sps[i] = sp
                kb = kblocks[i]
                for j, (K_sb, Q_sb, r0, rows) in enumerate(terms):
                    k.op("pe", lambda e: e.matmul(sp[:, 0:n], lhsT=K_sb[r0:r0 + rows, kb * 128:(kb + 1) * 128], rhs=Q_sb[r0:r0 + rows, qc0:qc0 + n],
                                                  start=(j == 0), stop=(j == len(terms) - 1)), [K_sb, Q_sb], [sp])
            score(0)
            for i in range(nk):
                if i + 1 < nk:
                    score(i + 1)
                pt = pt_[self.pti % 3]
                self.pti += 1
                sp = sps.pop(i)
                k.op("act", lambda e: e.activation(out=pt[:, 0:n], in_=sp[:, 0:n], func=AF.Exp, scale=scale), [sp], [pt])
                k.op("pe", lambda e: e.matmul(O[0:M, 0:n], lhsT=vfn(kblocks[i]), rhs=pt[:, 0:n], start=(i == 0), stop=(i == nk - 1)), [self.vbuf, pt], [O])
                if ones_sum:
                    k.op("pe", lambda e: e.matmul(Sps[:, 0:n], lhsT=self.C_("ones", True), rhs=pt[:, 0:n], start=(i == 0), stop=(i == nk - 1)), [self.cbf, pt], [Sps])
            epi(qi, t0, n, O, Sps)

    def attn_common(self, st):
        self.pt_ = [self.sb(st, [128, 512], BF16, "pt") for _ in range(3)]
        self.sci = 0
        self.pti = 0

    def qtiles(self, lat):
        if lat:
            return [(t0, t0, n) for (t0, n) in self.tiles[1:]]
        return [(0, 0, self.C)]

    def phase_diff(self, l):
        k = self.k
        W = self.W
        T, NB = self.T, self.NB
        lam_init = 0.8 - 0.6 * math.exp(-0.3 * l)
        QD = self.QD
        KD = self.KD
        with contextlib.ExitStack() as st:
            gq = self.sb(st, [128, 2], F32, "dg")
            for j, nm_ in enumerate(("diff_qn_g", "diff_kn_g")):
                for hh in range(2):
                    k.dma("sp", gq[hh * 64:(hh + 1) * 64, j:j + 1], W[nm_][l].rearrange("(p o) -> p o", o=1), [], [gq], gq, allow_slow_non_contiguous=True)
            u_ = [self.sb(st, [128, 512], F32, "du") for _ in range(2)]
            sq_ = [self.sb(st, [128, 512], F32, "dsq") for _ in range(2)]
            rs_ = [self.sb(st, [128, 512], F32, "drs") for _ in range(2)]
            cs_ = [self.sb(st, [128, 2, 512], F32, "dcs") for _ in range(2)]
            o_ = [self.sb(st, [128, 512], F32, "do") for _ in range(2)]
            ob_ = [self.sb(st, [128, 512], BF16, "dob") for _ in range(2)]
            it = 0
            for j, (col0, dst) in enumerate(((DQ, QD), (DK, KD))):
                for h in range(4):
                    r0 = self.urow(col0 + h * 128)
                    for ti, (t0, n) in enumerate(self.tiles):
                        u, sq, rs, cs, o, ob = u_[it % 2], sq_[it % 2], rs_[it % 2], cs_[it % 2], o_[it % 2], ob_[it % 2]
                        it += 1
                        k.dma("sp", u[:, 0:n], self.U.t[r0:r0 + 128, t0:t0 + n], [], [u], u)
                        k.dma("sp", cs[:, :, 0:n], self.roped_in[:, :, t0:t0 + n], [], [cs], cs)
                        k.op("act", lambda e: e.activation(out=sq[:, 0:n], in_=u[:, 0:n], func=AF.Square), [u], [sq])
                        ps = self.psum()
                        k.op("pe", lambda e: e.matmul(ps[:, 0:n], lhsT=self.C_("bd64"), rhs=sq[:, 0:n], start=True, stop=True), [self.cst, sq], [ps])
                        self.rstd_from(ps, n, rs, 1.0 / 64)
                        k.op("dve", lambda e: e.scalar_tensor_tensor(out=u[:, 0:n], in0=u[:, 0:n], scalar=gq[:, j:j + 1], in1=rs[:, 0:n], op0=ALU.mult, op1=ALU.mult), [u, gq, rs], [u])
                        ps2 = self.psum()
                        k.op("pe", lambda e: e.matmul(ps2[:, 0:n], lhsT=self.C_("perm64"), rhs=u[:, 0:n], start=True, stop=True), [self.cst, u], [ps2])
                        k.op("dve", lambda e: e.tensor_tensor(out=o[:, 0:n], in0=u[:, 0:n], in1=cs[:, 0, 0:n], op=ALU.mult), [u, cs], [o])
                        k.op("dve", lambda e: e.tensor_tensor(out=sq[:, 0:n], in0=ps2[:, 0:n], in1=cs[:, 1, 0:n], op=ALU.mult), [ps2, cs], [sq])
                        k.op("dve", lambda e: e.tensor_tensor(out=ob[:, 0:n], in0=o[:, 0:n], in1=sq[:, 0:n], op=ALU.add), [o, sq], [ob])
                        k.dma("pool", dst.t[h * 128:(h + 1) * 128, t0:t0 + n], ob[:, 0:n], [ob], [], ob)
            k.barrier()
        with contextlib.ExitStack() as st:
            self.attn_common(st)
            lamt2 = self.sb(st, [128, 256], F32, "lamt")
            k.dma("sp", lamt2[:, :], W["diff_lambda"][l:l + 1].rearrange("o a b -> o (a b)").partition_broadcast(128), [], [lamt2], lamt2)

            lv = self.sb(st, [128, 4], F32, "lv")
            lt = self.sb(st, [128, 2, 64], F32, "lt")
            for j in range(2):
                k.op("dve", lambda e: e.tensor_tensor(out=lt[:, j, :], in0=lamt2[:, (2 * j) * 64:(2 * j + 1) * 64], in1=lamt2[:, (2 * j + 1) * 64:(2 * j + 2) * 64], op=ALU.mult), [lamt2], [lt])
            k.op("dve", lambda e: e.reduce_sum(out=lv[:, 0:2], in_=lt[:, :, :], axis=mybir.AxisListType.X), [lt], [lv])
            k.op("act", lambda e: e.activation(out=lv[:, 0:2], in_=lv[:, 0:2], func=AF.Exp), [lv], [lv])
            k.op("dve", lambda e: e.scalar_tensor_tensor(out=lv[:, 2:3], in0=lv[:, 1:2], scalar=-lam_init, in1=lv[:, 0:1], op0=ALU.add, op1=ALU.subtract), [lv], [lv])
            sg = self.vec_col(st, W["diff_sub_g"][l], 128, "sg")
            k.op("dve", lambda e: e.tensor_scalar(out=sg[:, :], in0=sg[:, :], scalar1=(1.0 - lam_init), scalar2=None, op0=ALU.mult), [sg], [sg])
            qh = self.sb(st, [128, T], BF16, "dqh")
            khm = [self.sb(st, [128, T], BF16, "dkh") for _ in range(2)]
            for m_ in range(2):
                k.op("pool", lambda e: e.memset(khm[m_][:, :], 0.0), [], [khm[m_]])
            vh = self.sb(st, [128, NB, 128], BF16, "dvh")
            self.vbuf = vh
            ra = [self.sb(st, [128, 512], F32, "ra") for _ in range(2)]
            aa = [self.sb(st, [128, 512], F32, "aa") for _ in range(2)]
            dd = self.sb(st, [128, 512], F32, "dd")
            yb = [self.sb(st, [128, 512], BF16, "dyb") for _ in range(2)]
            for h in range(4):
                k.dma("sp", qh[:, :], QD.t[h * 128:(h + 1) * 128, :], [], [qh], qh)
                for m_ in range(2):
                    k.dma("sp", khm[m_][m_ * 64:(m_ + 1) * 64, :], KD.t[h * 128 + m_ * 64:h * 128 + (m_ + 1) * 64, :], [], [khm[m_]], khm[m_])
                k.dma("sp", vh[:, :, :], self.VD.t[:, h * 128:(h + 1) * 128].rearrange("(b p) d -> p b d", p=128), [], [vh], vh)
                passes = [(True, list(range(NB)))]
                if not self.last:
                    passes.append((False, list(range(self.C // 128))))
                for lat, kbl in passes:
                    for qi, qt in enumerate(self.qtiles(lat)):
                        res = {}
                        for m in range(2):
                            def epi(qi_, t0, n, O, Sps, m=m):
                                res[m] = (O, Sps)
                            self.attend(st, [(khm[m], qh, 0, 128)], lambda kb: vh[:, kb, :], 128, kbl, [qt], 64 ** -0.5, True, epi, ptag=m)
                        (qc0, t0, n) = qt
                        for m in range(2):
                            O, Sps = res[m]
                            k.op("dve", lambda e: e.reciprocal(out=ra[m][:, 0:n], in_=Sps[:, 0:n]), [Sps], [ra[m]])
                            k.op("dve", lambda e: e.tensor_tensor(out=aa[m][:, 0:n], in0=O[:, 0:n], in1=ra[m][:, 0:n], op=ALU.mult), [O, ra[m]], [aa[m]])
                        k.op("dve", lambda e: e.scalar_tensor_tensor(out=dd[:, 0:n], in0=aa[1][:, 0:n], scalar=lv[:, 2:3], in1=aa[0][:, 0:n], op0=ALU.mult, op1=ALU.add), [aa[0], aa[1], lv], [dd])
                        k.op("act", lambda e: e.activation(out=aa[0][:, 0:n], in_=dd[:, 0:n], func=AF.Square), [dd], [aa[0]])
                        ps = self.psb[self.sci % 4]
                        self.sci += 1
                        k.op("pe", lambda e: e.matmul(ps[:, 0:n], lhsT=self.C_("ones"), rhs=aa[0][:, 0:n], start=True, stop=True), [self.cst, aa[0]], [ps])
                        self.rstd_from(ps, n, ra[0], 1.0 / 128)
                        y = yb[qi % 2]
                        k.op("dve", lambda e: e.scalar_tensor_tensor(out=y[:, 0:n], in0=dd[:, 0:n], scalar=sg[:, 0:1], in1=ra[0][:, 0:n], op0=ALU.mult, op1=ALU.mult), [dd, sg, ra[0]], [y])
                        k.dma("pool", self.Y.t[1024 + h * 128:1024 + (h + 1) * 128, t0:t0 + n], y[:, 0:n], [y], [], y)
            k.barrier()

    def phase_mla(self, l):
        k = self.k
        W = self.W
        T, NB = self.T, self.NB
        with contextlib.ExitStack() as st:
            cn = self.sb(st, [128, 5, T], BF16, "cn")
            RK = self.sb(st, [32, T], F32, "RK")
            SQPE = self.sb(st, [32, T], F32, "SQPE")
            gkv = self.sb(st, [128, 5], F32, "gkv")
            k.dma("sp", gkv[:, 0:2], W["mla_kv_lora_g"][l].rearrange("(kc p) -> p kc", p=128), [], [gkv], gkv, allow_slow_non_contiguous=True)
            k.dma("sp", gkv[:, 2:5], W["mla_q_lora_g"][l].rearrange("(kc p) -> p kc", p=128), [], [gkv], gkv, allow_slow_non_contiguous=True)
            gk = self.sb(st, [128, 4], F32, "gk")
            for j, nm_ in enumerate(("mla_kn_g", "mla_qn_g")):
                k.dma("sp", gk[0:64, 2 * j:2 * j + 1], W[nm_][l, 0:64].rearrange("(p o) -> p o", o=1), [], [gk], gk, allow_slow_non_contiguous=True)
                k.dma("sp", gk[0:32, 2 * j + 1:2 * j + 2], W[nm_][l, 64:96].rearrange("(p o) -> p o", o=1), [], [gk], gk, allow_slow_non_contiguous=True)
            with contextlib.ExitStack() as st2:
                u_ = [self.sb(st2, [128, 3, 512], F32, "mu") for _ in range(2)]
                sq_ = [self.sb(st2, [128, 3, 512], F32, "msq") for _ in range(2)]
                rs_ = [self.sb(st2, [128, 512], F32, "mrs") for _ in range(2)]
                it = 0
                for (col0, kc_n, dst0, nfeat) in ((MCKV, 2, 0, 256), (MCQ, 3, 2, 384)):
                    r0 = self.urow(col0)
                    for ti, (t0, n) in enumerate(self.tiles):
                        u, sq, rs = u_[it % 2], sq_[it % 2], rs_[it % 2]
                        it += 1
                        k.dma("sp", u[:, 0:kc_n, 0:n], self.U.t[r0:r0 + kc_n * 128, t0:t0 + n].rearrange("(kc p) t -> p kc t", p=128), [], [u], u)
                        k.op("act", lambda e: e.activation(out=sq[:, 0:kc_n, 0:n], in_=u[:, 0:kc_n, 0:n], func=AF.Square), [u], [sq])
                        ps = self.psum()
                        for kc in range(kc_n):
                            k.op("pe", lambda e: e.matmul(ps[:, 0:n], lhsT=self.C_("ones"), rhs=sq[:, kc, 0:n], start=(kc == 0), stop=(kc == kc_n - 1)), [self.cst, sq], [ps])
                        self.rstd_from(ps, n, rs, 1.0 / nfeat)
                        for kc in range(kc_n):
                            k.op("dve", lambda e: e.scalar_tensor_tensor(out=cn[:, dst0 + kc, t0:t0 + n], in0=u[:, kc, 0:n], scalar=gkv[:, dst0 + kc:dst0 + kc + 1], in1=rs[:, 0:n],
                                                                         op0=ALU.mult, op1=ALU.mult), [u, gkv, rs], [cn])
                r0 = self.urow(MKPE)
                kp = self.sb(st2, [32, T], F32, "kp")
                k.dma("sp", kp[:, :], self.U.t[r0:r0 + 32, :], [], [kp], kp)
                k.op("act", lambda e: e.activation(out=SQPE[:, :], in_=kp[:, :], func=AF.Square), [kp], [SQPE])
                k.op("dve", lambda e: e.tensor_scalar(out=kp[:, :], in0=kp[:, :], scalar1=gk[0:32, 1:2], scalar2=None, op0=ALU.mult), [kp, gk], [kp])
                self.rope32(st2, kp, RK)
                k.barrier()
            with contextlib.ExitStack() as st2:
                sq_ = [self.sb(st2, [64, 512], F32, "ksq") for _ in range(2)]
                rs_ = [self.sb(st2, [128, 512], F32, "krs") for _ in range(2)]
                kn_ = [self.sb(st2, [64, 512], BF16, "kn") for _ in range(2)]
                kr_ = [self.sb(st2, [32, 512], BF16, "kr") for _ in range(2)]
                cnt = [0]
                wkv = W["mla_w_ukv"][l].rearrange("(kc p) c -> p kc c", p=128)

                def epik(ch, ti, t0, n, ps, rows):
                    h = ch[0] // 128
                    i = cnt[0] % 2
                    cnt[0] += 1
                    sq, rs, kn, kr = sq_[i], rs_[i], kn_[i], kr_[i]
                    k.op("act", lambda e: e.activation(out=sq[:, 0:n], in_=ps[0:64, 0:n], func=AF.Square), [ps], [sq])
                    p2 = self.psum()
                    k.op("pe", lambda e: e.matmul(p2[:, 0:n], lhsT=self.cst[0:64, 1, :], rhs=sq[0:64, 0:n], start=True, stop=False), [self.cst, sq], [p2])
                    k.op("pe", lambda e: e.matmul(p2[:, 0:n], lhsT=self.cst[0:32, 1, :], rhs=SQPE[0:32, t0:t0 + n], start=False, stop=True), [self.cst, SQPE], [p2])
                    self.rstd_from(p2, n, rs, 1.0 / 96)
                    k.op("dve", lambda e: e.scalar_tensor_tensor(out=kn[:, 0:n], in0=ps[0:64, 0:n], scalar=gk[0:64, 0:1], in1=rs[0:64, 0:n], op0=ALU.mult, op1=ALU.mult), [ps, gk, rs], [kn])
                    k.op("dve", lambda e: e.tensor_tensor(out=kr[:, 0:n], in0=RK[:, t0:t0 + n], in1=rs[0:32, 0:n], op=ALU.mult), [RK, rs], [kr])
                    k.dma("pool", self.KN.t[h * 64:(h + 1) * 64, t0:t0 + n], kn[:, 0:n], [kn], [], kn)
                    k.dma("pool", self.KR.t[h * 32:(h + 1) * 32, t0:t0 + n], kr[:, 0:n], [kr], [], kr)
                self.gemm_fm(st2, cn, 2, lambda c0, c1: wkv[:, :, c0:c1], [(h * 128, h * 128 + 64) for h in range(8)], self.tiles, epik)
                va_ = [self.sb(st2, [128, 8, 65], BF16, "va") for _ in range(2)]
                for v in va_:
                    k.op("dve", lambda e: e.memset(v[:, :, :], 1.0), [], [v])

                def epiv(blk, ps):
                    v = va_[blk % 2]
                    k.op("act", lambda e: e.copy(out=v[:, :, 0:64], in_=ps[:, 0:512].rearrange("p (h d) -> p h d", h=8)), [ps], [v])
                    k.dma("pool", self.VA.t[blk * 128:(blk + 1) * 128, :], v[:, :, :].rearrange("p h d -> p (h d)"), [v], [], v)
                wv5 = W["mla_w_ukv"][l].rearrange("(kc p) (h two d) -> p kc h two d", p=128, two=2, d=64)

                def wsrc(s_):
                    for kc in range(2):
                        k.dma("sp", s_[:, kc, :].rearrange("p (h d) -> p h d", h=8), wv5[:, kc, :, 1, :], [], [s_], s_)
                self.gemm_tm(st2, cn, 2, wsrc, 512, list(range(NB)), epiv)
                k.barrier()
            with contextlib.ExitStack() as st2:
                sqn_ = [self.sb(st2, [64, 512], F32, "qsq") for _ in range(2)]
                sqr_ = [self.sb(st2, [32, 512], F32, "qsr") for _ in range(2)]
                rs_ = [self.sb(st2, [128, 512], F32, "qrs") for _ in range(2)]
                qn_ = [self.sb(st2, [64, 512], BF16, "qn") for _ in range(2)]
                qr_ = [self.sb(st2, [32, 512], BF16, "qr") for _ in range(2)]
                xr_ = [self.sb(st2, [32, 512], F32, "xr") for _ in range(2)]
                ro_ = [self.sb(st2, [32, 512], F32, "ro") for _ in range(2)]
                cs_ = [self.sb(st2, [32, 2, 512], F32, "qcs") for _ in range(2)]
                cnt = [0]
                held = {}
                wq = W["mla_w_uq"][l].rearrange("(kc p) c -> p kc c", p=128)

                def epiq(ch, ti, t0, n, ps, rows):
                    if rows == 64:
                        held["n"] = ps
                        return
                    psn, psr = held["n"], ps
                    h = ch[0] // 96
                    i = cnt[0] % 2
                    cnt[0] += 1
                    sqn, sqr, rs, qn, qr, xr, ro, cs = sqn_[i], sqr_[i], rs_[i], qn_[i], qr_[i], xr_[i], ro_[i], cs_[i]
                    k.dma("sp", cs[:, :, 0:n], self.ropem_in[:, :, t0:t0 + n], [], [cs], cs)
                    k.op("act", lambda e: e.activation(out=sqn[:, 0:n], in_=psn[0:64, 0:n], func=AF.Square), [psn], [sqn])
                    k.op("act", lambda e: e.activation(out=sqr[:, 0:n], in_=psr[0:32, 0:n], func=AF.Square), [psr], [sqr])
                    p2 = self.psum()
                    k.op("pe", lambda e: e.matmul(p2[:, 0:n], lhsT=self.cst[0:64, 1, :], rhs=sqn[0:64, 0:n], start=True, stop=False), [self.cst, sqn], [p2])
                    k.op("pe", lambda e: e.matmul(p2[:, 0:n], lhsT=self.cst[0:32, 1, :], rhs=sqr[0:32, 0:n], start=False, stop=True), [self.cst, sqr], [p2])
                    self.rstd_from(p2, n, rs, 1.0 / 96)
                    k.op("dve", lambda e: e.scalar_tensor_tensor(out=qn[:, 0:n], in0=psn[0:64, 0:n], scalar=gk[0:64, 2:3], in1=rs[0:64, 0:n], op0=ALU.mult, op1=ALU.mult), [psn, gk, rs], [qn])
                    k.op("dve", lambda e: e.tensor_scalar(out=xr[:, 0:n], in0=psr[0:32, 0:n], scalar1=gk[0:32, 3:4], scalar2=None, op0=ALU.mult), [psr, gk], [xr])
                    p3 = self.psum()
                    k.op("pe", lambda e: e.matmul(p3[0:32, 0:n], lhsT=self.cst[0:32, CN.index("perm32"), 0:32], rhs=xr[0:32, 0:n], start=True, stop=True), [self.cst, xr], [p3])
                    k.op("dve", lambda e: e.tensor_tensor(out=ro[:, 0:n], in0=p3[0:32, 0:n], in1=cs[:, 1, 0:n], op=ALU.mult), [p3, cs], [ro])
                    k.op("dve", lambda e: e.tensor_tensor(out=xr[:, 0:n], in0=xr[:, 0:n], in1=cs[:, 0, 0:n], op=ALU.mult), [xr, cs], [xr])
                    k.op("dve", lambda e: e.tensor_tensor(out=xr[:, 0:n], in0=xr[:, 0:n], in1=ro[:, 0:n], op=ALU.add), [xr, ro], [xr])
                    k.op("dve", lambda e: e.tensor_tensor(out=qr[:, 0:n], in0=xr[:, 0:n], in1=rs[0:32, 0:n], op=ALU.mult), [xr, rs], [qr])
                    k.dma("pool", self.QN.t[h * 64:(h + 1) * 64, t0:t0 + n], qn[:, 0:n], [qn], [], qn)
                    k.dma("pool", self.QR.t[h * 32:(h + 1) * 32, t0:t0 + n], qr[:, 0:n], [qr], [], qr)
                chq = []
                for h in range(8):
                    chq += [(h * 96, h * 96 + 64), (h * 96 + 64, h * 96 + 96)]
                self.gemm_fm(st2, Buf(cn.t[:, 2:5, :]), 3, lambda c0, c1: wq[:, :, c0:c1], chq, self.tiles, epiq)
                k.barrier()
        with contextlib.ExitStack() as st:
            self.attn_common(st)
            kqh = self.sb(st, [96, T], BF16, "kqh")
            qqh = self.sb(st, [96, T], BF16, "qqh")
            vah = self.sb(st, [128, NB, 65], BF16, "vah")
            self.vbuf = vah
            osb = [self.sb(st, [65, 512], F32, "osb") for _ in range(2)]
            rr = [self.sb(st, [64, 512], F32, "rr") for _ in range(2)]
            yb = [self.sb(st, [64, 512], BF16, "myb") for _ in range(2)]
            cnt = [0]
            for h in range(8):
                k.dma("sp", kqh[0:64, :], self.KN.t[h * 64:(h + 1) * 64, :], [], [kqh], kqh)
                k.dma("sp", kqh[64:96, :], self.KR.t[h * 32:(h + 1) * 32, :], [], [kqh], kqh)
                k.dma("sp", qqh[0:64, :], self.QN.t[h * 64:(h + 1) * 64, :], [], [qqh], qqh)
                k.dma("sp", qqh[64:96, :], self.QR.t[h * 32:(h + 1) * 32, :], [], [qqh], qqh)
                k.dma("sp", vah[:, :, :], self.VA.t[:, h * 65:(h + 1) * 65].rearrange("(b p) d -> p b d", p=128), [], [vah], vah)

                def epi(qi, t0, n, O, Sps):
                    i = cnt[0] % 2
                    cnt[0] += 1
                    o, r, y = osb[i], rr[i], yb[i]
                    k.op("act", lambda e: e.copy(out=o[0:65, 0:n], in_=O[0:65, 0:n]), [O], [o])
                    ps = self.psb[self.sci % 4]
                    self.sci += 1
                    k.op("pe", lambda e: e.matmul(ps[0:64, 0:n], lhsT=self.cst[0:65, CN.index("sel65"), 0:64], rhs=o[0:65, 0:n], start=True, stop=True), [self.cst, o], [ps])
                    k.op("dve", lambda e: e.reciprocal(out=r[:, 0:n], in_=ps[0:64, 0:n]), [ps], [r])
                    k.op("dve", lambda e: e.tensor_tensor(out=y[:, 0:n], in0=o[0:64, 0:n], in1=r[:, 0:n], op=ALU.mult), [o, r], [y])
                    k.dma("pool", self.Y.t[512 + h * 64:512 + (h + 1) * 64, t0:t0 + n], y[:, 0:n], [y], [], y)
                terms = [(kqh, qqh, 0, 96)]
                self.attend(st, terms, lambda kb: vah[:, kb, :], 65, list(range(NB)), self.qtiles(True), 96 ** -0.5, False, epi)
                if not self.last:
                    self.attend(st, terms, lambda kb: vah[:, kb, :], 65, list(range(self.C // 128)), self.qtiles(False), 96 ** -0.5, False, epi)
            k.barrier()

    def rope32(self, st, xin, dst):
        k = self.k
        cs_ = [self.sb(st, [32, 2, 512], F32, "rcs") for _ in range(2)]
        ro_ = [self.sb(st, [32, 512], F32, "rro") for _ in range(2)]
        for ti, (t0, n) in enumerate(self.tiles):
            cs, ro = cs_[ti % 2], ro_[ti % 2]
            k.dma("sp", cs[:, :, 0:n], self.ropem_in[:, :, t0:t0 + n], [], [cs], cs)
            ps = self.psum()
            k.op("pe", lambda e: e.matmul(ps[0:32, 0:n], lhsT=self.cst[0:32, CN.index("perm32"), 0:32], rhs=xin[0:32, t0:t0 + n], start=True, stop=True), [self.cst, xin], [ps])
            k.op("dve", lambda e: e.tensor_tensor(out=ro[:, 0:n], in0=ps[0:32, 0:n], in1=cs[:, 1, 0:n], op=ALU.mult), [ps, cs], [ro])
            k.op("dve", lambda e: e.tensor_tensor(out=dst[:, t0:t0 + n], in0=xin[0:32, t0:t0 + n], in1=cs[:, 0, 0:n], op=ALU.mult), [xin, cs], [dst])
            k.op("dve", lambda e: e.tensor_tensor(out=dst[:, t0:t0 + n], in0=dst[:, t0:t0 + n], in1=ro[:, 0:n], op=ALU.add), [dst, ro], [dst])

    def conv_silu(self, st, u, acc, wc, bias, out):
        k = self.k
        C, T = self.C, self.T
        k.op("dve", lambda e: e.tensor_scalar(out=acc[:, :], in0=u[:, :], scalar1=wc[:, 2:3], scalar2=None, op0=ALU.mult), [u, wc], [acc])
        for j in (0, 1, 3, 4):
            s = j - 2
            for (s0, s1) in ((0, C), (C, T)):
                a = max(s0, s0 - s)
                b = min(s1, s1 - s)
                k.op("dve", lambda e: e.scalar_tensor_tensor(out=acc[:, a:b], in0=u[:, a + s:b + s], scalar=wc[:, j:j + 1], in1=acc[:, a:b], op0=ALU.mult, op1=ALU.add), [u, wc, acc], [acc])
        if bias is None:
            k.op("act", lambda e: e.activation(out=out, in_=acc[:, :], func=AF.Silu), [acc], [acc])
        else:
            k.op("act", lambda e: e.activation(out=out, in_=acc[:, :], func=AF.Silu, bias=bias, scale=1.0), [acc, wc], [acc])

    def softplus(self, st, xb, xap, n):
        k = self.k
        t = self.sb(st, [128, n], F32, "spt")

        class _X:
            def __getitem__(s_, key):
                return xap
        x = _X()
        k.op("act", lambda e: e.activation(out=t[:, :], in_=xap, func=AF.Abs), [xb], [t])
        k.op("act", lambda e: e.activation(out=t[:, :], in_=t[:, :], func=AF.Exp, scale=-1.0), [t], [t])
        k.op("act", lambda e: e.activation(out=t[:, :], in_=t[:, :], func=AF.Ln, bias=self.epsb[:, 1:2], scale=1.0), [t, self.epsb], [t])
        k.op("dve", lambda e: e.scalar_tensor_tensor(out=xap, in0=xap, scalar=0.0, in1=t[:, :], op0=ALU.max, op1=ALU.add), [xb, t], [xb])

    def tok_scalars(self, st, col0, ncols, dst):
        k = self.k
        r0 = self.urow(col0)
        raw = self.sb(st, [ncols, self.T], F32, "tsr")
        k.dma("sp", raw[:, :], self.U.t[r0:r0 + ncols, :], [], [raw], raw)
        for blk in range(self.NB):
            ps = self.psum()
            k.op("pe", lambda e: e.transpose(out=ps[:, 0:ncols], in_=raw[0:ncols, blk * 128:(blk + 1) * 128], identity=self.cst[0:ncols, 0, 0:ncols]), [raw, self.cst], [ps])
            k.op("act", lambda e: e.copy(out=dst[:, blk, :], in_=ps[:, 0:ncols]), [ps], [dst])

    def blk_order(self, d):
        nc_ = self.C // 128
        if d == 0:
            return list(range(self.NB))
        return list(range(nc_ - 1, -1, -1)) + list(range(self.NB - 1, nc_ - 1, -1))

    def phase_ssm(self, l):
        k = self.k
        W = self.W
        T, NB = self.T, self.NB
        with contextlib.ExitStack() as st:
            XTOK = self.sb(st, [128, NB, 512], F32, "XTOK")
            BT = self.sb(st, [128, 2, T], BF16, "BT")
            CT = self.sb(st, [128, 2, T], BF16, "CT")
            BTOK = self.sb(st, [128, NB, 2, 128], BF16, "BTOK")
            DT = self.sb(st, [128, NB, 16], F32, "DT")
            DA = self.sb(st, [128, NB, 16], F32, "DA")
            ACUM = self.sb(st, [128, NB, 16], F32, "ACUM")
            ATOT = self.sb(st, [128, NB, 16], F32, "ATOT")
            CD = self.sb(st, [128, NB, 16], F32, "CD")
            DTDS = self.sb(st, [128, NB, 16], F32, "DTDS")
            with contextlib.ExitStack() as st2:
                wc_ = [self.sb(st2, [128, 6], F32, "swc") for _ in range(2)]
                st2a = contextlib.ExitStack()
                u_ = [self.sb(st2a, [128, T], F32, "su") for _ in range(1)] * 2
                acc_ = [self.sb(st2a, [128, T], F32, "sacc") for _ in range(1)] * 2
                for c in range(8):
                    u, acc, wc = u_[c % 2], acc_[c % 2], wc_[c % 2]
                    r0 = self.urow(SX + c * 128)
                    k.dma("sp", u[:, :], self.U.t[r0:r0 + 128, :], [], [u], u)
                    k.dma("sp", wc[:, 0:5], W["ssm_conv"][l][:, c * 128:(c + 1) * 128].rearrange("j c -> c j"), [], [wc], wc, allow_slow_non_contiguous=True)
                    k.dma("sp", wc[:, 5:6], W["ssm_conv_b"][l, c * 128:(c + 1) * 128].rearrange("(p o) -> p o", o=1), [], [wc], wc, allow_slow_non_contiguous=True)
                    if c < 4:
                        self.conv_silu(st2, u, acc, wc, wc[:, 5:6], acc[:, :])
                        k.dma("pool", self.XS.t[c * 128:(c + 1) * 128, :], acc[:, :], [acc], [], acc)
                        for blk in range(NB):
                            ps = self.psum()
                            k.op("pe", lambda e: e.transpose(out=ps[:, 0:128], in_=acc[:, blk * 128:(blk + 1) * 128], identity=self.C_("ident")), [acc, self.cst], [ps])
                            k.op("act", lambda e: e.copy(out=XTOK[:, blk, c * 128:(c + 1) * 128], in_=ps[:, 0:128]), [ps], [XTOK])
                    elif c < 6:
                        g = c - 4
                        self.conv_silu(st2, u, acc, wc, wc[:, 5:6], acc[:, :])
                        k.op("dve", lambda e: e.tensor_copy(out=BT[:, g, :], in_=acc[:, :]), [acc], [BT])
                        for blk in range(NB):
                            ps = self.psum()
                            k.op("pe", lambda e: e.transpose(out=ps[:, 0:128], in_=acc[:, blk * 128:(blk + 1) * 128], identity=self.C_("ident")), [acc, self.cst], [ps])
                            k.op("act", lambda e: e.copy(out=BTOK[:, blk, g, :], in_=ps[:, 0:128]), [ps], [BTOK])
                    else:
                        g = c - 6
                        self.conv_silu(st2, u, acc, wc, wc[:, 5:6], acc[:, :])
                        k.op("dve", lambda e: e.tensor_copy(out=CT[:, g, :], in_=acc[:, :]), [acc], [CT])
                k.barrier()
                st2a.close()
                self.tok_scalars(st2, SDT, 16, DT)
                pb = self.sb(st2, [128, 2, 16], F32, "spb")
                k.dma("sp", pb[:, 0, :], W["ssm_dt_bias"][l:l + 1].rearrange("o a b -> o (a b)").partition_broadcast(128), [], [pb], pb)
                k.dma("sp", pb[:, 1, :], W["ssm_a_log"][l:l + 1].rearrange("o a b -> o (a b)").partition_broadcast(128), [], [pb], pb)
                k.op("dve", lambda e: e.tensor_tensor(out=DT[:, :, :], in0=DT[:, :, :], in1=pb[:, 0, :].unsqueeze(1).to_broadcast([128, NB, 16]), op=ALU.add), [DT, pb], [DT])
                self.softplus(st2, DT, DT.t[:, :, :].rearrange("p b c -> p (b c)"), NB * 16)
                k.op("act", lambda e: e.activation(out=pb[:, 1, :], in_=pb[:, 1, :], func=AF.Exp), [pb], [pb])
                k.op("dve", lambda e: e.scalar_tensor_tensor(out=DA[:, :, :], in0=DT[:, :, :], scalar=-1.0, in1=pb[:, 1, :].unsqueeze(1).to_broadcast([128, NB, 16]), op0=ALU.mult, op1=ALU.mult), [DT, pb], [DA])
                self.cum_stats(st2, DA, ACUM, ATOT, 8)
                k.op("act", lambda e: e.activation(out=CD[:, :, :], in_=ATOT[:, :, :], func=AF.Exp), [ATOT], [CD])
                k.op("dve", lambda e: e.tensor_tensor(out=DTDS[:, :, :], in0=ATOT[:, :, :], in1=ACUM[:, :, :], op=ALU.subtract), [ATOT, ACUM], [DTDS])
                k.op("act", lambda e: e.activation(out=DTDS[:, :, :], in_=DTDS[:, :, :], func=AF.Exp), [DTDS], [DTDS])
                k.op("dve", lambda e: e.tensor_tensor(out=DTDS[:, :, :], in0=DTDS[:, :, :], in1=DT[:, :, :], op=ALU.mult), [DTDS, DT], [DTDS])
                k.barrier()
            ST = self.sb(st, [128, 4, 4, 64], F32, "ST")
            STb = self.sb(st, [128, 4, 4, 64], BF16, "STb")
            k.op("dve", lambda e: e.memset(ST[:, :, :, :], 0.0), [], [ST])
            k.op("dve", lambda e: e.memset(STb[:, :, :, :], 0.0), [], [STb])
            R = 2
            rhsb = [self.sb(st, [128, 4, 128], F32, "srhs") for _ in range(R)]
            Dm = [self.sb(st, [128, 4, 128], F32, "sD") for _ in range(R)]
            LT = [self.sb(st, [128, 4, 128], F32, "sLT") for _ in range(R)]
            RE = [self.sb(st, [128, 4, 128], F32, "sRE") for _ in range(R)]
            WT = [self.sb(st, [128, 4, 128], BF16, "sWT") for _ in range(R)]
            CdT = [self.sb(st, [128, 4, 128], BF16, "sCd") for _ in range(R)]
            xdt = [self.sb(st, [128, 4, 64], BF16, "sxdt") for _ in range(R)]
            xdd = [self.sb(st, [128, 4, 64], BF16, "sxdd") for _ in range(R)]
            SCs = [self.sb(st, [128, 128], F32, "sSC") for _ in range(R)]
            yo = [self.sb(st, [64, 4, 128], F32, "syo") for _ in range(R)]
            STs = {(d, g): Buf(None) for d in range(2) for g in range(2)}
            it = 0
            orders = [self.blk_order(0), self.blk_order(1)]
            for i in range(NB):
                for d in range(2):
                    blk = orders[d][i]
                    tri = self.C_("triF" if d == 0 else "triB")
                    mneg = self.C_("mnegF" if d == 0 else "mnegB")
                    for g in range(2):
                        j = it % R
                        it += 1
                        ch = d * 2 + g
                        sbuf_ = STs[(d, g)]
                        hs = slice(d * 8 + g * 4, d * 8 + g * 4 + 4)
                        k.op("dve", lambda e: e.tensor_tensor(out=rhsb[j][:, :, :], in0=tri.unsqueeze(1).to_broadcast([128, 4, 128]),
                                                              in1=DA[:, blk, hs].unsqueeze(2).to_broadcast([128, 4, 128]), op=ALU.mult), [self.cst, DA], [rhsb[j]])
                        pa = self.psum()
                        k.op("pe", lambda e: e.matmul(pa[:, :], lhsT=self.C_("ones"), rhs=rhsb[j][:, :, :].rearrange("p h c -> p (h c)"), start=True, stop=True), [self.cst, rhsb[j]], [pa])
                        for h in range(4):
                            k.op("dve", lambda e: e.scalar_tensor_tensor(out=Dm[j][:, h, :], in0=pa[:, h * 128:(h + 1) * 128], scalar=ACUM[:, blk, d * 8 + g * 4 + h:d * 8 + g * 4 + h + 1],
                                                                         in1=mneg, op0=ALU.subtract, op1=ALU.add), [pa, ACUM, self.cst], [Dm[j]])
                        k.op("act", lambda e: e.activation(out=LT[j][:, :, :], in_=Dm[j][:, :, :], func=AF.Exp), [Dm[j]], [LT[j]])
                        k.op("act", lambda e: e.activation(out=RE[j][:, :, :].rearrange("p h c -> p (h c)"), in_=pa[:, :], func=AF.Exp), [pa], [RE[j]])
                        psc = self.psum()
                        k.op("pe", lambda e: e.matmul(psc[:, 0:128], lhsT=BT[:, g, blk * 128:(blk + 1) * 128], rhs=CT[:, g, blk * 128:(blk + 1) * 128], start=True, stop=True), [BT, CT], [psc])
                        k.op("act", lambda e: e.copy(out=SCs[j][:, :], in_=psc[:, 0:128]), [psc], [SCs[j]])
                        k.op("dve", lambda e: e.tensor_tensor(out=WT[j][:, :, :], in0=LT[j][:, :, :], in1=SCs[j][:, :].unsqueeze(1).to_broadcast([128, 4, 128]), op=ALU.mult), [LT[j], SCs[j]], [WT[j]])
                        k.op("dve", lambda e: e.tensor_tensor(out=CdT[j][:, :, :], in0=RE[j][:, :, :], in1=CT[:, g, blk * 128:(blk + 1) * 128].unsqueeze(1).to_broadcast([128, 4, 128]), op=ALU.mult), [RE[j], CT], [CdT[j]])
                        xv = XTOK[:, blk, g * 256:(g + 1) * 256].rearrange("p (h q) -> p h q", h=4)
                        k.op("dve", lambda e: e.tensor_tensor(out=xdt[j][:, :, :], in0=xv, in1=DT[:, blk, hs].unsqueeze(2).to_broadcast([128, 4, 64]), op=ALU.mult), [XTOK, DT], [xdt[j]])
                        k.op("dve", lambda e: e.tensor_tensor(out=xdd[j][:, :, :], in0=xv, in1=DTDS[:, blk, hs].unsqueeze(2).to_broadcast([128, 4, 64]), op=ALU.mult), [XTOK, DTDS], [xdd[j]])
                        py = self.psum()
                        for h in range(4):
                            k.op("pe", lambda e: e.matmul(py[0:64, h * 128:(h + 1) * 128], lhsT=xdt[j][:, h, :], rhs=WT[j][:, h, :], start=True, stop=False), [xdt[j], WT[j]], [py])
                            k.op("pe", lambda e: e.matmul(py[0:64, h * 128:(h + 1) * 128], lhsT=STb[:, ch, h, :], rhs=CdT[j][:, h, :], start=False, stop=True), [sbuf_, CdT[j]], [py])
                        k.op("act", lambda e: e.copy(out=yo[j][:, :, :].rearrange("p h c -> p (h c)"), in_=py[0:64, :]), [py], [yo[j]])
                        k.dma("pool", self.YS.t[d, g * 256:(g + 1) * 256, blk * 128:(blk + 1) * 128].rearrange("(h p) c -> p h c", p=64), yo[j][:, :, :], [yo[j]], [], yo[j])
                        pst = self.psum()
                        for h in range(4):
                            k.op("pe", lambda e: e.matmul(pst[:, h * 64:(h + 1) * 64], lhsT=BTOK[:, blk, g, :], rhs=xdd[j][:, h, :], start=True, stop=True), [BTOK, xdd[j]], [pst])
                        k.op("dve", lambda e: e.tensor_tensor(out=ST[:, ch, :, :], in0=ST[:, ch, :, :], in1=CD[:, blk, hs].unsqueeze(2).to_broadcast([128, 4, 64]), op=ALU.mult), [sbuf_, CD], [sbuf_])
                        k.op("dve", lambda e: e.tensor_tensor(out=ST[:, ch, :, :], in0=ST[:, ch, :, :], in1=pst[:, 0:256].rearrange("p (h q) -> p h q", h=4), op=ALU.add), [sbuf_, pst], [sbuf_])
                        k.op("act", lambda e: e.copy(out=STb[:, ch, :, :], in_=ST[:, ch, :, :]), [sbuf_], [sbuf_])
            k.barrier()
        with contextlib.ExitStack() as st:
            dsk = self.sb(st, [128, 4], F32, "dsk")
            gn = self.sb(st, [128, 4], F32, "sgn")
            for c in range(4):
                for hh in range(2):
                    k.dma("sp", dsk[hh * 64:(hh + 1) * 64, c:c + 1], W["ssm_d"][l:l + 1, 2 * c + hh:2 * c + hh + 1].partition_broadcast(64), [], [dsk], dsk)
            k.dma("sp", gn[:, :], W["ssm_norm_g"][l].rearrange("(c p) -> p c", p=128), [], [gn], gn, allow_slow_non_contiguous=True)
            ya = [self.sb(st, [128, 2, 512], F32, "fya") for _ in range(2)]
            yb_ = [self.sb(st, [128, 2, 512], F32, "fyb") for _ in range(2)]
            xs_ = [self.sb(st, [128, 2, 512], F32, "fxs") for _ in range(2)]
            z_ = [self.sb(st, [128, 2, 512], F32, "fz") for _ in range(2)]
            sq_ = [self.sb(st, [128, 2, 512], F32, "fsq") for _ in range(2)]
            rs_ = [self.sb(st, [128, 512], F32, "frs") for _ in range(2)]
            ob_ = [self.sb(st, [128, 2, 512], BF16, "fob") for _ in range(2)]
            it = 0
            rz = self.urow(SZ)
            tiles = self.tiles[1:] if self.last else self.tiles
            for gi in range(2):
                for ti, (t0, n) in enumerate(tiles):
                    j = it % 2
                    it += 1
                    r0 = gi * 256
                    v = lambda ap: ap.rearrange("(c p) t -> p c t", p=128)
                    k.dma("sp", ya[j][:, :, 0:n], v(self.YS.t[0, r0:r0 + 256, t0:t0 + n]), [], [ya[j]], ya[j])
                    k.dma("sp", yb_[j][:, :, 0:n], v(self.YS.t[1, r0:r0 + 256, t0:t0 + n]), [], [yb_[j]], yb_[j])
                    k.dma("sp", xs_[j][:, :, 0:n], v(self.XS.t[r0:r0 + 256, t0:t0 + n]), [], [xs_[j]], xs_[j])
                    k.dma("sp", z_[j][:, :, 0:n], v(self.U.t[rz + r0:rz + r0 + 256, t0:t0 + n]), [], [z_[j]], z_[j])
                    k.op("dve", lambda e: e.tensor_tensor(out=ya[j][:, :, 0:n], in0=ya[j][:, :, 0:n], in1=yb_[j][:, :, 0:n], op=ALU.add), [ya[j], yb_[j]], [ya[j]])
                    k.op("act", lambda e: e.activation(out=z_[j][:, :, 0:n], in_=z_[j][:, :, 0:n], func=AF.Silu), [z_[j]], [z_[j]])
                    for cc in range(2):
                        c = gi * 2 + cc
                        k.op("dve", lambda e: e.scalar_tensor_tensor(out=ya[j][:, cc, 0:n], in0=xs_[j][:, cc, 0:n], scalar=dsk[:, c:c + 1], in1=ya[j][:, cc, 0:n], op0=ALU.mult, op1=ALU.add), [xs_[j], dsk, ya[j]], [ya[j]])
                    k.op("dve", lambda e: e.tensor_tensor(out=ya[j][:, :, 0:n], in0=ya[j][:, :, 0:n], in1=z_[j][:, :, 0:n], op=ALU.mult), [ya[j], z_[j]], [ya[j]])
                    k.op("act", lambda e: e.activation(out=sq_[j][:, :, 0:n], in_=ya[j][:, :, 0:n], func=AF.Square), [ya[j]], [sq_[j]])
                    ps = self.psum()
                    for cc in range(2):
                        k.op("pe", lambda e: e.matmul(ps[:, 0:n], lhsT=self.C_("ones"), rhs=sq_[j][:, cc, 0:n], start=(cc == 0), stop=(cc == 1)), [self.cst, sq_[j]], [ps])
                    self.rstd_from(ps, n, rs_[j], 1.0 / 256)
                    for cc in range(2):
                        c = gi * 2 + cc
                        k.op("dve", lambda e: e.scalar_tensor_tensor(out=ob_[j][:, cc, 0:n], in0=ya[j][:, cc, 0:n], scalar=gn[:, c:c + 1], in1=rs_[j][:, 0:n], op0=ALU.mult, op1=ALU.mult), [ya[j], gn, rs_[j]], [ob_[j]])
                    k.dma("pool", v(self.Y.t[1536 + r0:1536 + r0 + 256, t0:t0 + n]), ob_[j][:, :, 0:n], [ob_[j]], [], ob_[j])
            k.barrier()

    def cum_stats(self, st, G, GCUM, GTOT, nh):
        k = self.k
        NB = self.NB
        for d in range(2):
            tri = self.C_("triF" if d == 0 else "triB")
            ps = self.psum()
            k.op("pe", lambda e: e.matmul(ps[:, 0:NB * nh].rearrange("p (b h) -> p b h", h=nh), lhsT=tri, rhs=G[:, :, d * nh:(d + 1) * nh], start=True, stop=True), [self.cst, G], [ps])
            k.op("act", lambda e: e.copy(out=GCUM[:, :, d * nh:(d + 1) * nh], in_=ps[:, 0:NB * nh].rearrange("p (b h) -> p b h", h=nh)), [ps], [GCUM])
            ps2 = self.psum()
            k.op("pe", lambda e: e.matmul(ps2[:, 0:NB * nh].rearrange("p (b h) -> p b h", h=nh), lhsT=self.C_("ones"), rhs=G[:, :, d * nh:(d + 1) * nh], start=True, stop=True), [self.cst, G], [ps2])
            k.op("dve", lambda e: e.tensor_copy(out=GTOT[:, :, d * nh:(d + 1) * nh], in_=ps2[:, 0:NB * nh].rearrange("p (b h) -> p b h", h=nh)), [ps2], [GTOT])

    def phase_gdn(self, l):
        k = self.k
        W = self.W
        T, NB = self.T, self.NB
        with contextlib.ExitStack() as st:
            QT = self.sb(st, [128, 4, T], BF16, "gQT")
            KT = self.sb(st, [128, 4, T], BF16, "gKT")
            KTOK = self.sb(st, [128, NB, 4, 128], BF16, "gKTOK")
            VTOK = self.sb(st, [128, NB, 4, 128], BF16, "gVTOK")
            G = self.sb(st, [128, NB, 8], F32, "gG")
            GCUM = self.sb(st, [128, NB, 8], F32, "gGCUM")
            GTOT = self.sb(st, [128, NB, 8], F32, "gGTOT")
            EG = self.sb(st, [128, NB, 8], F32, "gEG")
            GL = self.sb(st, [128, NB, 8], F32, "gGL")
            KDS = self.sb(st, [128, NB, 8], F32, "gKDS")
            BETA = self.sb(st, [128, NB, 8], F32, "gBETA")
            NBETA = self.sb(st, [128, NB, 8], F32, "gNBETA")
            with contextlib.ExitStack() as st2:
                wc_ = [self.sb(st2, [128, 6], F32, "gwc") for _ in range(2)]
                sq_ = [self.sb(st2, [128, 512], F32, "gsq") for _ in range(2)]
                rs_ = [self.sb(st2, [128, 512], F32, "grs") for _ in range(2)]
                st2a = contextlib.ExitStack()
                u_ = [self.sb(st2a, [128, T], F32, "gu") for _ in range(1)] * 2
                acc_ = [self.sb(st2a, [128, T], F32, "gacc") for _ in range(1)] * 2
                it = 0
                for c in range(12):
                    u, acc, wc = u_[c % 2], acc_[c % 2], wc_[c % 2]
                    r0 = self.urow(c * 128)
                    kind, h = c // 4, c % 4
                    k.dma("sp", u[:, :], self.U.t[r0:r0 + 128, :], [], [u], u)
                    k.dma("sp", wc[:, 0:5], W["gdn_conv"][l][:, c * 128:(c + 1) * 128].rearrange("j c -> c j"), [], [wc], wc, allow_slow_non_contiguous=True)
                    self.conv_silu(st2, u, acc, wc, None, acc[:, :])
                    if kind < 2:
                        dstT = QT if kind == 0 else KT
                        for ti, (t0, n) in enumerate(self.tiles):
                            sq, rs = sq_[it % 2], rs_[it % 2]
                            it += 1
                            k.op("act", lambda e: e.activation(out=sq[:, 0:n], in_=acc[:, t0:t0 + n], func=AF.Square), [acc], [sq])
                            ps = self.psum()
                            k.op("pe", lambda e: e.matmul(ps[:, 0:n], lhsT=self.C_("ones"), rhs=sq[:, 0:n], start=True, stop=True), [self.cst, sq], [ps])
                            self.rstd_from(ps, n, rs, 1.0)
                            if kind == 0:
                                k.op("dve", lambda e: e.scalar_tensor_tensor(out=dstT[:, h, t0:t0 + n], in0=acc[:, t0:t0 + n], scalar=128.0 ** -0.5, in1=rs[:, 0:n], op0=ALU.mult, op1=ALU.mult), [acc, rs], [dstT])
                            else:
                                k.op("dve", lambda e: e.tensor_tensor(out=acc[:, t0:t0 + n], in0=acc[:, t0:t0 + n], in1=rs[:, 0:n], op=ALU.mult), [acc, rs], [acc])
                                k.op("act", lambda e: e.copy(out=dstT[:, h, t0:t0 + n], in_=acc[:, t0:t0 + n]), [acc], [dstT])
                    if kind >= 1:
                        dst = KTOK if kind == 1 else VTOK
                        for blk in range(NB):
                            ps = self.psum()
                            k.op("pe", lambda e: e.transpose(out=ps[:, 0:128], in_=acc[:, blk * 128:(blk + 1) * 128], identity=self.C_("ident")), [acc, self.cst], [ps])
                            k.op("act", lambda e: e.copy(out=dst[:, blk, h, :], in_=ps[:, 0:128]), [ps], [dst])
                k.barrier()
                st2a.close()
                AB = self.sb(st2, [128, NB, 16], F32, "gAB")
                self.tok_scalars(st2, GA, 16, AB)
                pb = self.sb(st2, [128, 2, 8], F32, "gpb")
                k.dma("sp", pb[:, 0, :], W["gdn_dt_bias"][l:l + 1].rearrange("o a b -> o (a b)").partition_broadcast(128), [], [pb], pb)
                k.dma("sp", pb[:, 1, :], W["gdn_a_log"][l:l + 1].rearrange("o a b -> o (a b)").partition_broadcast(128), [], [pb], pb)
                k.op("dve", lambda e: e.tensor_tensor(out=G[:, :, :], in0=AB[:, :, 0:8], in1=pb[:, 0, :].unsqueeze(1).to_broadcast([128, NB, 8]), op=ALU.add), [AB, pb], [G])
                self.softplus(st2, G, G.t[:, :, :].rearrange("p b c -> p (b c)"), NB * 8)
                k.op("act", lambda e: e.activation(out=pb[:, 1, :], in_=pb[:, 1, :], func=AF.Exp), [pb], [pb])
                k.op("dve", lambda e: e.scalar_tensor_tensor(out=G[:, :, :], in0=G[:, :, :], scalar=-1.0, in1=pb[:, 1, :].unsqueeze(1).to_broadcast([128, NB, 8]), op0=ALU.mult, op1=ALU.mult), [G, pb], [G])
                k.op("act", lambda e: e.activation(out=BETA[:, :, :], in_=AB[:, :, 8:16], func=AF.Sigmoid), [AB], [BETA])
                k.op("dve", lambda e: e.tensor_scalar(out=NBETA[:, :, :], in0=BETA[:, :, :], scalar1=-1.0, scalar2=None, op0=ALU.mult), [BETA], [NBETA])
                self.cum_stats(st2, G, GCUM, GTOT, 4)
                k.op("act", lambda e: e.activation(out=EG[:, :, :], in_=GCUM[:, :, :], func=AF.Exp), [GCUM], [EG])
                k.op("act", lambda e: e.activation(out=GL[:, :, :], in_=GTOT[:, :, :], func=AF.Exp), [GTOT], [GL])
                k.op("dve", lambda e: e.tensor_tensor(out=KDS[:, :, :], in0=GTOT[:, :, :], in1=GCUM[:, :, :], op=ALU.subtract), [GTOT, GCUM], [KDS])
                k.op("act", lambda e: e.activation(out=KDS[:, :, :], in_=KDS[:, :, :], func=AF.Exp), [KDS], [KDS])
                k.barrier()
            S_ = self.sb(st, [128, 8, 128], F32, "gS")
            Sb = self.sb(st, [128, 8, 128], BF16, "gSb")
            k.op("dve", lambda e: e.memset(S_[:, :, :], 0.0), [], [S_])
            k.op("dve", lambda e: e.memset(Sb[:, :, :], 0.0), [], [Sb])
            R = 3
            mk = lambda p, dt=F32: [self.sb(st, [128, 128], dt, p) for _ in range(R)]
            rhsb, Dm, E, RE, t1 = mk("grh"), mk("gDm"), mk("gE"), mk("gRE"), mk("gt1")
            AttnT, QgT, Kd, Xb, Rp, vnew = mk("gAt", BF16), mk("gQg", BF16), mk("gKd", BF16), mk("gXb", BF16), mk("gRp", BF16), mk("gvn", BF16)
            Pk = [mk("gP%d" % i) for i in range(1)]
            PTk = [mk("gPT%d" % i) for i in range(1)]
            XTb, CTb, Cb, Zb, Z2b = mk("gXTb", BF16), mk("gCTb", BF16), mk("gCb", BF16), mk("gZb", BF16), mk("gZ2b", BF16)
            GM = self.sb(st, [128, 14, 128], F32, "gGM")
            k.dma("sp", GM[:, :, :], self.gmask_in, [], [GM], GM)
            X = mk("gX")
            oo = mk("goo")
            Sbufs = {(h, d): Buf(None) for h in range(4) for d in range(2)}
            ident = self.C_("ident")
            orders = [self.blk_order(0), self.blk_order(1)]
            it = 0
            evi = [0]

            def evac(out_ap, ps_ap, rd, wr):
                evi[0] += 1
                if evi[0] % 2:
                    k.op("act", lambda e: e.copy(out=out_ap, in_=ps_ap), rd, wr)
                else:
                    k.op("dve", lambda e: e.tensor_copy(out=out_ap, in_=ps_ap), rd, wr)
            def make_unit(i, h, d, j):
                blk = orders[d][i]
                ci = d * 4 + h
                sb_ = Sbufs[(h, d)]
                tri = self.C_("triF" if d == 0 else "triB")
                mneg = self.C_("mnegF" if d == 0 else "mnegB")
                strict = self.C_("strF" if d == 0 else "strB")
                cs = slice(blk * 128, (blk + 1) * 128)
                sc = lambda Tn: Tn[:, blk, ci:ci + 1]

                def pre_fn():
                    k.op("dve", lambda e: e.tensor_scalar(out=rhsb[j][:, :], in0=tri, scalar1=sc(G), scalar2=None, op0=ALU.mult), [self.cst, G], [rhsb[j]])
                    yield
                    pa = self.psum()
                    k.op("pe", lambda e: e.matmul(pa[:, 0:128], lhsT=self.C_("ones"), rhs=rhsb[j][:, :], start=True, stop=True), [self.cst, rhsb[j]], [pa])
                    yield
                    k.op("dve", lambda e: e.scalar_tensor_tensor(out=Dm[j][:, :], in0=pa[:, 0:128], scalar=sc(GCUM), in1=mneg, op0=ALU.subtract, op1=ALU.add), [pa, GCUM, self.cst], [Dm[j]])
                    yield
                    k.op("act", lambda e: e.activation(out=E[j][:, :], in_=Dm[j][:, :], func=AF.Exp), [Dm[j]], [E[j]])
                    yield
                    k.op("act", lambda e: e.activation(out=RE[j][:, :], in_=pa[:, 0:128], func=AF.Exp), [pa], [RE[j]])
                    yield
                    pA = self.psum()
                    k.op("pe", lambda e: e.matmul(pA[:, 0:128], lhsT=KT[:, h, cs], rhs=KT[:, h, cs], start=True, stop=True), [KT], [pA])
                    yield
                    k.op("pe", lambda e: e.matmul(pA[:, 128:256], lhsT=KT[:, h, cs], rhs=QT[:, h, cs], start=True, stop=True), [KT, QT], [pA])
                    yield
                    k.op("dve", lambda e: e.scalar_tensor_tensor(out=t1[j][:, :], in0=pA[:, 0:128], scalar=sc(BETA), in1=E[j][:, :], op0=ALU.mult, op1=ALU.mult), [pA, BETA, E[j]], [t1[j]])
                    yield
                    P0, PT0 = Pk[0][j], PTk[0][j]
                    k.op("dve", lambda e: e.tensor_tensor(out=P0[:, :], in0=t1[j][:, :], in1=strict, op=ALU.mult), [t1[j], self.cst], [P0])
                    yield
                    k.op("dve", lambda e: e.tensor_tensor(out=AttnT[j][:, :], in0=pA[:, 128:256], in1=E[j][:, :], op=ALU.mult), [pA, E[j]], [AttnT[j]])
                    yield
                    k.op("dve", lambda e: e.tensor_tensor(out=QgT[j][:, :], in0=QT[:, h, cs], in1=RE[j][:, :], op=ALU.mult), [QT, RE[j]], [QgT[j]])
                    yield
                    k.op("dve", lambda e: e.tensor_scalar(out=Kd[j][:, :], in0=KTOK[:, blk, h, :], scalar1=sc(KDS), scalar2=None, op0=ALU.mult), [KTOK, KDS], [Kd[j]])
                    yield
                    pt = self.psum()
                    k.op("pe", lambda e: e.transpose(out=pt[:, 0:128], in_=P0[:, :], identity=ident), [P0, self.cst], [pt])
                    yield
                    evac(PT0[:, :], pt[:, 0:128], [pt], [PT0])
                    yield
                    mC = lambda lv: GM[:, (0 if d == 0 else 7) + lv, :]
                    mCT = lambda lv: GM[:, (7 if d == 0 else 0) + lv, :]
                    k.op("dve", lambda e: e.tensor_tensor(out=t1[j][:, :], in0=P0[:, :], in1=mC(0), op=ALU.mult), [P0, GM], [t1[j]])
                    yield
                    k.op("dve", lambda e: e.tensor_tensor(out=Xb[j][:, :], in0=ident, in1=t1[j][:, :], op=ALU.subtract), [self.cst, t1[j]], [Xb[j]])
                    yield
                    k.op("dve", lambda e: e.tensor_tensor(out=t1[j][:, :], in0=PT0[:, :], in1=mCT(0), op=ALU.mult), [PT0, GM], [t1[j]])
                    yield
                    k.op("dve", lambda e: e.tensor_tensor(out=XTb[j][:, :], in0=ident, in1=t1[j][:, :], op=ALU.subtract), [self.cst, t1[j]], [XTb[j]])
                    yield
                    for lv in range(1, 1 if 'gdn_noneu' in self.dbg else 7):
                        lastlv = (lv == 6)
                        k.op("dve", lambda e: e.tensor_tensor(out=CTb[j][:, :], in0=PT0[:, :], in1=mCT(lv), op=ALU.mult), [PT0, GM], [CTb[j]])
                        yield
                        pz = self.psum()
                        k.op("pe", lambda e: e.matmul(pz[:, 0:128], lhsT=CTb[j][:, :], rhs=Xb[j][:, :], start=True, stop=True), [CTb[j], Xb[j]], [pz])
                        yield
                        evac(Zb[j][:, :], pz[:, 0:128], [pz], [Zb[j]])
                        yield
                        py_ = self.psum()
                        k.op("pe", lambda e: e.matmul(py_[:, 0:128], lhsT=XTb[j][:, :], rhs=Zb[j][:, :], start=True, stop=True), [XTb[j], Zb[j]], [py_])
                        yield
                        if not lastlv:
                            k.op("pool", lambda e: e.tensor_tensor(out=Cb[j][:, :], in0=P0[:, :], in1=mC(lv), op=ALU.mult), [P0, GM], [Cb[j]])
                            yield
                            pz2 = self.psum()
                            k.op("pe", lambda e: e.matmul(pz2[:, 0:128], lhsT=Cb[j][:, :], rhs=XTb[j][:, :], start=True, stop=True), [Cb[j], XTb[j]], [pz2])
                            yield
                            evac(Z2b[j][:, :], pz2[:, 0:128], [pz2], [Z2b[j]])
                            yield
                            py2 = self.psum()
                            k.op("pe", lambda e: e.matmul(py2[:, 0:128], lhsT=Xb[j][:, :], rhs=Z2b[j][:, :], start=True, stop=True), [Xb[j], Z2b[j]], [py2])
                            yield
                        k.op("dve", lambda e: e.tensor_tensor(out=Xb[j][:, :], in0=Xb[j][:, :], in1=py_[:, 0:128], op=ALU.subtract), [Xb[j], py_], [Xb[j]])
                        yield
                        if not lastlv:
                            k.op("dve", lambda e: e.tensor_tensor(out=XTb[j][:, :], in0=XTb[j][:, :], in1=py2[:, 0:128], op=ALU.subtract), [XTb[j], py2], [XTb[j]])
                            yield

                def seq_fn():
                    pk = self.psum()
                    k.op("pe", lambda e: e.matmul(pk[:, 0:128], lhsT=KT[:, h, cs], rhs=Sb[:, ci, :], start=True, stop=True), [KT, sb_], [pk])
                    k.op("dve", lambda e: e.scalar_tensor_tensor(out=Rp[j][:, :], in0=pk[:, 0:128], scalar=sc(EG), in1=VTOK[:, blk, h, :], op0=ALU.mult, op1=ALU.subtract), [pk, EG, VTOK], [Rp[j]])
                    k.op("pe", lambda e: e.matmul(pk[:, 128:256], lhsT=Xb[j][:, :], rhs=Rp[j][:, :], start=True, stop=True), [Xb[j], Rp[j]], [pk])
                    k.op("dve", lambda e: e.tensor_scalar(out=vnew[j][:, :], in0=pk[:, 128:256], scalar1=sc(NBETA), scalar2=None, op0=ALU.mult), [pk, NBETA], [vnew[j]])
                    po = self.psum()
                    k.op("pe", lambda e: e.matmul(po[:, 0:128], lhsT=Sb[:, ci, :], rhs=QgT[j][:, :], start=True, stop=False), [sb_, QgT[j]], [po])
                    k.op("pe", lambda e: e.matmul(po[:, 0:128], lhsT=vnew[j][:, :], rhs=AttnT[j][:, :], start=False, stop=True), [vnew[j], AttnT[j]], [po])
                    k.op("act", lambda e: e.copy(out=oo[j][:, :], in_=po[:, 0:128]), [po], [oo[j]])
                    k.dma("pool", self.GO.t[d, h * 128:(h + 1) * 128, cs], oo[j][:, :], [oo[j]], [], oo[j])
                    k.op("pe", lambda e: e.matmul(po[:, 128:256], lhsT=Kd[j][:, :], rhs=vnew[j][:, :], start=True, stop=True), [Kd[j], vnew[j]], [po])
                    k.op("dve", lambda e: e.scalar_tensor_tensor(out=S_[:, ci, :], in0=S_[:, ci, :], scalar=sc(GL), in1=po[:, 128:256], op0=ALU.mult, op1=ALU.add), [sb_, GL, po], [sb_])
                    k.op("act", lambda e: e.copy(out=Sb[:, ci, :], in_=S_[:, ci, :]), [sb_], [sb_])
                return pre_fn, seq_fn
            ulist = []
            for i in range(0 if "gdn_noscan" in self.dbg else NB):
                for h in range(4):
                    for d in range(2):
                        ulist.append(make_unit(i, h, d, len(ulist) % R))
            def adv(g):
                try:
                    next(g)
                    return True
                except StopIteration:
                    return False
            gens = {}
            nU = len(ulist)
            for n_ in range(min(2, nU)):
                gens[n_] = ulist[n_][0]()
            for n_ in range(nU):
                while True:
                    a = adv(gens[n_])
                    if n_ + 1 in gens:
                        adv(gens[n_ + 1])
                    if not a:
                        break
                del gens[n_]
                ulist[n_][1]()
                if n_ + 2 < nU:
                    gens[n_ + 2] = ulist[n_ + 2][0]()
            k.barrier()
        with contextlib.ExitStack() as st:
            gn = self.vec_col(st, W["gdn_norm_g"][l], 128, "ggn")
            oa = [self.sb(st, [128, 512], F32, "goa") for _ in range(2)]
            ob = [self.sb(st, [128, 512], F32, "gob") for _ in range(2)]
            z_ = [self.sb(st, [128, 512], F32, "gz") for _ in range(2)]
            sq_ = [self.sb(st, [128, 512], F32, "gfsq") for _ in range(2)]
            rs_ = [self.sb(st, [128, 512], F32, "gfrs") for _ in range(2)]
            yb = [self.sb(st, [128, 512], BF16, "gyb") for _ in range(2)]
            tiles = self.tiles[1:] if self.last else self.tiles
            it = 0
            for h in range(4):
                rz = self.urow(GZ + h * 128)
                for ti, (t0, n) in enumerate(tiles):
                    j = it % 2
                    it += 1
                    k.dma("sp", oa[j][:, 0:n], self.GO.t[0, h * 128:(h + 1) * 128, t0:t0 + n], [], [oa[j]], oa[j])
                    k.dma("sp", ob[j][:, 0:n], self.GO.t[1, h * 128:(h + 1) * 128, t0:t0 + n], [], [ob[j]], ob[j])
                    k.dma("sp", z_[j][:, 0:n], self.U.t[rz:rz + 128, t0:t0 + n], [], [z_[j]], z_[j])
                    k.op("dve", lambda e: e.tensor_tensor(out=oa[j][:, 0:n], in0=oa[j][:, 0:n], in1=ob[j][:, 0:n], op=ALU.add), [oa[j], ob[j]], [oa[j]])
                    k.op("act", lambda e: e.activation(out=sq_[j][:, 0:n], in_=oa[j][:, 0:n], func=AF.Square), [oa[j]], [sq_[j]])
                    k.op("act", lambda e: e.activation(out=z_[j][:, 0:n], in_=z_[j][:, 0:n], func=AF.Silu), [z_[j]], [z_[j]])
                    ps = self.psum()
                    k.op("pe", lambda e: e.matmul(ps[:, 0:n], lhsT=self.C_("ones"), rhs=sq_[j][:, 0:n], start=True, stop=True), [self.cst, sq_[j]], [ps])
                    self.rstd_from(ps, n, rs_[j], 1.0 / 128)
                    k.op("dve", lambda e: e.scalar_tensor_tensor(out=oa[j][:, 0:n], in0=oa[j][:, 0:n], scalar=gn[:, 0:1], in1=rs_[j][:, 0:n], op0=ALU.mult, op1=ALU.mult), [oa[j], gn, rs_[j]], [oa[j]])
                    k.op("dve", lambda e: e.tensor_tensor(out=yb[j][:, 0:n], in0=oa[j][:, 0:n], in1=z_[j][:, 0:n], op=ALU.mult), [oa[j], z_[j]], [yb[j]])
                    k.dma("pool", self.Y.t[h * 128:(h + 1) * 128, t0:t0 + n], yb[j][:, 0:n], [yb[j]], [], yb[j])
            k.barrier()


_CACHE = {}


def kernel(**inputs):
    S, C, DEPTH = 4096, 256, 2
    inputs = {k_: np.asarray(v) for k_, v in inputs.items()}
    if "nc" not in _CACHE:
        _CACHE["nc"] = Mod(S, C, DEPTH).build()
    nc = _CACHE["nc"]
    consts = consts_np(S, C)
    in_maps = []
    for b in range(8):
        m = {"x": np.ascontiguousarray(inputs["x"][b], dtype=np.float32),
             "ctx": np.ascontiguousarray(inputs["ctx"][b], dtype=np.float32),
             "cc": np.ascontiguousarray(np.stack([inputs["c"][b], inputs["c_ctx"]]), dtype=np.float32)}
        m.update(consts)
        for n, sh in WSPEC:
            m[n] = np.ascontiguousarray(inputs[n], dtype=np.float32)
        in_maps.append(m)
    res = run_bass_kernel_spmd(nc, in_maps, core_ids=list(range(8)))
    return np.stack([np.asarray(r["out"], dtype=np.float32) for r in res.results], axis=0)
```

```python
import math
import contextlib
import numpy as np
import concourse.bass as bass
import concourse.mybir as mybir
from concourse.bass_utils import run_bass_kernel_spmd

F32 = mybir.dt.float32
BF16 = mybir.dt.bfloat16
ALU = mybir.AluOpType
AF = mybir.ActivationFunctionType


class Buf:
    __slots__ = ("t", "w", "r", "dsem", "name")

    def __init__(self, t, name=""):
        self.t = t
        self.w = {}
        self.r = {}
        self.dsem = None
        self.name = name

    def __getitem__(self, key):
        return self.t[key]


class KB:
    SEM_ROT = 30000

    def __init__(self, nc):
        self.nc = nc
        self.es = contextlib.ExitStack()
        self.engs = {"pe": nc.tensor, "dve": nc.vector, "act": nc.scalar,
                     "pool": nc.gpsimd, "sp": nc.sync}
        self.semh = {}
        self.cnt = {}
        self.isdma = {}
        self.cur = {}
        self.waited = {e: {} for e in self.engs}
        self.nsem = 0
        for e in self.engs:
            self.cur[e] = self.new_sem(False)
        self.ninstr = 0
        self.free_dsems = []
        self.free_dsems_q = {}
        self.phase_dsems = []
        self.persist = False
        self.dma_remap = {}

    def new_sem(self, isdma):
        key = self.nsem
        self.nsem += 1
        self.semh[key] = self.es.enter_context(self.nc.semaphore("s%d" % key))
        self.cnt[key] = 0
        self.isdma[key] = isdma
        return key

    def sb(self, stack, name, shape, dtype):
        t = stack.enter_context(self.nc.sbuf_tensor(name, list(shape), dtype))
        return Buf(t, name)

    def ps(self, stack, name, shape, dtype):
        t = stack.enter_context(self.nc.psum_tensor(name, list(shape), dtype))
        return Buf(t, name)

    def _waits(self, eng, reads, writes):
        need = {}
        for b in reads:
            for s, v in b.w.items():
                if need.get(s, 0) < v:
                    need[s] = v
        for b in writes:
            for d in (b.w, b.r):
                for s, v in d.items():
                    if eng == "pe" and s == self.cur["pe"] and d is b.w:
                        continue
                    if need.get(s, 0) < v:
                        need[s] = v
        e = self.engs[eng]
        wd = self.waited[eng]
        for s, v in need.items():
            if wd.get(s, 0) >= v:
                continue
            if self.isdma[s]:
                v = self.cnt[s]
            e.wait_ge(self.semh[s], v)
            wd[s] = v
            self.ninstr += 1

    def op(self, eng, fn, reads=(), writes=()):
        self._waits(eng, reads, writes)
        ins = fn(self.engs[eng])
        s = self.cur[eng]
        self.cnt[s] += 1
        ins.then_inc(self.semh[s], 1)
        tag = (s, self.cnt[s])
        self._mark(tag, reads, writes)
        if self.cnt[s] >= self.SEM_ROT:
            self.cur[eng] = self.new_sem(False)
        self.ninstr += 1
        return ins

    def _mark(self, tag, reads, writes):
        s, v = tag
        for b in writes:
            b.w = {s: v}
            b.r = {}
        for b in reads:
            if b not in writes:
                b.r[s] = v

    def dma(self, eng, out, in_, reads, writes, sembuf, **kw):
        eng = self.dma_remap.get(eng, eng)
        dmap = sembuf.dsem if isinstance(sembuf.dsem, dict) else {}
        sembuf.dsem = dmap
        if eng not in dmap:
            fl = self.free_dsems_q.setdefault(eng, [])
            if fl and not self.persist:
                dmap[eng] = fl.pop()
            else:
                dmap[eng] = self.new_sem(True)
            if not self.persist:
                self.phase_dsems.append((eng, dmap[eng]))
        self._waits(eng, reads, writes)
        ins = self.engs[eng].dma_start(out=out, in_=in_, **kw)
        s = sembuf.dsem[eng]
        self.cnt[s] += 16
        ins.then_inc(self.semh[s], 16)
        self._mark((s, self.cnt[s]), reads, writes)
        self.ninstr += 1
        return ins

    def barrier(self, engs=None):
        if engs is None:
            for q_, s_ in self.phase_dsems:
                self.free_dsems_q.setdefault(q_, []).append(s_)
            self.phase_dsems = []
        for eng in (engs or self.engs):
            e = self.engs[eng]
            wd = self.waited[eng]
            for s, v in self.cnt.items():
                if v > 0 and wd.get(s, 0) < v:
                    e.wait_ge(self.semh[s], v)
                    wd[s] = v
                    self.ninstr += 1


D = 1024
EPS = 1e-6
NEG = -30000.0
GQ, GK, GV, GZ, GA, GB_ = 0, 512, 1024, 1536, 2048, 2056
MCQ, MCKV, MKPE = 2064, 2448, 2704
DQ, DK, DV = 2736, 3248, 3760
SZ, SX, SB_, SC, SDT = 4272, 4784, 5296, 5552, 5808
GATE0 = 5824
CN = ["ident", "ones", "triF", "triB", "mnegF", "mnegB", "strF", "strB", "bd64", "perm64", "perm32", "sel65"]


def consts_np(S, C):
    T = C + S
    k = np.arange(128)
    d = {}
    d["ident"] = np.eye(128)
    d["ones"] = np.ones((128, 128))
    d["triF"] = (k[:, None] <= k[None, :])
    d["triB"] = (k[:, None] >= k[None, :])
    d["mnegF"] = np.where(k[None, :] >= k[:, None], 0.0, NEG)
    d["mnegB"] = np.where(k[None, :] <= k[:, None], 0.0, NEG)
    d["strF"] = (k[None, :] > k[:, None])
    d["strB"] = (k[None, :] < k[:, None])
    d["bd64"] = (k[:, None] // 64 == k[None, :] // 64)

    def perm(n_rot, total):
        P = np.zeros((128, 128))
        half = n_rot // 2
        qd = half // 2
        for base in range(0, total, half):
            for i in range(half):
                m = base + i
                if i < qd:
                    P[m + qd, m] = -1.0
                else:
                    P[m - qd, m] = 1.0
        return P
    d["perm64"] = perm(64, 128)
    d["perm32"] = perm(32, 32)
    s65 = np.zeros((128, 128))
    s65[64, :] = 1.0
    d["sel65"] = s65
    cst = np.stack([np.asarray(d[n], np.float32) for n in CN], axis=1)

    def rope(rot_dim):
        rows = S // 64
        row = np.repeat(np.arange(rows, dtype=np.float32), 64)
        col = np.tile(np.arange(64, dtype=np.float32), rows)
        quarter = rot_dim // 4
        inv = (np.float32(10000.0) ** (-np.arange(quarter, dtype=np.float32) / np.float32(quarter))).astype(np.float32)
        ar = row[:, None] * inv
        ac = col[:, None] * inv
        ang = np.concatenate([ar, ar, ac, ac], axis=-1).astype(np.float32)
        cos = np.ones((rot_dim, T), np.float32)
        sin = np.zeros((rot_dim, T), np.float32)
        cos[:, C:] = np.cos(ang).T
        sin[:, C:] = np.sin(ang).T
        return cos, sin
    cm, sm = rope(32)
    cd, sd = rope(64)
    ropem = np.stack([cm, sm], axis=1)
    roped = np.stack([np.tile(cd, (2, 1)), np.tile(sd, (2, 1))], axis=1)
    gm = []
    for lv in range(7):
        b = 1 << lv
        mU = ((k[:, None] // (2 * b) == k[None, :] // (2 * b)) & (k[:, None] % (2 * b) < b) & (k[None, :] % (2 * b) >= b))
        gm.append(mU)
    gm = gm + [m_.T for m_ in gm]
    gmask = np.stack([np.asarray(m_, np.float32) for m_ in gm], axis=1)
    return {"cst": np.ascontiguousarray(cst), "ropem": np.ascontiguousarray(ropem),
            "roped": np.ascontiguousarray(roped), "gmask": np.ascontiguousarray(gmask)}


WSPEC = [
    ("ada_w", [D, 6 * D]), ("ada_b", [6 * D]), ("norm1_g", [D]), ("norm2_g", [D]),
    ("w_in", [D, 9920]), ("gdn_conv", [5, 1536]), ("gdn_a_log", [2, 4]), ("gdn_dt_bias", [2, 4]),
    ("gdn_norm_g", [128]), ("mla_q_lora_g", [384]), ("mla_kv_lora_g", [256]),
    ("mla_w_uq", [384, 768]), ("mla_w_ukv", [256, 1024]), ("mla_qn_g", [96]), ("mla_kn_g", [96]),
    ("diff_qn_g", [64]), ("diff_kn_g", [64]), ("diff_lambda", [4, 64]), ("diff_sub_g", [128]),
    ("ssm_conv", [5, 1024]), ("ssm_conv_b", [1024]), ("ssm_a_log", [2, 8]), ("ssm_dt_bias", [2, 8]),
    ("ssm_d", [8]), ("ssm_norm_g", [512]), ("w_branch", [4, 512, D]), ("w_out", [D, D]),
    ("mlp_w1", [D, 4 * D]), ("mlp_w2", [4 * D, D]),
]


class Mod:
    def __init__(self, S, C, depth, dbg=()):
        self.S, self.C, self.depth = S, C, depth
        self.T = T = S + C
        self.NB = T // 128
        self.dbg = dbg
        nc = self.nc = bass.Bass("TRN2", target_bir_lowering=False)
        self.k = KB(nc)
        self.uid = 0
        di = lambda n, sh, dt=F32: nc.dram_tensor(n, list(sh), dt, kind="ExternalInput").ap()
        self.x_in = di("x", [S, D])
        self.ctx_in = di("ctx", [C, D])
        self.cc_in = di("cc", [2, D])
        self.cst_in = di("cst", [128, len(CN), 128])
        self.ropem_in = di("ropem", [32, 2, T])
        self.roped_in = di("roped", [128, 2, T])
        self.gmask_in = di("gmask", [128, 14, 128])
        self.W = {n: di(n, [depth] + sh) for n, sh in WSPEC}
        self.out = nc.dram_tensor("out", [S, D], F32, kind="ExternalOutput").ap()
        dscr = lambda n, sh, dt=F32: Buf(nc.dram_tensor(n, list(sh), dt, kind=("ExternalOutput" if n in dbg else "Internal")).ap(), n)
        self.XT = dscr("XT", [D, T])
        self.U = dscr("U", [80 * 128, T])
        self.Y = dscr("Y", [2048, T], BF16)
        self.MT = dscr("MT", [D, T])
        self.HID = dscr("HID", [4 * D, T], BF16)
        self.QD = dscr("QD", [512, T], BF16)
        self.KD = dscr("KD", [512, T], BF16)
        self.VD = dscr("VD", [T, 512], BF16)
        self.KN = dscr("KN", [8 * 64, T], BF16)
        self.KR = dscr("KR", [8 * 32, T], BF16)
        self.QN = dscr("QN", [8 * 64, T], BF16)
        self.QR = dscr("QR", [8 * 32, T], BF16)
        self.VA = dscr("VA", [T, 8 * 65], BF16)
        self.XS = dscr("XS", [512, T])
        self.YS = dscr("YS", [2, 512, T])
        self.GO = dscr("GO", [2, 512, T])
        self.tiles = [(0, C)] + [(C + 512 * i, 512) for i in range(S // 512)]
        self.uchunks = []
        slot = 0
        self.uslot = {}
        for (a, b) in [(0, 2048), (2048, 2064), (2064, 2448), (2448, 2704), (2704, 2736), (2736, 4272),
                       (4272, 5808), (5808, 5824), (5824, 9920)]:
            c = a
            while c < b:
                e = min(c + 128, b)
                self.uchunks.append((c, e, slot))
                self.uslot[c] = slot
                slot += 1
                c = e
        assert slot == 80

    def nm(self, p):
        self.uid += 1
        return "%s_%d" % (p, self.uid)

    def sb(self, st, shape, dt=F32, p="t"):
        return self.k.sb(st, self.nm(p), shape, dt)

    def psum(self):
        self.psi = (self.psi + 1) % len(self.psb)
        return self.psb[self.psi]

    def urow(self, col):
        return self.uslot[col] * 128

    def build(self):
        k = self.k
        with k.es, contextlib.ExitStack() as gs:
            self.psb = [k.ps(gs, "psb%d" % i, [128, 512], F32) for i in range(8)]
            self.psi = 0
            self.cst = self.sb(gs, [128, len(CN), 128], F32, "cst")
            k.persist = True
            k.dma("sp", self.cst[:, :, :], self.cst_in, [], [self.cst], self.cst)
            k.persist = False
            self.cbf = self.sb(gs, [128, len(CN), 128], BF16, "cbf")
            k.op("dve", lambda e: e.tensor_copy(out=self.cbf[:, :, :], in_=self.cst[:, :, :]), [self.cst], [self.cbf])
            self.epsb = self.sb(gs, [128, 4], F32, "epsb")
            k.op("dve", lambda e: e.memset(self.epsb[:, :], EPS), [], [self.epsb])
            k.op("dve", lambda e: e.memset(self.epsb[:, 1:2], 1.0), [self.epsb], [self.epsb])
            self.XTt = [Buf(None, "XTt%d" % i) for i in range(len(self.tiles))]
            self.MTt = [Buf(None, "MTt%d" % i) for i in range(len(self.tiles))]
            self.phase_init()
            for l in range(self.depth):
                self.layer(l)
            self.phase_final()
            k.barrier()
        return self.nc

    def C_(self, name, bf=False):
        i = CN.index(name)
        return (self.cbf if bf else self.cst)[:, i, :]

    def phase_init(self):
        k = self.k
        with contextlib.ExitStack() as st:
            xin = [self.sb(st, [128, D], F32, "xin") for _ in range(2)]
            xo = [self.sb(st, [128, 8, 128], F32, "xo") for _ in range(2)]
            XTv = self.XT.t.rearrange("(kc p) t -> p kc t", p=128)
            for blk in range(self.NB):
                src = self.ctx_in[blk * 128:(blk + 1) * 128, :] if blk < self.C // 128 else \
                    self.x_in[blk * 128 - self.C:(blk + 1) * 128 - self.C, :]
                xi = xin[blk % 2]
                o = xo[blk % 2]
                k.dma("sp", xi[:, :], src, [], [xi], xi)
                for half in range(2):
                    ps = self.psum()
                    for j in range(4):
                        kc = half * 4 + j
                        k.op("pe", lambda e: e.transpose(out=ps[:, j * 128:(j + 1) * 128], in_=xi[:, kc * 128:(kc + 1) * 128],
                                                         identity=self.C_("ident")), [xi, self.cst], [ps])
                    eng = "dve" if half == 0 else "act"
                    if eng == "dve":
                        k.op("dve", lambda e: e.tensor_copy(out=o[:, half * 4:half * 4 + 4, :], in_=ps[:, :].rearrange("p (a b) -> p a b", a=4)), [ps], [o])
                    else:
                        k.op("act", lambda e: e.copy(out=o[:, half * 4:half * 4 + 4, :], in_=ps[:, :].rearrange("p (a b) -> p a b", a=4)), [ps], [o])
                k.dma("pool", XTv[:, :, blk * 128:(blk + 1) * 128], o[:, :, :], [o], [], o)
            k.barrier()

    def phase_final(self):
        k = self.k
        with contextlib.ExitStack() as st:
            xi = [self.sb(st, [128, 8, 128], F32, "fxi") for _ in range(2)]
            xo = [self.sb(st, [128, D], F32, "fxo") for _ in range(2)]
            XTv = self.XT.t.rearrange("(kc p) t -> p kc t", p=128)
            dout = Buf(self.out, "out")
            for b in range(self.S // 128):
                blk = b + self.C // 128
                a = xi[b % 2]
                o = xo[b % 2]
                k.dma("sp", a[:, :, :], XTv[:, :, blk * 128:(blk + 1) * 128], [], [a], a)
                for half in range(2):
                    ps = self.psum()
                    for j in range(4):
                        kc = half * 4 + j
                        k.op("pe", lambda e: e.transpose(out=ps[:, j * 128:(j + 1) * 128], in_=a[:, kc, :],
                                                         identity=self.C_("ident")), [a, self.cst], [ps])
                    if half == 0:
                        k.op("dve", lambda e: e.tensor_copy(out=o[:, 0:512], in_=ps[:, :]), [ps], [o])
                    else:
                        k.op("act", lambda e: e.copy(out=o[:, 512:1024], in_=ps[:, :]), [ps], [o])
                k.dma("pool", self.out[b * 128:(b + 1) * 128, :], o[:, :], [o], [dout], o)
            k.barrier()

    def layer(self, l):
        k = self.k
        self.l = l
        self.last = (l == self.depth - 1) and ("forcectx" not in self.dbg)
        with contextlib.ExitStack() as ls:
            self.phase_mod(l, ls)
            self.phase_inproj(l)
            if "U" in self.dbg and l == 0 and "stopU" in self.dbg:
                return
            for ph in ("gdn", "mla", "diff", "ssm", "merge", "mlp"):
                if ph not in self.dbg:
                    getattr(self, "phase_" + ph)(l)
            k.barrier()

    def phase_mod(self, l, ls):
        k = self.k
        W = self.W
        self.modv = self.sb(ls, [128, 48, 2], F32, "modv")
        self.A1 = self.sb(ls, [128, 8, 2], F32, "A1")
        self.A2 = self.sb(ls, [128, 8, 2], F32, "A2")
        with contextlib.ExitStack() as st:
            cv = self.sb(st, [128, 2, 8], F32, "cv")
            sv = self.sb(st, [128, 8, 2], F32, "sv")
            ab = self.sb(st, [128, 48], F32, "ab")
            g12 = self.sb(st, [128, 2, 8], F32, "g12")
            k.dma("sp", cv[:, :, :], self.cc_in.rearrange("r (kc p) -> p r kc", p=128), [], [cv], cv, allow_slow_non_contiguous=True)
            k.dma("sp", ab[:, :], W["ada_b"][l].rearrange("(j p) -> p j", p=128), [], [ab], ab, allow_slow_non_contiguous=True)
            k.dma("sp", g12[:, 0, :], W["norm1_g"][l].rearrange("(kc p) -> p kc", p=128), [], [g12], g12, allow_slow_non_contiguous=True)
            k.dma("sp", g12[:, 1, :], W["norm2_g"][l].rearrange("(kc p) -> p kc", p=128), [], [g12], g12, allow_slow_non_contiguous=True)
            k.op("act", lambda e: e.activation(out=sv[:, :, :], in_=cv[:, :, :].rearrange("p r kc -> p kc r"), func=AF.Silu), [cv], [sv])
            wst = [self.sb(st, [128, 8, 512], F32, "adaw") for _ in range(2)]
            aw = W["ada_w"][l].rearrange("(kc p) c -> p kc c", p=128)
            ps = self.psum()
            for g in range(12):
                w = wst[g % 2]
                k.dma("sp", w[:, :, :], aw[:, :, g * 512:(g + 1) * 512], [], [w], w)
                for jj in range(4):
                    j = g * 4 + jj
                    for kc in range(8):
                        k.op("pe", lambda e: e.matmul(ps[:, j * 2:j * 2 + 2], lhsT=w[:, kc, jj * 128:(jj + 1) * 128], rhs=sv[:, kc, :],
                                                      start=(kc == 0), stop=(kc == 7)), [w, sv], [ps])
            k.op("dve", lambda e: e.tensor_tensor(out=self.modv[:, :, :], in0=ps[:, 0:96].rearrange("p (j r) -> p j r", r=2),
                                                  in1=ab[:, :].unsqueeze(2).to_broadcast([128, 48, 2]), op=ALU.add), [ps, ab], [self.modv])
            for (A, gi, mi) in ((self.A1, 0, 1), (self.A2, 1, 4)):
                k.op("dve", lambda e: e.scalar_tensor_tensor(out=A[:, :, :], in0=self.modv[:, mi * 8:mi * 8 + 8, :], scalar=1.0,
                                                             in1=g12[:, gi, :].unsqueeze(2).to_broadcast([128, 8, 2]), op0=ALU.add, op1=ALU.mult),
                     [self.modv, g12], [A])
            k.barrier()

    def mcol(self, ti):
        return 1 if ti == 0 else 0

    def norm_mod(self, st, A, shift_idx, dst):
        k = self.k
        xt_ = [self.sb(st, [128, 8, 512], F32, "nx") for _ in range(2)]
        sq_ = [self.sb(st, [128, 8, 512], F32, "nsq") for _ in range(2)]
        rs_ = [self.sb(st, [128, 512], F32, "nrs") for _ in range(2)]
        tmp_ = [self.sb(st, [128, 512], F32, "ntmp") for _ in range(3)]
        XTv = self.XT.t.rearrange("(kc p) t -> p kc t", p=128)
        ci = 0
        for ti, (t0, n) in enumerate(self.tiles):
            col = self.mcol(ti)
            xt, sq, rs = xt_[ti % 2], sq_[ti % 2], rs_[ti % 2]
            k.dma("sp", xt[:, :, 0:n], XTv[:, :, t0:t0 + n], [self.XTt[ti]], [xt], xt)
            k.op("act", lambda e: e.activation(out=sq[:, :, 0:n], in_=xt[:, :, 0:n], func=AF.Square), [xt], [sq])
            ps = self.psum()
            for kc in range(8):
                k.op("pe", lambda e: e.matmul(ps[:, 0:n], lhsT=self.C_("ones"), rhs=sq[:, kc, 0:n], start=(kc == 0), stop=(kc == 7)),
                     [self.cst, sq], [ps])
            self.rstd_from(ps, n, rs, 1.0 / D)
            for kc in range(8):
                tmp = tmp_[ci % 3]
                ci += 1
                k.op("dve", lambda e: e.scalar_tensor_tensor(out=tmp[:, 0:n], in0=xt[:, kc, 0:n], scalar=A[:, kc, col:col + 1], in1=rs[:, 0:n],
                                                             op0=ALU.mult, op1=ALU.mult), [xt, A, rs], [tmp])
                k.op("act", lambda e: e.activation(out=dst[:, kc, t0:t0 + n], in_=tmp[:, 0:n], func=AF.Identity,
                                                   bias=self.modv[:, shift_idx * 8 + kc, col:col + 1], scale=1.0), [tmp, self.modv], [dst])

    def gemm_fm(self, st, in_sb, KC, wsrc, chunks, tiles, epi, krows=128):
        k = self.k
        groups, cur = [], []
        for ch in chunks:
            if cur and (ch[0] != cur[-1][1] or ch[1] - cur[0][0] > 512):
                groups.append(cur)
                cur = []
            cur.append(ch)
        if cur:
            groups.append(cur)
        wst = [self.sb(st, [128, KC, 512], F32, "wst") for _ in range(2)]
        wbf = [self.sb(st, [128, KC, 512], BF16, "wbf") for _ in range(2)]
        for gi, g in enumerate(groups):
            c0, c1 = g[0][0], g[-1][1]
            w = c1 - c0
            s, b = wst[gi % 2], wbf[gi % 2]
            k.dma("sp", s[0:krows, :, 0:w], wsrc(c0, c1), [], [s], s)
            k.op("pool", lambda e: e.tensor_copy(out=b[0:krows, :, 0:w], in_=s[0:krows, :, 0:w]), [s], [b])
            for ti, (t0, n) in enumerate(tiles):
                for ch in g:
                    rows = ch[1] - ch[0]
                    off = ch[0] - c0
                    ps = self.psum()
                    for kk in range(KC):
                        k.op("pe", lambda e: e.matmul(ps[0:rows, 0:n], lhsT=b[0:krows, kk, off:off + rows], rhs=in_sb[0:krows, kk, t0:t0 + n],
                                                      start=(kk == 0), stop=(kk == KC - 1)), [b, in_sb], [ps])
                    epi(ch, ti, t0, n, ps, rows)

    def phase_inproj(self, l):
        k = self.k
        with contextlib.ExitStack() as st:
            hT = self.sb(st, [128, 8, self.T], BF16, "hT")
            with contextlib.ExitStack() as st2:
                self.norm_mod(st2, self.A1, 0, hT)
                k.barrier()
            stg = [self.sb(st, [128, 512], F32, "ustg") for _ in range(4)]
            cnt = [0]
            win = self.W["w_in"][l].rearrange("(kc p) c -> p kc c", p=128)

            def epi(ch, ti, t0, n, ps, rows):
                s = stg[cnt[0] % 4]
                if cnt[0] % 2 == 0:
                    k.op("dve", lambda e: e.tensor_copy(out=s[0:rows, 0:n], in_=ps[0:rows, 0:n]), [ps], [s])
                else:
                    k.op("act", lambda e: e.copy(out=s[0:rows, 0:n], in_=ps[0:rows, 0:n]), [ps], [s])
                cnt[0] += 1
                r0 = ch[2] * 128
                k.dma("pool", self.U.t[r0:r0 + rows, t0:t0 + n], s[0:rows, 0:n], [s], [], s)
            self.gemm_fm(st, hT, 8, lambda c0, c1: win[:, :, c0:c1], self.uchunks, self.tiles, epi)
            vst = [self.sb(st, [128, 512], BF16, "vst") for _ in range(2)]

            def epiv(blk, ps):
                v = vst[blk % 2]
                k.op("act", lambda e: e.copy(out=v[:, :], in_=ps[:, :]), [ps], [v])
                k.dma("pool", self.VD.t[blk * 128:(blk + 1) * 128, :], v[:, :], [v], [], v)
            self.gemm_tm(st, hT, 8, lambda s_: k.dma("sp", s_[:, :, :], win[:, :, DV:DV + 512], [], [s_], s_), 512, list(range(self.NB)), epiv)
            k.barrier()

    def gemm_tm(self, st, in_sb, KC, wsrc, width, blocks, epi, krows=128):
        k = self.k
        s = self.sb(st, [128, KC, width], F32, "wtm")
        b = self.sb(st, [128, KC, width], BF16, "wtmb")
        wsrc(s)
        k.op("pool", lambda e: e.tensor_copy(out=b[0:krows, :, :], in_=s[0:krows, :, :]), [s], [b])
        for blk in blocks:
            ps = self.psum()
            for kk in range(KC):
                k.op("pe", lambda e: e.matmul(ps[:, 0:width], lhsT=in_sb[0:krows, kk, blk * 128:(blk + 1) * 128], rhs=b[0:krows, kk, :],
                                              start=(kk == 0), stop=(kk == KC - 1)), [in_sb, b], [ps])
            epi(blk, ps)

    def rstd_from(self, ps, n, rs, scale, rows=128):
        k = self.k
        k.op("act", lambda e: e.activation(out=rs[0:rows, 0:n], in_=ps[0:rows, 0:n], func=AF.Ln, bias=self.epsb[0:rows, 0:1], scale=scale), [ps, self.epsb], [rs])
        k.op("act", lambda e: e.activation(out=rs[0:rows, 0:n], in_=rs[0:rows, 0:n], func=AF.Exp, scale=-0.5), [rs], [rs])

    def vec_col(self, st, src_ap, rows, p="vc"):
        t = self.sb(st, [128, 1], F32, p)
        self.k.dma("sp", t[0:rows, :], src_ap.rearrange("(p o) -> p o", o=1), [], [t], t, allow_slow_non_contiguous=True)
        return t

    def phase_merge(self, l):
        k = self.k
        W = self.W
        tiles = self.tiles[1:] if self.last else self.tiles
        MTv = self.MT.t
        for i in range(4):
            with contextlib.ExitStack() as st:
                yin = self.sb(st, [128, 4, self.T], BF16, "yin")
                k.dma("sp", yin[:, :, :], self.Y.t[i * 512:(i + 1) * 512, :].rearrange("(kc p) t -> p kc t", p=128), [], [yin], yin)
                gt_ = [self.sb(st, [128, 512], F32, "gt") for _ in range(3)]
                mt_ = [self.sb(st, [128, 512], F32, "mt") for _ in range(3)]
                cnt = [0]
                wb = W["w_branch"][l, i].rearrange("(kc p) c -> p kc c", p=128)

                def epi(ch, ti, t0, n, ps, rows):
                    c = ch[0] // 128
                    gt, mt = gt_[cnt[0] % 3], mt_[cnt[0] % 3]
                    cnt[0] += 1
                    r0 = self.urow(GATE0 + i * 1024 + c * 128)
                    k.dma("sp", gt[:, 0:n], self.U.t[r0:r0 + 128, t0:t0 + n], [], [gt], gt)
                    k.op("act", lambda e: e.activation(out=gt[:, 0:n], in_=gt[:, 0:n], func=AF.Sigmoid), [gt], [gt])
                    if i > 0:
                        k.dma("sp", mt[:, 0:n], MTv[c * 128:(c + 1) * 128, t0:t0 + n], [], [mt], mt)
                        k.op("dve", lambda e: e.tensor_tensor(out=gt[:, 0:n], in0=ps[:, 0:n], in1=gt[:, 0:n], op=ALU.mult), [ps, gt], [gt])
                        k.op("dve", lambda e: e.tensor_tensor(out=mt[:, 0:n], in0=mt[:, 0:n], in1=gt[:, 0:n], op=ALU.add), [mt, gt], [mt])
                    else:
                        k.op("dve", lambda e: e.tensor_tensor(out=mt[:, 0:n], in0=ps[:, 0:n], in1=gt[:, 0:n], op=ALU.mult), [ps, gt], [mt])
                    k.dma("pool", MTv[c * 128:(c + 1) * 128, t0:t0 + n], mt[:, 0:n], [mt], [], mt)
                self.gemm_fm(st, yin, 4, lambda c0, c1: wb[:, :, c0:c1], [(c * 128, (c + 1) * 128) for c in range(8)], tiles, epi)
                k.barrier()
        with contextlib.ExitStack() as st:
            mT = self.sb(st, [128, 8, self.T], BF16, "mTb")
            with contextlib.ExitStack() as st2:
                ml = [self.sb(st2, [128, 8, 512], F32, "ml") for _ in range(2)]
                for ti, (t0, n) in enumerate(tiles):
                    m = ml[ti % 2]
                    k.dma("sp", m[:, :, 0:n], MTv.rearrange("(kc p) t -> p kc t", p=128)[:, :, t0:t0 + n], [], [m], m)
                    k.op("dve", lambda e: e.tensor_copy(out=mT[:, :, t0:t0 + n], in_=m[:, :, 0:n]), [m], [mT])
                k.barrier()
            self.resid_gemm(st, mT, 8, self.W["w_out"][l].rearrange("(kc p) c -> p kc c", p=128), 16, tiles)
            k.barrier()

    def resid_gemm(self, st, in_sb, KC, wv, gate_idx, tiles):
        k = self.k
        xt_ = [self.sb(st, [128, 512], F32, "rx") for _ in range(4)]
        cnt = [0]
        XTv = self.XT.t

        def epi(ch, ti, t0, n, ps, rows):
            c = ch[0] // 128
            col = 1 if t0 == 0 else 0
            xt = xt_[cnt[0] % 4]
            cnt[0] += 1
            k.dma("sp", xt[:, 0:n], XTv[c * 128:(c + 1) * 128, t0:t0 + n], [], [xt], xt)
            k.op("dve", lambda e: e.scalar_tensor_tensor(out=xt[:, 0:n], in0=ps[:, 0:n], scalar=self.modv[:, gate_idx + c, col:col + 1], in1=xt[:, 0:n],
                                                         op0=ALU.mult, op1=ALU.add), [ps, self.modv, xt], [xt])
            k.dma("pool", XTv[c * 128:(c + 1) * 128, t0:t0 + n], xt[:, 0:n], [xt], [], xt)
        self.gemm_fm(st, in_sb, KC, lambda c0, c1: wv[:, :, c0:c1], [(c * 128, (c + 1) * 128) for c in range(8)], tiles, epi)

    def phase_mlp(self, l):
        k = self.k
        tiles = self.tiles[1:] if self.last else self.tiles
        with contextlib.ExitStack() as st:
            hT = self.sb(st, [128, 8, self.T], BF16, "h2T")
            with contextlib.ExitStack() as st2:
                self.norm_mod(st2, self.A2, 3, hT)
                k.barrier()
            stg = [self.sb(st, [128, 512], F32, "hs") for _ in range(3)]
            stb = [self.sb(st, [128, 512], BF16, "hb") for _ in range(3)]
            cnt = [0]
            w1 = self.W["mlp_w1"][l].rearrange("(kc p) c -> p kc c", p=128)

            def epi(ch, ti, t0, n, ps, rows):
                s, b = stg[cnt[0] % 3], stb[cnt[0] % 3]
                cnt[0] += 1
                k.op("dve", lambda e: e.tensor_scalar_max(out=s[:, 0:n], in0=ps[:, 0:n], scalar1=0.0), [ps], [s])
                k.op("act", lambda e: e.activation(out=b[:, 0:n], in_=s[:, 0:n], func=AF.Square), [s], [b])
                k.dma("pool", self.HID.t[ch[0]:ch[1], t0:t0 + n], b[:, 0:n], [b], [], b)
            self.gemm_fm(st, hT, 8, lambda c0, c1: w1[:, :, c0:c1], [(c * 128, (c + 1) * 128) for c in range(32)], tiles, epi)
            k.barrier()
        for q in range(4):
            with contextlib.ExitStack() as st:
                hin = self.sb(st, [128, 8, self.T], BF16, "hin")
                k.dma("sp", hin[:, :, :], self.HID.t[q * 1024:(q + 1) * 1024, :].rearrange("(kc p) t -> p kc t", p=128), [], [hin], hin)
                w2 = self.W["mlp_w2"][l, q * 1024:(q + 1) * 1024, :].rearrange("(kc p) c -> p kc c", p=128)
                self.resid_gemm(st, hin, 8, w2, 40, tiles)
                k.barrier()

    def attend(self, st, terms, vfn, M, kblocks, qtiles, scale, ones_sum, epi, ptag=0):
        k = self.k
        pt_ = self.pt_
        for qi, (qc0, t0, n) in enumerate(qtiles):
            O = self.psb[4 + 2 * ptag]
            Sps = self.psb[5 + 2 * ptag]
            nk = len(kblocks)
            sps = {}

            def score(i):
                sp = self.psb[self.sci % 4]
                self.sci += 1
                sps[i] = sp
                kb = kblocks[i]
                for j, (K_sb, Q_sb, r0, rows) in enumerate(terms):
                    k.op("pe", lambda e: e.matmul(sp[:, 0:n], lhsT=K_sb[r0:r0 + rows, kb * 128:(kb + 1) * 128], rhs=Q_sb[r0:r0 + rows, qc0:qc0 + n],
                                                  start=(j == 0), stop=(j == len(terms) - 1)), [K_sb, Q_sb], [sp])
            score(0)
            for i in range(nk):
                if i + 1 < nk:
                    score(i + 1)
                pt = pt_[self.pti % 3]
                self.pti += 1
                sp = sps.pop(i)
                k.op("act", lambda e: e.activation(out=pt[:, 0:n], in_=sp[:, 0:n], func=AF.Exp, scale=scale), [sp], [pt])
                k.op("pe", lambda e: e.matmul(O[0:M, 0:n], lhsT=vfn(kblocks[i]), rhs=pt[:, 0:n], start=(i == 0), stop=(i == nk - 1)), [self.vbuf, pt], [O])
                if ones_sum:
                    k.op("pe", lambda e: e.matmul(Sps[:, 0:n], lhsT=self.C_("ones", True), rhs=pt[:, 0:n], start=(i == 0), stop=(i == nk - 1)), [self.cbf, pt], [Sps])
            epi(qi, t0, n, O, Sps)

    def attn_common(self, st):
        self.pt_ = [self.sb(st, [128, 512], BF16, "pt") for _ in range(3)]
        self.sci = 0
        self.pti = 0

    def qtiles(self, lat):
        if lat:
            return [(t0, t0, n) for (t0, n) in self.tiles[1:]]
        return [(0, 0, self.C)]

    def phase_diff(self, l):
        k = self.k
        W = self.W
        T, NB = self.T, self.NB
        lam_init = 0.8 - 0.6 * math.exp(-0.3 * l)
        QD = self.QD
        KD = self.KD
        with contextlib.ExitStack() as st:
            gq = self.sb(st, [128, 2], F32, "dg")
            for j, nm_ in enumerate(("diff_qn_g", "diff_kn_g")):
                for hh in range(2):
                    k.dma("sp", gq[hh * 64:(hh + 1) * 64, j:j + 1], W[nm_][l].rearrange("(p o) -> p o", o=1), [], [gq], gq, allow_slow_non_contiguous=True)
            u_ = [self.sb(st, [128, 512], F32, "du") for _ in range(2)]
            sq_ = [self.sb(st, [128, 512], F32, "dsq") for _ in range(2)]
            rs_ = [self.sb(st, [128, 512], F32, "drs") for _ in range(2)]
            cs_ = [self.sb(st, [128, 2, 512], F32, "dcs") for _ in range(2)]
            o_ = [self.sb(st, [128, 512], F32, "do") for _ in range(2)]
            ob_ = [self.sb(st, [128, 512], BF16, "dob") for _ in range(2)]
            it = 0
            for j, (col0, dst) in enumerate(((DQ, QD), (DK, KD))):
                for h in range(4):
                    r0 = self.urow(col0 + h * 128)
                    for ti, (t0, n) in enumerate(self.tiles):
                        u, sq, rs, cs, o, ob = u_[it % 2], sq_[it % 2], rs_[it % 2], cs_[it % 2], o_[it % 2], ob_[it % 2]
                        it += 1
                        k.dma("sp", u[:, 0:n], self.U.t[r0:r0 + 128, t0:t0 + n], [], [u], u)
                        k.dma("sp", cs[:, :, 0:n], self.roped_in[:, :, t0:t0 + n], [], [cs], cs)
                        k.op("act", lambda e: e.activation(out=sq[:, 0:n], in_=u[:, 0:n], func=AF.Square), [u], [sq])
                        ps = self.psum()
                        k.op("pe", lambda e: e.matmul(ps[:, 0:n], lhsT=self.C_("bd64"), rhs=sq[:, 0:n], start=True, stop=True), [self.cst, sq], [ps])
                        self.rstd_from(ps, n, rs, 1.0 / 64)
                        k.op("dve", lambda e: e.scalar_tensor_tensor(out=u[:, 0:n], in0=u[:, 0:n], scalar=gq[:, j:j + 1], in1=rs[:, 0:n], op0=ALU.mult, op1=ALU.mult), [u, gq, rs], [u])
                        ps2 = self.psum()
                        k.op("pe", lambda e: e.matmul(ps2[:, 0:n], lhsT=self.C_("perm64"), rhs=u[:, 0:n], start=True, stop=True), [self.cst, u], [ps2])
                        k.op("dve", lambda e: e.tensor_tensor(out=o[:, 0:n], in0=u[:, 0:n], in1=cs[:, 0, 0:n], op=ALU.mult), [u, cs], [o])
                        k.op("dve", lambda e: e.tensor_tensor(out=sq[:, 0:n], in0=ps2[:, 0:n], in1=cs[:, 1, 0:n], op=ALU.mult), [ps2, cs], [sq])
                        k.op("dve", lambda e: e.tensor_tensor(out=ob[:, 0:n], in0=o[:, 0:n], in1=sq[:, 0:n], op=ALU.add), [o, sq], [ob])
                        k.dma("pool", dst.t[h * 128:(h + 1) * 128, t0:t0 + n], ob[:, 0:n], [ob], [], ob)
            k.barrier()
        with contextlib.ExitStack() as st:
            self.attn_common(st)
            lamt2 = self.sb(st, [128, 256], F32, "lamt")
            k.dma("sp", lamt2[:, :], W["diff_lambda"][l:l + 1].rearrange("o a b -> o (a b)").partition_broadcast(128), [], [lamt2], lamt2)

            lv = self.sb(st, [128, 4], F32, "lv")
            lt = self.sb(st, [128, 2, 64], F32, "lt")
            for j in range(2):
                k.op("dve", lambda e: e.tensor_tensor(out=lt[:, j, :], in0=lamt2[:, (2 * j) * 64:(2 * j + 1) * 64], in1=lamt2[:, (2 * j + 1) * 64:(2 * j + 2) * 64], op=ALU.mult), [lamt2], [lt])
            k.op("dve", lambda e: e.reduce_sum(out=lv[:, 0:2], in_=lt[:, :, :], axis=mybir.AxisListType.X), [lt], [lv])
            k.op("act", lambda e: e.activation(out=lv[:, 0:2], in_=lv[:, 0:2], func=AF.Exp), [lv], [lv])
            k.op("dve", lambda e: e.scalar_tensor_tensor(out=lv[:, 2:3], in0=lv[:, 1:2], scalar=-lam_init, in1=lv[:, 0:1], op0=ALU.add, op1=ALU.subtract), [lv], [lv])
            sg = self.vec_col(st, W["diff_sub_g"][l], 128, "sg")
            k.op("dve", lambda e: e.tensor_scalar(out=sg[:, :], in0=sg[:, :], scalar1=(1.0 - lam_init), scalar2=None, op0=ALU.mult), [sg], [sg])
            qh = self.sb(st, [128, T], BF16, "dqh")
            khm = [self.sb(st, [128, T], BF16, "dkh") for _ in range(2)]
            for m_ in range(2):
                k.op("pool", lambda e: e.memset(khm[m_][:, :], 0.0), [], [khm[m_]])
            vh = self.sb(st, [128, NB, 128], BF16, "dvh")
            self.vbuf = vh
            ra = [self.sb(st, [128, 512], F32, "ra") for _ in range(2)]
            aa = [self.sb(st, [128, 512], F32, "aa") for _ in range(2)]
            dd = self.sb(st, [128, 512], F32, "dd")
            yb = [self.sb(st, [128, 512], BF16, "dyb") for _ in range(2)]
            for h in range(4):
                k.dma("sp", qh[:, :], QD.t[h * 128:(h + 1) * 128, :], [], [qh], qh)
                for m_ in range(2):
                    k.dma("sp", khm[m_][m_ * 64:(m_ + 1) * 64, :], KD.t[h * 128 + m_ * 64:h * 128 + (m_ + 1) * 64, :], [], [khm[m_]], khm[m_])
                k.dma("sp", vh[:, :, :], self.VD.t[:, h * 128:(h + 1) * 128].rearrange("(b p) d -> p b d", p=128), [], [vh], vh)
                passes = [(True, list(range(NB)))]
                if not self.last:
                    passes.append((False, list(range(self.C // 128))))
                for lat, kbl in passes:
                    for qi, qt in enumerate(self.qtiles(lat)):
                        res = {}
                        for m in range(2):
                            def epi(qi_, t0, n, O, Sps, m=m):
                                res[m] = (O, Sps)
                            self.attend(st, [(khm[m], qh, 0, 128)], lambda kb: vh[:, kb, :], 128, kbl, [qt], 64 ** -0.5, True, epi, ptag=m)
                        (qc0, t0, n) = qt
                        for m in range(2):
                            O, Sps = res[m]
                            k.op("dve", lambda e: e.reciprocal(out=ra[m][:, 0:n], in_=Sps[:, 0:n]), [Sps], [ra[m]])
                            k.op("dve", lambda e: e.tensor_tensor(out=aa[m][:, 0:n], in0=O[:, 0:n], in1=ra[m][:, 0:n], op=ALU.mult), [O, ra[m]], [aa[m]])
                        k.op("dve", lambda e: e.scalar_tensor_tensor(out=dd[:, 0:n], in0=aa[1][:, 0:n], scalar=lv[:, 2:3], in1=aa[0][:, 0:n], op0=ALU.mult, op1=ALU.add), [aa[0], aa[1], lv], [dd])
                        k.op("act", lambda e: e.activation(out=aa[0][:, 0:n], in_=dd[:, 0:n], func=AF.Square), [dd], [aa[0]])
                        ps = self.psb[self.sci % 4]
                        self.sci += 1
                        k.op("pe", lambda e: e.matmul(ps[:, 0:n], lhsT=self.C_("ones"), rhs=aa[0][:, 0:n], start=True, stop=True), [self.cst, aa[0]], [ps])
                        self.rstd_from(ps, n, ra[0], 1.0 / 128)
                        y = yb[qi % 2]
                        k.op("dve", lambda e: e.scalar_tensor_tensor(out=y[:, 0:n], in0=dd[:, 0:n], scalar=sg[:, 0:1], in1=ra[0][:, 0:n], op0=ALU.mult, op1=ALU.mult), [dd, sg, ra[0]], [y])
                        k.dma("pool", self.Y.t[1024 + h * 128:1024 + (h + 1) * 128, t0:t0 + n], y[:, 0:n], [y], [], y)
            k.barrier()

    def phase_mla(self, l):
        k = self.k
        W = self.W
        T, NB = self.T, self.NB
        with contextlib.ExitStack() as st:
            cn = self.sb(st, [128, 5, T], BF16, "cn")
            RK = self.sb(st, [32, T], F32, "RK")
            SQPE = self.sb(st, [32, T], F32, "SQPE")
            gkv = self.sb(st, [128, 5], F32, "gkv")
            k.dma("sp", gkv[:, 0:2], W["mla_kv_lora_g"][l].rearrange("(kc p) -> p kc", p=128), [], [gkv], gkv, allow_slow_non_contiguous=True)
            k.dma("sp", gkv[:, 2:5], W["mla_q_lora_g"][l].rearrange("(kc p) -> p kc", p=128), [], [gkv], gkv, allow_slow_non_contiguous=True)
            gk = self.sb(st, [128, 4], F32, "gk")
            for j, nm_ in enumerate(("mla_kn_g", "mla_qn_g")):
                k.dma("sp", gk[0:64, 2 * j:2 * j + 1], W[nm_][l, 0:64].rearrange("(p o) -> p o", o=1), [], [gk], gk, allow_slow_non_contiguous=True)
                k.dma("sp", gk[0:32, 2 * j + 1:2 * j + 2], W[nm_][l, 64:96].rearrange("(p o) -> p o", o=1), [], [gk], gk, allow_slow_non_contiguous=True)
            with contextlib.ExitStack() as st2:
                u_ = [self.sb(st2, [128, 3, 512], F32, "mu") for _ in range(2)]
                sq_ = [self.sb(st2, [128, 3, 512], F32, "msq") for _ in range(2)]
                rs_ = [self.sb(st2, [128, 512], F32, "mrs") for _ in range(2)]
                it = 0
                for (col0, kc_n, dst0, nfeat) in ((MCKV, 2, 0, 256), (MCQ, 3, 2, 384)):
                    r0 = self.urow(col0)
                    for ti, (t0, n) in enumerate(self.tiles):
                        u, sq, rs = u_[it % 2], sq_[it % 2], rs_[it % 2]
                        it += 1
                        k.dma("sp", u[:, 0:kc_n, 0:n], self.U.t[r0:r0 + kc_n * 128, t0:t0 + n].rearrange("(kc p) t -> p kc t", p=128), [], [u], u)
                        k.op("act", lambda e: e.activation(out=sq[:, 0:kc_n, 0:n], in_=u[:, 0:kc_n, 0:n], func=AF.Square), [u], [sq])
                        ps = self.psum()
                        for kc in range(kc_n):
                            k.op("pe", lambda e: e.matmul(ps[:, 0:n], lhsT=self.C_("ones"), rhs=sq[:, kc, 0:n], start=(kc == 0), stop=(kc == kc_n - 1)), [self.cst, sq], [ps])
                        self.rstd_from(ps, n, rs, 1.0 / nfeat)
                        for kc in range(kc_n):
                            k.op("dve", lambda e: e.scalar_tensor_tensor(out=cn[:, dst0 + kc, t0:t0 + n], in0=u[:, kc, 0:n], scalar=gkv[:, dst0 + kc:dst0 + kc + 1], in1=rs[:, 0:n],
                                                                         op0=ALU.mult, op1=ALU.mult), [u, gkv, rs], [cn])
                r0 = self.urow(MKPE)
                kp = self.sb(st2, [32, T], F32, "kp")
                k.dma("sp", kp[:, :], self.U.t[r0:r0 + 32, :], [], [kp], kp)
                k.op("act", lambda e: e.activation(out=SQPE[:, :], in_=kp[:, :], func=AF.Square), [kp], [SQPE])
                k.op("dve", lambda e: e.tensor_scalar(out=kp[:, :], in0=kp[:, :], scalar1=gk[0:32, 1:2], scalar2=None, op0=ALU.mult), [kp, gk], [kp])
                self.rope32(st2, kp, RK)
                k.barrier()
            with contextlib.ExitStack() as st2:
                sq_ = [self.sb(st2, [64, 512], F32, "ksq") for _ in range(2)]
                rs_ = [self.sb(st2, [128, 512], F32, "krs") for _ in range(2)]
                kn_ = [self.sb(st2, [64, 512], BF16, "kn") for _ in range(2)]
                kr_ = [self.sb(st2, [32, 512], BF16, "kr") for _ in range(2)]
                cnt = [0]
                wkv = W["mla_w_ukv"][l].rearrange("(kc p) c -> p kc c", p=128)

                def epik(ch, ti, t0, n, ps, rows):
                    h = ch[0] // 128
                    i = cnt[0] % 2
                    cnt[0] += 1
                    sq, rs, kn, kr = sq_[i], rs_[i], kn_[i], kr_[i]
                    k.op("act", lambda e: e.activation(out=sq[:, 0:n], in_=ps[0:64, 0:n], func=AF.Square), [ps], [sq])
                    p2 = self.psum()
                    k.op("pe", lambda e: e.matmul(p2[:, 0:n], lhsT=self.cst[0:64, 1, :], rhs=sq[0:64, 0:n], start=True, stop=False), [self.cst, sq], [p2])
                    k.op("pe", lambda e: e.matmul(p2[:, 0:n], lhsT=self.cst[0:32, 1, :], rhs=SQPE[0:32, t0:t0 + n], start=False, stop=True), [self.cst, SQPE], [p2])
                    self.rstd_from(p2, n, rs, 1.0 / 96)
                    k.op("dve", lambda e: e.scalar_tensor_tensor(out=kn[:, 0:n], in0=ps[0:64, 0:n], scalar=gk[0:64, 0:1], in1=rs[0:64, 0:n], op0=ALU.mult, op1=ALU.mult), [ps, gk, rs], [kn])
                    k.op("dve", lambda e: e.tensor_tensor(out=kr[:, 0:n], in0=RK[:, t0:t0 + n], in1=rs[0:32, 0:n], op=ALU.mult), [RK, rs], [kr])
                    k.dma("pool", self.KN.t[h * 64:(h + 1) * 64, t0:t0 + n], kn[:, 0:n], [kn], [], kn)
                    k.dma("pool", self.KR.t[h * 32:(h + 1) * 32, t0:t0 + n], kr[:, 0:n], [kr], [], kr)
                self.gemm_fm(st2, cn, 2, lambda c0, c1: wkv[:, :, c0:c1], [(h * 128, h * 128 + 64) for h in range(8)], self.tiles, epik)
                va_ = [self.sb(st2, [128, 8, 65], BF16, "va") for _ in range(2)]
                for v in va_:
                    k.op("dve", lambda e: e.memset(v[:, :, :], 1.0), [], [v])

                def epiv(blk, ps):
                    v = va_[blk % 2]
                    k.op("act", lambda e: e.copy(out=v[:, :, 0:64], in_=ps[:, 0:512].rearrange("p (h d) -> p h d", h=8)), [ps], [v])
                    k.dma("pool", self.VA.t[blk * 128:(blk + 1) * 128, :], v[:, :, :].rearrange("p h d -> p (h d)"), [v], [], v)
                wv5 = W["mla_w_ukv"][l].rearrange("(kc p) (h two d) -> p kc h two d", p=128, two=2, d=64)

                def wsrc(s_):
                    for kc in range(2):
                        k.dma("sp", s_[:, kc, :].rearrange("p (h d) -> p h d", h=8), wv5[:, kc, :, 1, :], [], [s_], s_)
                self.gemm_tm(st2, cn, 2, wsrc, 512, list(range(NB)), epiv)
                k.barrier()
            with contextlib.ExitStack() as st2:
                sqn_ = [self.sb(st2, [64, 512], F32, "qsq") for _ in range(2)]
                sqr_ = [self.sb(st2, [32, 512], F32, "qsr") for _ in range(2)]
                rs_ = [self.sb(st2, [128, 512], F32, "qrs") for _ in range(2)]
                qn_ = [self.sb(st2, [64, 512], BF16, "qn") for _ in range(2)]
                qr_ = [self.sb(st2, [32, 512], BF16, "qr") for _ in range(2)]
                xr_ = [self.sb(st2, [32, 512], F32, "xr") for _ in range(2)]
                ro_ = [self.sb(st2, [32, 512], F32, "ro") for _ in range(2)]
                cs_ = [self.sb(st2, [32, 2, 512], F32, "qcs") for _ in range(2)]
                cnt = [0]
                held = {}
                wq = W["mla_w_uq"][l].rearrange("(kc p) c -> p kc c", p=128)

                def epiq(ch, ti, t0, n, ps, rows):
                    if rows == 64:
                        held["n"] = ps
                        return
                    psn, psr = held["n"], ps
                    h = ch[0] // 96
                    i = cnt[0] % 2
                    cnt[0] += 1
                    sqn, sqr, rs, qn, qr, xr, ro, cs = sqn_[i], sqr_[i], rs_[i], qn_[i], qr_[i], xr_[i], ro_[i], cs_[i]
                    k.dma("sp", cs[:, :, 0:n], self.ropem_in[:, :, t0:t0 + n], [], [cs], cs)
                    k.op("act", lambda e: e.activation(out=sqn[:, 0:n], in_=psn[0:64, 0:n], func=AF.Square), [psn], [sqn])
                    k.op("act", lambda e: e.activation(out=sqr[:, 0:n], in_=psr[0:32, 0:n], func=AF.Square), [psr], [sqr])
                    p2 = self.psum()
                    k.op("pe", lambda e: e.matmul(p2[:, 0:n], lhsT=self.cst[0:64, 1, :], rhs=sqn[0:64, 0:n], start=True, stop=False), [self.cst, sqn], [p2])
                    k.op("pe", lambda e: e.matmul(p2[:, 0:n], lhsT=self.cst[0:32, 1, :], rhs=sqr[0:32, 0:n], start=False, stop=True), [self.cst, sqr], [p2])
                    self.rstd_from(p2, n, rs, 1.0 / 96)
                    k.op("dve", lambda e: e.scalar_tensor_tensor(out=qn[:, 0:n], in0=psn[0:64, 0:n], scalar=gk[0:64, 2:3], in1=rs[0:64, 0:n], op0=ALU.mult, op1=ALU.mult), [psn, gk, rs], [qn])
                    k.op("dve", lambda e: e.tensor_scalar(out=xr[:, 0:n], in0=psr[0:32, 0:n], scalar1=gk[0:32, 3:4], scalar2=None, op0=ALU.mult), [psr, gk], [xr])
                    p3 = self.psum()
                    k.op("pe", lambda e: e.matmul(p3[0:32, 0:n], lhsT=self.cst[0:32, CN.index("perm32"), 0:32], rhs=xr[0:32, 0:n], start=True, stop=True), [self.cst, xr], [p3])
                    k.op("dve", lambda e: e.tensor_tensor(out=ro[:, 0:n], in0=p3[0:32, 0:n], in1=cs[:, 1, 0:n], op=ALU.mult), [p3, cs], [ro])
                    k.op("dve", lambda e: e.tensor_tensor(out=xr[:, 0:n], in0=xr[:, 0:n], in1=cs[:, 0, 0:n], op=ALU.mult), [xr, cs], [xr])
                    k.op("dve", lambda e: e.tensor_tensor(out=xr[:, 0:n], in0=xr[:, 0:n], in1=ro[:, 0:n], op=ALU.add), [xr, ro], [xr])
                    k.op("dve", lambda e: e.tensor_tensor(out=qr[:, 0:n], in0=xr[:, 0:n], in1=rs[0:32, 0:n], op=ALU.mult), [xr, rs], [qr])
                    k.dma("pool", self.QN.t[h * 64:(h + 1) * 64, t0:t0 + n], qn[:, 0:n], [qn], [], qn)
                    k.dma("pool", self.QR.t[h * 32:(h + 1) * 32, t0:t0 + n], qr[:, 0:n], [qr], [], qr)
                chq = []
                for h in range(8):
                    chq += [(h * 96, h * 96 + 64), (h * 96 + 64, h * 96 + 96)]
                self.gemm_fm(st2, Buf(cn.t[:, 2:5, :]), 3, lambda c0, c1: wq[:, :, c0:c1], chq, self.tiles, epiq)
                k.barrier()
        with contextlib.ExitStack() as st:
            self.attn_common(st)
            kqh = self.sb(st, [96, T], BF16, "kqh")
            qqh = self.sb(st, [96, T], BF16, "qqh")
            vah = self.sb(st, [128, NB, 65], BF16, "vah")
            self.vbuf = vah
            osb = [self.sb(st, [65, 512], F32, "osb") for _ in range(2)]
            rr = [self.sb(st, [64, 512], F32, "rr") for _ in range(2)]
            yb = [self.sb(st, [64, 512], BF16, "myb") for _ in range(2)]
            cnt = [0]
            for h in range(8):
                k.dma("sp", kqh[0:64, :], self.KN.t[h * 64:(h + 1) * 64, :], [], [kqh], kqh)
                k.dma("sp", kqh[64:96, :], self.KR.t[h * 32:(h + 1) * 32, :], [], [kqh], kqh)
                k.dma("sp", qqh[0:64, :], self.QN.t[h * 64:(h + 1) * 64, :], [], [qqh], qqh)
                k.dma("sp", qqh[64:96, :], self.QR.t[h * 32:(h + 1) * 32, :], [], [qqh], qqh)
                k.dma("sp", vah[:, :, :], self.VA.t[:, h * 65:(h + 1) * 65].rearrange("(b p) d -> p b d", p=128), [], [vah], vah)

                def epi(qi, t0, n, O, Sps):
                    i = cnt[0] % 2
                    cnt[0] += 1
                    o, r, y = osb[i], rr[i], yb[i]
                    k.op("act", lambda e: e.copy(out=o[0:65, 0:n], in_=O[0:65, 0:n]), [O], [o])
                    ps = self.psb[self.sci % 4]
                    self.sci += 1
                    k.op("pe", lambda e: e.matmul(ps[0:64, 0:n], lhsT=self.cst[0:65, CN.index("sel65"), 0:64], rhs=o[0:65, 0:n], start=True, stop=True), [self.cst, o], [ps])
                    k.op("dve", lambda e: e.reciprocal(out=r[:, 0:n], in_=ps[0:64, 0:n]), [ps], [r])
                    k.op("dve", lambda e: e.tensor_tensor(out=y[:, 0:n], in0=o[0:64, 0:n], in1=r[:, 0:n], op=ALU.mult), [o, r], [y])
                    k.dma("pool", self.Y.t[512 + h * 64:512 + (h + 1) * 64, t0:t0 + n], y[:, 0:n], [y], [], y)
                terms = [(kqh, qqh, 0, 96)]
                self.attend(st, terms, lambda kb: vah[:, kb, :], 65, list(range(NB)), self.qtiles(True), 96 ** -0.5, False, epi)
                if not self.last:
                    self.attend(st, terms, lambda kb: vah[:, kb, :], 65, list(range(self.C // 128)), self.qtiles(False), 96 ** -0.5, False, epi)
            k.barrier()

    def rope32(self, st, xin, dst):
        k = self.k
        cs_ = [self.sb(st, [32, 2, 512], F32, "rcs") for _ in range(2)]
        ro_ = [self.sb(st, [32, 512], F32, "rro") for _ in range(2)]
        for ti, (t0, n) in enumerate(self.tiles):
            cs, ro = cs_[ti % 2], ro_[ti % 2]
            k.dma("sp", cs[:, :, 0:n], self.ropem_in[:, :, t0:t0 + n], [], [cs], cs)
            ps = self.psum()
            k.op("pe", lambda e: e.matmul(ps[0:32, 0:n], lhsT=self.cst[0:32, CN.index("perm32"), 0:32], rhs=xin[0:32, t0:t0 + n], start=True, stop=True), [self.cst, xin], [ps])
            k.op("dve", lambda e: e.tensor_tensor(out=ro[:, 0:n], in0=ps[0:32, 0:n], in1=cs[:, 1, 0:n], op=ALU.mult), [ps, cs], [ro])
            k.op("dve", lambda e: e.tensor_tensor(out=dst[:, t0:t0 + n], in0=xin[0:32, t0:t0 + n], in1=cs[:, 0, 0:n], op=ALU.mult), [xin, cs], [dst])
            k.op("dve", lambda e: e.tensor_tensor(out=dst[:, t0:t0 + n], in0=dst[:, t0:t0 + n], in1=ro[:, 0:n], op=ALU.add), [dst, ro], [dst])

    def conv_silu(self, st, u, acc, wc, bias, out):
        k = self.k
        C, T = self.C, self.T
        k.op("dve", lambda e: e.tensor_scalar(out=acc[:, :], in0=u[:, :], scalar1=wc[:, 2:3], scalar2=None, op0=ALU.mult), [u, wc], [acc])
        for j in (0, 1, 3, 4):
            s = j - 2
            for (s0, s1) in ((0, C), (C, T)):
                a = max(s0, s0 - s)
                b = min(s1, s1 - s)
                k.op("dve", lambda e: e.scalar_tensor_tensor(out=acc[:, a:b], in0=u[:, a + s:b + s], scalar=wc[:, j:j + 1], in1=acc[:, a:b], op0=ALU.mult, op1=ALU.add), [u, wc, acc], [acc])
        if bias is None:
            k.op("act", lambda e: e.activation(out=out, in_=acc[:, :], func=AF.Silu), [acc], [acc])
        else:
            k.op("act", lambda e: e.activation(out=out, in_=acc[:, :], func=AF.Silu, bias=bias, scale=1.0), [acc, wc], [acc])

    def softplus(self, st, xb, xap, n):
        k = self.k
        t = self.sb(st, [128, n], F32, "spt")

        class _X:
            def __getitem__(s_, key):
                return xap
        x = _X()
        k.op("act", lambda e: e.activation(out=t[:, :], in_=xap, func=AF.Abs), [xb], [t])
        k.op("act", lambda e: e.activation(out=t[:, :], in_=t[:, :], func=AF.Exp, scale=-1.0), [t], [t])
        k.op("act", lambda e: e.activation(out=t[:, :], in_=t[:, :], func=AF.Ln, bias=self.epsb[:, 1:2], scale=1.0), [t, self.epsb], [t])
        k.op("dve", lambda e: e.scalar_tensor_tensor(out=xap, in0=xap, scalar=0.0, in1=t[:, :], op0=ALU.max, op1=ALU.add), [xb, t], [xb])

    def tok_scalars(self, st, col0, ncols, dst):
        k = self.k
        r0 = self.urow(col0)
        raw = self.sb(st, [ncols, self.T], F32, "tsr")
        k.dma("sp", raw[:, :], self.U.t[r0:r0 + ncols, :], [], [raw], raw)
        for blk in range(self.NB):
            ps = self.psum()
            k.op("pe", lambda e: e.transpose(out=ps[:, 0:ncols], in_=raw[0:ncols, blk * 128:(blk + 1) * 128], identity=self.cst[0:ncols, 0, 0:ncols]), [raw, self.cst], [ps])
            k.op("act", lambda e: e.copy(out=dst[:, blk, :], in_=ps[:, 0:ncols]), [ps], [dst])

    def blk_order(self, d):
        nc_ = self.C // 128
        if d == 0:
            return list(range(self.NB))
        return list(range(nc_ - 1, -1, -1)) + list(range(self.NB - 1, nc_ - 1, -1))

    def phase_ssm(self, l):
        k = self.k
        W = self.W
        T, NB = self.T, self.NB
        with contextlib.ExitStack() as st:
            XTOK = self.sb(st, [128, NB, 512], F32, "XTOK")
            BT = self.sb(st, [128, 2, T], BF16, "BT")
            CT = self.sb(st, [128, 2, T], BF16, "CT")
            BTOK = self.sb(st, [128, NB, 2, 128], BF16, "BTOK")
            DT = self.sb(st, [128, NB, 16], F32, "DT")
            DA = self.sb(st, [128, NB, 16], F32, "DA")
            ACUM = self.sb(st, [128, NB, 16], F32, "ACUM")
            ATOT = self.sb(st, [128, NB, 16], F32, "ATOT")
            CD = self.sb(st, [128, NB, 16], F32, "CD")
            DTDS = self.sb(st, [128, NB, 16], F32, "DTDS")
            with contextlib.ExitStack() as st2:
                wc_ = [self.sb(st2, [128, 6], F32, "swc") for _ in range(2)]
                st2a = contextlib.ExitStack()
                u_ = [self.sb(st2a, [128, T], F32, "su") for _ in range(1)] * 2
                acc_ = [self.sb(st2a, [128, T], F32, "sacc") for _ in range(1)] * 2
                for c in range(8):
                    u, acc, wc = u_[c % 2], acc_[c % 2], wc_[c % 2]
                    r0 = self.urow(SX + c * 128)
                    k.dma("sp", u[:, :], self.U.t[r0:r0 + 128, :], [], [u], u)
                    k.dma("sp", wc[:, 0:5], W["ssm_conv"][l][:, c * 128:(c + 1) * 128].rearrange("j c -> c j"), [], [wc], wc, allow_slow_non_contiguous=True)
                    k.dma("sp", wc[:, 5:6], W["ssm_conv_b"][l, c * 128:(c + 1) * 128].rearrange("(p o) -> p o", o=1), [], [wc], wc, allow_slow_non_contiguous=True)
                    if c < 4:
                        self.conv_silu(st2, u, acc, wc, wc[:, 5:6], acc[:, :])
                        k.dma("pool", self.XS.t[c * 128:(c + 1) * 128, :], acc[:, :], [acc], [], acc)
                        for blk in range(NB):
                            ps = self.psum()
                            k.op("pe", lambda e: e.transpose(out=ps[:, 0:128], in_=acc[:, blk * 128:(blk + 1) * 128], identity=self.C_("ident")), [acc, self.cst], [ps])
                            k.op("act", lambda e: e.copy(out=XTOK[:, blk, c * 128:(c + 1) * 128], in_=ps[:, 0:128]), [ps], [XTOK])
                    elif c < 6:
                        g = c - 4
                        self.conv_silu(st2, u, acc, wc, wc[:, 5:6], acc[:, :])
                        k.op("dve", lambda e: e.tensor_copy(out=BT[:, g, :], in_=acc[:, :]), [acc], [BT])
                        for blk in range(NB):
                            ps = self.psum()
                            k.op("pe", lambda e: e.transpose(out=ps[:, 0:128], in_=acc[:, blk * 128:(blk + 1) * 128], identity=self.C_("ident")), [acc, self.cst], [ps])
                            k.op("act", lambda e: e.copy(out=BTOK[:, blk, g, :], in_=ps[:, 0:128]), [ps], [BTOK])
                    else:
                        g = c - 6
                        self.conv_silu(st2, u, acc, wc, wc[:, 5:6], acc[:, :])
                        k.op("dve", lambda e: e.tensor_copy(out=CT[:, g, :], in_=acc[:, :]), [acc], [CT])
                k.barrier()
                st2a.close()
                self.tok_scalars(st2, SDT, 16, DT)
                pb = self.sb(st2, [128, 2, 16], F32, "spb")
                k.dma("sp", pb[:, 0, :], W["ssm_dt_bias"][l:l + 1].rearrange("o a b -> o (a b)").partition_broadcast(128), [], [pb], pb)
                k.dma("sp", pb[:, 1, :], W["ssm_a_log"][l:l + 1].rearrange("o a b -> o (a b)").partition_broadcast(128), [], [pb], pb)
                k.op("dve", lambda e: e.tensor_tensor(out=DT[:, :, :], in0=DT[:, :, :], in1=pb[:, 0, :].unsqueeze(1).to_broadcast([128, NB, 16]), op=ALU.add), [DT, pb], [DT])
                self.softplus(st2, DT, DT.t[:, :, :].rearrange("p b c -> p (b c)"), NB * 16)
                k.op("act", lambda e: e.activation(out=pb[:, 1, :], in_=pb[:, 1, :], func=AF.Exp), [pb], [pb])
                k.op("dve", lambda e: e.scalar_tensor_tensor(out=DA[:, :, :], in0=DT[:, :, :], scalar=-1.0, in1=pb[:, 1, :].unsqueeze(1).to_broadcast([128, NB, 16]), op0=ALU.mult, op1=ALU.mult), [DT, pb], [DA])
                self.cum_stats(st2, DA, ACUM, ATOT, 8)
                k.op("act", lambda e: e.activation(out=CD[:, :, :], in_=ATOT[:, :, :], func=AF.Exp), [ATOT], [CD])
                k.op("dve", lambda e: e.tensor_tensor(out=DTDS[:, :, :], in0=ATOT[:, :, :], in1=ACUM[:, :, :], op=ALU.subtract), [ATOT, ACUM], [DTDS])
                k.op("act", lambda e: e.activation(out=DTDS[:, :, :], in_=DTDS[:, :, :], func=AF.Exp), [DTDS], [DTDS])
                k.op("dve", lambda e: e.tensor_tensor(out=DTDS[:, :, :], in0=DTDS[:, :, :], in1=DT[:, :, :], op=ALU.mult), [DTDS, DT], [DTDS])
                k.barrier()
            ST = self.sb(st, [128, 4, 4, 64], F32, "ST")
            STb = self.sb(st, [128, 4, 4, 64], BF16, "STb")
            k.op("dve", lambda e: e.memset(ST[:, :, :, :], 0.0), [], [ST])
            k.op("dve", lambda e: e.memset(STb[:, :, :, :], 0.0), [], [STb])
            R = 2
            rhsb = [self.sb(st, [128, 4, 128], F32, "srhs") for _ in range(R)]
            Dm = [self.sb(st, [128, 4, 128], F32, "sD") for _ in range(R)]
            LT = [self.sb(st, [128, 4, 128], F32, "sLT") for _ in range(R)]
            RE = [self.sb(st, [128, 4, 128], F32, "sRE") for _ in range(R)]
            WT = [self.sb(st, [128, 4, 128], BF16, "sWT") for _ in range(R)]
            CdT = [self.sb(st, [128, 4, 128], BF16, "sCd") for _ in range(R)]
            xdt = [self.sb(st, [128, 4, 64], BF16, "sxdt") for _ in range(R)]
            xdd = [self.sb(st, [128, 4, 64], BF16, "sxdd") for _ in range(R)]
            SCs = [self.sb(st, [128, 128], F32, "sSC") for _ in range(R)]
            yo = [self.sb(st, [64, 4, 128], F32, "syo") for _ in range(R)]
            STs = {(d, g): Buf(None) for d in range(2) for g in range(2)}
            it = 0
            orders = [self.blk_order(0), self.blk_order(1)]
            for i in range(NB):
                for d in range(2):
                    blk = orders[d][i]
                    tri = self.C_("triF" if d == 0 else "triB")
                    mneg = self.C_("mnegF" if d == 0 else "mnegB")
                    for g in range(2):
                        j = it % R
                        it += 1
                        ch = d * 2 + g
                        sbuf_ = STs[(d, g)]
                        hs = slice(d * 8 + g * 4, d * 8 + g * 4 + 4)
                        k.op("dve", lambda e: e.tensor_tensor(out=rhsb[j][:, :, :], in0=tri.unsqueeze(1).to_broadcast([128, 4, 128]),
                                                              in1=DA[:, blk, hs].unsqueeze(2).to_broadcast([128, 4, 128]), op=ALU.mult), [self.cst, DA], [rhsb[j]])
                        pa = self.psum()
                        k.op("pe", lambda e: e.matmul(pa[:, :], lhsT=self.C_("ones"), rhs=rhsb[j][:, :, :].rearrange("p h c -> p (h c)"), start=True, stop=True), [self.cst, rhsb[j]], [pa])
                        for h in range(4):
                            k.op("dve", lambda e: e.scalar_tensor_tensor(out=Dm[j][:, h, :], in0=pa[:, h * 128:(h + 1) * 128], scalar=ACUM[:, blk, d * 8 + g * 4 + h:d * 8 + g * 4 + h + 1],
                                                                         in1=mneg, op0=ALU.subtract, op1=ALU.add), [pa, ACUM, self.cst], [Dm[j]])
                        k.op("act", lambda e: e.activation(out=LT[j][:, :, :], in_=Dm[j][:, :, :], func=AF.Exp), [Dm[j]], [LT[j]])
                        k.op("act", lambda e: e.activation(out=RE[j][:, :, :].rearrange("p h c -> p (h c)"), in_=pa[:, :], func=AF.Exp), [pa], [RE[j]])
                        psc = self.psum()
                        k.op("pe", lambda e: e.matmul(psc[:, 0:128], lhsT=BT[:, g, blk * 128:(blk + 1) * 128], rhs=CT[:, g, blk * 128:(blk + 1) * 128], start=True, stop=True), [BT, CT], [psc])
                        k.op("act", lambda e: e.copy(out=SCs[j][:, :], in_=psc[:, 0:128]), [psc], [SCs[j]])
                        k.op("dve", lambda e: e.tensor_tensor(out=WT[j][:, :, :], in0=LT[j][:, :, :], in1=SCs[j][:, :].unsqueeze(1).to_broadcast([128, 4, 128]), op=ALU.mult), [LT[j], SCs[j]], [WT[j]])
                        k.op("dve", lambda e: e.tensor_tensor(out=CdT[j][:, :, :], in0=RE[j][:, :, :], in1=CT[:, g, blk * 128:(blk + 1) * 128].unsqueeze(1).to_broadcast([128, 4, 128]), op=ALU.mult), [RE[j], CT], [CdT[j]])
                        xv = XTOK[:, blk, g * 256:(g + 1) * 256].rearrange("p (h q) -> p h q", h=4)
                        k.op("dve", lambda e: e.tensor_tensor(out=xdt[j][:, :, :], in0=xv, in1=DT[:, blk, hs].unsqueeze(2).to_broadcast([128, 4, 64]), op=ALU.mult), [XTOK, DT], [xdt[j]])
                        k.op("dve", lambda e: e.tensor_tensor(out=xdd[j][:, :, :], in0=xv, in1=DTDS[:, blk, hs].unsqueeze(2).to_broadcast([128, 4, 64]), op=ALU.mult), [XTOK, DTDS], [xdd[j]])
                        py = self.psum()
                        for h in range(4):
                            k.op("pe", lambda e: e.matmul(py[0:64, h * 128:(h + 1) * 128], lhsT=xdt[j][:, h, :], rhs=WT[j][:, h, :], start=True, stop=False), [xdt[j], WT[j]], [py])
                            k.op("pe", lambda e: e.matmul(py[0:64, h * 128:(h + 1) * 128], lhsT=STb[:, ch, h, :], rhs=CdT[j][:, h, :], start=False, stop=True), [sbuf_, CdT[j]], [py])
                        k.op("act", lambda e: e.copy(out=yo[j][:, :, :].rearrange("p h c -> p (h c)"), in_=py[0:64, :]), [py], [yo[j]])
                        k.dma("pool", self.YS.t[d, g * 256:(g + 1) * 256, blk * 128:(blk + 1) * 128].rearrange("(h p) c -> p h c", p=64), yo[j][:, :, :], [yo[j]], [], yo[j])
                        pst = self.psum()
                        for h in range(4):
                            k.op("pe", lambda e: e.matmul(pst[:, h * 64:(h + 1) * 64], lhsT=BTOK[:, blk, g, :], rhs=xdd[j][:, h, :], start=True, stop=True), [BTOK, xdd[j]], [pst])
                        k.op("dve", lambda e: e.tensor_tensor(out=ST[:, ch, :, :], in0=ST[:, ch, :, :], in1=CD[:, blk, hs].unsqueeze(2).to_broadcast([128, 4, 64]), op=ALU.mult), [sbuf_, CD], [sbuf_])
                        k.op("dve", lambda e: e.tensor_tensor(out=ST[:, ch, :, :], in0=ST[:, ch, :, :], in1=pst[:, 0:256].rearrange("p (h q) -> p h q", h=4), op=ALU.add), [sbuf_, pst], [sbuf_])
                        k.op("act", lambda e: e.copy(out=STb[:, ch, :, :], in_=ST[:, ch, :, :]), [sbuf_], [sbuf_])
            k.barrier()
        with contextlib.ExitStack() as st:
            dsk = self.sb(st, [128, 4], F32, "dsk")
            gn = self.sb(st, [128, 4], F32, "sgn")
            for c in range(4):
                for hh in range(2):
                    k.dma("sp", dsk[hh * 64:(hh + 1) * 64, c:c + 1], W["ssm_d"][l:l + 1, 2 * c + hh:2 * c + hh + 1].partition_broadcast(64), [], [dsk], dsk)
            k.dma("sp", gn[:, :], W["ssm_norm_g"][l].rearrange("(c p) -> p c", p=128), [], [gn], gn, allow_slow_non_contiguous=True)
            ya = [self.sb(st, [128, 2, 512], F32, "fya") for _ in range(2)]
            yb_ = [self.sb(st, [128, 2, 512], F32, "fyb") for _ in range(2)]
            xs_ = [self.sb(st, [128, 2, 512], F32, "fxs") for _ in range(2)]
            z_ = [self.sb(st, [128, 2, 512], F32, "fz") for _ in range(2)]
            sq_ = [self.sb(st, [128, 2, 512], F32, "fsq") for _ in range(2)]
            rs_ = [self.sb(st, [128, 512], F32, "frs") for _ in range(2)]
            ob_ = [self.sb(st, [128, 2, 512], BF16, "fob") for _ in range(2)]
            it = 0
            rz = self.urow(SZ)
            tiles = self.tiles[1:] if self.last else self.tiles
            for gi in range(2):
                for ti, (t0, n) in enumerate(tiles):
                    j = it % 2
                    it += 1
                    r0 = gi * 256
                    v = lambda ap: ap.rearrange("(c p) t -> p c t", p=128)
                    k.dma("sp", ya[j][:, :, 0:n], v(self.YS.t[0, r0:r0 + 256, t0:t0 + n]), [], [ya[j]], ya[j])
                    k.dma("sp", yb_[j][:, :, 0:n], v(self.YS.t[1, r0:r0 + 256, t0:t0 + n]), [], [yb_[j]], yb_[j])
                    k.dma("sp", xs_[j][:, :, 0:n], v(self.XS.t[r0:r0 + 256, t0:t0 + n]), [], [xs_[j]], xs_[j])
                    k.dma("sp", z_[j][:, :, 0:n], v(self.U.t[rz + r0:rz + r0 + 256, t0:t0 + n]), [], [z_[j]], z_[j])
                    k.op("dve", lambda e: e.tensor_tensor(out=ya[j][:, :, 0:n], in0=ya[j][:, :, 0:n], in1=yb_[j][:, :, 0:n], op=ALU.add), [ya[j], yb_[j]], [ya[j]])
                    k.op("act", lambda e: e.activation(out=z_[j][:, :, 0:n], in_=z_[j][:, :, 0:n], func=AF.Silu), [z_[j]], [z_[j]])
                    for cc in range(2):
                        c = gi * 2 + cc
                        k.op("dve", lambda e: e.scalar_tensor_tensor(out=ya[j][:, cc, 0:n], in0=xs_[j][:, cc, 0:n], scalar=dsk[:, c:c + 1], in1=ya[j][:, cc, 0:n], op0=ALU.mult, op1=ALU.add), [xs_[j], dsk, ya[j]], [ya[j]])
                    k.op("dve", lambda e: e.tensor_tensor(out=ya[j][:, :, 0:n], in0=ya[j][:, :, 0:n], in1=z_[j][:, :, 0:n], op=ALU.mult), [ya[j], z_[j]], [ya[j]])
                    k.op("act", lambda e: e.activation(out=sq_[j][:, :, 0:n], in_=ya[j][:, :, 0:n], func=AF.Square), [ya[j]], [sq_[j]])
                    ps = self.psum()
                    for cc in range(2):
                        k.op("pe", lambda e: e.matmul(ps[:, 0:n], lhsT=self.C_("ones"), rhs=sq_[j][:, cc, 0:n], start=(cc == 0), stop=(cc == 1)), [self.cst, sq_[j]], [ps])
                    self.rstd_from(ps, n, rs_[j], 1.0 / 256)
                    for cc in range(2):
                        c = gi * 2 + cc
                        k.op("dve", lambda e: e.scalar_tensor_tensor(out=ob_[j][:, cc, 0:n], in0=ya[j][:, cc, 0:n], scalar=gn[:, c:c + 1], in1=rs_[j][:, 0:n], op0=ALU.mult, op1=ALU.mult), [ya[j], gn, rs_[j]], [ob_[j]])
                    k.dma("pool", v(self.Y.t[1536 + r0:1536 + r0 + 256, t0:t0 + n]), ob_[j][:, :, 0:n], [ob_[j]], [], ob_[j])
            k.barrier()

    def cum_stats(self, st, G, GCUM, GTOT, nh):
        k = self.k
        NB = self.NB
        for d in range(2):
            tri = self.C_("triF" if d == 0 else "triB")
            ps = self.psum()
            k.op("pe", lambda e: e.matmul(ps[:, 0:NB * nh].rearrange("p (b h) -> p b h", h=nh), lhsT=tri, rhs=G[:, :, d * nh:(d + 1) * nh], start=True, stop=True), [self.cst, G], [ps])
            k.op("act", lambda e: e.copy(out=GCUM[:, :, d * nh:(d + 1) * nh], in_=ps[:, 0:NB * nh].rearrange("p (b h) -> p b h", h=nh)), [ps], [GCUM])
            ps2 = self.psum()
            k.op("pe", lambda e: e.matmul(ps2[:, 0:NB * nh].rearrange("p (b h) -> p b h", h=nh), lhsT=self.C_("ones"), rhs=G[:, :, d * nh:(d + 1) * nh], start=True, stop=True), [self.cst, G], [ps2])
            k.op("dve", lambda e: e.tensor_copy(out=GTOT[:, :, d * nh:(d + 1) * nh], in_=ps2[:, 0:NB * nh].rearrange("p (b h) -> p b h", h=nh)), [ps2], [GTOT])

    def phase_gdn(self, l):
        k = self.k
        W = self.W
        T, NB = self.T, self.NB
        with contextlib.ExitStack() as st:
            QT = self.sb(st, [128, 4, T], BF16, "gQT")
            KT = self.sb(st, [128, 4, T], BF16, "gKT")
            KTOK = self.sb(st, [128, NB, 4, 128], BF16, "gKTOK")
            VTOK = self.sb(st, [128, NB, 4, 128], BF16, "gVTOK")
            G = self.sb(st, [128, NB, 8], F32, "gG")
            GCUM = self.sb(st, [128, NB, 8], F32, "gGCUM")
            GTOT = self.sb(st, [128, NB, 8], F32, "gGTOT")
            EG = self.sb(st, [128, NB, 8], F32, "gEG")
            GL = self.sb(st, [128, NB, 8], F32, "gGL")
            KDS = self.sb(st, [128, NB, 8], F32, "gKDS")
            BETA = self.sb(st, [128, NB, 8], F32, "gBETA")
            NBETA = self.sb(st, [128, NB, 8], F32, "gNBETA")
            with contextlib.ExitStack() as st2:
                wc_ = [self.sb(st2, [128, 6], F32, "gwc") for _ in range(2)]
                sq_ = [self.sb(st2, [128, 512], F32, "gsq") for _ in range(2)]
                rs_ = [self.sb(st2, [128, 512], F32, "grs") for _ in range(2)]
                st2a = contextlib.ExitStack()
                u_ = [self.sb(st2a, [128, T], F32, "gu") for _ in range(1)] * 2
                acc_ = [self.sb(st2a, [128, T], F32, "gacc") for _ in range(1)] * 2
                it = 0
                for c in range(12):
                    u, acc, wc = u_[c % 2], acc_[c % 2], wc_[c % 2]
                    r0 = self.urow(c * 128)
                    kind, h = c // 4, c % 4
                    k.dma("sp", u[:, :], self.U.t[r0:r0 + 128, :], [], [u], u)
                    k.dma("sp", wc[:, 0:5], W["gdn_conv"][l][:, c * 128:(c + 1) * 128].rearrange("j c -> c j"), [], [wc], wc, allow_slow_non_contiguous=True)
                    self.conv_silu(st2, u, acc, wc, None, acc[:, :])
                    if kind < 2:
                        dstT = QT if kind == 0 else KT
                        for ti, (t0, n) in enumerate(self.tiles):
                            sq, rs = sq_[it % 2], rs_[it % 2]
                            it += 1
                            k.op("act", lambda e: e.activation(out=sq[:, 0:n], in_=acc[:, t0:t0 + n], func=AF.Square), [acc], [sq])
                            ps = self.psum()
                            k.op("pe", lambda e: e.matmul(ps[:, 0:n], lhsT=self.C_("ones"), rhs=sq[:, 0:n], start=True, stop=True), [self.cst, sq], [ps])
                            self.rstd_from(ps, n, rs, 1.0)
                            if kind == 0:
                                k.op("dve", lambda e: e.scalar_tensor_tensor(out=dstT[:, h, t0:t0 + n], in0=acc[:, t0:t0 + n], scalar=128.0 ** -0.5, in1=rs[:, 0:n], op0=ALU.mult, op1=ALU.mult), [acc, rs], [dstT])
                            else:
                                k.op("dve", lambda e: e.tensor_tensor(out=acc[:, t0:t0 + n], in0=acc[:, t0:t0 + n], in1=rs[:, 0:n], op=ALU.mult), [acc, rs], [acc])
                                k.op("act", lambda e: e.copy(out=dstT[:, h, t0:t0 + n], in_=acc[:, t0:t0 + n]), [acc], [dstT])
                    if kind >= 1:
                        dst = KTOK if kind == 1 else VTOK
                        for blk in range(NB):
                            ps = self.psum()
                            k.op("pe", lambda e: e.transpose(out=ps[:, 0:128], in_=acc[:, blk * 128:(blk + 1) * 128], identity=self.C_("ident")), [acc, self.cst], [ps])
                            k.op("act", lambda e: e.copy(out=dst[:, blk, h, :], in_=ps[:, 0:128]), [ps], [dst])
                k.barrier()
                st2a.close()
                AB = self.sb(st2, [128, NB, 16], F32, "gAB")
                self.tok_scalars(st2, GA, 16, AB)
                pb = self.sb(st2, [128, 2, 8], F32, "gpb")
                k.dma("sp", pb[:, 0, :], W["gdn_dt_bias"][l:l + 1].rearrange("o a b -> o (a b)").partition_broadcast(128), [], [pb], pb)
                k.dma("sp", pb[:, 1, :], W["gdn_a_log"][l:l + 1].rearrange("o a b -> o (a b)").partition_broadcast(128), [], [pb], pb)
                k.op("dve", lambda e: e.tensor_tensor(out=G[:, :, :], in0=AB[:, :, 0:8], in1=pb[:, 0, :].unsqueeze(1).to_broadcast([128, NB, 8]), op=ALU.add), [AB, pb], [G])
                self.softplus(st2, G, G.t[:, :, :].rearrange("p b c -> p (b c)"), NB * 8)
                k.op("act", lambda e: e.activation(out=pb[:, 1, :], in_=pb[:, 1, :], func=AF.Exp), [pb], [pb])
                k.op("dve", lambda e: e.scalar_tensor_tensor(out=G[:, :, :], in0=G[:, :, :], scalar=-1.0, in1=pb[:, 1, :].unsqueeze(1).to_broadcast([128, NB, 8]), op0=ALU.mult, op1=ALU.mult), [G, pb], [G])
                k.op("act", lambda e: e.activation(out=BETA[:, :, :], in_=AB[:, :, 8:16], func=AF.Sigmoid), [AB], [BETA])
                k.op("dve", lambda e: e.tensor_scalar(out=NBETA[:, :, :], in0=BETA[:, :, :], scalar1=-1.0, scalar2=None, op0=ALU.mult), [BETA], [NBETA])
                self.cum_stats(st2, G, GCUM, GTOT, 4)
                k.op("act", lambda e: e.activation(out=EG[:, :, :], in_=GCUM[:, :, :], func=AF.Exp), [GCUM], [EG])
                k.op("act", lambda e: e.activation(out=GL[:, :, :], in_=GTOT[:, :, :], func=AF.Exp), [GTOT], [GL])
                k.op("dve", lambda e: e.tensor_tensor(out=KDS[:, :, :], in0=GTOT[:, :, :], in1=GCUM[:, :, :], op=ALU.subtract), [GTOT, GCUM], [KDS])
                k.op("act", lambda e: e.activation(out=KDS[:, :, :], in_=KDS[:, :, :], func=AF.Exp), [KDS], [KDS])
                k.barrier()
            S_ = self.sb(st, [128, 8, 128], F32, "gS")
            Sb = self.sb(st, [128, 8, 128], BF16, "gSb")
            k.op("dve", lambda e: e.memset(S_[:, :, :], 0.0), [], [S_])
            k.op("dve", lambda e: e.memset(Sb[:, :, :], 0.0), [], [Sb])
            R = 3
            mk = lambda p, dt=F32: [self.sb(st, [128, 128], dt, p) for _ in range(R)]
            rhsb, Dm, E, RE, t1 = mk("grh"), mk("gDm"), mk("gE"), mk("gRE"), mk("gt1")
            AttnT, QgT, Kd, Xb, Rp, vnew = mk("gAt", BF16), mk("gQg", BF16), mk("gKd", BF16), mk("gXb", BF16), mk("gRp", BF16), mk("gvn", BF16)
            Pk = [mk("gP%d" % i) for i in range(1)]
            PTk = [mk("gPT%d" % i) for i in range(1)]
            XTb, CTb, Cb, Zb, Z2b = mk("gXTb", BF16), mk("gCTb", BF16), mk("gCb", BF16), mk("gZb", BF16), mk("gZ2b", BF16)
            GM = self.sb(st, [128, 14, 128], F32, "gGM")
            k.dma("sp", GM[:, :, :], self.gmask_in, [], [GM], GM)
            X = mk("gX")
            oo = mk("goo")
            Sbufs = {(h, d): Buf(None) for h in range(4) for d in range(2)}
            ident = self.C_("ident")
            orders = [self.blk_order(0), self.blk_order(1)]
            it = 0
            evi = [0]

            def evac(out_ap, ps_ap, rd, wr):
                evi[0] += 1
                if evi[0] % 2:
                    k.op("act", lambda e: e.copy(out=out_ap, in_=ps_ap), rd, wr)
                else:
                    k.op("dve", lambda e: e.tensor_copy(out=out_ap, in_=ps_ap), rd, wr)
            ppi = [0]

            def pre_ps():
                ppi[0] = (ppi[0] + 1) % 6
                return self.psb[ppi[0]]

            def make_unit(i, h, d, j):
                blk = orders[d][i]
                ci = d * 4 + h
                sb_ = Sbufs[(h, d)]
                tri = self.C_("triF" if d == 0 else "triB")
                mneg = self.C_("mnegF" if d == 0 else "mnegB")
                strict = self.C_("strF" if d == 0 else "strB")
                cs = slice(blk * 128, (blk + 1) * 128)
                sc = lambda Tn: Tn[:, blk, ci:ci + 1]

                def pre_fn():
                    k.op("dve", lambda e: e.tensor_scalar(out=rhsb[j][:, :], in0=tri, scalar1=sc(G), scalar2=None, op0=ALU.mult), [self.cst, G], [rhsb[j]])
                    yield
                    pa = pre_ps()
                    k.op("pe", lambda e: e.matmul(pa[:, 0:128], lhsT=self.C_("ones"), rhs=rhsb[j][:, :], start=True, stop=True), [self.cst, rhsb[j]], [pa])
                    yield
                    k.op("dve", lambda e: e.scalar_tensor_tensor(out=Dm[j][:, :], in0=pa[:, 0:128], scalar=sc(GCUM), in1=mneg, op0=ALU.subtract, op1=ALU.add), [pa, GCUM, self.cst], [Dm[j]])
                    yield
                    k.op("act", lambda e: e.activation(out=E[j][:, :], in_=Dm[j][:, :], func=AF.Exp), [Dm[j]], [E[j]])
                    yield
                    k.op("act", lambda e: e.activation(out=RE[j][:, :], in_=pa[:, 0:128], func=AF.Exp), [pa], [RE[j]])
                    yield
                    pA = pre_ps()
                    k.op("pe", lambda e: e.matmul(pA[:, 0:128], lhsT=KT[:, h, cs], rhs=KT[:, h, cs], start=True, stop=True), [KT], [pA])
                    yield
                    k.op("pe", lambda e: e.matmul(pA[:, 128:256], lhsT=KT[:, h, cs], rhs=QT[:, h, cs], start=True, stop=True), [KT, QT], [pA])
                    yield
                    k.op("dve", lambda e: e.scalar_tensor_tensor(out=t1[j][:, :], in0=pA[:, 0:128], scalar=sc(BETA), in1=E[j][:, :], op0=ALU.mult, op1=ALU.mult), [pA, BETA, E[j]], [t1[j]])
                    yield
                    P0, PT0 = Pk[0][j], PTk[0][j]
                    k.op("dve", lambda e: e.tensor_tensor(out=P0[:, :], in0=t1[j][:, :], in1=strict, op=ALU.mult), [t1[j], self.cst], [P0])
                    yield
                    k.op("dve", lambda e: e.tensor_tensor(out=AttnT[j][:, :], in0=pA[:, 128:256], in1=E[j][:, :], op=ALU.mult), [pA, E[j]], [AttnT[j]])
                    yield
                    k.op("dve", lambda e: e.tensor_tensor(out=QgT[j][:, :], in0=QT[:, h, cs], in1=RE[j][:, :], op=ALU.mult), [QT, RE[j]], [QgT[j]])
                    yield
                    k.op("dve", lambda e: e.tensor_scalar(out=Kd[j][:, :], in0=KTOK[:, blk, h, :], scalar1=sc(KDS), scalar2=None, op0=ALU.mult), [KTOK, KDS], [Kd[j]])
                    yield
                    pt = pre_ps()
                    k.op("pe", lambda e: e.transpose(out=pt[:, 0:128], in_=P0[:, :], identity=ident), [P0, self.cst], [pt])
                    yield
                    evac(PT0[:, :], pt[:, 0:128], [pt], [PT0])
                    yield
                    mC = lambda lv: GM[:, (0 if d == 0 else 7) + lv, :]
                    mCT = lambda lv: GM[:, (7 if d == 0 else 0) + lv, :]
                    k.op("dve", lambda e: e.tensor_tensor(out=t1[j][:, :], in0=P0[:, :], in1=mC(0), op=ALU.mult), [P0, GM], [t1[j]])
                    yield
                    k.op("dve", lambda e: e.tensor_tensor(out=Xb[j][:, :], in0=ident, in1=t1[j][:, :], op=ALU.subtract), [self.cst, t1[j]], [Xb[j]])
                    yield
                    k.op("dve", lambda e: e.tensor_tensor(out=t1[j][:, :], in0=PT0[:, :], in1=mCT(0), op=ALU.mult), [PT0, GM], [t1[j]])
                    yield
                    k.op("dve", lambda e: e.tensor_tensor(out=XTb[j][:, :], in0=ident, in1=t1[j][:, :], op=ALU.subtract), [self.cst, t1[j]], [XTb[j]])
                    yield
                    for lv in range(1, 1 if 'gdn_noneu' in self.dbg else 7):
                        lastlv = (lv == 6)
                        k.op("dve", lambda e: e.tensor_tensor(out=CTb[j][:, :], in0=PT0[:, :], in1=mCT(lv), op=ALU.mult), [PT0, GM], [CTb[j]])
                        yield
                        pz = pre_ps()
                        k.op("pe", lambda e: e.matmul(pz[:, 0:128], lhsT=CTb[j][:, :], rhs=Xb[j][:, :], start=True, stop=True), [CTb[j], Xb[j]], [pz])
                        yield
                        evac(Zb[j][:, :], pz[:, 0:128], [pz], [Zb[j]])
                        yield
                        py_ = pre_ps()
                        k.op("pe", lambda e: e.matmul(py_[:, 0:128], lhsT=XTb[j][:, :], rhs=Zb[j][:, :], start=True, stop=True), [XTb[j], Zb[j]], [py_])
                        yield
                        if not lastlv:
                            k.op("pool", lambda e: e.tensor_tensor(out=Cb[j][:, :], in0=P0[:, :], in1=mC(lv), op=ALU.mult), [P0, GM], [Cb[j]])
                            yield
                            pz2 = pre_ps()
                            k.op("pe", lambda e: e.matmul(pz2[:, 0:128], lhsT=Cb[j][:, :], rhs=XTb[j][:, :], start=True, stop=True), [Cb[j], XTb[j]], [pz2])
                            yield
                            evac(Z2b[j][:, :], pz2[:, 0:128], [pz2], [Z2b[j]])
                            yield
                            py2 = pre_ps()
                            k.op("pe", lambda e: e.matmul(py2[:, 0:128], lhsT=Xb[j][:, :], rhs=Z2b[j][:, :], start=True, stop=True), [Xb[j], Z2b[j]], [py2])
                            yield
                        k.op("dve", lambda e: e.tensor_tensor(out=Xb[j][:, :], in0=Xb[j][:, :], in1=py_[:, 0:128], op=ALU.subtract), [Xb[j], py_], [Xb[j]])
                        yield
                        if not lastlv:
                            k.op("dve", lambda e: e.tensor_tensor(out=XTb[j][:, :], in0=XTb[j][:, :], in1=py2[:, 0:128], op=ALU.subtract), [XTb[j], py2], [XTb[j]])
                            yield

                def seq_fn():
                    pk = self.psb[6]
                    k.op("pe", lambda e: e.matmul(pk[:, 0:128], lhsT=KT[:, h, cs], rhs=Sb[:, ci, :], start=True, stop=True), [KT, sb_], [pk])
                    yield
                    k.op("dve", lambda e: e.scalar_tensor_tensor(out=Rp[j][:, :], in0=pk[:, 0:128], scalar=sc(EG), in1=VTOK[:, blk, h, :], op0=ALU.mult, op1=ALU.subtract), [pk, EG, VTOK], [Rp[j]])
                    yield
                    k.op("pe", lambda e: e.matmul(pk[:, 128:256], lhsT=Xb[j][:, :], rhs=Rp[j][:, :], start=True, stop=True), [Xb[j], Rp[j]], [pk])
                    yield
                    k.op("dve", lambda e: e.tensor_scalar(out=vnew[j][:, :], in0=pk[:, 128:256], scalar1=sc(NBETA), scalar2=None, op0=ALU.mult), [pk, NBETA], [vnew[j]])
                    yield
                    po = self.psb[7]
                    k.op("pe", lambda e: e.matmul(po[:, 0:128], lhsT=Sb[:, ci, :], rhs=QgT[j][:, :], start=True, stop=False), [sb_, QgT[j]], [po])
                    yield
                    k.op("pe", lambda e: e.matmul(po[:, 0:128], lhsT=vnew[j][:, :], rhs=AttnT[j][:, :], start=False, stop=True), [vnew[j], AttnT[j]], [po])
                    yield
                    k.op("act", lambda e: e.copy(out=oo[j][:, :], in_=po[:, 0:128]), [po], [oo[j]])
                    yield
                    k.dma("pool", self.GO.t[d, h * 128:(h + 1) * 128, cs], oo[j][:, :], [oo[j]], [], oo[j])
                    yield
                    k.op("pe", lambda e: e.matmul(po[:, 128:256], lhsT=Kd[j][:, :], rhs=vnew[j][:, :], start=True, stop=True), [Kd[j], vnew[j]], [po])
                    yield
                    k.op("dve", lambda e: e.scalar_tensor_tensor(out=S_[:, ci, :], in0=S_[:, ci, :], scalar=sc(GL), in1=po[:, 128:256], op0=ALU.mult, op1=ALU.add), [sb_, GL, po], [sb_])
                    yield
                    k.op("act", lambda e: e.copy(out=Sb[:, ci, :], in_=S_[:, ci, :]), [sb_], [sb_])
                    yield
                return pre_fn, seq_fn
            ulist = []
            for i in range(0 if "gdn_noscan" in self.dbg else NB):
                for h in range(4):
                    for d in range(2):
                        ulist.append(make_unit(i, h, d, len(ulist) % R))
            def adv(g):
                try:
                    next(g)
                    return True
                except StopIteration:
                    return False
            nU = len(ulist)
            active, pre_done = {}, set()
            nxt, seq_unit, seq_gen, seq_done = 0, 0, None, -1
            while seq_done < nU - 1:
                while len(active) < 2 and nxt < nU and nxt <= seq_done + 3:
                    active[nxt] = ulist[nxt][0]()
                    nxt += 1
                for u_ in sorted(active):
                    if not adv(active[u_]):
                        del active[u_]
                        pre_done.add(u_)
                if seq_gen is None and seq_unit in pre_done:
                    seq_gen = ulist[seq_unit][1]()
                if seq_gen is not None and not adv(seq_gen):
                    seq_gen = None
                    seq_done = seq_unit
                    seq_unit += 1
            k.barrier()
        with contextlib.ExitStack() as st:
            gn = self.vec_col(st, W["gdn_norm_g"][l], 128, "ggn")
            oa = [self.sb(st, [128, 512], F32, "goa") for _ in range(2)]
            ob = [self.sb(st, [128, 512], F32, "gob") for _ in range(2)]
            z_ = [self.sb(st, [128, 512], F32, "gz") for _ in range(2)]
            sq_ = [self.sb(st, [128, 512], F32, "gfsq") for _ in range(2)]
            rs_ = [self.sb(st, [128, 512], F32, "gfrs") for _ in range(2)]
            yb = [self.sb(st, [128, 512], BF16, "gyb") for _ in range(2)]
            tiles = self.tiles[1:] if self.last else self.tiles
            it = 0
            for h in range(4):
                rz = self.urow(GZ + h * 128)
                for ti, (t0, n) in enumerate(tiles):
                    j = it % 2
                    it += 1
                    k.dma("sp", oa[j][:, 0:n], self.GO.t[0, h * 128:(h + 1) * 128, t0:t0 + n], [], [oa[j]], oa[j])
                    k.dma("sp", ob[j][:, 0:n], self.GO.t[1, h * 128:(h + 1) * 128, t0:t0 + n], [], [ob[j]], ob[j])
                    k.dma("sp", z_[j][:, 0:n], self.U.t[rz:rz + 128, t0:t0 + n], [], [z_[j]], z_[j])
                    k.op("dve", lambda e: e.tensor_tensor(out=oa[j][:, 0:n], in0=oa[j][:, 0:n], in1=ob[j][:, 0:n], op=ALU.add), [oa[j], ob[j]], [oa[j]])
                    k.op("act", lambda e: e.activation(out=sq_[j][:, 0:n], in_=oa[j][:, 0:n], func=AF.Square), [oa[j]], [sq_[j]])
                    k.op("act", lambda e: e.activation(out=z_[j][:, 0:n], in_=z_[j][:, 0:n], func=AF.Silu), [z_[j]], [z_[j]])
                    ps = self.psum()
                    k.op("pe", lambda e: e.matmul(ps[:, 0:n], lhsT=self.C_("ones"), rhs=sq_[j][:, 0:n], start=True, stop=True), [self.cst, sq_[j]], [ps])
                    self.rstd_from(ps, n, rs_[j], 1.0 / 128)
                    k.op("dve", lambda e: e.scalar_tensor_tensor(out=oa[j][:, 0:n], in0=oa[j][:, 0:n], scalar=gn[:, 0:1], in1=rs_[j][:, 0:n], op0=ALU.mult, op1=ALU.mult), [oa[j], gn, rs_[j]], [oa[j]])
                    k.op("dve", lambda e: e.tensor_tensor(out=yb[j][:, 0:n], in0=oa[j][:, 0:n], in1=z_[j][:, 0:n], op=ALU.mult), [oa[j], z_[j]], [yb[j]])
                    k.dma("pool", self.Y.t[h * 128:(h + 1) * 128, t0:t0 + n], yb[j][:, 0:n], [yb[j]], [], yb[j])
            k.barrier()


_CACHE = {}


def kernel(**inputs):
    S, C, DEPTH = 4096, 256, 2
    inputs = {k_: np.asarray(v) for k_, v in inputs.items()}
    if "nc" not in _CACHE:
        _CACHE["nc"] = Mod(S, C, DEPTH).build()
    nc = _CACHE["nc"]
    consts = consts_np(S, C)
    in_maps = []
    for b in range(8):
        m = {"x": np.ascontiguousarray(inputs["x"][b], dtype=np.float32),
             "ctx": np.ascontiguousarray(inputs["ctx"][b], dtype=np.float32),
             "cc": np.ascontiguousarray(np.stack([inputs["c"][b], inputs["c_ctx"]]), dtype=np.float32)}
        m.update(consts)
        for n, sh in WSPEC:
            m[n] = np.ascontiguousarray(inputs[n], dtype=np.float32)
        in_maps.append(m)
    res = run_bass_kernel_spmd(nc, in_maps, core_ids=list(range(8)))
    return np.stack([np.asarray(r["out"], dtype=np.float32) for r in res.results], axis=0)
```

```python
import math
import contextlib
import numpy as np
import concourse.bass as bass
import concourse.mybir as mybir
from concourse.bass_utils import run_bass_kernel_spmd

F32 = mybir.dt.float32
BF16 = mybir.dt.bfloat16
ALU = mybir.AluOpType
AF = mybir.ActivationFunctionType


class Buf:
    __slots__ = ("t", "w", "r", "dsem", "name")

    def __init__(self, t, name=""):
        self.t = t
        self.w = {}
        self.r = {}
        self.dsem = None
        self.name = name

    def __getitem__(self, key):
        return self.t[key]


class KB:
    SEM_ROT = 30000

    def __init__(self, nc):
        self.nc = nc
        self.es = contextlib.ExitStack()
        self.engs = {"pe": nc.tensor, "dve": nc.vector, "act": nc.scalar,
                     "pool": nc.gpsimd, "sp": nc.sync}
        self.semh = {}
        self.cnt = {}
        self.isdma = {}
        self.cur = {}
        self.waited = {e: {} for e in self.engs}
        self.nsem = 0
        for e in self.engs:
            self.cur[e] = self.new_sem(False)
        self.ninstr = 0
        self.free_dsems = []
        self.free_dsems_q = {}
        self.phase_dsems = []
        self.persist = False
        self.dma_remap = {}

    def new_sem(self, isdma):
        key = self.nsem
        self.nsem += 1
        self.semh[key] = self.es.enter_context(self.nc.semaphore("s%d" % key))
        self.cnt[key] = 0
        self.isdma[key] = isdma
        return key

    def sb(self, stack, name, shape, dtype):
        t = stack.enter_context(self.nc.sbuf_tensor(name, list(shape), dtype))
        return Buf(t, name)

    def ps(self, stack, name, shape, dtype):
        t = stack.enter_context(self.nc.psum_tensor(name, list(shape), dtype))
        return Buf(t, name)

    def _waits(self, eng, reads, writes):
        need = {}
        for b in reads:
            for s, v in b.w.items():
                if need.get(s, 0) < v:
                    need[s] = v
        for b in writes:
            for d in (b.w, b.r):
                for s, v in d.items():
                    if eng == "pe" and s == self.cur["pe"] and d is b.w:
                        continue
                    if need.get(s, 0) < v:
                        need[s] = v
        e = self.engs[eng]
        wd = self.waited[eng]
        for s, v in need.items():
            if wd.get(s, 0) >= v:
                continue
            if self.isdma[s]:
                v = self.cnt[s]
            e.wait_ge(self.semh[s], v)
            wd[s] = v
            self.ninstr += 1

    def op(self, eng, fn, reads=(), writes=()):
        self._waits(eng, reads, writes)
        ins = fn(self.engs[eng])
        s = self.cur[eng]
        self.cnt[s] += 1
        ins.then_inc(self.semh[s], 1)
        tag = (s, self.cnt[s])
        self._mark(tag, reads, writes)
        if self.cnt[s] >= self.SEM_ROT:
            self.cur[eng] = self.new_sem(False)
        self.ninstr += 1
        return ins

    def _mark(self, tag, reads, writes):
        s, v = tag
        for b in writes:
            b.w = {s: v}
            b.r = {}
        for b in reads:
            if b not in writes:
                b.r[s] = v

    def dma(self, eng, out, in_, reads, writes, sembuf, **kw):
        eng = self.dma_remap.get(eng, eng)
        dmap = sembuf.dsem if isinstance(sembuf.dsem, dict) else {}
        sembuf.dsem = dmap
        if eng not in dmap:
            fl = self.free_dsems_q.setdefault(eng, [])
            if fl and not self.persist:
                dmap[eng] = fl.pop()
            else:
                dmap[eng] = self.new_sem(True)
            if not self.persist:
                self.phase_dsems.append((eng, dmap[eng]))
        self._waits(eng, reads, writes)
        ins = self.engs[eng].dma_start(out=out, in_=in_, **kw)
        s = sembuf.dsem[eng]
        self.cnt[s] += 16
        ins.then_inc(self.semh[s], 16)
        self._mark((s, self.cnt[s]), reads, writes)
        self.ninstr += 1
        return ins

    def barrier(self, engs=None):
        if engs is None:
            for q_, s_ in self.phase_dsems:
                self.free_dsems_q.setdefault(q_, []).append(s_)
            self.phase_dsems = []
        for eng in (engs or self.engs):
            e = self.engs[eng]
            wd = self.waited[eng]
            for s, v in self.cnt.items():
                if v > 0 and wd.get(s, 0) < v:
                    e.wait_ge(self.semh[s], v)
                    wd[s] = v
                    self.ninstr += 1


D = 1024
EPS = 1e-6
NEG = -30000.0
GQ, GK, GV, GZ, GA, GB_ = 0, 512, 1024, 1536, 2048, 2056
MCQ, MCKV, MKPE = 2064, 2448, 2704
DQ, DK, DV = 2736, 3248, 3760
SZ, SX, SB_, SC, SDT = 4272, 4784, 5296, 5552, 5808
GATE0 = 5824
CN = ["ident", "ones", "triF", "triB", "mnegF", "mnegB", "strF", "strB", "bd64", "perm64", "perm32", "sel65"]


def consts_np(S, C):
    T = C + S
    k = np.arange(128)
    d = {}
    d["ident"] = np.eye(128)
    d["ones"] = np.ones((128, 128))
    d["triF"] = (k[:, None] <= k[None, :])
    d["triB"] = (k[:, None] >= k[None, :])
    d["mnegF"] = np.where(k[None, :] >= k[:, None], 0.0, NEG)
    d["mnegB"] = np.where(k[None, :] <= k[:, None], 0.0, NEG)
    d["strF"] = (k[None, :] > k[:, None])
    d["strB"] = (k[None, :] < k[:, None])
    d["bd64"] = (k[:, None] // 64 == k[None, :] // 64)

    def perm(n_rot, total):
        P = np.zeros((128, 128))
        half = n_rot // 2
        qd = half // 2
        for base in range(0, total, half):
            for i in range(half):
                m = base + i
                if i < qd:
                    P[m + qd, m] = -1.0
                else:
                    P[m - qd, m] = 1.0
        return P
    d["perm64"] = perm(64, 128)
    d["perm32"] = perm(32, 32)
    s65 = np.zeros((128, 128))
    s65[64, :] = 1.0
    d["sel65"] = s65
    cst = np.stack([np.asarray(d[n], np.float32) for n in CN], axis=1)

    def rope(rot_dim):
        rows = S // 64
        row = np.repeat(np.arange(rows, dtype=np.float32), 64)
        col = np.tile(np.arange(64, dtype=np.float32), rows)
        quarter = rot_dim // 4
        inv = (np.float32(10000.0) ** (-np.arange(quarter, dtype=np.float32) / np.float32(quarter))).astype(np.float32)
        ar = row[:, None] * inv
        ac = col[:, None] * inv
        ang = np.concatenate([ar, ar, ac, ac], axis=-1).astype(np.float32)
        cos = np.ones((rot_dim, T), np.float32)
        sin = np.zeros((rot_dim, T), np.float32)
        cos[:, C:] = np.cos(ang).T
        sin[:, C:] = np.sin(ang).T
        return cos, sin
    cm, sm = rope(32)
    cd, sd = rope(64)
    ropem = np.stack([cm, sm], axis=1)
    roped = np.stack([np.tile(cd, (2, 1)), np.tile(sd, (2, 1))], axis=1)
    gm = []
    for lv in range(7):
        b = 1 << lv
        mU = ((k[:, None] // (2 * b) == k[None, :] // (2 * b)) & (k[:, None] % (2 * b) < b) & (k[None, :] % (2 * b) >= b))
        gm.append(mU)
    gm = gm + [m_.T for m_ in gm]
    gmask = np.stack([np.asarray(m_, np.float32) for m_ in gm], axis=1)
    return {"cst": np.ascontiguousarray(cst), "ropem": np.ascontiguousarray(ropem),
            "roped": np.ascontiguousarray(roped), "gmask": np.ascontiguousarray(gmask)}


WSPEC = [
    ("ada_w", [D, 6 * D]), ("ada_b", [6 * D]), ("norm1_g", [D]), ("norm2_g", [D]),
    ("w_in", [D, 9920]), ("gdn_conv", [5, 1536]), ("gdn_a_log", [2, 4]), ("gdn_dt_bias", [2, 4]),
    ("gdn_norm_g", [128]), ("mla_q_lora_g", [384]), ("mla_kv_lora_g", [256]),
    ("mla_w_uq", [384, 768]), ("mla_w_ukv", [256, 1024]), ("mla_qn_g", [96]), ("mla_kn_g", [96]),
    ("diff_qn_g", [64]), ("diff_kn_g", [64]), ("diff_lambda", [4, 64]), ("diff_sub_g", [128]),
    ("ssm_conv", [5, 1024]), ("ssm_conv_b", [1024]), ("ssm_a_log", [2, 8]), ("ssm_dt_bias", [2, 8]),
    ("ssm_d", [8]), ("ssm_norm_g", [512]), ("w_branch", [4, 512, D]), ("w_out", [D, D]),
    ("mlp_w1", [D, 4 * D]), ("mlp_w2", [4 * D, D]),
]


class Mod:
    def __init__(self, S, C, depth, dbg=()):
        self.S, self.C, self.depth = S, C, depth
        self.T = T = S + C
        self.NB = T // 128
        self.dbg = dbg
        nc = self.nc = bass.Bass("TRN2", target_bir_lowering=False)
        self.k = KB(nc)
        self.uid = 0
        di = lambda n, sh, dt=F32: nc.dram_tensor(n, list(sh), dt, kind="ExternalInput").ap()
        self.x_in = di("x", [S, D])
        self.ctx_in = di("ctx", [C, D])
        self.cc_in = di("cc", [2, D])
        self.cst_in = di("cst", [128, len(CN), 128])
        self.ropem_in = di("ropem", [32, 2, T])
        self.roped_in = di("roped", [128, 2, T])
        self.gmask_in = di("gmask", [128, 14, 128])
        self.W = {n: di(n, [depth] + sh) for n, sh in WSPEC}
        self.out = nc.dram_tensor("out", [S, D], F32, kind="ExternalOutput").ap()
        dscr = lambda n, sh, dt=F32: Buf(nc.dram_tensor(n, list(sh), dt, kind=("ExternalOutput" if n in dbg else "Internal")).ap(), n)
        self.XT = dscr("XT", [D, T])
        self.U = dscr("U", [80 * 128, T])
        self.Y = dscr("Y", [2048, T], BF16)
        self.MT = dscr("MT", [D, T])
        self.HID = dscr("HID", [4 * D, T], BF16)
        self.QD = dscr("QD", [512, T], BF16)
        self.KD = dscr("KD", [512, T], BF16)
        self.VD = dscr("VD", [T, 512], BF16)
        self.KN = dscr("KN", [8 * 64, T], BF16)
        self.KR = dscr("KR", [8 * 32, T], BF16)
        self.QN = dscr("QN", [8 * 64, T], BF16)
        self.QR = dscr("QR", [8 * 32, T], BF16)
        self.VA = dscr("VA", [T, 8 * 65], BF16)
        self.XS = dscr("XS", [512, T])
        self.YS = dscr("YS", [2, 512, T])
        self.GO = dscr("GO", [2, 512, T])
        self.tiles = [(0, C)] + [(C + 512 * i, 512) for i in range(S // 512)]
        self.uchunks = []
        slot = 0
        self.uslot = {}
        for (a, b) in [(0, 2048), (2048, 2064), (2064, 2448), (2448, 2704), (2704, 2736), (2736, 4272),
                       (4272, 5808), (5808, 5824), (5824, 9920)]:
            c = a
            while c < b:
                e = min(c + 128, b)
                self.uchunks.append((c, e, slot))
                self.uslot[c] = slot
                slot += 1
                c = e
        assert slot == 80

    def nm(self, p):
        self.uid += 1
        return "%s_%d" % (p, self.uid)

    def sb(self, st, shape, dt=F32, p="t"):
        return self.k.sb(st, self.nm(p), shape, dt)

    def psum(self):
        self.psi = (self.psi + 1) % len(self.psb)
        return self.psb[self.psi]

    def urow(self, col):
        return self.uslot[col] * 128

    def build(self):
        k = self.k
        with k.es, contextlib.ExitStack() as gs:
            self.psb = [k.ps(gs, "psb%d" % i, [128, 512], F32) for i in range(8)]
            self.psi = 0
            self.cst = self.sb(gs, [128, len(CN), 128], F32, "cst")
            k.persist = True
            k.dma("sp", self.cst[:, :, :], self.cst_in, [], [self.cst], self.cst)
            k.persist = False
            self.cbf = self.sb(gs, [128, len(CN), 128], BF16, "cbf")
            k.op("dve", lambda e: e.tensor_copy(out=self.cbf[:, :, :], in_=self.cst[:, :, :]), [self.cst], [self.cbf])
            self.epsb = self.sb(gs, [128, 4], F32, "epsb")
            k.op("dve", lambda e: e.memset(self.epsb[:, :], EPS), [], [self.epsb])
            k.op("dve", lambda e: e.memset(self.epsb[:, 1:2], 1.0), [self.epsb], [self.epsb])
            self.XTt = [Buf(None, "XTt%d" % i) for i in range(len(self.tiles))]
            self.MTt = [Buf(None, "MTt%d" % i) for i in range(len(self.tiles))]
            self.phase_init()
            for l in range(self.depth):
                self.layer(l)
            self.phase_final()
            k.barrier()
        return self.nc

    def C_(self, name, bf=False):
        i = CN.index(name)
        return (self.cbf if bf else self.cst)[:, i, :]

    def phase_init(self):
        k = self.k
        with contextlib.ExitStack() as st:
            xin = [self.sb(st, [128, D], F32, "xin") for _ in range(2)]
            xo = [self.sb(st, [128, 8, 128], F32, "xo") for _ in range(2)]
            XTv = self.XT.t.rearrange("(kc p) t -> p kc t", p=128)
            for blk in range(self.NB):
                src = self.ctx_in[blk * 128:(blk + 1) * 128, :] if blk < self.C // 128 else \
                    self.x_in[blk * 128 - self.C:(blk + 1) * 128 - self.C, :]
                xi = xin[blk % 2]
                o = xo[blk % 2]
                k.dma("sp", xi[:, :], src, [], [xi], xi)
                for half in range(2):
                    ps = self.psum()
                    for j in range(4):
                        kc = half * 4 + j
                        k.op("pe", lambda e: e.transpose(out=ps[:, j * 128:(j + 1) * 128], in_=xi[:, kc * 128:(kc + 1) * 128],
                                                         identity=self.C_("ident")), [xi, self.cst], [ps])
                    eng = "dve" if half == 0 else "act"
                    if eng == "dve":
                        k.op("dve", lambda e: e.tensor_copy(out=o[:, half * 4:half * 4 + 4, :], in_=ps[:, :].rearrange("p (a b) -> p a b", a=4)), [ps], [o])
                    else:
                        k.op("act", lambda e: e.copy(out=o[:, half * 4:half * 4 + 4, :], in_=ps[:, :].rearrange("p (a b) -> p a b", a=4)), [ps], [o])
                k.dma("pool", XTv[:, :, blk * 128:(blk + 1) * 128], o[:, :, :], [o], [], o)
            k.barrier()

    def phase_final(self):
        k = self.k
        with contextlib.ExitStack() as st:
            xi = [self.sb(st, [128, 8, 128], F32, "fxi") for _ in range(2)]
            xo = [self.sb(st, [128, D], F32, "fxo") for _ in range(2)]
            XTv = self.XT.t.rearrange("(kc p) t -> p kc t", p=128)
            dout = Buf(self.out, "out")
            for b in range(self.S // 128):
                blk = b + self.C // 128
                a = xi[b % 2]
                o = xo[b % 2]
                k.dma("sp", a[:, :, :], XTv[:, :, blk * 128:(blk + 1) * 128], [], [a], a)
                for half in range(2):
                    ps = self.psum()
                    for j in range(4):
                        kc = half * 4 + j
                        k.op("pe", lambda e: e.transpose(out=ps[:, j * 128:(j + 1) * 128], in_=a[:, kc, :],
                                                         identity=self.C_("ident")), [a, self.cst], [ps])
                    if half == 0:
                        k.op("dve", lambda e: e.tensor_copy(out=o[:, 0:512], in_=ps[:, :]), [ps], [o])
                    else:
                        k.op("act", lambda e: e.copy(out=o[:, 512:1024], in_=ps[:, :]), [ps], [o])
                k.dma("pool", self.out[b * 128:(b + 1) * 128, :], o[:, :], [o], [dout], o)
            k.barrier()

    def layer(self, l):
        k = self.k
        self.l = l
        self.last = (l == self.depth - 1) and ("forcectx" not in self.dbg)
        with contextlib.ExitStack() as ls:
            self.phase_mod(l, ls)
            self.phase_inproj(l)
            if "U" in self.dbg and l == 0 and "stopU" in self.dbg:
                return
            for ph in ("gdn", "mla", "diff", "ssm", "merge", "mlp"):
                if ph not in self.dbg:
                    getattr(self, "phase_" + ph)(l)
            k.barrier()

    def phase_mod(self, l, ls):
        k = self.k
        W = self.W
        self.modv = self.sb(ls, [128, 48, 2], F32, "modv")
        self.A1 = self.sb(ls, [128, 8, 2], F32, "A1")
        self.A2 = self.sb(ls, [128, 8, 2], F32, "A2")
        with contextlib.ExitStack() as st:
            cv = self.sb(st, [128, 2, 8], F32, "cv")
            sv = self.sb(st, [128, 8, 2], F32, "sv")
            ab = self.sb(st, [128, 48], F32, "ab")
            g12 = self.sb(st, [128, 2, 8], F32, "g12")
            k.dma("sp", cv[:, :, :], self.cc_in.rearrange("r (kc p) -> p r kc", p=128), [], [cv], cv, allow_slow_non_contiguous=True)
            k.dma("sp", ab[:, :], W["ada_b"][l].rearrange("(j p) -> p j", p=128), [], [ab], ab, allow_slow_non_contiguous=True)
            k.dma("sp", g12[:, 0, :], W["norm1_g"][l].rearrange("(kc p) -> p kc", p=128), [], [g12], g12, allow_slow_non_contiguous=True)
            k.dma("sp", g12[:, 1, :], W["norm2_g"][l].rearrange("(kc p) -> p kc", p=128), [], [g12], g12, allow_slow_non_contiguous=True)
            k.op("act", lambda e: e.activation(out=sv[:, :, :], in_=cv[:, :, :].rearrange("p r kc -> p kc r"), func=AF.Silu), [cv], [sv])
            wst = [self.sb(st, [128, 8, 512], F32, "adaw") for _ in range(2)]
            aw = W["ada_w"][l].rearrange("(kc p) c -> p kc c", p=128)
            ps = self.psum()
            for g in range(12):
                w = wst[g % 2]
                k.dma("sp", w[:, :, :], aw[:, :, g * 512:(g + 1) * 512], [], [w], w)
                for jj in range(4):
                    j = g * 4 + jj
                    for kc in range(8):
                        k.op("pe", lambda e: e.matmul(ps[:, j * 2:j * 2 + 2], lhsT=w[:, kc, jj * 128:(jj + 1) * 128], rhs=sv[:, kc, :],
                                                      start=(kc == 0), stop=(kc == 7)), [w, sv], [ps])
            k.op("dve", lambda e: e.tensor_tensor(out=self.modv[:, :, :], in0=ps[:, 0:96].rearrange("p (j r) -> p j r", r=2),
                                                  in1=ab[:, :].unsqueeze(2).to_broadcast([128, 48, 2]), op=ALU.add), [ps, ab], [self.modv])
            for (A, gi, mi) in ((self.A1, 0, 1), (self.A2, 1, 4)):
                k.op("dve", lambda e: e.scalar_tensor_tensor(out=A[:, :, :], in0=self.modv[:, mi * 8:mi * 8 + 8, :], scalar=1.0,
                                                             in1=g12[:, gi, :].unsqueeze(2).to_broadcast([128, 8, 2]), op0=ALU.add, op1=ALU.mult),
                     [self.modv, g12], [A])
            k.barrier()

    def mcol(self, ti):
        return 1 if ti == 0 else 0

    def norm_mod(self, st, A, shift_idx, dst):
        k = self.k
        xt_ = [self.sb(st, [128, 8, 512], F32, "nx") for _ in range(2)]
        sq_ = [self.sb(st, [128, 8, 512], F32, "nsq") for _ in range(2)]
        rs_ = [self.sb(st, [128, 512], F32, "nrs") for _ in range(2)]
        tmp_ = [self.sb(st, [128, 512], F32, "ntmp") for _ in range(3)]
        XTv = self.XT.t.rearrange("(kc p) t -> p kc t", p=128)
        ci = 0
        for ti, (t0, n) in enumerate(self.tiles):
            col = self.mcol(ti)
            xt, sq, rs = xt_[ti % 2], sq_[ti % 2], rs_[ti % 2]
            k.dma("sp", xt[:, :, 0:n], XTv[:, :, t0:t0 + n], [self.XTt[ti]], [xt], xt)
            k.op("act", lambda e: e.activation(out=sq[:, :, 0:n], in_=xt[:, :, 0:n], func=AF.Square), [xt], [sq])
            ps = self.psum()
            for kc in range(8):
                k.op("pe", lambda e: e.matmul(ps[:, 0:n], lhsT=self.C_("ones"), rhs=sq[:, kc, 0:n], start=(kc == 0), stop=(kc == 7)),
                     [self.cst, sq], [ps])
            self.rstd_from(ps, n, rs, 1.0 / D)
            for kc in range(8):
                tmp = tmp_[ci % 3]
                ci += 1
                k.op("dve", lambda e: e.scalar_tensor_tensor(out=tmp[:, 0:n], in0=xt[:, kc, 0:n], scalar=A[:, kc, col:col + 1], in1=rs[:, 0:n],
                                                             op0=ALU.mult, op1=ALU.mult), [xt, A, rs], [tmp])
                k.op("act", lambda e: e.activation(out=dst[:, kc, t0:t0 + n], in_=tmp[:, 0:n], func=AF.Identity,
                                                   bias=self.modv[:, shift_idx * 8 + kc, col:col + 1], scale=1.0), [tmp, self.modv], [dst])

    def gemm_fm(self, st, in_sb, KC, wsrc, chunks, tiles, epi, krows=128):
        k = self.k
        groups, cur = [], []
        for ch in chunks:
            if cur and (ch[0] != cur[-1][1] or ch[1] - cur[0][0] > 512):
                groups.append(cur)
                cur = []
            cur.append(ch)
        if cur:
            groups.append(cur)
        wst = [self.sb(st, [128, KC, 512], F32, "wst") for _ in range(2)]
        wbf = [self.sb(st, [128, KC, 512], BF16, "wbf") for _ in range(2)]
        for gi, g in enumerate(groups):
            c0, c1 = g[0][0], g[-1][1]
            w = c1 - c0
            s, b = wst[gi % 2], wbf[gi % 2]
            k.dma("sp", s[0:krows, :, 0:w], wsrc(c0, c1), [], [s], s)
            k.op("pool", lambda e: e.tensor_copy(out=b[0:krows, :, 0:w], in_=s[0:krows, :, 0:w]), [s], [b])
            for ti, (t0, n) in enumerate(tiles):
                for ch in g:
                    rows = ch[1] - ch[0]
                    off = ch[0] - c0
                    ps = self.psum()
                    for kk in range(KC):
                        k.op("pe", lambda e: e.matmul(ps[0:rows, 0:n], lhsT=b[0:krows, kk, off:off + rows], rhs=in_sb[0:krows, kk, t0:t0 + n],
                                                      start=(kk == 0), stop=(kk == KC - 1)), [b, in_sb], [ps])
                    epi(ch, ti, t0, n, ps, rows)

    def phase_inproj(self, l):
        k = self.k
        with contextlib.ExitStack() as st:
            hT = self.sb(st, [128, 8, self.T], BF16, "hT")
            with contextlib.ExitStack() as st2:
                self.norm_mod(st2, self.A1, 0, hT)
                k.barrier()
            stg = [self.sb(st, [128, 512], F32, "ustg") for _ in range(4)]
            cnt = [0]
            win = self.W["w_in"][l].rearrange("(kc p) c -> p kc c", p=128)

            def epi(ch, ti, t0, n, ps, rows):
                s = stg[cnt[0] % 4]
                if cnt[0] % 2 == 0:
                    k.op("dve", lambda e: e.tensor_copy(out=s[0:rows, 0:n], in_=ps[0:rows, 0:n]), [ps], [s])
                else:
                    k.op("act", lambda e: e.copy(out=s[0:rows, 0:n], in_=ps[0:rows, 0:n]), [ps], [s])
                cnt[0] += 1
                r0 = ch[2] * 128
                k.dma("pool", self.U.t[r0:r0 + rows, t0:t0 + n], s[0:rows, 0:n], [s], [], s)
            self.gemm_fm(st, hT, 8, lambda c0, c1: win[:, :, c0:c1], self.uchunks, self.tiles, epi)
            vst = [self.sb(st, [128, 512], BF16, "vst") for _ in range(2)]

            def epiv(blk, ps):
                v = vst[blk % 2]
                k.op("act", lambda e: e.copy(out=v[:, :], in_=ps[:, :]), [ps], [v])
                k.dma("pool", self.VD.t[blk * 128:(blk + 1) * 128, :], v[:, :], [v], [], v)
            self.gemm_tm(st, hT, 8, lambda s_: k.dma("sp", s_[:, :, :], win[:, :, DV:DV + 512], [], [s_], s_), 512, list(range(self.NB)), epiv)
            k.barrier()

    def gemm_tm(self, st, in_sb, KC, wsrc, width, blocks, epi, krows=128):
        k = self.k
        s = self.sb(st, [128, KC, width], F32, "wtm")
        b = self.sb(st, [128, KC, width], BF16, "wtmb")
        wsrc(s)
        k.op("pool", lambda e: e.tensor_copy(out=b[0:krows, :, :], in_=s[0:krows, :, :]), [s], [b])
        for blk in blocks:
            ps = self.psum()
            for kk in range(KC):
                k.op("pe", lambda e: e.matmul(ps[:, 0:width], lhsT=in_sb[0:krows, kk, blk * 128:(blk + 1) * 128], rhs=b[0:krows, kk, :],
                                              start=(kk == 0), stop=(kk == KC - 1)), [in_sb, b], [ps])
            epi(blk, ps)

    def rstd_from(self, ps, n, rs, scale, rows=128):
        k = self.k
        k.op("act", lambda e: e.activation(out=rs[0:rows, 0:n], in_=ps[0:rows, 0:n], func=AF.Ln, bias=self.epsb[0:rows, 0:1], scale=scale), [ps, self.epsb], [rs])
        k.op("act", lambda e: e.activation(out=rs[0:rows, 0:n], in_=rs[0:rows, 0:n], func=AF.Exp, scale=-0.5), [rs], [rs])

    def vec_col(self, st, src_ap, rows, p="vc"):
        t = self.sb(st, [128, 1], F32, p)
        self.k.dma("sp", t[0:rows, :], src_ap.rearrange("(p o) -> p o", o=1), [], [t], t, allow_slow_non_contiguous=True)
        return t

    def phase_merge(self, l):
        k = self.k
        W = self.W
        tiles = self.tiles[1:] if self.last else self.tiles
        MTv = self.MT.t
        for i in range(4):
            with contextlib.ExitStack() as st:
                yin = self.sb(st, [128, 4, self.T], BF16, "yin")
                k.dma("sp", yin[:, :, :], self.Y.t[i * 512:(i + 1) * 512, :].rearrange("(kc p) t -> p kc t", p=128), [], [yin], yin)
                gt_ = [self.sb(st, [128, 512], F32, "gt") for _ in range(3)]
                mt_ = [self.sb(st, [128, 512], F32, "mt") for _ in range(3)]
                cnt = [0]
                wb = W["w_branch"][l, i].rearrange("(kc p) c -> p kc c", p=128)

                def epi(ch, ti, t0, n, ps, rows):
                    c = ch[0] // 128
                    gt, mt = gt_[cnt[0] % 3], mt_[cnt[0] % 3]
                    cnt[0] += 1
                    r0 = self.urow(GATE0 + i * 1024 + c * 128)
                    k.dma("sp", gt[:, 0:n], self.U.t[r0:r0 + 128, t0:t0 + n], [], [gt], gt)
                    k.op("act", lambda e: e.activation(out=gt[:, 0:n], in_=gt[:, 0:n], func=AF.Sigmoid), [gt], [gt])
                    if i > 0:
                        k.dma("sp", mt[:, 0:n], MTv[c * 128:(c + 1) * 128, t0:t0 + n], [], [mt], mt)
                        k.op("dve", lambda e: e.tensor_tensor(out=gt[:, 0:n], in0=ps[:, 0:n], in1=gt[:, 0:n], op=ALU.mult), [ps, gt], [gt])
                        k.op("dve", lambda e: e.tensor_tensor(out=mt[:, 0:n], in0=mt[:, 0:n], in1=gt[:, 0:n], op=ALU.add), [mt, gt], [mt])
                    else:
                        k.op("dve", lambda e: e.tensor_tensor(out=mt[:, 0:n], in0=ps[:, 0:n], in1=gt[:, 0:n], op=ALU.mult), [ps, gt], [mt])
                    k.dma("pool", MTv[c * 128:(c + 1) * 128, t0:t0 + n], mt[:, 0:n], [mt], [], mt)
                self.gemm_fm(st, yin, 4, lambda c0, c1: wb[:, :, c0:c1], [(c * 128, (c + 1) * 128) for c in range(8)], tiles, epi)
                k.barrier()
        with contextlib.ExitStack() as st:
            mT = self.sb(st, [128, 8, self.T], BF16, "mTb")
            with contextlib.ExitStack() as st2:
                ml = [self.sb(st2, [128, 8, 512], F32, "ml") for _ in range(2)]
                for ti, (t0, n) in enumerate(tiles):
                    m = ml[ti % 2]
                    k.dma("sp", m[:, :, 0:n], MTv.rearrange("(kc p) t -> p kc t", p=128)[:, :, t0:t0 + n], [], [m], m)
                    k.op("dve", lambda e: e.tensor_copy(out=mT[:, :, t0:t0 + n], in_=m[:, :, 0:n]), [m], [mT])
                k.barrier()
            self.resid_gemm(st, mT, 8, self.W["w_out"][l].rearrange("(kc p) c -> p kc c", p=128), 16, tiles)
            k.barrier()

    def resid_gemm(self, st, in_sb, KC, wv, gate_idx, tiles):
        k = self.k
        xt_ = [self.sb(st, [128, 512], F32, "rx") for _ in range(4)]
        cnt = [0]
        XTv = self.XT.t

        def epi(ch, ti, t0, n, ps, rows):
            c = ch[0] // 128
            col = 1 if t0 == 0 else 0
            xt = xt_[cnt[0] % 4]
            cnt[0] += 1
            k.dma("sp", xt[:, 0:n], XTv[c * 128:(c + 1) * 128, t0:t0 + n], [], [xt], xt)
            k.op("dve", lambda e: e.scalar_tensor_tensor(out=xt[:, 0:n], in0=ps[:, 0:n], scalar=self.modv[:, gate_idx + c, col:col + 1], in1=xt[:, 0:n],
                                                         op0=ALU.mult, op1=ALU.add), [ps, self.modv, xt], [xt])
            k.dma("pool", XTv[c * 128:(c + 1) * 128, t0:t0 + n], xt[:, 0:n], [xt], [], xt)
        self.gemm_fm(st, in_sb, KC, lambda c0, c1: wv[:, :, c0:c1], [(c * 128, (c + 1) * 128) for c in range(8)], tiles, epi)

    def phase_mlp(self, l):
        k = self.k
        tiles = self.tiles[1:] if self.last else self.tiles
        with contextlib.ExitStack() as st:
            hT = self.sb(st, [128, 8, self.T], BF16, "h2T")
            with contextlib.ExitStack() as st2:
                self.norm_mod(st2, self.A2, 3, hT)
                k.barrier()
            stg = [self.sb(st, [128, 512], F32, "hs") for _ in range(3)]
            stb = [self.sb(st, [128, 512], BF16, "hb") for _ in range(3)]
            cnt = [0]
            w1 = self.W["mlp_w1"][l].rearrange("(kc p) c -> p kc c", p=128)

            def epi(ch, ti, t0, n, ps, rows):
                s, b = stg[cnt[0] % 3], stb[cnt[0] % 3]
                cnt[0] += 1
                k.op("dve", lambda e: e.tensor_scalar_max(out=s[:, 0:n], in0=ps[:, 0:n], scalar1=0.0), [ps], [s])
                k.op("act", lambda e: e.activation(out=b[:, 0:n], in_=s[:, 0:n], func=AF.Square), [s], [b])
                k.dma("pool", self.HID.t[ch[0]:ch[1], t0:t0 + n], b[:, 0:n], [b], [], b)
            self.gemm_fm(st, hT, 8, lambda c0, c1: w1[:, :, c0:c1], [(c * 128, (c + 1) * 128) for c in range(32)], tiles, epi)
            k.barrier()
        for q in range(4):
            with contextlib.ExitStack() as st:
                hin = self.sb(st, [128, 8, self.T], BF16, "hin")
                k.dma("sp", hin[:, :, :], self.HID.t[q * 1024:(q + 1) * 1024, :].rearrange("(kc p) t -> p kc t", p=128), [], [hin], hin)
                w2 = self.W["mlp_w2"][l, q * 1024:(q + 1) * 1024, :].rearrange("(kc p) c -> p kc c", p=128)
                self.resid_gemm(st, hin, 8, w2, 40, tiles)
                k.barrier()

    def attend(self, st, terms, vfn, M, kblocks, qtiles, scale, ones_sum, epi, ptag=0):
        k = self.k
        pt_ = self.pt_
        for qi, (qc0, t0, n) in enumerate(qtiles):
            O = self.psb[4 + 2 * ptag]
            Sps = self.psb[5 + 2 * ptag]
            nk = len(kblocks)
            sps = {}

            def score(i):
                sp = self.psb[self.sci % 4]
                self.sci += 1
                sps[i] = sp
                kb = kblocks[i]
                for j, (K_sb, Q_sb, r0, rows) in enumerate(terms):
                    k.op("pe", lambda e: e.matmul(sp[:, 0:n], lhsT=K_sb[r0:r0 + rows, kb * 128:(kb + 1) * 128], rhs=Q_sb[r0:r0 + rows, qc0:qc0 + n],
                                                  start=(j == 0), stop=(j == len(terms) - 1)), [K_sb, Q_sb], [sp])
            score(0)
            for i in range(nk):
                if i + 1 < nk:
                    score(i + 1)
                pt = pt_[self.pti % 3]
                self.pti += 1
                sp = sps.pop(i)
                k.op("act", lambda e: e.activation(out=pt[:, 0:n], in_=sp[:, 0:n], func=AF.Exp, scale=scale), [sp], [pt])
                k.op("pe", lambda e: e.matmul(O[0:M, 0:n], lhsT=vfn(kblocks[i]), rhs=pt[:, 0:n], start=(i == 0), stop=(i == nk - 1)), [self.vbuf, pt], [O])
                if ones_sum:
                    k.op("pe", lambda e: e.matmul(Sps[:, 0:n], lhsT=self.C_("ones", True), rhs=pt[:, 0:n], start=(i == 0), stop=(i == nk - 1)), [self.cbf, pt], [Sps])
            epi(qi, t0, n, O, Sps)

    def attn_common(self, st):
        self.pt_ = [self.sb(st, [128, 512], BF16, "pt") for _ in range(3)]
        self.sci = 0
        self.pti = 0

    def qtiles(self, lat):
        if lat:
            return [(t0, t0, n) for (t0, n) in self.tiles[1:]]
        return [(0, 0, self.C)]

    def phase_diff(self, l):
        k = self.k
        W = self.W
        T, NB = self.T, self.NB
        lam_init = 0.8 - 0.6 * math.exp(-0.3 * l)
        QD = self.QD
        KD = self.KD
        with contextlib.ExitStack() as st:
            gq = self.sb(st, [128, 2], F32, "dg")
            for j, nm_ in enumerate(("diff_qn_g", "diff_kn_g")):
                for hh in range(2):
                    k.dma("sp", gq[hh * 64:(hh + 1) * 64, j:j + 1], W[nm_][l].rearrange("(p o) -> p o", o=1), [], [gq], gq, allow_slow_non_contiguous=True)
            u_ = [self.sb(st, [128, 512], F32, "du") for _ in range(2)]
            sq_ = [self.sb(st, [128, 512], F32, "dsq") for _ in range(2)]
            rs_ = [self.sb(st, [128, 512], F32, "drs") for _ in range(2)]
            cs_ = [self.sb(st, [128, 2, 512], F32, "dcs") for _ in range(2)]
            o_ = [self.sb(st, [128, 512], F32, "do") for _ in range(2)]
            ob_ = [self.sb(st, [128, 512], BF16, "dob") for _ in range(2)]
            it = 0
            for j, (col0, dst) in enumerate(((DQ, QD), (DK, KD))):
                for h in range(4):
                    r0 = self.urow(col0 + h * 128)
                    for ti, (t0, n) in enumerate(self.tiles):
                        u, sq, rs, cs, o, ob = u_[it % 2], sq_[it % 2], rs_[it % 2], cs_[it % 2], o_[it % 2], ob_[it % 2]
                        it += 1
                        k.dma("sp", u[:, 0:n], self.U.t[r0:r0 + 128, t0:t0 + n], [], [u], u)
                        k.dma("sp", cs[:, :, 0:n], self.roped_in[:, :, t0:t0 + n], [], [cs], cs)
                        k.op("act", lambda e: e.activation(out=sq[:, 0:n], in_=u[:, 0:n], func=AF.Square), [u], [sq])
                        ps = self.psum()
                        k.op("pe", lambda e: e.matmul(ps[:, 0:n], lhsT=self.C_("bd64"), rhs=sq[:, 0:n], start=True, stop=True), [self.cst, sq], [ps])
                        self.rstd_from(ps, n, rs, 1.0 / 64)
                        k.op("dve", lambda e: e.scalar_tensor_tensor(out=u[:, 0:n], in0=u[:, 0:n], scalar=gq[:, j:j + 1], in1=rs[:, 0:n], op0=ALU.mult, op1=ALU.mult), [u, gq, rs], [u])
                        ps2 = self.psum()
                        k.op("pe", lambda e: e.matmul(ps2[:, 0:n], lhsT=self.C_("perm64"), rhs=u[:, 0:n], start=True, stop=True), [self.cst, u], [ps2])
                        k.op("dve", lambda e: e.tensor_tensor(out=o[:, 0:n], in0=u[:, 0:n], in1=cs[:, 0, 0:n], op=ALU.mult), [u, cs], [o])
                        k.op("dve", lambda e: e.tensor_tensor(out=sq[:, 0:n], in0=ps2[:, 0:n], in1=cs[:, 1, 0:n], op=ALU.mult), [ps2, cs], [sq])
                        k.op("dve", lambda e: e.tensor_tensor(out=ob[:, 0:n], in0=o[:, 0:n], in1=sq[:, 0:n], op=ALU.add), [o, sq], [ob])
                        k.dma("pool", dst.t[h * 128:(h + 1) * 128, t0:t0 + n], ob[:, 0:n], [ob], [], ob)
            k.barrier()
        with contextlib.ExitStack() as st:
            self.attn_common(st)
            lamt2 = self.sb(st, [128, 256], F32, "lamt")
            k.dma("sp", lamt2[:, :], W["diff_lambda"][l:l + 1].rearrange("o a b -> o (a b)").partition_broadcast(128), [], [lamt2], lamt2)

            lv = self.sb(st, [128, 4], F32, "lv")
            lt = self.sb(st, [128, 2, 64], F32, "lt")
            for j in range(2):
                k.op("dve", lambda e: e.tensor_tensor(out=lt[:, j, :], in0=lamt2[:, (2 * j) * 64:(2 * j + 1) * 64], in1=lamt2[:, (2 * j + 1) * 64:(2 * j + 2) * 64], op=ALU.mult), [lamt2], [lt])
            k.op("dve", lambda e: e.reduce_sum(out=lv[:, 0:2], in_=lt[:, :, :], axis=mybir.AxisListType.X), [lt], [lv])
            k.op("act", lambda e: e.activation(out=lv[:, 0:2], in_=lv[:, 0:2], func=AF.Exp), [lv], [lv])
            k.op("dve", lambda e: e.scalar_tensor_tensor(out=lv[:, 2:3], in0=lv[:, 1:2], scalar=-lam_init, in1=lv[:, 0:1], op0=ALU.add, op1=ALU.subtract), [lv], [lv])
            sg = self.vec_col(st, W["diff_sub_g"][l], 128, "sg")
            k.op("dve", lambda e: e.tensor_scalar(out=sg[:, :], in0=sg[:, :], scalar1=(1.0 - lam_init), scalar2=None, op0=ALU.mult), [sg], [sg])
            qh = self.sb(st, [128, T], BF16, "dqh")
            khm = [self.sb(st, [128, T], BF16, "dkh") for _ in range(2)]
            for m_ in range(2):
                k.op("pool", lambda e: e.memset(khm[m_][:, :], 0.0), [], [khm[m_]])
            vh = self.sb(st, [128, NB, 128], BF16, "dvh")
            self.vbuf = vh
            ra = [self.sb(st, [128, 512], F32, "ra") for _ in range(2)]
            aa = [self.sb(st, [128, 512], F32, "aa") for _ in range(2)]
            dd = self.sb(st, [128, 512], F32, "dd")
            yb = [self.sb(st, [128, 512], BF16, "dyb") for _ in range(2)]
            for h in range(4):
                k.dma("sp", qh[:, :], QD.t[h * 128:(h + 1) * 128, :], [], [qh], qh)
                for m_ in range(2):
                    k.dma("sp", khm[m_][m_ * 64:(m_ + 1) * 64, :], KD.t[h * 128 + m_ * 64:h * 128 + (m_ + 1) * 64, :], [], [khm[m_]], khm[m_])
                k.dma("sp", vh[:, :, :], self.VD.t[:, h * 128:(h + 1) * 128].rearrange("(b p) d -> p b d", p=128), [], [vh], vh)
                passes = [(True, list(range(NB)))]
                if not self.last:
                    passes.append((False, list(range(self.C // 128))))
                for lat, kbl in passes:
                    for qi, qt in enumerate(self.qtiles(lat)):
                        res = {}
                        for m in range(2):
                            def epi(qi_, t0, n, O, Sps, m=m):
                                res[m] = (O, Sps)
                            self.attend(st, [(khm[m], qh, 0, 128)], lambda kb: vh[:, kb, :], 128, kbl, [qt], 64 ** -0.5, True, epi, ptag=m)
                        (qc0, t0, n) = qt
                        for m in range(2):
                            O, Sps = res[m]
                            k.op("dve", lambda e: e.reciprocal(out=ra[m][:, 0:n], in_=Sps[:, 0:n]), [Sps], [ra[m]])
                            k.op("dve", lambda e: e.tensor_tensor(out=aa[m][:, 0:n], in0=O[:, 0:n], in1=ra[m][:, 0:n], op=ALU.mult), [O, ra[m]], [aa[m]])
                        k.op("dve", lambda e: e.scalar_tensor_tensor(out=dd[:, 0:n], in0=aa[1][:, 0:n], scalar=lv[:, 2:3], in1=aa[0][:, 0:n], op0=ALU.mult, op1=ALU.add), [aa[0], aa[1], lv], [dd])
                        k.op("act", lambda e: e.activation(out=aa[0][:, 0:n], in_=dd[:, 0:n], func=AF.Square), [dd], [aa[0]])
                        ps = self.psb[self.sci % 4]
                        self.sci += 1
                        k.op("pe", lambda e: e.matmul(ps[:, 0:n], lhsT=self.C_("ones"), rhs=aa[0][:, 0:n], start=True, stop=True), [self.cst, aa[0]], [ps])
                        self.rstd_from(ps, n, ra[0], 1.0 / 128)
                        y = yb[qi % 2]
                        k.op("dve", lambda e: e.scalar_tensor_tensor(out=y[:, 0:n], in0=dd[:, 0:n], scalar=sg[:, 0:1], in1=ra[0][:, 0:n], op0=ALU.mult, op1=ALU.mult), [dd, sg, ra[0]], [y])
                        k.dma("pool", self.Y.t[1024 + h * 128:1024 + (h + 1) * 128, t0:t0 + n], y[:, 0:n], [y], [], y)
            k.barrier()

    def phase_mla(self, l):
        k = self.k
        W = self.W
        T, NB = self.T, self.NB
        with contextlib.ExitStack() as st:
            cn = self.sb(st, [128, 5, T], BF16, "cn")
            RK = self.sb(st, [32, T], F32, "RK")
            SQPE = self.sb(st, [32, T], F32, "SQPE")
            gkv = self.sb(st, [128, 5], F32, "gkv")
            k.dma("sp", gkv[:, 0:2], W["mla_kv_lora_g"][l].rearrange("(kc p) -> p kc", p=128), [], [gkv], gkv, allow_slow_non_contiguous=True)
            k.dma("sp", gkv[:, 2:5], W["mla_q_lora_g"][l].rearrange("(kc p) -> p kc", p=128), [], [gkv], gkv, allow_slow_non_contiguous=True)
            gk = self.sb(st, [128, 4], F32, "gk")
            for j, nm_ in enumerate(("mla_kn_g", "mla_qn_g")):
                k.dma("sp", gk[0:64, 2 * j:2 * j + 1], W[nm_][l, 0:64].rearrange("(p o) -> p o", o=1), [], [gk], gk, allow_slow_non_contiguous=True)
                k.dma("sp", gk[0:32, 2 * j + 1:2 * j + 2], W[nm_][l, 64:96].rearrange("(p o) -> p o", o=1), [], [gk], gk, allow_slow_non_contiguous=True)
            with contextlib.ExitStack() as st2:
                u_ = [self.sb(st2, [128, 3, 512], F32, "mu") for _ in range(2)]
                sq_ = [self.sb(st2, [128, 3, 512], F32, "msq") for _ in range(2)]
                rs_ = [self.sb(st2, [128, 512], F32, "mrs") for _ in range(2)]
                it = 0
                for (col0, kc_n, dst0, nfeat) in ((MCKV, 2, 0, 256), (MCQ, 3, 2, 384)):
                    r0 = self.urow(col0)
                    for ti, (t0, n) in enumerate(self.tiles):
                        u, sq, rs = u_[it % 2], sq_[it % 2], rs_[it % 2]
                        it += 1
                        k.dma("sp", u[:, 0:kc_n, 0:n], self.U.t[r0:r0 + kc_n * 128, t0:t0 + n].rearrange("(kc p) t -> p kc t", p=128), [], [u], u)
                        k.op("act", lambda e: e.activation(out=sq[:, 0:kc_n, 0:n], in_=u[:, 0:kc_n, 0:n], func=AF.Square), [u], [sq])
                        ps = self.psum()
                        for kc in range(kc_n):
                            k.op("pe", lambda e: e.matmul(ps[:, 0:n], lhsT=self.C_("ones"), rhs=sq[:, kc, 0:n], start=(kc == 0), stop=(kc == kc_n - 1)), [self.cst, sq], [ps])
                        self.rstd_from(ps, n, rs, 1.0 / nfeat)
                        for kc in range(kc_n):
                            k.op("dve", lambda e: e.scalar_tensor_tensor(out=cn[:, dst0 + kc, t0:t0 + n], in0=u[:, kc, 0:n], scalar=gkv[:, dst0 + kc:dst0 + kc + 1], in1=rs[:, 0:n],
                                                                         op0=ALU.mult, op1=ALU.mult), [u, gkv, rs], [cn])
                r0 = self.urow(MKPE)
                kp = self.sb(st2, [32, T], F32, "kp")
                k.dma("sp", kp[:, :], self.U.t[r0:r0 + 32, :], [], [kp], kp)
                k.op("act", lambda e: e.activation(out=SQPE[:, :], in_=kp[:, :], func=AF.Square), [kp], [SQPE])
                k.op("dve", lambda e: e.tensor_scalar(out=kp[:, :], in0=kp[:, :], scalar1=gk[0:32, 1:2], scalar2=None, op0=ALU.mult), [kp, gk], [kp])
                self.rope32(st2, kp, RK)
                k.barrier()
            with contextlib.ExitStack() as st2:
                sq_ = [self.sb(st2, [64, 512], F32, "ksq") for _ in range(2)]
                rs_ = [self.sb(st2, [128, 512], F32, "krs") for _ in range(2)]
                kn_ = [self.sb(st2, [64, 512], BF16, "kn") for _ in range(2)]
                kr_ = [self.sb(st2, [32, 512], BF16, "kr") for _ in range(2)]
                cnt = [0]
                wkv = W["mla_w_ukv"][l].rearrange("(kc p) c -> p kc c", p=128)

                def epik(ch, ti, t0, n, ps, rows):
                    h = ch[0] // 128
                    i = cnt[0] % 2
                    cnt[0] += 1
                    sq, rs, kn, kr = sq_[i], rs_[i], kn_[i], kr_[i]
                    k.op("act", lambda e: e.activation(out=sq[:, 0:n], in_=ps[0:64, 0:n], func=AF.Square), [ps], [sq])
                    p2 = self.psum()
                    k.op("pe", lambda e: e.matmul(p2[:, 0:n], lhsT=self.cst[0:64, 1, :], rhs=sq[0:64, 0:n], start=True, stop=False), [self.cst, sq], [p2])
                    k.op("pe", lambda e: e.matmul(p2[:, 0:n], lhsT=self.cst[0:32, 1, :], rhs=SQPE[0:32, t0:t0 + n], start=False, stop=True), [self.cst, SQPE], [p2])
                    self.rstd_from(p2, n, rs, 1.0 / 96)
                    k.op("dve", lambda e: e.scalar_tensor_tensor(out=kn[:, 0:n], in0=ps[0:64, 0:n], scalar=gk[0:64, 0:1], in1=rs[0:64, 0:n], op0=ALU.mult, op1=ALU.mult), [ps, gk, rs], [kn])
                    k.op("dve", lambda e: e.tensor_tensor(out=kr[:, 0:n], in0=RK[:, t0:t0 + n], in1=rs[0:32, 0:n], op=ALU.mult), [RK, rs], [kr])
                    k.dma("pool", self.KN.t[h * 64:(h + 1) * 64, t0:t0 + n], kn[:, 0:n], [kn], [], kn)
                    k.dma("pool", self.KR.t[h * 32:(h + 1) * 32, t0:t0 + n], kr[:, 0:n], [kr], [], kr)
                self.gemm_fm(st2, cn, 2, lambda c0, c1: wkv[:, :, c0:c1], [(h * 128, h * 128 + 64) for h in range(8)], self.tiles, epik)
                va_ = [self.sb(st2, [128, 8, 65], BF16, "va") for _ in range(2)]
                for v in va_:
                    k.op("dve", lambda e: e.memset(v[:, :, :], 1.0), [], [v])

                def epiv(blk, ps):
                    v = va_[blk % 2]
                    k.op("act", lambda e: e.copy(out=v[:, :, 0:64], in_=ps[:, 0:512].rearrange("p (h d) -> p h d", h=8)), [ps], [v])
                    k.dma("pool", self.VA.t[blk * 128:(blk + 1) * 128, :], v[:, :, :].rearrange("p h d -> p (h d)"), [v], [], v)
                wv5 = W["mla_w_ukv"][l].rearrange("(kc p) (h two d) -> p kc h two d", p=128, two=2, d=64)

                def wsrc(s_):
                    for kc in range(2):
                        k.dma("sp", s_[:, kc, :].rearrange("p (h d) -> p h d", h=8), wv5[:, kc, :, 1, :], [], [s_], s_)
                self.gemm_tm(st2, cn, 2, wsrc, 512, list(range(NB)), epiv)
                k.barrier()
            with contextlib.ExitStack() as st2:
                sqn_ = [self.sb(st2, [64, 512], F32, "qsq") for _ in range(2)]
                sqr_ = [self.sb(st2, [32, 512], F32, "qsr") for _ in range(2)]
                rs_ = [self.sb(st2, [128, 512], F32, "qrs") for _ in range(2)]
                qn_ = [self.sb(st2, [64, 512], BF16, "qn") for _ in range(2)]
                qr_ = [self.sb(st2, [32, 512], BF16, "qr") for _ in range(2)]
                xr_ = [self.sb(st2, [32, 512], F32, "xr") for _ in range(2)]
                ro_ = [self.sb(st2, [32, 512], F32, "ro") for _ in range(2)]
                cs_ = [self.sb(st2, [32, 2, 512], F32, "qcs") for _ in range(2)]
                cnt = [0]
                held = {}
                wq = W["mla_w_uq"][l].rearrange("(kc p) c -> p kc c", p=128)

                def epiq(ch, ti, t0, n, ps, rows):
                    if rows == 64:
                        held["n"] = ps
                        return
                    psn, psr = held["n"], ps
                    h = ch[0] // 96
                    i = cnt[0] % 2
                    cnt[0] += 1
                    sqn, sqr, rs, qn, qr, xr, ro, cs = sqn_[i], sqr_[i], rs_[i], qn_[i], qr_[i], xr_[i], ro_[i], cs_[i]
                    k.dma("sp", cs[:, :, 0:n], self.ropem_in[:, :, t0:t0 + n], [], [cs], cs)
                    k.op("act", lambda e: e.activation(out=sqn[:, 0:n], in_=psn[0:64, 0:n], func=AF.Square), [psn], [sqn])
                    k.op("act", lambda e: e.activation(out=sqr[:, 0:n], in_=psr[0:32, 0:n], func=AF.Square), [psr], [sqr])
                    p2 = self.psum()
                    k.op("pe", lambda e: e.matmul(p2[:, 0:n], lhsT=self.cst[0:64, 1, :], rhs=sqn[0:64, 0:n], start=True, stop=False), [self.cst, sqn], [p2])
                    k.op("pe", lambda e: e.matmul(p2[:, 0:n], lhsT=self.cst[0:32, 1, :], rhs=sqr[0:32, 0:n], start=False, stop=True), [self.cst, sqr], [p2])
                    self.rstd_from(p2, n, rs, 1.0 / 96)
                    k.op("dve", lambda e: e.scalar_tensor_tensor(out=qn[:, 0:n], in0=psn[0:64, 0:n], scalar=gk[0:64, 2:3], in1=rs[0:64, 0:n], op0=ALU.mult, op1=ALU.mult), [psn, gk, rs], [qn])
                    k.op("dve", lambda e: e.tensor_scalar(out=xr[:, 0:n], in0=psr[0:32, 0:n], scalar1=gk[0:32, 3:4], scalar2=None, op0=ALU.mult), [psr, gk], [xr])
                    p3 = self.psum()
                    k.op("pe", lambda e: e.matmul(p3[0:32, 0:n], lhsT=self.cst[0:32, CN.index("perm32"), 0:32], rhs=xr[0:32, 0:n], start=True, stop=True), [self.cst, xr], [p3])
                    k.op("dve", lambda e: e.tensor_tensor(out=ro[:, 0:n], in0=p3[0:32, 0:n], in1=cs[:, 1, 0:n], op=ALU.mult), [p3, cs], [ro])
                    k.op("dve", lambda e: e.tensor_tensor(out=xr[:, 0:n], in0=xr[:, 0:n], in1=cs[:, 0, 0:n], op=ALU.mult), [xr, cs], [xr])
                    k.op("dve", lambda e: e.tensor_tensor(out=xr[:, 0:n], in0=xr[:, 0:n], in1=ro[:, 0:n], op=ALU.add), [xr, ro], [xr])
                    k.op("dve", lambda e: e.tensor_tensor(out=qr[:, 0:n], in0=xr[:, 0:n], in1=rs[0:32, 0:n], op=ALU.mult), [xr, rs], [qr])
                    k.dma("pool", self.QN.t[h * 64:(h + 1) * 64, t0:t0 + n], qn[:, 0:n], [qn], [], qn)
                    k.dma("pool", self.QR.t[h * 32:(h + 1) * 32, t0:t0 + n], qr[:, 0:n], [qr], [], qr)
                chq = []
                for h in range(8):
                    chq += [(h * 96, h * 96 + 64), (h * 96 + 64, h * 96 + 96)]
                self.gemm_fm(st2, Buf(cn.t[:, 2:5, :]), 3, lambda c0, c1: wq[:, :, c0:c1], chq, self.tiles, epiq)
                k.barrier()
        with contextlib.ExitStack() as st:
            self.attn_common(st)
            kqh = self.sb(st, [96, T], BF16, "kqh")
            qqh = self.sb(st, [96, T], BF16, "qqh")
            vah = self.sb(st, [128, NB, 65], BF16, "vah")
            self.vbuf = vah
            osb = [self.sb(st, [65, 512], F32, "osb") for _ in range(2)]
            rr = [self.sb(st, [64, 512], F32, "rr") for _ in range(2)]
            yb = [self.sb(st, [64, 512], BF16, "myb") for _ in range(2)]
            cnt = [0]
            for h in range(8):
                k.dma("sp", kqh[0:64, :], self.KN.t[h * 64:(h + 1) * 64, :], [], [kqh], kqh)
                k.dma("sp", kqh[64:96, :], self.KR.t[h * 32:(h + 1) * 32, :], [], [kqh], kqh)
                k.dma("sp", qqh[0:64, :], self.QN.t[h * 64:(h + 1) * 64, :], [], [qqh], qqh)
                k.dma("sp", qqh[64:96, :], self.QR.t[h * 32:(h + 1) * 32, :], [], [qqh], qqh)
                k.dma("sp", vah[:, :, :], self.VA.t[:, h * 65:(h + 1) * 65].rearrange("(b p) d -> p b d", p=128), [], [vah], vah)

                def epi(qi, t0, n, O, Sps):
                    i = cnt[0] % 2
                    cnt[0] += 1
                    o, r, y = osb[i], rr[i], yb[i]
                    k.op("act", lambda e: e.copy(out=o[0:65, 0:n], in_=O[0:65, 0:n]), [O], [o])
                    ps = self.psb[self.sci % 4]
                    self.sci += 1
                    k.op("pe", lambda e: e.matmul(ps[0:64, 0:n], lhsT=self.cst[0:65, CN.index("sel65"), 0:64], rhs=o[0:65, 0:n], start=True, stop=True), [self.cst, o], [ps])
                    k.op("dve", lambda e: e.reciprocal(out=r[:, 0:n], in_=ps[0:64, 0:n]), [ps], [r])
                    k.op("dve", lambda e: e.tensor_tensor(out=y[:, 0:n], in0=o[0:64, 0:n], in1=r[:, 0:n], op=ALU.mult), [o, r], [y])
                    k.dma("pool", self.Y.t[512 + h * 64:512 + (h + 1) * 64, t0:t0 + n], y[:, 0:n], [y], [], y)
                terms = [(kqh, qqh, 0, 96)]
                self.attend(st, terms, lambda kb: vah[:, kb, :], 65, list(range(NB)), self.qtiles(True), 96 ** -0.5, False, epi)
                if not self.last:
                    self.attend(st, terms, lambda kb: vah[:, kb, :], 65, list(range(self.C // 128)), self.qtiles(False), 96 ** -0.5, False, epi)
            k.barrier()

    def rope32(self, st, xin, dst):
        k = self.k
        cs_ = [self.sb(st, [32, 2, 512], F32, "rcs") for _ in range(2)]
        ro_ = [self.sb(st, [32, 512], F32, "rro") for _ in range(2)]
        for ti, (t0, n) in enumerate(self.tiles):
            cs, ro = cs_[ti % 2], ro_[ti % 2]
            k.dma("sp", cs[:, :, 0:n], self.ropem_in[:, :, t0:t0 + n], [], [cs], cs)
            ps = self.psum()
            k.op("pe", lambda e: e.matmul(ps[0:32, 0:n], lhsT=self.cst[0:32, CN.index("perm32"), 0:32], rhs=xin[0:32, t0:t0 + n], start=True, stop=True), [self.cst, xin], [ps])
            k.op("dve", lambda e: e.tensor_tensor(out=ro[:, 0:n], in0=ps[0:32, 0:n], in1=cs[:, 1, 0:n], op=ALU.mult), [ps, cs], [ro])
            k.op("dve", lambda e: e.tensor_tensor(out=dst[:, t0:t0 + n], in0=xin[0:32, t0:t0 + n], in1=cs[:, 0, 0:n], op=ALU.mult), [xin, cs], [dst])
            k.op("dve", lambda e: e.tensor_tensor(out=dst[:, t0:t0 + n], in0=dst[:, t0:t0 + n], in1=ro[:, 0:n], op=ALU.add), [dst, ro], [dst])

    def conv_silu(self, st, u, acc, wc, bias, out):
        k = self.k
        C, T = self.C, self.T
        k.op("dve", lambda e: e.tensor_scalar(out=acc[:, :], in0=u[:, :], scalar1=wc[:, 2:3], scalar2=None, op0=ALU.mult), [u, wc], [acc])
        for j in (0, 1, 3, 4):
            s = j - 2
            for (s0, s1) in ((0, C), (C, T)):
                a = max(s0, s0 - s)
                b = min(s1, s1 - s)
                k.op("dve", lambda e: e.scalar_tensor_tensor(out=acc[:, a:b], in0=u[:, a + s:b + s], scalar=wc[:, j:j + 1], in1=acc[:, a:b], op0=ALU.mult, op1=ALU.add), [u, wc, acc], [acc])
        if bias is None:
            k.op("act", lambda e: e.activation(out=out, in_=acc[:, :], func=AF.Silu), [acc], [acc])
        else:
            k.op("act", lambda e: e.activation(out=out, in_=acc[:, :], func=AF.Silu, bias=bias, scale=1.0), [acc, wc], [acc])

    def softplus(self, st, xb, xap, n):
        k = self.k
        t = self.sb(st, [128, n], F32, "spt")

        class _X:
            def __getitem__(s_, key):
                return xap
        x = _X()
        k.op("act", lambda e: e.activation(out=t[:, :], in_=xap, func=AF.Abs), [xb], [t])
        k.op("act", lambda e: e.activation(out=t[:, :], in_=t[:, :], func=AF.Exp, scale=-1.0), [t], [t])
        k.op("act", lambda e: e.activation(out=t[:, :], in_=t[:, :], func=AF.Ln, bias=self.epsb[:, 1:2], scale=1.0), [t, self.epsb], [t])
        k.op("dve", lambda e: e.scalar_tensor_tensor(out=xap, in0=xap, scalar=0.0, in1=t[:, :], op0=ALU.max, op1=ALU.add), [xb, t], [xb])

    def tok_scalars(self, st, col0, ncols, dst):
        k = self.k
        r0 = self.urow(col0)
        raw = self.sb(st, [ncols, self.T], F32, "tsr")
        k.dma("sp", raw[:, :], self.U.t[r0:r0 + ncols, :], [], [raw], raw)
        for blk in range(self.NB):
            ps = self.psum()
            k.op("pe", lambda e: e.transpose(out=ps[:, 0:ncols], in_=raw[0:ncols, blk * 128:(blk + 1) * 128], identity=self.cst[0:ncols, 0, 0:ncols]), [raw, self.cst], [ps])
            k.op("act", lambda e: e.copy(out=dst[:, blk, :], in_=ps[:, 0:ncols]), [ps], [dst])

    def blk_order(self, d):
        nc_ = self.C // 128
        if d == 0:
            return list(range(self.NB))
        return list(range(nc_ - 1, -1, -1)) + list(range(self.NB - 1, nc_ - 1, -1))

    def phase_ssm(self, l):
        k = self.k
        W = self.W
        T, NB = self.T, self.NB
        with contextlib.ExitStack() as st:
            XTOK = self.sb(st, [128, NB, 512], F32, "XTOK")
            BT = self.sb(st, [128, 2, T], BF16, "BT")
            CT = self.sb(st, [128, 2, T], BF16, "CT")
            BTOK = self.sb(st, [128, NB, 2, 128], BF16, "BTOK")
            DT = self.sb(st, [128, NB, 16], F32, "DT")
            DA = self.sb(st, [128, NB, 16], F32, "DA")
            ACUM = self.sb(st, [128, NB, 16], F32, "ACUM")
            ATOT = self.sb(st, [128, NB, 16], F32, "ATOT")
            CD = self.sb(st, [128, NB, 16], F32, "CD")
            DTDS = self.sb(st, [128, NB, 16], F32, "DTDS")
            with contextlib.ExitStack() as st2:
                wc_ = [self.sb(st2, [128, 6], F32, "swc") for _ in range(2)]
                st2a = contextlib.ExitStack()
                u_ = [self.sb(st2a, [128, T], F32, "su") for _ in range(1)] * 2
                acc_ = [self.sb(st2a, [128, T], F32, "sacc") for _ in range(1)] * 2
                for c in range(8):
                    u, acc, wc = u_[c % 2], acc_[c % 2], wc_[c % 2]
                    r0 = self.urow(SX + c * 128)
                    k.dma("sp", u[:, :], self.U.t[r0:r0 + 128, :], [], [u], u)
                    k.dma("sp", wc[:, 0:5], W["ssm_conv"][l][:, c * 128:(c + 1) * 128].rearrange("j c -> c j"), [], [wc], wc, allow_slow_non_contiguous=True)
                    k.dma("sp", wc[:, 5:6], W["ssm_conv_b"][l, c * 128:(c + 1) * 128].rearrange("(p o) -> p o", o=1), [], [wc], wc, allow_slow_non_contiguous=True)
                    if c < 4:
                        self.conv_silu(st2, u, acc, wc, wc[:, 5:6], acc[:, :])
                        k.dma("pool", self.XS.t[c * 128:(c + 1) * 128, :], acc[:, :], [acc], [], acc)
                        for blk in range(NB):
                            ps = self.psum()
                            k.op("pe", lambda e: e.transpose(out=ps[:, 0:128], in_=acc[:, blk * 128:(blk + 1) * 128], identity=self.C_("ident")), [acc, self.cst], [ps])
                            k.op("act", lambda e: e.copy(out=XTOK[:, blk, c * 128:(c + 1) * 128], in_=ps[:, 0:128]), [ps], [XTOK])
                    elif c < 6:
                        g = c - 4
                        self.conv_silu(st2, u, acc, wc, wc[:, 5:6], acc[:, :])
                        k.op("dve", lambda e: e.tensor_copy(out=BT[:, g, :], in_=acc[:, :]), [acc], [BT])
                        for blk in range(NB):
                            ps = self.psum()
                            k.op("pe", lambda e: e.transpose(out=ps[:, 0:128], in_=acc[:, blk * 128:(blk + 1) * 128], identity=self.C_("ident")), [acc, self.cst], [ps])
                            k.op("act", lambda e: e.copy(out=BTOK[:, blk, g, :], in_=ps[:, 0:128]), [ps], [BTOK])
                    else:
                        g = c - 6
                        self.conv_silu(st2, u, acc, wc, wc[:, 5:6], acc[:, :])
                        k.op("dve", lambda e: e.tensor_copy(out=CT[:, g, :], in_=acc[:, :]), [acc], [CT])
                k.barrier()
                st2a.close()
                self.tok_scalars(st2, SDT, 16, DT)
                pb = self.sb(st2, [128, 2, 16], F32, "spb")
                k.dma("sp", pb[:, 0, :], W["ssm_dt_bias"][l:l + 1].rearrange("o a b -> o (a b)").partition_broadcast(128), [], [pb], pb)
                k.dma("sp", pb[:, 1, :], W["ssm_a_log"][l:l + 1].rearrange("o a b -> o (a b)").partition_broadcast(128), [], [pb], pb)
                k.op("dve", lambda e: e.tensor_tensor(out=DT[:, :, :], in0=DT[:, :, :], in1=pb[:, 0, :].unsqueeze(1).to_broadcast([128, NB, 16]), op=ALU.add), [DT, pb], [DT])
                self.softplus(st2, DT, DT.t[:, :, :].rearrange("p b c -> p (b c)"), NB * 16)
                k.op("act", lambda e: e.activation(out=pb[:, 1, :], in_=pb[:, 1, :], func=AF.Exp), [pb], [pb])
                k.op("dve", lambda e: e.scalar_tensor_tensor(out=DA[:, :, :], in0=DT[:, :, :], scalar=-1.0, in1=pb[:, 1, :].unsqueeze(1).to_broadcast([128, NB, 16]), op0=ALU.mult, op1=ALU.mult), [DT, pb], [DA])
                self.cum_stats(st2, DA, ACUM, ATOT, 8)
                k.op("act", lambda e: e.activation(out=CD[:, :, :], in_=ATOT[:, :, :], func=AF.Exp), [ATOT], [CD])
                k.op("dve", lambda e: e.tensor_tensor(out=DTDS[:, :, :], in0=ATOT[:, :, :], in1=ACUM[:, :, :], op=ALU.subtract), [ATOT, ACUM], [DTDS])
                k.op("act", lambda e: e.activation(out=DTDS[:, :, :], in_=DTDS[:, :, :], func=AF.Exp), [DTDS], [DTDS])
                k.op("dve", lambda e: e.tensor_tensor(out=DTDS[:, :, :], in0=DTDS[:, :, :], in1=DT[:, :, :], op=ALU.mult), [DTDS, DT], [DTDS])
                k.barrier()
            ST = self.sb(st, [128, 4, 4, 64], F32, "ST")
            STb = self.sb(st, [128, 4, 4, 64], BF16, "STb")
            k.op("dve", lambda e: e.memset(ST[:, :, :, :], 0.0), [], [ST])
            k.op("dve", lambda e: e.memset(STb[:, :, :, :], 0.0), [], [STb])
            R = 3
            rhsb = [self.sb(st, [128, 4, 128], F32, "srhs") for _ in range(R)]
            Dm = [self.sb(st, [128, 4, 128], F32, "sD") for _ in range(R)]
            LT = [self.sb(st, [128, 4, 128], F32, "sLT") for _ in range(R)]
            RE = [self.sb(st, [128, 4, 128], F32, "sRE") for _ in range(R)]
            WT = [self.sb(st, [128, 4, 128], BF16, "sWT") for _ in range(R)]
            CdT = [self.sb(st, [128, 4, 128], BF16, "sCd") for _ in range(R)]
            xdt = [self.sb(st, [128, 4, 64], BF16, "sxdt") for _ in range(R)]
            xdd = [self.sb(st, [128, 4, 64], BF16, "sxdd") for _ in range(R)]
            SCs = [self.sb(st, [128, 128], F32, "sSC") for _ in range(R)]
            yo = [self.sb(st, [64, 4, 128], F32, "syo") for _ in range(R)]
            STs = {(d, g): Buf(None) for d in range(2) for g in range(2)}
            it = 0
            orders = [self.blk_order(0), self.blk_order(1)]
            sppi = [0]

            def spre_ps():
                sppi[0] = (sppi[0] + 1) % 6
                return self.psb[sppi[0]]

            def make_sunit(i, d, g, j):
                blk = orders[d][i]
                tri = self.C_("triF" if d == 0 else "triB")
                mneg = self.C_("mnegF" if d == 0 else "mnegB")
                ch = d * 2 + g
                sbuf_ = STs[(d, g)]
                hs = slice(d * 8 + g * 4, d * 8 + g * 4 + 4)

                def pre_fn():
                    k.op("dve", lambda e: e.tensor_tensor(out=rhsb[j][:, :, :], in0=tri.unsqueeze(1).to_broadcast([128, 4, 128]),
                                                          in1=DA[:, blk, hs].unsqueeze(2).to_broadcast([128, 4, 128]), op=ALU.mult), [self.cst, DA], [rhsb[j]])
                    yield
                    pa = spre_ps()
                    k.op("pe", lambda e: e.matmul(pa[:, :], lhsT=self.C_("ones"), rhs=rhsb[j][:, :, :].rearrange("p h c -> p (h c)"), start=True, stop=True), [self.cst, rhsb[j]], [pa])
                    yield
                    for h in range(4):
                        k.op("dve", lambda e: e.scalar_tensor_tensor(out=Dm[j][:, h, :], in0=pa[:, h * 128:(h + 1) * 128], scalar=ACUM[:, blk, d * 8 + g * 4 + h:d * 8 + g * 4 + h + 1],
                                                                     in1=mneg, op0=ALU.subtract, op1=ALU.add), [pa, ACUM, self.cst], [Dm[j]])
                        yield
                    k.op("act", lambda e: e.activation(out=LT[j][:, :, :], in_=Dm[j][:, :, :], func=AF.Exp), [Dm[j]], [LT[j]])
                    yield
                    k.op("act", lambda e: e.activation(out=RE[j][:, :, :].rearrange("p h c -> p (h c)"), in_=pa[:, :], func=AF.Exp), [pa], [RE[j]])
                    yield
                    psc = spre_ps()
                    k.op("pe", lambda e: e.matmul(psc[:, 0:128], lhsT=BT[:, g, blk * 128:(blk + 1) * 128], rhs=CT[:, g, blk * 128:(blk + 1) * 128], start=True, stop=True), [BT, CT], [psc])
                    yield
                    k.op("act", lambda e: e.copy(out=SCs[j][:, :], in_=psc[:, 0:128]), [psc], [SCs[j]])
                    yield
                    k.op("dve", lambda e: e.tensor_tensor(out=WT[j][:, :, :], in0=LT[j][:, :, :], in1=SCs[j][:, :].unsqueeze(1).to_broadcast([128, 4, 128]), op=ALU.mult), [LT[j], SCs[j]], [WT[j]])
                    yield
                    k.op("dve", lambda e: e.tensor_tensor(out=CdT[j][:, :, :], in0=RE[j][:, :, :], in1=CT[:, g, blk * 128:(blk + 1) * 128].unsqueeze(1).to_broadcast([128, 4, 128]), op=ALU.mult), [RE[j], CT], [CdT[j]])
                    yield
                    xv = XTOK[:, blk, g * 256:(g + 1) * 256].rearrange("p (h q) -> p h q", h=4)
                    k.op("dve", lambda e: e.tensor_tensor(out=xdt[j][:, :, :], in0=xv, in1=DT[:, blk, hs].unsqueeze(2).to_broadcast([128, 4, 64]), op=ALU.mult), [XTOK, DT], [xdt[j]])
                    yield
                    k.op("dve", lambda e: e.tensor_tensor(out=xdd[j][:, :, :], in0=xv, in1=DTDS[:, blk, hs].unsqueeze(2).to_broadcast([128, 4, 64]), op=ALU.mult), [XTOK, DTDS], [xdd[j]])
                    yield

                def seq_fn():
                    py = self.psb[6]
                    for h in range(4):
                        k.op("pe", lambda e: e.matmul(py[0:64, h * 128:(h + 1) * 128], lhsT=xdt[j][:, h, :], rhs=WT[j][:, h, :], start=True, stop=False), [xdt[j], WT[j]], [py])
                        yield
                        k.op("pe", lambda e: e.matmul(py[0:64, h * 128:(h + 1) * 128], lhsT=STb[:, ch, h, :], rhs=CdT[j][:, h, :], start=False, stop=True), [sbuf_, CdT[j]], [py])
                        yield
                    k.op("act", lambda e: e.copy(out=yo[j][:, :, :].rearrange("p h c -> p (h c)"), in_=py[0:64, :]), [py], [yo[j]])
                    yield
                    k.dma("pool", self.YS.t[d, g * 256:(g + 1) * 256, blk * 128:(blk + 1) * 128].rearrange("(h p) c -> p h c", p=64), yo[j][:, :, :], [yo[j]], [], yo[j])
                    yield
                    pst = self.psb[7]
                    for h in range(4):
                        k.op("pe", lambda e: e.matmul(pst[:, h * 64:(h + 1) * 64], lhsT=BTOK[:, blk, g, :], rhs=xdd[j][:, h, :], start=True, stop=True), [BTOK, xdd[j]], [pst])
                        yield
                    k.op("dve", lambda e: e.tensor_tensor(out=ST[:, ch, :, :], in0=ST[:, ch, :, :], in1=CD[:, blk, hs].unsqueeze(2).to_broadcast([128, 4, 64]), op=ALU.mult), [sbuf_, CD], [sbuf_])
                    yield
                    k.op("dve", lambda e: e.tensor_tensor(out=ST[:, ch, :, :], in0=ST[:, ch, :, :], in1=pst[:, 0:256].rearrange("p (h q) -> p h q", h=4), op=ALU.add), [sbuf_, pst], [sbuf_])
                    yield
                    k.op("act", lambda e: e.copy(out=STb[:, ch, :, :], in_=ST[:, ch, :, :]), [sbuf_], [sbuf_])
                    yield
                return pre_fn, seq_fn
            sul = []
            for i in range(NB):
                for d in range(2):
                    for g in range(2):
                        sul.append(make_sunit(i, d, g, len(sul) % R))
            def sadv(g_):
                try:
                    next(g_)
                    return True
                except StopIteration:
                    return False
            nU = len(sul)
            active, pre_done = {}, set()
            nxt, seq_unit, seq_gen, seq_done = 0, 0, None, -1
            while seq_done < nU - 1:
                while len(active) < 2 and nxt < nU and nxt <= seq_done + 3:
                    active[nxt] = sul[nxt][0]()
                    nxt += 1
                for u_ in sorted(active):
                    if not sadv(active[u_]):
                        del active[u_]
                        pre_done.add(u_)
                if seq_gen is None and seq_unit in pre_done:
                    seq_gen = sul[seq_unit][1]()
                if seq_gen is not None and not sadv(seq_gen):
                    seq_gen = None
                    seq_done = seq_unit
                    seq_unit += 1
            k.barrier()
        with contextlib.ExitStack() as st:
            dsk = self.sb(st, [128, 4], F32, "dsk")
            gn = self.sb(st, [128, 4], F32, "sgn")
            for c in range(4):
                for hh in range(2):
                    k.dma("sp", dsk[hh * 64:(hh + 1) * 64, c:c + 1], W["ssm_d"][l:l + 1, 2 * c + hh:2 * c + hh + 1].partition_broadcast(64), [], [dsk], dsk)
            k.dma("sp", gn[:, :], W["ssm_norm_g"][l].rearrange("(c p) -> p c", p=128), [], [gn], gn, allow_slow_non_contiguous=True)
            ya = [self.sb(st, [128, 2, 512], F32, "fya") for _ in range(2)]
            yb_ = [self.sb(st, [128, 2, 512], F32, "fyb") for _ in range(2)]
            xs_ = [self.sb(st, [128, 2, 512], F32, "fxs") for _ in range(2)]
            z_ = [self.sb(st, [128, 2, 512], F32, "fz") for _ in range(2)]
            sq_ = [self.sb(st, [128, 2, 512], F32, "fsq") for _ in range(2)]
            rs_ = [self.sb(st, [128, 512], F32, "frs") for _ in range(2)]
            ob_ = [self.sb(st, [128, 2, 512], BF16, "fob") for _ in range(2)]
            it = 0
            rz = self.urow(SZ)
            tiles = self.tiles[1:] if self.last else self.tiles
            for gi in range(2):
                for ti, (t0, n) in enumerate(tiles):
                    j = it % 2
                    it += 1
                    r0 = gi * 256
                    v = lambda ap: ap.rearrange("(c p) t -> p c t", p=128)
                    k.dma("sp", ya[j][:, :, 0:n], v(self.YS.t[0, r0:r0 + 256, t0:t0 + n]), [], [ya[j]], ya[j])
                    k.dma("sp", yb_[j][:, :, 0:n], v(self.YS.t[1, r0:r0 + 256, t0:t0 + n]), [], [yb_[j]], yb_[j])
                    k.dma("sp", xs_[j][:, :, 0:n], v(self.XS.t[r0:r0 + 256, t0:t0 + n]), [], [xs_[j]], xs_[j])
                    k.dma("sp", z_[j][:, :, 0:n], v(self.U.t[rz + r0:rz + r0 + 256, t0:t0 + n]), [], [z_[j]], z_[j])
                    k.op("dve", lambda e: e.tensor_tensor(out=ya[j][:, :, 0:n], in0=ya[j][:, :, 0:n], in1=yb_[j][:, :, 0:n], op=ALU.add), [ya[j], yb_[j]], [ya[j]])
                    k.op("act", lambda e: e.activation(out=z_[j][:, :, 0:n], in_=z_[j][:, :, 0:n], func=AF.Silu), [z_[j]], [z_[j]])
                    for cc in range(2):
                        c = gi * 2 + cc
                        k.op("dve", lambda e: e.scalar_tensor_tensor(out=ya[j][:, cc, 0:n], in0=xs_[j][:, cc, 0:n], scalar=dsk[:, c:c + 1], in1=ya[j][:, cc, 0:n], op0=ALU.mult, op1=ALU.add), [xs_[j], dsk, ya[j]], [ya[j]])
                    k.op("dve", lambda e: e.tensor_tensor(out=ya[j][:, :, 0:n], in0=ya[j][:, :, 0:n], in1=z_[j][:, :, 0:n], op=ALU.mult), [ya[j], z_[j]], [ya[j]])
                    k.op("act", lambda e: e.activation(out=sq_[j][:, :, 0:n], in_=ya[j][:, :, 0:n], func=AF.Square), [ya[j]], [sq_[j]])
                    ps = self.psum()
                    for cc in range(2):
                        k.op("pe", lambda e: e.matmul(ps[:, 0:n], lhsT=self.C_("ones"), rhs=sq_[j][:, cc, 0:n], start=(cc == 0), stop=(cc == 1)), [self.cst, sq_[j]], [ps])
                    self.rstd_from(ps, n, rs_[j], 1.0 / 256)
                    for cc in range(2):
                        c = gi * 2 + cc
                        k.op("dve", lambda e: e.scalar_tensor_tensor(out=ob_[j][:, cc, 0:n], in0=ya[j][:, cc, 0:n], scalar=gn[:, c:c + 1], in1=rs_[j][:, 0:n], op0=ALU.mult, op1=ALU.mult), [ya[j], gn, rs_[j]], [ob_[j]])
                    k.dma("pool", v(self.Y.t[1536 + r0:1536 + r0 + 256, t0:t0 + n]), ob_[j][:, :, 0:n], [ob_[j]], [], ob_[j])
            k.barrier()

    def cum_stats(self, st, G, GCUM, GTOT, nh):
        k = self.k
        NB = self.NB
        for d in range(2):
            tri = self.C_("triF" if d == 0 else "triB")
            ps = self.psum()
            k.op("pe", lambda e: e.matmul(ps[:, 0:NB * nh].rearrange("p (b h) -> p b h", h=nh), lhsT=tri, rhs=G[:, :, d * nh:(d + 1) * nh], start=True, stop=True), [self.cst, G], [ps])
            k.op("act", lambda e: e.copy(out=GCUM[:, :, d * nh:(d + 1) * nh], in_=ps[:, 0:NB * nh].rearrange("p (b h) -> p b h", h=nh)), [ps], [GCUM])
            ps2 = self.psum()
            k.op("pe", lambda e: e.matmul(ps2[:, 0:NB * nh].rearrange("p (b h) -> p b h", h=nh), lhsT=self.C_("ones"), rhs=G[:, :, d * nh:(d + 1) * nh], start=True, stop=True), [self.cst, G], [ps2])
            k.op("dve", lambda e: e.tensor_copy(out=GTOT[:, :, d * nh:(d + 1) * nh], in_=ps2[:, 0:NB * nh].rearrange("p (b h) -> p b h", h=nh)), [ps2], [GTOT])

    def phase_gdn(self, l):
        k = self.k
        W = self.W
        T, NB = self.T, self.NB
        with contextlib.ExitStack() as st:
            QT = self.sb(st, [128, 4, T], BF16, "gQT")
            KT = self.sb(st, [128, 4, T], BF16, "gKT")
            KTOK = self.sb(st, [128, NB, 4, 128], BF16, "gKTOK")
            VTOK = self.sb(st, [128, NB, 4, 128], BF16, "gVTOK")
            G = self.sb(st, [128, NB, 8], F32, "gG")
            GCUM = self.sb(st, [128, NB, 8], F32, "gGCUM")
            GTOT = self.sb(st, [128, NB, 8], F32, "gGTOT")
            EG = self.sb(st, [128, NB, 8], F32, "gEG")
            GL = self.sb(st, [128, NB, 8], F32, "gGL")
            KDS = self.sb(st, [128, NB, 8], F32, "gKDS")
            BETA = self.sb(st, [128, NB, 8], F32, "gBETA")
            NBETA = self.sb(st, [128, NB, 8], F32, "gNBETA")
            with contextlib.ExitStack() as st2:
                wc_ = [self.sb(st2, [128, 6], F32, "gwc") for _ in range(2)]
                sq_ = [self.sb(st2, [128, 512], F32, "gsq") for _ in range(2)]
                rs_ = [self.sb(st2, [128, 512], F32, "grs") for _ in range(2)]
                st2a = contextlib.ExitStack()
                u_ = [self.sb(st2a, [128, T], F32, "gu") for _ in range(1)] * 2
                acc_ = [self.sb(st2a, [128, T], F32, "gacc") for _ in range(1)] * 2
                it = 0
                for c in range(12):
                    u, acc, wc = u_[c % 2], acc_[c % 2], wc_[c % 2]
                    r0 = self.urow(c * 128)
                    kind, h = c // 4, c % 4
                    k.dma("sp", u[:, :], self.U.t[r0:r0 + 128, :], [], [u], u)
                    k.dma("sp", wc[:, 0:5], W["gdn_conv"][l][:, c * 128:(c + 1) * 128].rearrange("j c -> c j"), [], [wc], wc, allow_slow_non_contiguous=True)
                    self.conv_silu(st2, u, acc, wc, None, acc[:, :])
                    if kind < 2:
                        dstT = QT if kind == 0 else KT
                        for ti, (t0, n) in enumerate(self.tiles):
                            sq, rs = sq_[it % 2], rs_[it % 2]
                            it += 1
                            k.op("act", lambda e: e.activation(out=sq[:, 0:n], in_=acc[:, t0:t0 + n], func=AF.Square), [acc], [sq])
                            ps = self.psum()
                            k.op("pe", lambda e: e.matmul(ps[:, 0:n], lhsT=self.C_("ones"), rhs=sq[:, 0:n], start=True, stop=True), [self.cst, sq], [ps])
                            self.rstd_from(ps, n, rs, 1.0)
                            if kind == 0:
                                k.op("dve", lambda e: e.scalar_tensor_tensor(out=dstT[:, h, t0:t0 + n], in0=acc[:, t0:t0 + n], scalar=128.0 ** -0.5, in1=rs[:, 0:n], op0=ALU.mult, op1=ALU.mult), [acc, rs], [dstT])
                            else:
                                k.op("dve", lambda e: e.tensor_tensor(out=acc[:, t0:t0 + n], in0=acc[:, t0:t0 + n], in1=rs[:, 0:n], op=ALU.mult), [acc, rs], [acc])
                                k.op("act", lambda e: e.copy(out=dstT[:, h, t0:t0 + n], in_=acc[:, t0:t0 + n]), [acc], [dstT])
                    if kind >= 1:
                        dst = KTOK if kind == 1 else VTOK
                        for blk in range(NB):
                            ps = self.psum()
                            k.op("pe", lambda e: e.transpose(out=ps[:, 0:128], in_=acc[:, blk * 128:(blk + 1) * 128], identity=self.C_("ident")), [acc, self.cst], [ps])
                            k.op("act", lambda e: e.copy(out=dst[:, blk, h, :], in_=ps[:, 0:128]), [ps], [dst])
                k.barrier()
                st2a.close()
                AB = self.sb(st2, [128, NB, 16], F32, "gAB")
                self.tok_scalars(st2, GA, 16, AB)
                pb = self.sb(st2, [128, 2, 8], F32, "gpb")
                k.dma("sp", pb[:, 0, :], W["gdn_dt_bias"][l:l + 1].rearrange("o a b -> o (a b)").partition_broadcast(128), [], [pb], pb)
                k.dma("sp", pb[:, 1, :], W["gdn_a_log"][l:l + 1].rearrange("o a b -> o (a b)").partition_broadcast(128), [], [pb], pb)
                k.op("dve", lambda e: e.tensor_tensor(out=G[:, :, :], in0=AB[:, :, 0:8], in1=pb[:, 0, :].unsqueeze(1).to_broadcast([128, NB, 8]), op=ALU.add), [AB, pb], [G])
                self.softplus(st2, G, G.t[:, :, :].rearrange("p b c -> p (b c)"), NB * 8)
                k.op("act", lambda e: e.activation(out=pb[:, 1, :], in_=pb[:, 1, :], func=AF.Exp), [pb], [pb])
                k.op("dve", lambda e: e.scalar_tensor_tensor(out=G[:, :, :], in0=G[:, :, :], scalar=-1.0, in1=pb[:, 1, :].unsqueeze(1).to_broadcast([128, NB, 8]), op0=ALU.mult, op1=ALU.mult), [G, pb], [G])
                k.op("act", lambda e: e.activation(out=BETA[:, :, :], in_=AB[:, :, 8:16], func=AF.Sigmoid), [AB], [BETA])
                k.op("dve", lambda e: e.tensor_scalar(out=NBETA[:, :, :], in0=BETA[:, :, :], scalar1=-1.0, scalar2=None, op0=ALU.mult), [BETA], [NBETA])
                self.cum_stats(st2, G, GCUM, GTOT, 4)
                k.op("act", lambda e: e.activation(out=EG[:, :, :], in_=GCUM[:, :, :], func=AF.Exp), [GCUM], [EG])
                k.op("act", lambda e: e.activation(out=GL[:, :, :], in_=GTOT[:, :, :], func=AF.Exp), [GTOT], [GL])
                k.op("dve", lambda e: e.tensor_tensor(out=KDS[:, :, :], in0=GTOT[:, :, :], in1=GCUM[:, :, :], op=ALU.subtract), [GTOT, GCUM], [KDS])
                k.op("act", lambda e: e.activation(out=KDS[:, :, :], in_=KDS[:, :, :], func=AF.Exp), [KDS], [KDS])
                k.barrier()
            S_ = self.sb(st, [128, 8, 128], F32, "gS")
            Sb = self.sb(st, [128, 8, 128], BF16, "gSb")
            k.op("dve", lambda e: e.memset(S_[:, :, :], 0.0), [], [S_])
            k.op("dve", lambda e: e.memset(Sb[:, :, :], 0.0), [], [Sb])
            R = 3
            mk = lambda p, dt=F32: [self.sb(st, [128, 128], dt, p) for _ in range(R)]
            rhsb, Dm, E, RE, t1 = mk("grh"), mk("gDm"), mk("gE"), mk("gRE"), mk("gt1")
            AttnT, QgT, Kd, Xb, Rp, vnew = mk("gAt", BF16), mk("gQg", BF16), mk("gKd", BF16), mk("gXb", BF16), mk("gRp", BF16), mk("gvn", BF16)
            Pk = [mk("gP%d" % i) for i in range(1)]
            PTk = [mk("gPT%d" % i) for i in range(1)]
            XTb, CTb, Cb, Zb, Z2b = mk("gXTb", BF16), mk("gCTb", BF16), mk("gCb", BF16), mk("gZb", BF16), mk("gZ2b", BF16)
            GM = self.sb(st, [128, 14, 128], F32, "gGM")
            k.dma("sp", GM[:, :, :], self.gmask_in, [], [GM], GM)
            X = mk("gX")
            oo = mk("goo")
            Sbufs = {(h, d): Buf(None) for h in range(4) for d in range(2)}
            ident = self.C_("ident")
            orders = [self.blk_order(0), self.blk_order(1)]
            it = 0
            evi = [0]

            def evac(out_ap, ps_ap, rd, wr):
                evi[0] += 1
                if evi[0] % 2:
                    k.op("act", lambda e: e.copy(out=out_ap, in_=ps_ap), rd, wr)
                else:
                    k.op("dve", lambda e: e.tensor_copy(out=out_ap, in_=ps_ap), rd, wr)
            ppi = [0]

            def pre_ps():
                ppi[0] = (ppi[0] + 1) % 6
                return self.psb[ppi[0]]

            def make_unit(i, h, d, j):
                blk = orders[d][i]
                ci = d * 4 + h
                sb_ = Sbufs[(h, d)]
                tri = self.C_("triF" if d == 0 else "triB")
                mneg = self.C_("mnegF" if d == 0 else "mnegB")
                strict = self.C_("strF" if d == 0 else "strB")
                cs = slice(blk * 128, (blk + 1) * 128)
                sc = lambda Tn: Tn[:, blk, ci:ci + 1]

                def pre_fn():
                    k.op("dve", lambda e: e.tensor_scalar(out=rhsb[j][:, :], in0=tri, scalar1=sc(G), scalar2=None, op0=ALU.mult), [self.cst, G], [rhsb[j]])
                    yield
                    pa = pre_ps()
                    k.op("pe", lambda e: e.matmul(pa[:, 0:128], lhsT=self.C_("ones"), rhs=rhsb[j][:, :], start=True, stop=True), [self.cst, rhsb[j]], [pa])
                    yield
                    k.op("dve", lambda e: e.scalar_tensor_tensor(out=Dm[j][:, :], in0=pa[:, 0:128], scalar=sc(GCUM), in1=mneg, op0=ALU.subtract, op1=ALU.add), [pa, GCUM, self.cst], [Dm[j]])
                    yield
                    k.op("act", lambda e: e.activation(out=E[j][:, :], in_=Dm[j][:, :], func=AF.Exp), [Dm[j]], [E[j]])
                    yield
                    k.op("act", lambda e: e.activation(out=RE[j][:, :], in_=pa[:, 0:128], func=AF.Exp), [pa], [RE[j]])
                    yield
                    pA = pre_ps()
                    k.op("pe", lambda e: e.matmul(pA[:, 0:128], lhsT=KT[:, h, cs], rhs=KT[:, h, cs], start=True, stop=True), [KT], [pA])
                    yield
                    k.op("pe", lambda e: e.matmul(pA[:, 128:256], lhsT=KT[:, h, cs], rhs=QT[:, h, cs], start=True, stop=True), [KT, QT], [pA])
                    yield
                    k.op("dve", lambda e: e.scalar_tensor_tensor(out=t1[j][:, :], in0=pA[:, 0:128], scalar=sc(BETA), in1=E[j][:, :], op0=ALU.mult, op1=ALU.mult), [pA, BETA, E[j]], [t1[j]])
                    yield
                    P0, PT0 = Pk[0][j], PTk[0][j]
                    k.op("dve", lambda e: e.tensor_tensor(out=P0[:, :], in0=t1[j][:, :], in1=strict, op=ALU.mult), [t1[j], self.cst], [P0])
                    yield
                    k.op("dve", lambda e: e.tensor_tensor(out=AttnT[j][:, :], in0=pA[:, 128:256], in1=E[j][:, :], op=ALU.mult), [pA, E[j]], [AttnT[j]])
                    yield
                    k.op("dve", lambda e: e.tensor_tensor(out=QgT[j][:, :], in0=QT[:, h, cs], in1=RE[j][:, :], op=ALU.mult), [QT, RE[j]], [QgT[j]])
                    yield
                    k.op("dve", lambda e: e.tensor_scalar(out=Kd[j][:, :], in0=KTOK[:, blk, h, :], scalar1=sc(KDS), scalar2=None, op0=ALU.mult), [KTOK, KDS], [Kd[j]])
                    yield
                    pt = pre_ps()
                    k.op("pe", lambda e: e.transpose(out=pt[:, 0:128], in_=P0[:, :], identity=ident), [P0, self.cst], [pt])
                    yield
                    evac(PT0[:, :], pt[:, 0:128], [pt], [PT0])
                    yield
                    mC = lambda lv: GM[:, (0 if d == 0 else 7) + lv, :]
                    mCT = lambda lv: GM[:, (7 if d == 0 else 0) + lv, :]
                    k.op("dve", lambda e: e.tensor_tensor(out=t1[j][:, :], in0=P0[:, :], in1=mC(0), op=ALU.mult), [P0, GM], [t1[j]])
                    yield
                    k.op("dve", lambda e: e.tensor_tensor(out=Xb[j][:, :], in0=ident, in1=t1[j][:, :], op=ALU.subtract), [self.cst, t1[j]], [Xb[j]])
                    yield
                    k.op("dve", lambda e: e.tensor_tensor(out=t1[j][:, :], in0=PT0[:, :], in1=mCT(0), op=ALU.mult), [PT0, GM], [t1[j]])
                    yield
                    k.op("dve", lambda e: e.tensor_tensor(out=XTb[j][:, :], in0=ident, in1=t1[j][:, :], op=ALU.subtract), [self.cst, t1[j]], [XTb[j]])
                    yield
                    for lv in range(1, 1 if 'gdn_noneu' in self.dbg else 7):
                        lastlv = (lv == 6)
                        k.op("dve", lambda e: e.tensor_tensor(out=CTb[j][:, :], in0=PT0[:, :], in1=mCT(lv), op=ALU.mult), [PT0, GM], [CTb[j]])
                        yield
                        pz = pre_ps()
                        k.op("pe", lambda e: e.matmul(pz[:, 0:128], lhsT=CTb[j][:, :], rhs=Xb[j][:, :], start=True, stop=True), [CTb[j], Xb[j]], [pz])
                        yield
                        evac(Zb[j][:, :], pz[:, 0:128], [pz], [Zb[j]])
                        yield
                        py_ = pre_ps()
                        k.op("pe", lambda e: e.matmul(py_[:, 0:128], lhsT=XTb[j][:, :], rhs=Zb[j][:, :], start=True, stop=True), [XTb[j], Zb[j]], [py_])
                        yield
                        if not lastlv:
                            k.op("pool", lambda e: e.tensor_tensor(out=Cb[j][:, :], in0=P0[:, :], in1=mC(lv), op=ALU.mult), [P0, GM], [Cb[j]])
                            yield
                            pz2 = pre_ps()
                            k.op("pe", lambda e: e.matmul(pz2[:, 0:128], lhsT=Cb[j][:, :], rhs=XTb[j][:, :], start=True, stop=True), [Cb[j], XTb[j]], [pz2])
                            yield
                            evac(Z2b[j][:, :], pz2[:, 0:128], [pz2], [Z2b[j]])
                            yield
                            py2 = pre_ps()
                            k.op("pe", lambda e: e.matmul(py2[:, 0:128], lhsT=Xb[j][:, :], rhs=Z2b[j][:, :], start=True, stop=True), [Xb[j], Z2b[j]], [py2])
                            yield
                        k.op("dve", lambda e: e.tensor_tensor(out=Xb[j][:, :], in0=Xb[j][:, :], in1=py_[:, 0:128], op=ALU.subtract), [Xb[j], py_], [Xb[j]])
                        yield
                        if not lastlv:
                            k.op("dve", lambda e: e.tensor_tensor(out=XTb[j][:, :], in0=XTb[j][:, :], in1=py2[:, 0:128], op=ALU.subtract), [XTb[j], py2], [XTb[j]])
                            yield

                def seq_fn():
                    pk = self.psb[6]
                    k.op("pe", lambda e: e.matmul(pk[:, 0:128], lhsT=KT[:, h, cs], rhs=Sb[:, ci, :], start=True, stop=True), [KT, sb_], [pk])
                    yield
                    k.op("dve", lambda e: e.scalar_tensor_tensor(out=Rp[j][:, :], in0=pk[:, 0:128], scalar=sc(EG), in1=VTOK[:, blk, h, :], op0=ALU.mult, op1=ALU.subtract), [pk, EG, VTOK], [Rp[j]])
                    yield
                    k.op("pe", lambda e: e.matmul(pk[:, 128:256], lhsT=Xb[j][:, :], rhs=Rp[j][:, :], start=True, stop=True), [Xb[j], Rp[j]], [pk])
                    yield
                    k.op("dve", lambda e: e.tensor_scalar(out=vnew[j][:, :], in0=pk[:, 128:256], scalar1=sc(NBETA), scalar2=None, op0=ALU.mult), [pk, NBETA], [vnew[j]])
                    yield
                    po = self.psb[7]
                    k.op("pe", lambda e: e.matmul(po[:, 0:128], lhsT=Sb[:, ci, :], rhs=QgT[j][:, :], start=True, stop=False), [sb_, QgT[j]], [po])
                    yield
                    k.op("pe", lambda e: e.matmul(po[:, 0:128], lhsT=vnew[j][:, :], rhs=AttnT[j][:, :], start=False, stop=True), [vnew[j], AttnT[j]], [po])
                    yield
                    k.op("act", lambda e: e.copy(out=oo[j][:, :], in_=po[:, 0:128]), [po], [oo[j]])
                    yield
                    k.dma("pool", self.GO.t[d, h * 128:(h + 1) * 128, cs], oo[j][:, :], [oo[j]], [], oo[j])
                    yield
                    k.op("pe", lambda e: e.matmul(po[:, 128:256], lhsT=Kd[j][:, :], rhs=vnew[j][:, :], start=True, stop=True), [Kd[j], vnew[j]], [po])
                    yield
                    k.op("dve", lambda e: e.scalar_tensor_tensor(out=S_[:, ci, :], in0=S_[:, ci, :], scalar=sc(GL), in1=po[:, 128:256], op0=ALU.mult, op1=ALU.add), [sb_, GL, po], [sb_])
                    yield
                    k.op("act", lambda e: e.copy(out=Sb[:, ci, :], in_=S_[:, ci, :]), [sb_], [sb_])
                    yield
                return pre_fn, seq_fn
            ulist = []
            for i in range(0 if "gdn_noscan" in self.dbg else NB):
                for h in range(4):
                    for d in range(2):
                        ulist.append(make_unit(i, h, d, len(ulist) % R))
            def adv(g):
                try:
                    next(g)
                    return True
                except StopIteration:
                    return False
            nU = len(ulist)
            active, pre_done = {}, set()
            nxt, seq_unit, seq_gen, seq_done = 0, 0, None, -1
            while seq_done < nU - 1:
                while len(active) < 2 and nxt < nU and nxt <= seq_done + 3:
                    active[nxt] = ulist[nxt][0]()
                    nxt += 1
                for u_ in sorted(active):
                    if not adv(active[u_]):
                        del active[u_]
                        pre_done.add(u_)
                if seq_gen is None and seq_unit in pre_done:
                    seq_gen = ulist[seq_unit][1]()
                if seq_gen is not None and not adv(seq_gen):
                    seq_gen = None
                    seq_done = seq_unit
                    seq_unit += 1
            k.barrier()
        with contextlib.ExitStack() as st:
            gn = self.vec_col(st, W["gdn_norm_g"][l], 128, "ggn")
            oa = [self.sb(st, [128, 512], F32, "goa") for _ in range(2)]
            ob = [self.sb(st, [128, 512], F32, "gob") for _ in range(2)]
            z_ = [self.sb(st, [128, 512], F32, "gz") for _ in range(2)]
            sq_ = [self.sb(st, [128, 512], F32, "gfsq") for _ in range(2)]
            rs_ = [self.sb(st, [128, 512], F32, "gfrs") for _ in range(2)]
            yb = [self.sb(st, [128, 512], BF16, "gyb") for _ in range(2)]
            tiles = self.tiles[1:] if self.last else self.tiles
            it = 0
            for h in range(4):
                rz = self.urow(GZ + h * 128)
                for ti, (t0, n) in enumerate(tiles):
                    j = it % 2
                    it += 1
                    k.dma("sp", oa[j][:, 0:n], self.GO.t[0, h * 128:(h + 1) * 128, t0:t0 + n], [], [oa[j]], oa[j])
                    k.dma("sp", ob[j][:, 0:n], self.GO.t[1, h * 128:(h + 1) * 128, t0:t0 + n], [], [ob[j]], ob[j])
                    k.dma("sp", z_[j][:, 0:n], self.U.t[rz:rz + 128, t0:t0 + n], [], [z_[j]], z_[j])
                    k.op("dve", lambda e: e.tensor_tensor(out=oa[j][:, 0:n], in0=oa[j][:, 0:n], in1=ob[j][:, 0:n], op=ALU.add), [oa[j], ob[j]], [oa[j]])
                    k.op("act", lambda e: e.activation(out=sq_[j][:, 0:n], in_=oa[j][:, 0:n], func=AF.Square), [oa[j]], [sq_[j]])
                    k.op("act", lambda e: e.activation(out=z_[j][:, 0:n], in_=z_[j][:, 0:n], func=AF.Silu), [z_[j]], [z_[j]])
                    ps = self.psum()
                    k.op("pe", lambda e: e.matmul(ps[:, 0:n], lhsT=self.C_("ones"), rhs=sq_[j][:, 0:n], start=True, stop=True), [self.cst, sq_[j]], [ps])
                    self.rstd_from(ps, n, rs_[j], 1.0 / 128)
                    k.op("dve", lambda e: e.scalar_tensor_tensor(out=oa[j][:, 0:n], in0=oa[j][:, 0:n], scalar=gn[:, 0:1], in1=rs_[j][:, 0:n], op0=ALU.mult, op1=ALU.mult), [oa[j], gn, rs_[j]], [oa[j]])
                    k.op("dve", lambda e: e.tensor_tensor(out=yb[j][:, 0:n], in0=oa[j][:, 0:n], in1=z_[j][:, 0:n], op=ALU.mult), [oa[j], z_[j]], [yb[j]])
                    k.dma("pool", self.Y.t[h * 128:(h + 1) * 128, t0:t0 + n], yb[j][:, 0:n], [yb[j]], [], yb[j])
            k.barrier()


_CACHE = {}


def kernel(**inputs):
    S, C, DEPTH = 4096, 256, 2
    inputs = {k_: np.asarray(v) for k_, v in inputs.items()}
    if "nc" not in _CACHE:
        _CACHE["nc"] = Mod(S, C, DEPTH).build()
    nc = _CACHE["nc"]
    consts = consts_np(S, C)
    in_maps = []
    for b in range(8):
        m = {"x": np.ascontiguousarray(inputs["x"][b], dtype=np.float32),
             "ctx": np.ascontiguousarray(inputs["ctx"][b], dtype=np.float32),
             "cc": np.ascontiguousarray(np.stack([inputs["c"][b], inputs["c_ctx"]]), dtype=np.float32)}
        m.update(consts)
        for n, sh in WSPEC:
            m[n] = np.ascontiguousarray(inputs[n], dtype=np.float32)
        in_maps.append(m)
    res = run_bass_kernel_spmd(nc, in_maps, core_ids=list(range(8)))
    return np.stack([np.asarray(r["out"], dtype=np.float32) for r in res.results], axis=0)
```
